# Optimizing a Trainium2 kernel written in Bass

```python
import math
import jax
import jax.numpy as jnp
from jax import lax
import numpy as np

D_MODEL = 1024
BATCH = 8
SEQ = 4096
DEPTH = 2

GRID_W = 64
CTX_LEN = 256
N_MOD = 6
EPS = 1e-6
NEG_INF = -1e30
ROPE_BASE = 10000.0
BLOCK = 128
DA_HEADS = 4
DA_QK_DIM = 32
DA_V_DIM = 64
HEAD_DIM = 64
SW_HEADS = 6
SW_KV_HEADS = 2
SW_GROUP = SW_HEADS // SW_KV_HEADS
WINDOW = 128
LRU_WIDTH = 384
LRU_BLOCKS = 6
LRU_BLOCK_DIM = LRU_WIDTH // LRU_BLOCKS
LRU_CONV = 4
LRU_C = 8.0
D_MIX = DA_HEADS * DA_V_DIM + SW_HEADS * HEAD_DIM + LRU_WIDTH
IN_SIZES = (DA_HEADS * 2 * DA_QK_DIM, DA_HEADS * 2 * DA_QK_DIM, DA_HEADS * DA_V_DIM,
            SW_HEADS * HEAD_DIM, SW_KV_HEADS * HEAD_DIM, SW_KV_HEADS * HEAD_DIM,
            LRU_WIDTH, LRU_WIDTH)
D_IN = sum(IN_SIZES)
D_FF = 2816
FFN_CONV = 3

kernel_name = 'hybrid_diffusion_parallel_heads'

F32 = jnp.float32


def rms_norm(x, gain):
    xf = x.astype(F32)
    y = xf * lax.rsqrt(jnp.mean(xf * xf, axis=-1, keepdims=True) + EPS)
    return (y * gain.astype(F32)).astype(x.dtype)


def modulate(h, shift, scale):
    return h * (1 + scale) + shift


def axial_rope_tables(rows, head_dim):
    row = jnp.repeat(jnp.arange(rows, dtype=F32), GRID_W)
    col = jnp.tile(jnp.arange(GRID_W, dtype=F32), rows)
    quarter = head_dim // 4
    inv_freq = ROPE_BASE ** (-jnp.arange(quarter, dtype=F32) / quarter)
    ang = jnp.concatenate([row[:, None] * inv_freq, col[:, None] * inv_freq], axis=-1)
    return jnp.cos(ang), jnp.sin(ang)


def apply_rope(x, cos, sin):
    xf = x.astype(F32)
    half = x.shape[-1] // 2
    x1, x2 = xf[..., :half], xf[..., half:]
    shape = (1, cos.shape[0]) + (1,) * (x.ndim - 3) + (half,)
    cs, sn = cos.reshape(shape), sin.reshape(shape)
    return jnp.concatenate([x1 * cs - x2 * sn, x1 * sn + x2 * cs], axis=-1).astype(x.dtype)


def dwconv(x, w, b, left):
    k_width = w.shape[0]
    t_len = x.shape[1]
    xp = jnp.pad(x, ((0, 0), (left, k_width - 1 - left), (0, 0)))
    out = b
    for k in range(k_width):
        out = out + xp[:, k:k + t_len] * w[k]
    return out


def linear_scan(a, b, h0):
    def combine(l, r):
        return (l[0] * r[0], r[0] * l[1] + r[1])
    a_cum, b_cum = lax.associative_scan(combine, (a, b), axis=1)
    return a_cum * h0[:, None] + b_cum


def da_qkv(aq, ak, av, q_gain, k_gain, rope):
    bsz, t_len = aq.shape[:2]
    q = rms_norm(aq.reshape(bsz, t_len, DA_HEADS, 2, DA_QK_DIM), q_gain)
    k = rms_norm(ak.reshape(bsz, t_len, DA_HEADS, 2, DA_QK_DIM), k_gain)
    v = av.reshape(bsz, t_len, DA_HEADS, DA_V_DIM)
    if rope is not None:
        q = apply_rope(q, rope[0], rope[1])
        k = apply_rope(k, rope[0], rope[1])
    return q, k, v


def diff_attention(q, k, v, qc, kc, vc, lam, lambda_init, sub_gain, ctx_out):
    bsz, n_tok = q.shape[:2]
    scale = DA_QK_DIM ** -0.5

    def attend(qb, keys, vals):
        s = jnp.einsum('bqhmd,bkhmd->bhmqk', qb, keys).astype(F32) * scale
        p = jax.nn.softmax(s, axis=-1)
        w = p[:, :, 0] - lam * p[:, :, 1]
        return jnp.einsum('bhqk,bkhd->bqhd', w.astype(vals.dtype), vals)

    def finish(o):
        return (rms_norm(o, sub_gain) * (1.0 - lambda_init)).reshape(o.shape[0], o.shape[1], -1)

    k_all = jnp.concatenate([kc, k], axis=1)
    v_all = jnp.concatenate([vc, v], axis=1)
    n_blk = n_tok // BLOCK
    qb = q.reshape((bsz, n_blk, BLOCK) + q.shape[2:]).swapaxes(0, 1)
    o = lax.map(lambda blk: attend(blk, k_all, v_all), qb)
    y = finish(o.swapaxes(0, 1).reshape(bsz, n_tok, DA_HEADS, DA_V_DIM))
    yc = finish(attend(qc, kc, vc)) if ctx_out else None
    return y, yc


def sw_qkv(bq, bk, bv, q_gain, k_gain, rope):
    bsz, t_len = bq.shape[:2]
    q = rms_norm(bq.reshape(bsz, t_len, SW_HEADS, HEAD_DIM), q_gain)
    k = rms_norm(bk.reshape(bsz, t_len, SW_KV_HEADS, HEAD_DIM), k_gain)
    v = bv.reshape(bsz, t_len, SW_KV_HEADS, HEAD_DIM)
    if rope is not None:
        q = apply_rope(q, rope[0], rope[1])
        k = apply_rope(k, rope[0], rope[1])
    return q, k, v


def window_attention(q, k, v, qc, kc, vc, sink, ctx_out):
    bsz, n_tok = q.shape[:2]
    n_ctx = kc.shape[1]
    n_blk = n_tok // BLOCK
    scale = HEAD_DIM ** -0.5
    sink_l = sink.astype(F32).reshape(SW_KV_HEADS, SW_GROUP, 1, 1)

    def band(t):
        tp = jnp.pad(t, ((0, 0), (BLOCK, BLOCK), (0, 0), (0, 0)))
        tp = tp.reshape(bsz, n_blk + 2, BLOCK, SW_KV_HEADS, HEAD_DIM)
        bd = jnp.concatenate([tp[:, :-2], tp[:, 1:-1], tp[:, 2:]], axis=2)
        return bd.swapaxes(0, 1)

    kb, vb = band(k), band(v)
    qb = q.reshape(bsz, n_blk, BLOCK, SW_KV_HEADS, SW_GROUP, HEAD_DIM).swapaxes(0, 1)
    qi = jnp.arange(BLOCK)
    kj = jnp.arange(3 * BLOCK) - BLOCK

    def one_block(args):
        n, qblk, kblk, vblk = args
        qpos = n * BLOCK + qi
        kpos = n * BLOCK + kj
        valid = ((jnp.abs(qpos[:, None] - kpos[None, :]) <= WINDOW)
                 & (kpos >= 0)[None, :] & (kpos < n_tok)[None, :])
        s_lat = jnp.einsum('bqkgd,bjkd->bkgqj', qblk, kblk).astype(F32) * scale
        s_lat = jnp.where(valid, s_lat, NEG_INF)
        s_ctx = jnp.einsum('bqkgd,bckd->bkgqc', qblk, kc).astype(F32) * scale
        sink_col = jnp.broadcast_to(sink_l, s_ctx.shape[:-1] + (1,))
        p = jax.nn.softmax(jnp.concatenate([s_ctx, s_lat, sink_col], axis=-1), axis=-1)
        p_ctx = p[..., :n_ctx]
        p_lat = p[..., n_ctx:n_ctx + 3 * BLOCK]
        return (jnp.einsum('bkgqc,bckd->bqkgd', p_ctx.astype(vc.dtype), vc)
                + jnp.einsum('bkgqj,bjkd->bqkgd', p_lat.astype(vblk.dtype), vblk))

    o = lax.map(one_block, (jnp.arange(n_blk), qb, kb, vb))
    y = o.swapaxes(0, 1).reshape(bsz, n_tok, SW_HEADS * HEAD_DIM)
    yc = None
    if ctx_out:
        qcg = qc.reshape(bsz, n_ctx, SW_KV_HEADS, SW_GROUP, HEAD_DIM)
        s = jnp.einsum('bqkgd,bckd->bkgqc', qcg, kc).astype(F32) * scale
        sink_col = jnp.broadcast_to(sink_l, s.shape[:-1] + (1,))
        p = jax.nn.softmax(jnp.concatenate([s, sink_col], axis=-1), axis=-1)[..., :n_ctx]
        yc = jnp.einsum('bkgqc,bckd->bqkgd', p.astype(vc.dtype), vc).reshape(bsz, n_ctx, -1)
    return y, yc


def rglru_gates(xs, conv_w, conv_b, wa, ba, wx, bx, lam_param):
    xc = dwconv(xs, conv_w, conv_b, LRU_CONV - 1)
    xblk = xc.reshape(xc.shape[0], xc.shape[1], LRU_BLOCKS, LRU_BLOCK_DIM)
    r = jax.nn.sigmoid(jnp.einsum('bthi,hij->bthj', xblk, wa).reshape(xc.shape) + ba)
    i = jax.nn.sigmoid(jnp.einsum('bthi,hij->bthj', xblk, wx).reshape(xc.shape) + bx)
    log_a = -LRU_C * r.astype(F32) * jax.nn.softplus(-lam_param.astype(F32))
    a = jnp.exp(log_a)
    b = jnp.sqrt(-jnp.expm1(2.0 * log_a)) * (i * xc).astype(F32)
    return a, b


def rglru_mixer(cx, cg, cxc, cgc, lp, ctx_out):
    bsz = cx.shape[0]
    lat_sum = 0.0
    ctx_sum = 0.0
    for d in range(2):
        flip = (lambda t: t[:, ::-1]) if d == 1 else (lambda t: t)
        params = (lp['lru_conv_w'][d], lp['lru_conv_b'][d], lp['lru_wa'][d], lp['lru_ba'][d],
                  lp['lru_wx'][d], lp['lru_bx'][d], lp['lru_lambda'][d])
        a_c, b_c = rglru_gates(flip(cxc), *params)
        h_c = linear_scan(a_c, b_c, jnp.zeros((bsz, LRU_WIDTH), F32))
        a_l, b_l = rglru_gates(flip(cx), *params)
        h_l = linear_scan(a_l, b_l, h_c[:, -1])
        lat_sum = lat_sum + flip(h_l)
        ctx_sum = ctx_sum + flip(h_c)
    y = (lat_sum * jax.nn.gelu(cg.astype(F32), approximate=True)).astype(cx.dtype)
    yc = (ctx_sum * jax.nn.gelu(cgc.astype(F32), approximate=True)).astype(cxc.dtype) if ctx_out else None
    return y, yc


def conv_ffn(h, lp):
    up = h @ lp['w_up']
    gate, val = jnp.split(up, 2, axis=-1)
    gate = dwconv(gate, lp['ffn_conv_w'], lp['ffn_conv_b'], (FFN_CONV - 1) // 2)
    return (jax.nn.silu(gate) * val) @ lp['w_down']


def hybrid_layer(x, xc, c, c_ctx, lp, lambda_init, rope_a, rope_b, ctx_out):
    mod = jax.nn.silu(c) @ lp['w_mod'] + lp['b_mod']
    mod_c = jax.nn.silu(c_ctx) @ lp['w_mod'] + lp['b_mod']
    sh1, sc1, g1, sh2, sc2, g2 = [m[:, None] for m in jnp.split(mod, N_MOD, axis=-1)]
    sh1c, sc1c, g1c, sh2c, sc2c, g2c = jnp.split(mod_c, N_MOD, axis=-1)

    h = modulate(rms_norm(x, lp['norm1_gain']), sh1, sc1)
    hc = modulate(rms_norm(xc, lp['norm1_gain']), sh1c, sc1c)
    split_pts = np.cumsum(IN_SIZES)[:-1].tolist()
    aq, ak, av, bq, bk, bv, cx, cg = jnp.split(h @ lp['w_in'], split_pts, axis=-1)
    aqc, akc, avc, bqc, bkc, bvc, cxc, cgc = jnp.split(hc @ lp['w_in'], split_pts, axis=-1)

    lam = (jnp.exp(jnp.sum(lp['da_lam_q1'].astype(F32) * lp['da_lam_k1'].astype(F32)))
           - jnp.exp(jnp.sum(lp['da_lam_q2'].astype(F32) * lp['da_lam_k2'].astype(F32)))
           + lambda_init)
    qa, ka, va = da_qkv(aq, ak, av, lp['da_q_gain'], lp['da_k_gain'], rope_a)
    qac, kac, vac = da_qkv(aqc, akc, avc, lp['da_q_gain'], lp['da_k_gain'], None)
    ya, yac = diff_attention(qa, ka, va, qac, kac, vac, lam, lambda_init, lp['da_sub_gain'], ctx_out)

    qb, kb, vb = sw_qkv(bq, bk, bv, lp['sw_q_gain'], lp['sw_k_gain'], rope_b)
    qbc, kbc, vbc = sw_qkv(bqc, bkc, bvc, lp['sw_q_gain'], lp['sw_k_gain'], None)
    yb, ybc = window_attention(qb, kb, vb, qbc, kbc, vbc, lp['sw_sink'], ctx_out)

    yc_, ycc = rglru_mixer(cx, cg, cxc, cgc, lp, ctx_out)

    y = jnp.concatenate([ya, yb, yc_], axis=-1) @ lp['w_out']
    x = x + g1 * y
    x = x + g2 * conv_ffn(modulate(rms_norm(x, lp['norm2_gain']), sh2, sc2), lp)
    if ctx_out:
        yctx = jnp.concatenate([yac, ybc, ycc], axis=-1) @ lp['w_out']
        xc = xc + g1c * yctx
        xc = xc + g2c * conv_ffn(modulate(rms_norm(xc, lp['norm2_gain']), sh2c, sc2c), lp)
    return x, xc


def setup_inputs(seed: int = 0) -> dict:
    key = jax.random.key(seed)
    ks = iter(jax.random.split(key, 40))
    D = D_MODEL

    def nrm(shape, scale):
        return jax.random.normal(next(ks), shape, F32) * scale

    def gain(shape):
        return 1.0 + nrm(shape, 0.02)

    x = nrm((BATCH, SEQ, D), 1.0)
    c = nrm((BATCH, D), 1.0)
    ctx = nrm((BATCH, CTX_LEN, D), 1.0)
    c_ctx = nrm((D,), 1.0)
    w_mod = nrm((DEPTH, D, N_MOD * D), 0.3 * D ** -0.5)
    b_mod = nrm((DEPTH, N_MOD * D), 0.02)
    norm1_gain = gain((DEPTH, D))
    norm2_gain = gain((DEPTH, D))
    w_in = nrm((DEPTH, D, D_IN), D ** -0.5)
    da_q_gain = gain((DEPTH, DA_QK_DIM))
    da_k_gain = gain((DEPTH, DA_QK_DIM))
    da_lam_q1 = nrm((DEPTH, DA_QK_DIM), 0.1)
    da_lam_k1 = nrm((DEPTH, DA_QK_DIM), 0.1)
    da_lam_q2 = nrm((DEPTH, DA_QK_DIM), 0.1)
    da_lam_k2 = nrm((DEPTH, DA_QK_DIM), 0.1)
    da_sub_gain = gain((DEPTH, DA_V_DIM))
    sw_q_gain = gain((DEPTH, HEAD_DIM))
    sw_k_gain = gain((DEPTH, HEAD_DIM))
    sw_sink = nrm((DEPTH, SW_HEADS), 1.0)
    lru_conv_w = nrm((DEPTH, 2, LRU_CONV, LRU_WIDTH), LRU_CONV ** -0.5)
    lru_conv_b = nrm((DEPTH, 2, LRU_WIDTH), 0.02)
    lru_wa = nrm((DEPTH, 2, LRU_BLOCKS, LRU_BLOCK_DIM, LRU_BLOCK_DIM), LRU_BLOCK_DIM ** -0.5)
    lru_ba = nrm((DEPTH, 2, LRU_WIDTH), 0.02)
    lru_wx = nrm((DEPTH, 2, LRU_BLOCKS, LRU_BLOCK_DIM, LRU_BLOCK_DIM), LRU_BLOCK_DIM ** -0.5)
    lru_bx = nrm((DEPTH, 2, LRU_WIDTH), 0.02)
    a_min, a_max = 0.9 ** (1.0 / LRU_C), 0.999 ** (1.0 / LRU_C)
    a = jax.random.uniform(next(ks), (DEPTH, 2, LRU_WIDTH), F32, a_min, a_max)
    lru_lambda = jnp.log(a) - jnp.log1p(-a)
    w_out = nrm((DEPTH, D_MIX, D), D_MIX ** -0.5)
    w_up = nrm((DEPTH, D, 2 * D_FF), D ** -0.5)
    ffn_conv_w = nrm((DEPTH, FFN_CONV, D_FF), FFN_CONV ** -0.5)
    ffn_conv_b = nrm((DEPTH, D_FF), 0.02)
    w_down = nrm((DEPTH, D_FF, D), D_FF ** -0.5)
    return {'x': x, 'c': c, 'ctx': ctx, 'c_ctx': c_ctx, 'w_mod': w_mod, 'b_mod': b_mod,
            'norm1_gain': norm1_gain, 'norm2_gain': norm2_gain, 'w_in': w_in,
            'da_q_gain': da_q_gain, 'da_k_gain': da_k_gain, 'da_lam_q1': da_lam_q1,
            'da_lam_k1': da_lam_k1, 'da_lam_q2': da_lam_q2, 'da_lam_k2': da_lam_k2,
            'da_sub_gain': da_sub_gain, 'sw_q_gain': sw_q_gain, 'sw_k_gain': sw_k_gain,
            'sw_sink': sw_sink, 'lru_conv_w': lru_conv_w, 'lru_conv_b': lru_conv_b,
            'lru_wa': lru_wa, 'lru_ba': lru_ba, 'lru_wx': lru_wx, 'lru_bx': lru_bx,
            'lru_lambda': lru_lambda, 'w_out': w_out, 'w_up': w_up, 'ffn_conv_w': ffn_conv_w,
            'ffn_conv_b': ffn_conv_b, 'w_down': w_down}


def reference(x, c, ctx, c_ctx, w_mod, b_mod, norm1_gain, norm2_gain, w_in, da_q_gain, da_k_gain,
              da_lam_q1, da_lam_k1, da_lam_q2, da_lam_k2, da_sub_gain, sw_q_gain, sw_k_gain,
              sw_sink, lru_conv_w, lru_conv_b, lru_wa, lru_ba, lru_wx, lru_bx, lru_lambda,
              w_out, w_up, ffn_conv_w, ffn_conv_b, w_down):
    n_tok = x.shape[1]
    rows = n_tok // GRID_W
    rope_a = axial_rope_tables(rows, DA_QK_DIM)
    rope_b = axial_rope_tables(rows, HEAD_DIM)
    xc = ctx
    for i in range(DEPTH):
        lp = dict(w_mod=w_mod[i], b_mod=b_mod[i], norm1_gain=norm1_gain[i], norm2_gain=norm2_gain[i],
                  w_in=w_in[i], da_q_gain=da_q_gain[i], da_k_gain=da_k_gain[i],
                  da_lam_q1=da_lam_q1[i], da_lam_k1=da_lam_k1[i], da_lam_q2=da_lam_q2[i],
                  da_lam_k2=da_lam_k2[i], da_sub_gain=da_sub_gain[i], sw_q_gain=sw_q_gain[i],
                  sw_k_gain=sw_k_gain[i], sw_sink=sw_sink[i], lru_conv_w=lru_conv_w[i],
                  lru_conv_b=lru_conv_b[i], lru_wa=lru_wa[i], lru_ba=lru_ba[i], lru_wx=lru_wx[i],
                  lru_bx=lru_bx[i], lru_lambda=lru_lambda[i], w_out=w_out[i], w_up=w_up[i],
                  ffn_conv_w=ffn_conv_w[i], ffn_conv_b=ffn_conv_b[i], w_down=w_down[i])
        lambda_init = 0.8 - 0.6 * math.exp(-0.3 * i)
        x, xc = hybrid_layer(x, xc, c, c_ctx, lp, lambda_init, rope_a, rope_b, i < DEPTH - 1)
    return x
```

```python
import math
import numpy as np
import concourse.bass as bass
import concourse.mybir as mybir
from concourse.bass_utils import run_bass_kernel_spmd

F32 = mybir.dt.float32
BF16 = mybir.dt.bfloat16
U8 = mybir.dt.uint8
AF = mybir.ActivationFunctionType
ALU = mybir.AluOpType

ENGS = ("pe", "act", "dve", "pool", "sp")
NDMASEM = 40

D = 1024
NCTX = 256
NLAT = 4096
T = NCTX + NLAT
NS = 163
EPS = 1e-6
ARENA = 190 * 1024
SCALE_A = 32 ** -0.5
SCALE_B = 64 ** -0.5


class Buf:
    __slots__ = ("name", "w", "r")

    def __init__(self, name):
        self.name = name
        self.w = []
        self.r = []


class Sched:
    def __init__(self):
        self.q = {e: [] for e in ENGS}
        self.cnt = {e: 0 for e in ENGS}
        self.seen = {e: {} for e in ENGS}
        self.dma_n = 0
        self.dma_np = 0
        self.dma_tot = [0] * NDMASEM

    def bufs(self, name, n):
        return [Buf(f"{name}{i}") for i in range(n)]

    def _deps(self, eng, reads, writes):
        need = {}
        for b in reads:
            for (k, v) in b.w:
                if need.get(k, 0) < v:
                    need[k] = v
        for b in writes:
            for (k, v) in b.w:
                if need.get(k, 0) < v:
                    need[k] = v
            for (k, v) in b.r:
                if need.get(k, 0) < v:
                    need[k] = v
        seen = self.seen[eng]
        out = []
        for k, v in need.items():
            if seen.get(k, 0) < v:
                seen[k] = v
                out.append((k, v))
        return out

    def _commit(self, token, reads, writes):
        for b in writes:
            b.w = [token]
            b.r = []
        for b in reads:
            if b not in writes:
                b.r.append(token)
                if len(b.r) > 24:
                    m = {}
                    for (k, v) in b.r:
                        if m.get(k, 0) < v:
                            m[k] = v
                    b.r = list(m.items())

    def op(self, eng, fn, reads=(), writes=()):
        waits = self._deps(eng, reads, writes)
        self.cnt[eng] += 1
        token = (eng, self.cnt[eng])
        self._commit(token, reads, writes)
        self.q[eng].append((waits, fn, token))
        return token

    def dma(self, eng, fn, reads=(), writes=()):
        if eng == "pool":
            s = NDMASEM - 8 + self.dma_np % 8
            self.dma_np += 1
        else:
            s = self.dma_n % (NDMASEM - 8)
            self.dma_n += 1
        key = ("dma", s)
        waits = self._deps(eng, reads, writes)
        prev = self.dma_tot[s]
        if prev > 0 and self.seen[eng].get(key, 0) < prev:
            self.seen[eng][key] = prev
            waits.append((key, prev))
        self.dma_tot[s] = prev + 16
        token = (key, prev + 16)
        self._commit(token, reads, writes)
        self.q[eng].append((waits, fn, token))
        return token

    def barrier(self):
        for eng in ENGS:
            waits = []
            seen = self.seen[eng]
            for k in ENGS:
                v = self.cnt[k]
                if v > 0 and seen.get(k, 0) < v:
                    seen[k] = v
                    waits.append((k, v))
            for s in range(NDMASEM):
                v = self.dma_tot[s]
                key = ("dma", s)
                if v > 0 and seen.get(key, 0) < v:
                    seen[key] = v
                    waits.append((key, v))
            self.q[eng].append((waits, None, None))

    def emit(self, nc, sems, dsems):
        def semof(k):
            return dsems[k[1]] if isinstance(k, tuple) else sems[k]

        def run(eng):
            def body(h):
                for (waits, fn, token) in self.q[eng]:
                    for (k, v) in waits:
                        h.wait_ge(semof(k), v)
                    if fn is None:
                        continue
                    ins = fn(h)
                    k, v = token
                    ins.then_inc(semof(k), 16 if isinstance(k, tuple) else 1)
            return body

        with nc.Block() as block:
            block.tensor(run("pe"))
            block.scalar(run("act"))
            block.vector(run("dve"))
            block.gpsimd(run("pool"))
            block.sync(run("sp"))


class SemCtx:
    def __init__(self, nc):
        self.nc = nc
        self.stack = []

    def __enter__(self):
        sems = {}
        for e in ENGS:
            g = self.nc.semaphore("s_" + e)
            sems[e] = g.__enter__()
            self.stack.append(g)
        dsems = []
        for i in range(NDMASEM):
            g = self.nc.semaphore(f"d{i}")
            dsems.append(g.__enter__())
            self.stack.append(g)
        return sems, dsems

    def __exit__(self, *a):
        for g in reversed(self.stack):
            g.__exit__(None, None, None)
        return False


class Rot:
    def __init__(self, items):
        self.items = items
        self.i = 0

    def next(self):
        it = self.items[self.i % len(self.items)]
        self.i += 1
        return it


def build_program(nlayers=2, dbg=False, stop_after=None):
    nc = bass.Bass("TRN2", target_bir_lowering=False)

    def din(name, shape, dt=F32):
        return nc.dram_tensor(name, shape, dt, kind="ExternalInput").ap()

    xT_in = din("xT", [D, T])
    cT_in = din("cT", [128, 8, 2])
    w_mod = din("w_mod", [2, D, 6144])
    b_mod2 = din("b_mod2", [2, 2, 6144])
    w_in = din("w_in", [2, D, 2304])
    w_out = din("w_out", [2, D, D])
    w_up = din("w_up", [2, D, 5632])
    w_down = din("w_down", [2, 2816, D])
    lru_bd = din("lru_bd", [2, 12, 128, 128])
    smalls = din("smalls", [2, 128, NS])
    lamv = din("lamv", [2, 128, 4, 32])
    cmats = din("cmats", [7, 128, 128])
    rope = din("rope", [4, 128, T])
    out = nc.dram_tensor("out", [D, NLAT], F32, kind="ExternalOutput").ap()

    skind = "ExternalOutput" if dbg else "Internal"

    def dscr(name, shape, dt):
        return nc.dram_tensor(name, shape, dt, kind=skind).ap()

    xTs = dscr("xTs", [D, T], F32)
    qk = dscr("qk", [9, 128, T], BF16)
    vtok = dscr("vtok", [T, 768], BF16)
    cxs = dscr("cxs", [3, 128, T], F32)
    cgs = dscr("cgs", [3, 128, T], BF16)
    mix = dscr("mix", [D, T], BF16)
    h2s = dscr("h2s", [D, T + 4], BF16)
    modd = dscr("modd", [128, 96], F32) if dbg else None

    S = Sched()
    groups = [(0, 256, 1)] + [(256 + 512 * i, 512, 0) for i in range(8)]

    with (
        nc.sbuf_tensor("arena", [128, ARENA], U8) as arena,
        nc.sbuf_tensor("cst", [128, 4], F32) as cst,
        nc.sbuf_tensor("cmb", [128, 4, 128], BF16) as cmb,
        nc.sbuf_tensor("msk", [128, 2, 128], BF16) as msk,
        nc.sbuf_tensor("ident", [128, 128], F32) as ident,
        nc.sbuf_tensor("onesb", [128, 128], BF16) as onesb,
        nc.sbuf_tensor("smt", [128, NS], F32) as smt,
        nc.sbuf_tensor("ctile", [128, 8, 2], F32) as ctile,
        nc.sbuf_tensor("sc", [128, 8, 2], F32) as sc,
        nc.sbuf_tensor("modT", [128, 48, 2], F32) as modT,
        nc.sbuf_tensor("A1", [128, 8, 2], F32) as A1,
        nc.sbuf_tensor("A2", [128, 8, 2], F32) as A2,
        nc.sbuf_tensor("lamt", [128, 4, 32], F32) as lamt,
        nc.sbuf_tensor("sm2", [128, 64], F32) as sm2,
        nc.psum_tensor("ps", [128, 8, 512], F32) as ps,
        SemCtx(nc) as (sems, dsems),
    ):
        off = [0]

        def areset():
            off[0] = 0

        def alloc(shape, dt):
            esz = 2 if dt == BF16 else 4
            n = int(np.prod(shape))
            nb = (n * esz + 63) // 64 * 64
            assert off[0] + nb <= ARENA, (off[0], nb)
            a = arena[:, off[0]:off[0] + n * esz].bitcast(dt)
            off[0] += nb
            if len(shape) == 2:
                a = a.rearrange("p (a b) -> p a b", a=shape[0])
            elif len(shape) == 3:
                a = a.rearrange("p (a b c) -> p a b c", a=shape[0], b=shape[1])
            return a

        nb_ = [0]

        def B(name="b"):
            nb_[0] += 1
            return Buf(f"{name}{nb_[0]}")

        psb = [B("ps") for _ in range(8)]
        b_cst, b_cmb, b_msk, b_ident, b_ones, b_smt, b_ct, b_sc, b_modT, b_A, b_lam, b_sm2 = [B("c") for _ in range(12)]

        def psflat(b0, n):
            return ps[:, b0:b0 + (n + 511) // 512, :].rearrange("p b n -> p (b n)")[:, 0:n]

        def rev(ap2d):
            (pst, pn), (fs, fn_) = ap2d.ap
            from concourse.ap import AP
            return AP(ap2d.tensor, ap2d.offset + (fn_ - 1) * fs, [[pst, pn], [-fs, fn_]])

        S.op("pool", lambda g: g.memset(cst[:, 0:1], EPS), writes=[b_cst])
        S.op("pool", lambda g: g.memset(cst[:, 1:2], 1.0), writes=[b_cst])
        S.op("pool", lambda g: g.memset(cst[:, 2:3], 0.0), writes=[b_cst])
        S.op("pool", lambda g: g.memset(onesb[:], 1.0), writes=[b_ones])
        S.dma("pool", lambda g: g.dma_start(out=cmb[:], in_=cmats[0:4].rearrange("c p n -> p c n")), writes=[b_cmb])
        S.dma("pool", lambda g: g.dma_start(out=msk[:], in_=cmats[5:7].rearrange("c p n -> p c n")), writes=[b_msk])
        S.dma("sp", lambda g: g.dma_start(out=ident[:], in_=cmats[4]), writes=[b_ident])
        S.dma("sp", lambda g: g.dma_start(out=ctile[:], in_=cT_in), writes=[b_ct])
        S.op("act", lambda g: g.activation(out=sc[:], in_=ctile[:], func=AF.Silu), reads=[b_ct], writes=[b_sc])

        def run_layer(l):
            last = (l == nlayers - 1)
            ctx_out = not last
            lam_init = 0.8 - 0.6 * math.exp(-0.3 * l)
            xsrc = xT_in if l == 0 else xTs
            xsrc_k = xsrc.rearrange("(k p) t -> p k t", p=128)
            xTs_k = xTs.rearrange("(k p) t -> p k t", p=128)

            sh = {}
            def _ph():
                S.barrier()
                areset()
                S.dma("sp", lambda g, l=l: g.dma_start(out=smt[:], in_=smalls[l]), writes=[b_smt])
                S.dma("sp", lambda g, l=l: g.dma_start(out=lamt[:], in_=lamv[l]), writes=[b_lam])
                bm = alloc([6144], F32)
                modrow = alloc([6144], F32)
                wms = [alloc([8, 512], F32) for _ in range(2)]
                b_bm, b_modrow = B(), B()
                b_wm = [B(), B()]
                S.dma("sp", lambda g, l=l: g.dma_start(out=bm[0:2, :], in_=b_mod2[l]), writes=[b_bm])
                wmk = w_mod[l].rearrange("(k p) n -> p k n", p=128)
                for cg in range(12):
                    wm, bw = wms[cg % 2], b_wm[cg % 2]
                    S.dma("sp", lambda g, wm=wm, cg=cg: g.dma_start(out=wm, in_=wmk[:, :, cg * 512:(cg + 1) * 512]), writes=[bw])

                    def mm(g, wm=wm, cg=cg):
                        for k in range(8):
                            ins = g.matmul(ps[0:2, cg % 2, :], lhsT=sc[:, k, :], rhs=wm[:, k, :], start=(k == 0), stop=(k == 7))
                        return ins
                    S.op("pe", mm, reads=[bw, b_sc], writes=[psb[cg % 2]])
                    S.op("dve", lambda g, cg=cg: g.tensor_tensor(out=modrow[0:2, cg * 512:(cg + 1) * 512], in0=ps[0:2, cg % 2, :],
                                                                 in1=bm[0:2, cg * 512:(cg + 1) * 512], op=ALU.add),
                         reads=[psb[cg % 2], b_bm], writes=[b_modrow])

                def tr(g):
                    for j in range(48):
                        ins = g.transpose(ps[:, 2, 2 * j:2 * j + 2], modrow[0:2, j * 128:(j + 1) * 128], ident[0:2, 0:2])
                    return ins
                S.op("pe", tr, reads=[b_modrow, b_ident], writes=[psb[2]])
                S.op("dve", lambda g: g.tensor_copy(out=modT[:].rearrange("p a b -> p (a b)"), in_=ps[:, 2, 0:96]), reads=[psb[2]], writes=[b_modT])
                for v in range(2):
                    S.op("dve", lambda g, v=v: g.scalar_tensor_tensor(out=A1[:, :, v], in0=modT[:, 8:16, v], scalar=1.0, in1=smt[:, 0:8],
                                                                      op0=ALU.add, op1=ALU.mult), reads=[b_modT, b_smt], writes=[b_A])
                    S.op("dve", lambda g, v=v: g.scalar_tensor_tensor(out=A2[:, :, v], in0=modT[:, 32:40, v], scalar=1.0, in1=smt[:, 8:16],
                                                                      op0=ALU.add, op1=ALU.mult), reads=[b_modT, b_smt], writes=[b_A])
                if dbg and l == 0:
                    S.dma("sp", lambda g: g.dma_start(out=modd, in_=modT[:].rearrange("p a b -> p (a b)")), reads=[b_modT])
                S.op("act", lambda g: g.mul(out=sm2[:, 0:1], in_=smt[:, 20:21], mul=float(1.0 - lam_init)), reads=[b_smt], writes=[b_sm2])
                S.op("act", lambda g: g.activation(out=sm2[:, 2:8], in_=smt[:, 21:27], func=AF.Exp), reads=[b_smt], writes=[b_sm2])
                S.op("dve", lambda g: g.tensor_tensor(out=lamt[:, 0, :], in0=lamt[:, 0, :], in1=lamt[:, 1, :], op=ALU.mult), reads=[b_lam], writes=[b_lam])
                S.op("dve", lambda g: g.tensor_tensor(out=lamt[:, 2, :], in0=lamt[:, 2, :], in1=lamt[:, 3, :], op=ALU.mult), reads=[b_lam], writes=[b_lam])
                S.op("dve", lambda g: g.tensor_reduce(out=sm2[:, 20:21], in_=lamt[:, 0, :], axis=mybir.AxisListType.X, op=ALU.add), reads=[b_lam], writes=[b_sm2])
                S.op("dve", lambda g: g.tensor_reduce(out=sm2[:, 21:22], in_=lamt[:, 2, :], axis=mybir.AxisListType.X, op=ALU.add), reads=[b_lam], writes=[b_sm2])
                S.op("act", lambda g: g.activation(out=sm2[:, 22:24], in_=sm2[:, 20:22], func=AF.Exp), reads=[b_sm2], writes=[b_sm2])
                S.op("dve", lambda g: g.scalar_tensor_tensor(out=sm2[:, 1:2], in0=sm2[:, 23:24], scalar=float(-lam_init), in1=sm2[:, 22:23],
                                                             op0=ALU.add, op1=ALU.subtract), reads=[b_sm2], writes=[b_sm2])
                L_ = smt[:, 69:75]
                S.op("dve", lambda g: g.tensor_scalar_mul(out=sm2[:, 24:30], in0=L_, scalar1=-1.0), reads=[b_smt], writes=[b_sm2])
                S.op("dve", lambda g: g.tensor_tensor(out=sm2[:, 48:54], in0=sm2[:, 24:30], in1=L_, op=ALU.max), reads=[b_smt, b_sm2], writes=[b_sm2])
                S.op("act", lambda g: g.activation(out=sm2[:, 30:36], in_=sm2[:, 48:54], func=AF.Exp, scale=-1.0), reads=[b_sm2], writes=[b_sm2])
                S.op("dve", lambda g: g.tensor_scalar_add(out=sm2[:, 36:42], in0=sm2[:, 30:36], scalar1=1.0), reads=[b_sm2], writes=[b_sm2])
                S.op("act", lambda g: g.activation(out=sm2[:, 42:48], in_=sm2[:, 36:42], func=AF.Ln), reads=[b_sm2], writes=[b_sm2])
                S.op("dve", lambda g: g.tensor_scalar(out=sm2[:, 36:42], in0=sm2[:, 36:42], scalar1=-1.0, scalar2=1e-30, op0=ALU.add, op1=ALU.max),
                     reads=[b_sm2], writes=[b_sm2])
                S.op("dve", lambda g: g.reciprocal(out=sm2[:, 54:60], in_=sm2[:, 36:42]), reads=[b_sm2], writes=[b_sm2])
                S.op("dve", lambda g: g.tensor_tensor(out=sm2[:, 30:36], in0=sm2[:, 30:36], in1=sm2[:, 54:60], op=ALU.mult), reads=[b_sm2], writes=[b_sm2])
                S.op("dve", lambda g: g.tensor_tensor(out=sm2[:, 30:36], in0=sm2[:, 30:36], in1=sm2[:, 42:48], op=ALU.mult), reads=[b_sm2], writes=[b_sm2])
                S.op("dve", lambda g: g.tensor_scalar_max(out=sm2[:, 24:30], in0=sm2[:, 24:30], scalar1=0.0), reads=[b_sm2], writes=[b_sm2])
                S.op("dve", lambda g: g.tensor_tensor(out=sm2[:, 24:30], in0=sm2[:, 24:30], in1=sm2[:, 30:36], op=ALU.add), reads=[b_sm2], writes=[b_sm2])
                S.op("dve", lambda g: g.tensor_scalar_mul(out=sm2[:, 8:14], in0=sm2[:, 24:30], scalar1=-8.0), reads=[b_sm2], writes=[b_sm2])
                S.op("dve", lambda g: g.tensor_scalar_mul(out=sm2[:, 14:20], in0=sm2[:, 24:30], scalar1=-16.0), reads=[b_sm2], writes=[b_sm2])
                if stop_after == ("M", l):
                    return True

                return False
            if _ph():
                return True
            def _ph():
                S.barrier()
                areset()
                win = alloc([8, 2304], BF16)
                b_win = B()
                wik = w_in[l].rearrange("(k p) n -> p k n", p=128)
                for hh in range(2):
                    S.dma("pool", lambda g, hh=hh: g.dma_start(out=win[:, :, hh * 1152:(hh + 1) * 1152], in_=wik[:, :, hh * 1152:(hh + 1) * 1152]), writes=[b_win])
                xgs = [(alloc([8, 512], F32), B()) for _ in range(2)]
                rps = [(alloc([4, 512], F32), B()) for _ in range(2)]
                hTs = [(alloc([8, 512], BF16), B()) for _ in range(2)]
                sq, b_sq = alloc([8, 512], BF16), B()
                tmp8, b_tmp8 = alloc([8, 512], F32), B()
                rstd, b_rstd = alloc([512], F32), B()
                NSET = 3
                sets = [dict(qf=alloc([2, 512], F32), sqb=alloc([2, 512], BF16), rr=alloc([2, 512], F32), qn=alloc([2, 512], F32), qo=alloc([2, 512], BF16),
                             b={n: B() for n in ("qf", "sqb", "rr", "qn", "qo")}, pb=1 + 2 * i) for i in range(NSET)]
                vos = [(alloc([6, 128], BF16), B()) for _ in range(2)]
                for (vo, bvo) in vos:
                    S.op("pool", lambda g, vo=vo: g.memset(vo[:, :, 64:128], 1.0), writes=[bvo])
                ropek = rope.rearrange("r p t -> p r t")
                from concourse.ap import AP as _AP

                def bc3(ap2, nb):
                    (pst, pn), (fs, fn_) = ap2.ap
                    return _AP(ap2.tensor, ap2.offset, [[pst, pn], [0, nb], [fs, fn_]])

                batches = [(0, 2, 16, 0), (2, 2, 17, 0), (4, 2, 18, 1), (6, 1, 18, 1), (7, 2, 19, 1)]

                def prologue(gi):
                    t0, W, v = groups[gi]
                    xg, bxg = xgs[gi % 2]
                    rp, brp = rps[gi % 2]
                    hT, bhT = hTs[gi % 2]

                    def s0():
                        S.dma("sp", lambda g: g.dma_start(out=xg[:, :, 0:W], in_=xsrc_k[:, :, t0:t0 + W]), writes=[bxg])
                        S.dma("sp", lambda g: g.dma_start(out=rp[:, :, 0:W], in_=ropek[:, :, t0:t0 + W]), writes=[brp])

                    def s1():
                        S.op("act", lambda g: g.activation(out=sq[:, :, 0:W], in_=xg[:, :, 0:W], func=AF.Square), reads=[bxg], writes=[b_sq])

                    def s2():
                        def ssmm(g):
                            for k in range(8):
                                ins = g.matmul(ps[:, 0, 0:W], lhsT=onesb[:], rhs=sq[:, k, 0:W], start=(k == 0), stop=(k == 7))
                            return ins
                        S.op("pe", ssmm, reads=[b_sq, b_ones], writes=[psb[0]])

                    def s3():
                        S.op("act", lambda g: g.activation(out=rstd[:, 0:W], in_=ps[:, 0, 0:W], func=AF.Ln, scale=1.0 / D, bias=cst[:, 0:1]),
                             reads=[psb[0], b_cst], writes=[b_rstd])
                        S.op("act", lambda g: g.activation(out=rstd[:, 0:W], in_=rstd[:, 0:W], func=AF.Exp, scale=-0.5), reads=[b_rstd], writes=[b_rstd])

                    def s4():
                        S.op("dve", lambda g: g.tensor_tensor(out=tmp8[:, :, 0:W], in0=xg[:, :, 0:W], in1=bc3(rstd[:, 0:W], 8), op=ALU.mult),
                             reads=[bxg, b_rstd], writes=[b_tmp8])

                    def s5():
                        for k in range(8):
                            S.op("act", lambda g, k=k: g.activation(
                                out=hT[:, k, 0:W], in_=tmp8[:, k, 0:W], func=AF.Identity, scale=A1[:, k, v:v + 1], bias=modT[:, k, v:v + 1]),
                                reads=[b_tmp8, b_modT, b_A], writes=[bhT])
                    return [s0, s1, s2, s3, s4, s5]

                def proj(oc0, nb, pb0, hT, W):
                    def f(g):
                        for i in range(nb):
                            for k in range(8):
                                ins = g.matmul(ps[:, pb0 + i, 0:W], lhsT=win[:, k, (oc0 + i) * 128:(oc0 + i + 1) * 128], rhs=hT[:, k, 0:W], start=(k == 0), stop=(k == 7))
                        return ins
                    return f

                def qkbatch(gi, bi, st):
                    t0, W, v = groups[gi]
                    rp, brp = rps[gi % 2]
                    hT, bhT = hTs[gi % 2]
                    oc0, nb, gcol, mi = batches[bi]
                    bb = st["b"]
                    pb0 = st["pb"]
                    pbs = psb[pb0:pb0 + nb]
                    inv = 1.0 / 32 if mi == 0 else 1.0 / 64
                    rc, rsn = (0, 1) if mi == 0 else (2, 3)
                    qf, sqb, rr, qn, qo = (st[n][:, 0:nb, 0:W] for n in ("qf", "sqb", "rr", "qn", "qo"))
                    pv = ps[:, pb0:pb0 + nb, 0:W]

                    def s0():
                        S.op("pe", proj(oc0, nb, pb0, hT, W), reads=[b_win, bhT], writes=pbs)

                    def s1():
                        S.op("act", lambda g: g.activation(out=qf, in_=pv, func=AF.Identity), reads=pbs, writes=[bb["qf"]])
                        S.op("act", lambda g: g.activation(out=sqb, in_=pv, func=AF.Square), reads=pbs, writes=[bb["sqb"]])

                    def s2():
                        def smm(g):
                            for i in range(nb):
                                ins = g.matmul(ps[:, pb0 + i, 0:W], lhsT=cmb[:, mi, :], rhs=st["sqb"][:, i, 0:W], start=True, stop=True)
                            return ins
                        S.op("pe", smm, reads=[bb["sqb"], b_cmb], writes=pbs)

                    def s3():
                        S.op("act", lambda g: g.activation(out=rr, in_=pv, func=AF.Ln, scale=inv, bias=cst[:, 0:1]), reads=pbs + [b_cst], writes=[bb["rr"]])
                        S.op("act", lambda g: g.activation(out=rr, in_=rr, func=AF.Exp, scale=-0.5), reads=[bb["rr"]], writes=[bb["rr"]])

                    def s4():
                        S.op("dve", lambda g: g.scalar_tensor_tensor(out=qn, in0=qf, scalar=smt[:, gcol:gcol + 1], in1=rr, op0=ALU.mult, op1=ALU.mult),
                             reads=[bb["qf"], bb["rr"], b_smt], writes=[bb["qn"]])

                    def s5():
                        S.op("dve", lambda g: g.tensor_copy(out=sqb, in_=qn), reads=[bb["qn"]], writes=[bb["sqb"]])

                    def s6():
                        def rmm(g):
                            for i in range(nb):
                                ins = g.matmul(ps[:, pb0 + i, 0:W], lhsT=cmb[:, 2 + mi, :], rhs=st["sqb"][:, i, 0:W], start=True, stop=True)
                            return ins
                        S.op("pe", rmm, reads=[bb["sqb"], b_cmb], writes=pbs)
                        S.op("pool", lambda g: g.tensor_tensor(out=qf, in0=qn, in1=bc3(rp[:, rc, 0:W], nb), op=ALU.mult), reads=[bb["qn"], brp], writes=[bb["qf"]])

                    def s7():
                        S.op("dve", lambda g: g.tensor_tensor(out=rr, in0=pv, in1=bc3(rp[:, rsn, 0:W], nb), op=ALU.mult), reads=pbs + [brp], writes=[bb["rr"]])

                    def s8():
                        S.op("pool", lambda g: g.tensor_tensor(out=qo, in0=qf, in1=rr, op=ALU.add), reads=[bb["qf"], bb["rr"]], writes=[bb["qo"]])
                        S.dma("sp", lambda g: g.dma_start(out=qk[oc0:oc0 + nb].rearrange("c p t -> p c t")[:, :, t0:t0 + W], in_=qo), reads=[bb["qo"]])
                    return [s0, s1, s2, s3, s4, s5, s6, s7, s8]

                def cbatch(gi, which, st, c0, nb):
                    t0, W, v = groups[gi]
                    hT, bhT = hTs[gi % 2]
                    bb = st["b"]
                    pb0 = st["pb"]
                    pbs = psb[pb0:pb0 + nb]
                    pv = ps[:, pb0:pb0 + nb, 0:W]

                    def s0():
                        S.op("pe", proj((9 if which == 0 else 12) + c0, nb, pb0, hT, W), reads=[b_win, bhT], writes=pbs)

                    def s1():
                        if which == 0:
                            S.op("act", lambda g: g.activation(out=st["qn"][:, 0:nb, 0:W], in_=pv, func=AF.Identity), reads=pbs, writes=[bb["qn"]])
                            S.dma("sp", lambda g: g.dma_start(out=cxs[c0:c0 + nb].rearrange("c p t -> p c t")[:, :, t0:t0 + W], in_=st["qn"][:, 0:nb, 0:W]), reads=[bb["qn"]])
                        else:
                            S.op("act", lambda g: g.activation(out=st["qo"][:, 0:nb, 0:W], in_=pv, func=AF.Gelu_apprx_tanh), reads=pbs, writes=[bb["qo"]])
                            S.dma("sp", lambda g: g.dma_start(out=cgs[c0:c0 + nb].rearrange("c p t -> p c t")[:, :, t0:t0 + W], in_=st["qo"][:, 0:nb, 0:W]), reads=[bb["qo"]])
                    return [s0, s1]

                def vitem(gi):
                    t0, W, v = groups[gi]
                    hT, bhT = hTs[gi % 2]

                    def mk(tt):
                        def s():
                            def vmm(g):
                                for k in range(8):
                                    ins = g.matmul(ps[:, 7, 0:384], lhsT=hT[:, k, tt * 128:(tt + 1) * 128], rhs=win[:, k, 1920:2304], start=(k == 0), stop=(k == 7))
                                return ins
                            S.op("pe", vmm, reads=[b_win, bhT], writes=[psb[7]])
                            vo, bvo = vos[tt % 2]
                            S.op("dve", lambda g: g.tensor_copy(out=vo[:, :, 0:64], in_=ps[:, 7, 0:384].rearrange("p (h d) -> p h d", h=6)), reads=[psb[7]], writes=[bvo])
                            S.dma("sp", lambda g: g.dma_start(out=vtok[t0 + tt * 128:t0 + (tt + 1) * 128, :], in_=vo[:].rearrange("p h d -> p (h d)")), reads=[bvo])
                        return s
                    return [mk(tt) for tt in range(W // 128)]

                items = []
                pidx = {}
                nset = [0]

                def add(stages, res=None, deps=()):
                    items.append((stages, res, list(deps)))
                    return len(items) - 1

                pidx[0] = add(prologue(0))
                for gi in range(len(groups)):
                    for bi in range(len(batches)):
                        st = sets[nset[0] % NSET]
                        add(qkbatch(gi, bi, st), res=("set", nset[0] % NSET), deps=[pidx[gi]])
                        nset[0] += 1
                        if bi == 1 and gi + 1 < len(groups):
                            pidx[gi + 1] = add(prologue(gi + 1), res=("pro",))
                    for which in range(2):
                        for (c0, nb) in ((0, 2), (2, 1)):
                            st = sets[nset[0] % NSET]
                            add(cbatch(gi, which, st, c0, nb), res=("set", nset[0] % NSET), deps=[pidx[gi]])
                            nset[0] += 1
                    add(vitem(gi), res=("v",), deps=[pidx[gi]])
                SK = 3
                start, end_ = [], []
                resend = {}
                for i, (stages, res, deps) in enumerate(items):
                    s = 0 if i == 0 else start[i - 1] + SK
                    if res is not None and res in resend:
                        s = max(s, resend[res])
                    for dI in deps:
                        s = max(s, end_[dI])
                    start.append(s)
                    end_.append(s + len(stages))
                    if res is not None:
                        resend[res] = s + len(stages)
                tmax = max(end_)
                for t in range(tmax):
                    for i, (stages, res, deps) in enumerate(items):
                        k = t - start[i]
                        if 0 <= k < len(stages):
                            stages[k]()
                if stop_after == ("A", l):
                    return True
                return False
            if _ph():
                return True
            def _ph():
                S.barrier()
                areset()
                Vt, b_Vt = alloc([34, 768], BF16), B()
                vtk = vtok.rearrange("(kt p) c -> p kt c", p=128)
                for q4 in range(0, 34, 9):
                    q5 = min(34, q4 + 9)
                    S.dma("sp", lambda g, q4=q4, q5=q5: g.dma_start(out=Vt[:, q4:q5, :], in_=vtk[:, q4:q5, :]), writes=[b_Vt])
                b1_mark = off[0]
                sh.update(Vt=Vt, b_Vt=b_Vt, b1_mark=b1_mark)
                KT, b_KT = alloc([2, T], BF16), B()
                S.dma("sp", lambda g: g.dma_start(out=KT, in_=qk[2:4].rearrange("c p t -> p c t")), writes=[b_KT])
                QTR = Rot([(alloc([2, 512], BF16), B()) for _ in range(2)])
                ER = [Rot([(alloc([512], BF16), B()) for _ in range(2)]) for _ in range(4)]
                evR = Rot([dict(o=alloc([4, 512], F32), l=alloc([4, 512], F32), bo=B(), bl=B()) for _ in range(2)])
                finR = Rot([dict(oo=alloc([512], F32), osq=alloc([512], BF16), rr=alloc([512], F32), y=alloc([512], BF16),
                                 b={n: B() for n in ("oo", "osq", "rr", "y")}) for _ in range(4)])
                sbR = Rot([0, 1, 2])
                pending = []

                def flush():
                    while pending:
                        pending.pop(0)()
                for gi, (t0, W, v) in enumerate(groups):
                    if gi == 0 and not ctx_out:
                        continue
                    nkt = 2 if gi == 0 else 34
                    QT, bQT = QTR.next()
                    S.dma("sp", lambda g, QT=QT, t0=t0, W=W: g.dma_start(out=QT[:, :, 0:W], in_=qk[0:2].rearrange("c p t -> p c t")[:, :, t0:t0 + W]), writes=[bQT])
                    for c in range(2):
                        Ecur = [None] * 4
                        Eprev = [None] * 4
                        for kt in range(nkt + 1):
                            if kt == min(22, nkt):
                                flush()
                            if kt < nkt:
                                for j in range(4):
                                    E, bE = ER[j].next()
                                    Ecur[j] = (E, bE)
                                    sb = j
                                    S.op("pe", lambda g, j=j, c=c, kt=kt, QT=QT, W=W, sb=sb: g.matmul(
                                        ps[:, sb, 0:W], lhsT=KT[32 * j:32 * j + 32, c, kt * 128:(kt + 1) * 128], rhs=QT[32 * j:32 * j + 32, c, 0:W],
                                        start=True, stop=True, tile_position=(32 * j, 0)), reads=[b_KT, bQT], writes=[psb[sb]])
                                    S.op("act", lambda g, sb=sb, E=E, W=W: g.activation(out=E[:, 0:W], in_=ps[:, sb, 0:W], func=AF.Exp, scale=SCALE_A),
                                         reads=[psb[sb]], writes=[bE])
                            if kt >= 1:
                                for j in range(4):
                                    E, bE = Eprev[j]
                                    h = 2 * c + j // 2
                                    S.op("pe", lambda g, j=j, E=E, h=h, kt=kt, W=W, nkt=nkt: g.matmul(
                                        ps[:, 4 + j, 0:W], lhsT=Vt[:, kt - 1, h * 128:(h + 1) * 128], rhs=E[:, 0:W], start=(kt == 1), stop=(kt == nkt)),
                                        reads=[bE, b_Vt], writes=[psb[4 + j]])
                            Eprev = list(Ecur)
                        ev = evR.next()
                        S.op("dve", lambda g, ev=ev, W=W: g.tensor_copy(out=ev["o"][0:64, :, 0:W], in_=ps[0:64, 4:8, 0:W]), reads=psb[4:8], writes=[ev["bo"]])
                        S.op("dve", lambda g, ev=ev, W=W: g.tensor_copy(out=ev["l"][0:64, :, 0:W], in_=ps[64:128, 4:8, 0:W]), reads=psb[4:8], writes=[ev["bl"]])
                        S.op("dve", lambda g, ev=ev, W=W: g.reciprocal(out=ev["l"][0:64, :, 0:W], in_=ev["l"][0:64, :, 0:W]), reads=[ev["bl"]], writes=[ev["bl"]])
                        S.op("pool", lambda g, ev=ev, W=W: g.tensor_tensor(out=ev["o"][0:64, :, 0:W], in0=ev["o"][0:64, :, 0:W], in1=ev["l"][0:64, :, 0:W], op=ALU.mult),
                             reads=[ev["bo"], ev["bl"]], writes=[ev["bo"]])
                        for hh in range(2):
                            h = 2 * c + hh
                            f = finR.next()
                            fb = f["b"]
                            S.op("dve", lambda g, f=f, ev=ev, hh=hh, W=W: g.scalar_tensor_tensor(
                                out=f["oo"][0:64, 0:W], in0=ev["o"][0:64, 2 * hh + 1, 0:W], scalar=sm2[0:64, 1:2], in1=ev["o"][0:64, 2 * hh, 0:W],
                                op0=ALU.mult, op1=ALU.add), reads=[ev["bo"], b_sm2], writes=[fb["oo"]])
                            S.op("pool", lambda g, f=f, W=W: g.tensor_tensor(out=f["osq"][0:64, 0:W], in0=f["oo"][0:64, 0:W], in1=f["oo"][0:64, 0:W], op=ALU.mult),
                                 reads=[fb["oo"]], writes=[fb["osq"]])

                            def late(f=f, fb=fb, h=h, t0=t0, W=W):
                                S.op("pe", lambda g: g.matmul(ps[0:64, 0, 0:W], lhsT=onesb[0:64, 0:64], rhs=f["osq"][0:64, 0:W], start=True, stop=True),
                                     reads=[fb["osq"], b_ones], writes=[psb[0]])
                                S.op("act", lambda g: g.activation(out=f["rr"][0:64, 0:W], in_=ps[0:64, 0, 0:W], func=AF.Ln, scale=1.0 / 64, bias=cst[0:64, 0:1]),
                                     reads=[psb[0], b_cst], writes=[fb["rr"]])
                                S.op("act", lambda g: g.activation(out=f["rr"][0:64, 0:W], in_=f["rr"][0:64, 0:W], func=AF.Exp, scale=-0.5),
                                     reads=[fb["rr"]], writes=[fb["rr"]])
                                S.op("dve", lambda g: g.scalar_tensor_tensor(out=f["y"][0:64, 0:W], in0=f["oo"][0:64, 0:W], scalar=sm2[0:64, 0:1], in1=f["rr"][0:64, 0:W],
                                                                             op0=ALU.mult, op1=ALU.mult), reads=[fb["oo"], fb["rr"], b_sm2], writes=[fb["y"]])
                                S.dma("sp", lambda g: g.dma_start(out=mix[h * 64:(h + 1) * 64, t0:t0 + W], in_=f["y"][0:64, 0:W]), reads=[fb["y"]])
                            pending.append(late)
                flush()
                if stop_after == ("B1", l):
                    return True
                return False
            if _ph():
                return True
            def _ph():
                Vt, b_Vt, b1_mark = sh["Vt"], sh["b_Vt"], sh["b1_mark"]
                S.barrier()
                off[0] = b1_mark
                KB, b_KB = alloc([2, T], BF16), B()
                S.dma("sp", lambda g: g.dma_start(out=KB, in_=qk[7:9].rearrange("c p t -> p c t")), writes=[b_KB])
                QBs = [(alloc([3, 512], BF16), B()) for _ in range(2)]
                EBR = Rot([(alloc([640], BF16), B()) for _ in range(4)])
                fbR = Rot([dict(ls=alloc([512], F32), rl=alloc([512], F32), y=alloc([512], BF16), b={n: B() for n in ("ls", "rl", "y")}) for _ in range(3)])
                sbR = Rot([0, 2])
                abR = Rot([4, 5, 6, 7])
                units = []
                ng = 0
                for gi, (t0, W, v) in enumerate(groups):
                    if gi == 0 and not ctx_out:
                        continue
                    QB, bQB = QBs[ng % 2]
                    ng += 1
                    for hq in range(6):
                        ab = abR.next()
                        nqb = W // 128
                        for qb in range(nqb):
                            tt = t0 // 128 + qb
                            if gi == 0:
                                keys = [(0, None), (1, None)]
                            else:
                                n = tt - 2
                                keys = [(0, None), (1, None)]
                                if n > 0:
                                    keys.append((tt - 1, 0))
                                keys.append((tt, None))
                                if n < 31:
                                    keys.append((tt + 1, 1))
                            units.append(dict(gi=gi, t0=t0, W=W, hq=hq, qb=qb, keys=keys, ab=ab, QB=QB, bQB=bQB, first=(hq == 0 and qb == 0), lastq=(qb == nqb - 1)))

                def front(u):
                    t0, W, hq, qb, keys, QB, bQB = u["t0"], u["W"], u["hq"], u["qb"], u["keys"], u["QB"], u["bQB"]
                    c, half, kv = hq // 2, hq % 2, hq // 3
                    p0 = half * 64
                    if u["first"]:
                        S.dma("sp", lambda g: g.dma_start(out=QB[:, :, 0:W], in_=qk[4:7].rearrange("c p t -> p c t")[:, :, t0:t0 + W]), writes=[bQB])
                    nk = len(keys)
                    b0 = sbR.next()

                    def qkmm(g):
                        for i, (kt, m) in enumerate(keys):
                            ins = g.matmul(ps[:, b0 + i // 4, (i % 4) * 128:(i % 4 + 1) * 128], lhsT=KB[p0:p0 + 64, kv, kt * 128:(kt + 1) * 128],
                                           rhs=QB[p0:p0 + 64, c, qb * 128:(qb + 1) * 128], start=True, stop=True, tile_position=(p0, 0))
                        return ins
                    S.op("pe", qkmm, reads=[b_KB, bQB], writes=[psb[b0], psb[b0 + 1]])
                    E, bE = EBR.next()
                    u["E"], u["bE"] = E, bE
                    S.op("act", lambda g: g.activation(out=E[:, 0:nk * 128], in_=psflat(b0, nk * 128), func=AF.Exp, scale=SCALE_B),
                         reads=[psb[b0], psb[b0 + 1]], writes=[bE])
                    for i, (kt, m) in enumerate(keys):
                        if m is not None:
                            S.op("dve", lambda g, i=i, m=m: g.tensor_tensor(out=E[:, i * 128:(i + 1) * 128], in0=E[:, i * 128:(i + 1) * 128], in1=msk[:, m, :], op=ALU.mult),
                                 reads=[bE, b_msk], writes=[bE])

                def back(u):
                    t0, W, hq, qb, keys, ab = u["t0"], u["W"], u["hq"], u["qb"], u["keys"], u["ab"]
                    kv = hq // 3
                    E, bE = u["E"], u["bE"]

                    def pvmm(g):
                        for i, (kt, m) in enumerate(keys):
                            ins = g.matmul(ps[:, ab, qb * 128:(qb + 1) * 128], lhsT=Vt[:, kt, (4 + kv) * 128:(5 + kv) * 128], rhs=E[:, i * 128:(i + 1) * 128],
                                           start=(i == 0), stop=(i == len(keys) - 1))
                        return ins
                    S.op("pe", pvmm, reads=[bE, b_Vt], writes=[psb[ab]])
                    if u["lastq"]:
                        f = fbR.next()
                        fb = f["b"]
                        S.op("dve", lambda g: g.tensor_scalar_add(out=f["ls"][0:64, 0:W], in0=ps[64:128, ab, 0:W], scalar1=sm2[0:64, 2 + hq:3 + hq]),
                             reads=[psb[ab], b_sm2], writes=[fb["ls"]])
                        S.op("dve", lambda g: g.reciprocal(out=f["rl"][0:64, 0:W], in_=f["ls"][0:64, 0:W]), reads=[fb["ls"]], writes=[fb["rl"]])
                        S.op("dve", lambda g: g.tensor_tensor(out=f["y"][0:64, 0:W], in0=ps[0:64, ab, 0:W], in1=f["rl"][0:64, 0:W], op=ALU.mult),
                             reads=[psb[ab], fb["rl"]], writes=[fb["y"]])
                        S.dma("sp", lambda g: g.dma_start(out=mix[256 + hq * 64:256 + (hq + 1) * 64, t0:t0 + W], in_=f["y"][0:64, 0:W]), reads=[fb["y"]])

                LAG = 2
                for idx in range(len(units) + LAG):
                    if idx < len(units):
                        front(units[idx])
                    if idx >= LAG:
                        back(units[idx - LAG])
                if stop_after == ("B2", l):
                    return True
                return False
            if _ph():
                return True
            def _ph():
                S.barrier()
                areset()
                bdw, b_bdw = alloc([12, 128], BF16), B()
                S.dma("pool", lambda g, l=l: g.dma_start(out=bdw, in_=lru_bd[l].rearrange("c p n -> p c n")), writes=[b_bdw])
                xx, b_xx = alloc([T], F32), B()
                xc, b_xc = alloc([T], F32), B()
                xcb, b_xcb = alloc([T], BF16), B()
                rr_, b_r = alloc([T], F32), B()
                ii_, b_i = alloc([T], F32), B()
                aa_, b_a = alloc([T], F32), B()
                mm_, b_m = alloc([T], F32), B()
                hh_ = [alloc([T], F32), alloc([T], F32)]
                b_h = [B(), B()]
                gg_, b_g = alloc([T], BF16), B()
                yy_, b_y = alloc([T], BF16), B()
                segs = [(0, NCTX), (NCTX, T)]
                for cc in range(3):
                    S.dma("sp", lambda g, cc=cc: g.dma_start(out=xx, in_=cxs[cc]), writes=[b_xx])
                    S.dma("sp", lambda g, cc=cc: g.dma_start(out=gg_, in_=cgs[cc]), writes=[b_g])
                    for d in range(2):
                        wcol = lambda k, d=d, cc=cc: smt[:, 27 + d * 12 + k * 3 + cc:28 + d * 12 + k * 3 + cc]
                        bcol = smt[:, 51 + d * 3 + cc:52 + d * 3 + cc]
                        S.op("dve", lambda g, wcol=wcol, bcol=bcol: g.tensor_scalar(out=xc, in0=xx, scalar1=wcol(3), scalar2=bcol, op0=ALU.mult, op1=ALU.add),
                             reads=[b_xx, b_smt], writes=[b_xc])
                        for k in range(3):
                            s_ = 3 - k
                            for (a_, e_) in segs:
                                if d == 0:
                                    dst, src = (a_ + s_, e_), (a_, e_ - s_)
                                else:
                                    dst, src = (a_, e_ - s_), (a_ + s_, e_)
                                eng = "dve"
                                S.op(eng, lambda g, dst=dst, src=src, wcol=wcol, k=k: g.scalar_tensor_tensor(
                                    out=xc[:, dst[0]:dst[1]], in0=xx[:, src[0]:src[1]], scalar=wcol(k), in1=xc[:, dst[0]:dst[1]], op0=ALU.mult, op1=ALU.add),
                                    reads=[b_xx, b_xc, b_smt], writes=[b_xc])
                        S.op("pool", lambda g: g.tensor_copy(out=xcb, in_=xc), reads=[b_xc], writes=[b_xcb])
                        ia = (d * 2 + 0) * 3 + cc
                        ix = (d * 2 + 1) * 3 + cc
                        for sg0 in range(0, T, 2048):
                            sgw = min(2048, T - sg0)

                            def gmm(g, sg0=sg0, sgw=sgw, ia=ia, ix=ix):
                                for q0 in range(0, sgw, 512):
                                    w_ = min(512, sgw - q0)
                                    g.matmul(ps[:, q0 // 512, 0:w_], lhsT=bdw[:, ia, :], rhs=xcb[:, sg0 + q0:sg0 + q0 + w_], start=True, stop=True)
                                    ins = g.matmul(ps[:, 4 + q0 // 512, 0:w_], lhsT=bdw[:, ix, :], rhs=xcb[:, sg0 + q0:sg0 + q0 + w_], start=True, stop=True)
                                return ins
                            S.op("pe", gmm, reads=[b_xcb, b_bdw], writes=psb)
                            S.op("act", lambda g, sg0=sg0, sgw=sgw, d=d, cc=cc: g.activation(
                                out=rr_[:, sg0:sg0 + sgw], in_=psflat(0, sgw), func=AF.Sigmoid, bias=smt[:, 57 + d * 3 + cc:58 + d * 3 + cc]),
                                reads=psb[0:4] + [b_smt], writes=[b_r])
                            S.op("act", lambda g, sg0=sg0, sgw=sgw, d=d, cc=cc: g.activation(
                                out=ii_[:, sg0:sg0 + sgw], in_=psflat(4, sgw), func=AF.Sigmoid, bias=smt[:, 63 + d * 3 + cc:64 + d * 3 + cc]),
                                reads=psb[4:8] + [b_smt], writes=[b_i])
                        S.op("act", lambda g, d=d, cc=cc: g.activation(out=aa_, in_=rr_, func=AF.Exp, scale=sm2[:, 8 + d * 3 + cc:9 + d * 3 + cc]),
                             reads=[b_r, b_sm2], writes=[b_a])
                        S.op("act", lambda g, d=d, cc=cc: g.activation(out=mm_, in_=rr_, func=AF.Exp, scale=sm2[:, 14 + d * 3 + cc:15 + d * 3 + cc]),
                             reads=[b_r, b_sm2], writes=[b_m])
                        S.op("act", lambda g: g.activation(out=mm_, in_=mm_, func=AF.Sqrt, scale=-1.0, bias=cst[:, 1:2]), reads=[b_m, b_cst], writes=[b_m])
                        S.op("pool", lambda g: g.tensor_tensor(out=ii_, in0=ii_, in1=xc, op=ALU.mult), reads=[b_i, b_xc], writes=[b_i])
                        S.op("dve", lambda g: g.tensor_tensor(out=mm_, in0=mm_, in1=ii_, op=ALU.mult), reads=[b_m, b_i], writes=[b_m])
                        hd = hh_[d]
                        if d == 0:
                            S.op("dve", lambda g, hd=hd: g.tensor_tensor_scan(out=hd, data0=aa_, data1=mm_, initial=0.0, op0=ALU.mult, op1=ALU.add),
                                 reads=[b_a, b_m], writes=[b_h[d]])
                        else:
                            S.op("dve", lambda g, hd=hd: g.tensor_tensor_scan(out=rev(hd[:, 0:NCTX]), data0=rev(aa_[:, 0:NCTX]), data1=rev(mm_[:, 0:NCTX]),
                                                                              initial=0.0, op0=ALU.mult, op1=ALU.add), reads=[b_a, b_m], writes=[b_h[d]])
                            S.op("dve", lambda g, hd=hd: g.tensor_tensor_scan(out=rev(hd[:, NCTX:T]), data0=rev(aa_[:, NCTX:T]), data1=rev(mm_[:, NCTX:T]),
                                                                              initial=hd[:, 0:1], op0=ALU.mult, op1=ALU.add), reads=[b_a, b_m, b_h[d]], writes=[b_h[d]])
                    S.op("pool", lambda g: g.tensor_tensor(out=hh_[0], in0=hh_[0], in1=hh_[1], op=ALU.add), reads=[b_h[0], b_h[1]], writes=[b_h[0]])
                    S.op("dve", lambda g: g.tensor_tensor(out=yy_, in0=hh_[0], in1=gg_, op=ALU.mult), reads=[b_h[0], b_g], writes=[b_y])
                    S.dma("sp", lambda g, cc=cc: g.dma_start(out=mix[640 + cc * 128:640 + (cc + 1) * 128, :], in_=yy_), reads=[b_y])
                if stop_after == ("B3", l):
                    return True

                return False
            if _ph():
                return True
            def _ph():
                S.barrier()
                areset()
                WSZ = 2 * 22528 + 22528
                save = off[0]
                off[0] = ARENA - WSZ
                ws0 = dict(wg=alloc([8, 1408], BF16), wv=alloc([8, 1408], BF16), wd=alloc([11, D], BF16), b_wg=B(), b_wv=B(), b_wd=B())
                off[0] = save
                wuk = w_up[l].rearrange("(k p) n -> p k n", p=128)
                wout, b_wout = alloc([8, D], BF16), B()
                S.dma("pool", lambda g: g.dma_start(out=wout, in_=w_out[l].rearrange("(k p) n -> p k n", p=128)), writes=[b_wout])
                S.dma("pool", lambda g: g.dma_start(out=ws0["wg"], in_=wuk[:, :, 0:1408]), writes=[ws0["b_wg"]])
                S.dma("pool", lambda g: g.dma_start(out=ws0["wv"], in_=wuk[:, :, 2816:2816 + 1408]), writes=[ws0["b_wv"]])
                S.dma("pool", lambda g: g.dma_start(out=ws0["wd"], in_=w_down[l][0:1408, :].rearrange("(c p) n -> p c n", p=128)), writes=[ws0["b_wd"]])
                sh["ws0"] = ws0
                mxs = [(alloc([8, 512], BF16), B()) for _ in range(2)]
                xgs = [(alloc([8, 512], F32), B()) for _ in range(2)]
                h2s_ = [(alloc([8, 514], BF16), B()) for _ in range(2)]
                sq, b_sq = alloc([8, 512], BF16), B()
                tmp8, b_tmp8 = alloc([8, 512], F32), B()
                rstd, b_rstd = alloc([512], F32), B()
                assert off[0] <= ARENA - WSZ
                for (h2, bh2) in h2s_:
                    S.op("pool", lambda g, h2=h2: g.memset(h2, 0.0), writes=[bh2])
                mixk = mix.rearrange("(k p) t -> p k t", p=128)
                h2sk = h2s.rearrange("(k p) t -> p k t", p=128)
                pbR = Rot([1, 2, 3, 4, 5, 6])
                from concourse.ap import AP as _AP

                def bc3(ap2, nb):
                    (pst, pn), (fs, fn_) = ap2.ap
                    return _AP(ap2.tensor, ap2.offset, [[pst, pn], [0, nb], [fs, fn_]])
                glist = [(gi, g_) for gi, g_ in enumerate(groups) if not (gi == 0 and not ctx_out)]

                def front(n):
                    gi, (t0, W, v) = glist[n]
                    mx, bmx = mxs[n % 2]
                    xg, bxg = xgs[n % 2]
                    for kh in range(2):
                        S.dma("sp", lambda g, kh=kh: g.dma_start(out=mx[:, 4 * kh:4 * kh + 4, 0:W], in_=mixk[:, 4 * kh:4 * kh + 4, t0:t0 + W]), writes=[bmx])
                    S.dma("sp", lambda g: g.dma_start(out=xg[:, :, 0:W], in_=xsrc_k[:, :, t0:t0 + W]), writes=[bxg])
                    for j in range(8):
                        pb = pbR.next()

                        def omm(g, j=j, pb=pb):
                            for c in range(8):
                                ins = g.matmul(ps[:, pb, 0:W], lhsT=wout[:, c, j * 128:(j + 1) * 128], rhs=mx[:, c, 0:W], start=(c == 0), stop=(c == 7))
                            return ins
                        S.op("pe", omm, reads=[b_wout, bmx], writes=[psb[pb]])
                        S.op("dve", lambda g, j=j, pb=pb: g.scalar_tensor_tensor(
                            out=xg[:, j, 0:W], in0=ps[:, pb, 0:W], scalar=modT[:, 16 + j, v:v + 1], in1=xg[:, j, 0:W], op0=ALU.mult, op1=ALU.add),
                            reads=[psb[pb], bxg, b_modT], writes=[bxg])
                    S.dma("sp", lambda g: g.dma_start(out=xTs_k[:, :, t0:t0 + W], in_=xg[:, :, 0:W]), reads=[bxg])

                def back(n):
                    gi, (t0, W, v) = glist[n]
                    xg, bxg = xgs[n % 2]
                    h2, bh2 = h2s_[n % 2]
                    S.op("act", lambda g: g.activation(out=sq[:, :, 0:W], in_=xg[:, :, 0:W], func=AF.Square), reads=[bxg], writes=[b_sq])

                    def ssmm2(g):
                        for k in range(8):
                            ins = g.matmul(ps[:, 0, 0:W], lhsT=onesb[:], rhs=sq[:, k, 0:W], start=(k == 0), stop=(k == 7))
                        return ins
                    S.op("pe", ssmm2, reads=[b_sq, b_ones], writes=[psb[0]])
                    S.op("act", lambda g: g.activation(out=rstd[:, 0:W], in_=ps[:, 0, 0:W], func=AF.Ln, scale=1.0 / D, bias=cst[:, 0:1]),
                         reads=[psb[0], b_cst], writes=[b_rstd])
                    S.op("act", lambda g: g.activation(out=rstd[:, 0:W], in_=rstd[:, 0:W], func=AF.Exp, scale=-0.5), reads=[b_rstd], writes=[b_rstd])
                    S.op("dve", lambda g: g.tensor_tensor(out=tmp8[:, :, 0:W], in0=xg[:, :, 0:W], in1=bc3(rstd[:, 0:W], 8), op=ALU.mult),
                         reads=[bxg, b_rstd], writes=[b_tmp8])
                    for k in range(8):
                        S.op("act", lambda g, k=k: g.activation(
                            out=h2[:, k, 1:1 + W], in_=tmp8[:, k, 0:W], func=AF.Identity, scale=A2[:, k, v:v + 1], bias=modT[:, 24 + k, v:v + 1]),
                            reads=[b_tmp8, b_modT, b_A], writes=[bh2])
                    if gi == 0:
                        S.dma("sp", lambda g: g.dma_start(out=h2sk[:, :, 0:258], in_=h2[:, :, 0:258]), reads=[bh2])
                    elif gi == 1:
                        S.dma("sp", lambda g: g.dma_start(out=h2sk[:, :, 258:258 + 513], in_=h2[:, :, 0:513]), reads=[bh2])
                    elif gi == 8:
                        S.dma("sp", lambda g: g.dma_start(out=h2sk[:, :, t0 + 3:t0 + 3 + 513], in_=h2[:, :, 1:514]), reads=[bh2])
                    else:
                        S.dma("sp", lambda g: g.dma_start(out=h2sk[:, :, t0 + 3:t0 + 3 + 512], in_=h2[:, :, 1:513]), reads=[bh2])

                for n in range(len(glist) + 1):
                    if n < len(glist):
                        front(n)
                    if n >= 1:
                        back(n - 1)
                if stop_after == ("C1", l):
                    return True
                return False
            if _ph():
                return True
            def _ph():
                h2sk = h2s.rearrange("(k p) t -> p k t", p=128)
                wins = []
                if ctx_out:
                    wins.append((0, 258, 0, 256, 1))
                for i in range(9):
                    wo = min(510, NLAT - 510 * i)
                    wins.append((258 + 510 * i, wo + 2, NCTX + 510 * i, wo, 0))
                S.barrier()
                areset()
                wsets = []
                wuk = w_up[l].rearrange("(k p) n -> p k n", p=128)
                wsets.append(sh["ws0"])
                for hf_ in range(1, 2):
                    ws = dict(wg=alloc([8, 1408], BF16), wv=alloc([8, 1408], BF16), wd=alloc([11, D], BF16), b_wg=B(), b_wv=B(), b_wd=B())
                    wsets.append(ws)
                    S.dma("pool", lambda g, hf_=hf_, ws=ws: g.dma_start(out=ws["wg"], in_=wuk[:, :, hf_ * 1408:(hf_ + 1) * 1408]), writes=[ws["b_wg"]])
                    S.dma("pool", lambda g, hf_=hf_, ws=ws: g.dma_start(out=ws["wv"], in_=wuk[:, :, 2816 + hf_ * 1408:2816 + (hf_ + 1) * 1408]), writes=[ws["b_wv"]])
                    S.dma("pool", lambda g, hf_=hf_, ws=ws: g.dma_start(out=ws["wd"], in_=w_down[l][hf_ * 1408:(hf_ + 1) * 1408, :].rearrange("(c p) n -> p c n", p=128)), writes=[ws["b_wd"]])
                hwR = Rot([(alloc([8, 512], BF16), B()) for _ in range(2)])
                xgR = Rot([(alloc([8, 512], F32), B()) for _ in range(1)])
                act, b_act = alloc([11, 512], BF16), B()
                cvR = Rot([(alloc([512], F32), B()) for _ in range(2)])
                sgR = Rot([(alloc([512], F32), B()) for _ in range(2)])
                gvR = Rot([(0, 1), (2, 3)])
                dbR = Rot([4, 5, 6, 7])
                bxw = [B() for _ in wins]
                assert off[0] <= ARENA - (2 * 22528 + 22528), off[0]
                for hf in range(2):
                    ws = wsets[hf]
                    wg, wv, wd, b_wg, b_wv, b_wd = ws["wg"], ws["wv"], ws["wd"], ws["b_wg"], ws["b_wv"], ws["b_wd"]
                    for wi, (cs, Wn, tk0, Wo, v) in enumerate(wins):
                        hw, bhw = hwR.next()
                        S.dma("sp", lambda g, hw=hw, cs=cs, Wn=Wn: g.dma_start(out=hw[:, :, 0:Wn], in_=h2sk[:, :, cs:cs + Wn]), writes=[bhw])
                        xg, bxg = xgR.next()
                        S.dma("sp", lambda g, xg=xg, tk0=tk0, Wo=Wo: g.dma_start(out=xg[:, :, 0:Wo], in_=xTs_k[:, :, tk0:tk0 + Wo]), reads=[bxw[wi]], writes=[bxg])
                        for ci in range(11):
                            c = hf * 11 + ci
                            gb, vb = gvR.next()

                            def umm(g, ci=ci, gb=gb, vb=vb, hw=hw, Wn=Wn, Wo=Wo, wg=wg, wv=wv):
                                for k in range(8):
                                    g.matmul(ps[:, gb, 0:Wn], lhsT=wg[:, k, ci * 128:(ci + 1) * 128], rhs=hw[:, k, 0:Wn], start=(k == 0), stop=(k == 7))
                                for k in range(8):
                                    ins = g.matmul(ps[:, vb, 0:Wo], lhsT=wv[:, k, ci * 128:(ci + 1) * 128], rhs=hw[:, k, 1:1 + Wo], start=(k == 0), stop=(k == 7))
                                return ins
                            S.op("pe", umm, reads=[b_wg, b_wv, bhw], writes=[psb[gb], psb[vb]])
                            cv, bcv = cvR.next()
                            sg, bsg = sgR.next()
                            w0 = smt[:, 75 + c:76 + c]
                            w1 = smt[:, 75 + 22 + c:76 + 22 + c]
                            w2 = smt[:, 75 + 44 + c:76 + 44 + c]
                            bb_ = smt[:, 141 + c:142 + c]
                            S.op("act", lambda g, cv=cv, gb=gb, Wo=Wo, w1=w1, bb_=bb_: g.activation(out=cv[:, 0:Wo], in_=ps[:, gb, 1:1 + Wo], func=AF.Identity, scale=w1, bias=bb_),
                                 reads=[psb[gb], b_smt], writes=[bcv])
                            S.op("dve", lambda g, cv=cv, gb=gb, Wo=Wo, w0=w0: g.scalar_tensor_tensor(out=cv[:, 0:Wo], in0=ps[:, gb, 0:Wo], scalar=w0, in1=cv[:, 0:Wo], op0=ALU.mult, op1=ALU.add),
                                 reads=[psb[gb], bcv, b_smt], writes=[bcv])
                            S.op("dve", lambda g, cv=cv, gb=gb, Wo=Wo, w2=w2: g.scalar_tensor_tensor(out=cv[:, 0:Wo], in0=ps[:, gb, 2:2 + Wo], scalar=w2, in1=cv[:, 0:Wo], op0=ALU.mult, op1=ALU.add),
                                 reads=[psb[gb], bcv, b_smt], writes=[bcv])
                            S.op("act", lambda g, cv=cv, sg=sg, Wo=Wo: g.activation(out=sg[:, 0:Wo], in_=cv[:, 0:Wo], func=AF.Silu), reads=[bcv], writes=[bsg])
                            S.op("dve", lambda g, sg=sg, vb=vb, ci=ci, Wo=Wo: g.tensor_tensor(out=act[:, ci, 0:Wo], in0=ps[:, vb, 0:Wo], in1=sg[:, 0:Wo], op=ALU.mult),
                                 reads=[psb[vb], bsg], writes=[b_act])
                        for j in range(8):
                            db = dbR.next()

                            def dmm(g, j=j, db=db, Wo=Wo, wd=wd):
                                for ci in range(11):
                                    ins = g.matmul(ps[:, db, 0:Wo], lhsT=wd[:, ci, j * 128:(j + 1) * 128], rhs=act[:, ci, 0:Wo], start=(ci == 0), stop=(ci == 10))
                                return ins
                            S.op("pe", dmm, reads=[b_wd, b_act], writes=[psb[db]])
                            S.op("dve", lambda g, j=j, db=db, xg=xg, Wo=Wo, v=v: g.scalar_tensor_tensor(
                                out=xg[:, j, 0:Wo], in0=ps[:, db, 0:Wo], scalar=modT[:, 40 + j, v:v + 1], in1=xg[:, j, 0:Wo], op0=ALU.mult, op1=ALU.add),
                                reads=[psb[db], bxg, b_modT], writes=[bxg])
                        if last and hf == 1:
                            outk = out.rearrange("(k p) t -> p k t", p=128)
                            S.dma("sp", lambda g, xg=xg, tk0=tk0, Wo=Wo: g.dma_start(out=outk[:, :, tk0 - NCTX:tk0 - NCTX + Wo], in_=xg[:, :, 0:Wo]), reads=[bxg])
                        else:
                            S.dma("sp", lambda g, xg=xg, tk0=tk0, Wo=Wo: g.dma_start(out=xTs_k[:, :, tk0:tk0 + Wo], in_=xg[:, :, 0:Wo]), reads=[bxg], writes=[bxw[wi]])
                    if stop_after == ("C2%d" % hf, l):
                        return True
                if stop_after is not None and stop_after[1] == l:
                    return True
                return False
            if _ph():
                return True
            return False

        for l in range(nlayers):
            if run_layer(l):
                break

        S.barrier()
        S.emit(nc, sems, dsems)
    return nc


def _rope_tables():
    pos = np.arange(NLAT)
    row = (pos // 64).astype(np.float32)
    col = (pos % 64).astype(np.float32)
    tabs = []
    for hd in (32, 64):
        quarter = hd // 4
        half = hd // 2
        inv_freq = (np.float32(10000.0) ** (-np.arange(quarter, dtype=np.float32) / np.float32(quarter))).astype(np.float32)
        ang = np.concatenate([row[:, None] * inv_freq[None, :], col[:, None] * inv_freq[None, :]], axis=-1).astype(np.float32)
        cos, sin = np.cos(ang).astype(np.float32), np.sin(ang).astype(np.float32)
        p = np.arange(128)
        d = p % hd
        j = d % half
        sign = np.where(d < half, -1.0, 1.0).astype(np.float32)
        C = np.ones((128, T), np.float32)
        Sg = np.zeros((128, T), np.float32)
        C[:, NCTX:] = cos[:, j].T
        Sg[:, NCTX:] = sin[:, j].T * sign[:, None]
        tabs += [C, Sg]
    return np.stack(tabs, 0)


def _const_mats():
    p = np.arange(128)
    m = np.zeros((7, 128, 128), np.float32)
    m[0] = (p[:, None] // 32 == p[None, :] // 32)
    m[1] = (p[:, None] // 64 == p[None, :] // 64)
    for idx, hd in ((2, 32), (3, 64)):
        perm = (p // hd) * hd + ((p % hd) + hd // 2) % hd
        m[idx] = (p[:, None] == perm[None, :])
    m[4] = np.eye(128, dtype=np.float32)
    m[5] = (p[:, None] >= p[None, :])
    m[6] = (p[:, None] <= p[None, :])
    return m


def _prep_shared(inp):
    f = lambda a: np.ascontiguousarray(np.asarray(a, dtype=np.float32))
    w_in = f(inp["w_in"])
    cols = np.concatenate([np.arange(0, 256), np.arange(256, 512), np.arange(768, 1152),
                           np.arange(1152, 1216), np.arange(1152, 1216), np.arange(1216, 1280), np.arange(1216, 1280),
                           np.arange(1408, 1792), np.arange(1792, 2176), np.arange(512, 768), np.arange(1280, 1408)])
    assert cols.size == 2304
    w_in_ext = np.ascontiguousarray(w_in[:, :, cols])
    b_mod2 = np.ascontiguousarray(np.repeat(f(inp["b_mod"])[:, None, :], 2, axis=1))
    wa, wx = f(inp["lru_wa"]), f(inp["lru_wx"])
    bd = np.zeros((2, 2, 2, 3, 128, 128), np.float32)
    for cc in range(3):
        for hb in range(2):
            bd[:, :, 0, cc, hb * 64:(hb + 1) * 64, hb * 64:(hb + 1) * 64] = wa[:, :, 2 * cc + hb]
            bd[:, :, 1, cc, hb * 64:(hb + 1) * 64, hb * 64:(hb + 1) * 64] = wx[:, :, 2 * cc + hb]
    bd = np.ascontiguousarray(bd.reshape(2, 12, 128, 128))
    p = np.arange(128)
    sm = np.zeros((2, 128, NS), np.float32)
    sm[:, :, 0:8] = f(inp["norm1_gain"]).reshape(2, 8, 128).transpose(0, 2, 1)
    sm[:, :, 8:16] = f(inp["norm2_gain"]).reshape(2, 8, 128).transpose(0, 2, 1)
    sm[:, :, 16] = f(inp["da_q_gain"])[:, p % 32]
    sm[:, :, 17] = f(inp["da_k_gain"])[:, p % 32]
    sm[:, :, 18] = f(inp["sw_q_gain"])[:, p % 64]
    sm[:, :, 19] = f(inp["sw_k_gain"])[:, p % 64]
    sm[:, :, 20] = f(inp["da_sub_gain"])[:, p % 64]
    sm[:, :, 21:27] = f(inp["sw_sink"])[:, None, :]
    cw = f(inp["lru_conv_w"]).reshape(2, 2, 4, 3, 128)
    sm[:, :, 27:51] = cw.transpose(0, 4, 1, 2, 3).reshape(2, 128, 24)
    for base, name in ((51, "lru_conv_b"), (57, "lru_ba"), (63, "lru_bx"), (69, "lru_lambda")):
        sm[:, :, base:base + 6] = f(inp[name]).reshape(2, 2, 3, 128).transpose(0, 3, 1, 2).reshape(2, 128, 6)
    fw = f(inp["ffn_conv_w"]).reshape(2, 3, 22, 128)
    sm[:, :, 75:141] = fw.transpose(0, 3, 1, 2).reshape(2, 128, 66)
    sm[:, :, 141:163] = f(inp["ffn_conv_b"]).reshape(2, 22, 128).transpose(0, 2, 1)
    lam = np.stack([f(inp["da_lam_q1"]), f(inp["da_lam_k1"]), f(inp["da_lam_q2"]), f(inp["da_lam_k2"])], axis=1)
    lamv = np.ascontiguousarray(np.broadcast_to(lam[:, None], (2, 128, 4, 32)))
    return {
        "w_mod": f(inp["w_mod"]), "b_mod2": b_mod2, "w_in": w_in_ext, "w_out": f(inp["w_out"]), "w_up": f(inp["w_up"]),
        "w_down": f(inp["w_down"]), "lru_bd": bd, "smalls": np.ascontiguousarray(sm), "lamv": lamv,
        "cmats": _const_mats(), "rope": _rope_tables(),
    }


def _prep_core(inp, b):
    x = np.asarray(inp["x"], dtype=np.float32)
    ctx = np.asarray(inp["ctx"], dtype=np.float32)
    c = np.asarray(inp["c"], dtype=np.float32)
    c_ctx = np.asarray(inp["c_ctx"], dtype=np.float32)
    xT = np.ascontiguousarray(np.concatenate([ctx[b].T, x[b].T], axis=1))
    cT = np.ascontiguousarray(np.stack([c[b].reshape(8, 128).T, c_ctx.reshape(8, 128).T], axis=-1))
    return {"xT": xT, "cT": cT}


_CACHE = {}


def kernel(**inputs):
    if "nc" not in _CACHE:
        _CACHE["nc"] = build_program()
    nc = _CACHE["nc"]
    shared = _prep_shared(inputs)
    n = 8
    in_maps = []
    for b in range(n):
        m = dict(shared)
        m.update(_prep_core(inputs, b))
        in_maps.append(m)
    res = run_bass_kernel_spmd(nc, in_maps, core_ids=list(range(n)))
    outs = [np.asarray(r["out"]).T for r in res.results]
    return np.ascontiguousarray(np.stack(outs, axis=0).astype(np.float32))
```

```python
import math
import numpy as np
import concourse.bass as bass
import concourse.mybir as mybir
from concourse.bass_utils import run_bass_kernel_spmd

F32 = mybir.dt.float32
BF16 = mybir.dt.bfloat16
U8 = mybir.dt.uint8
AF = mybir.ActivationFunctionType
ALU = mybir.AluOpType

ENGS = ("pe", "act", "dve", "pool", "sp")
NDMASEM = 44

D = 1024
NCTX = 256
NLAT = 4096
T = NCTX + NLAT
NS = 163
EPS = 1e-6
ARENA = 190 * 1024
SCALE_A = 32 ** -0.5
SCALE_B = 64 ** -0.5


class Buf:
    __slots__ = ("name", "w", "r")

    def __init__(self, name):
        self.name = name
        self.w = []
        self.r = []


class Sched:
    def __init__(self):
        self.q = {e: [] for e in ENGS}
        self.cnt = {e: 0 for e in ENGS}
        self.seen = {e: {} for e in ENGS}
        self.dma_n = 0
        self.dma_np = 0
        self.dma_tot = [0] * NDMASEM

    def bufs(self, name, n):
        return [Buf(f"{name}{i}") for i in range(n)]

    def _deps(self, eng, reads, writes):
        need = {}
        for b in reads:
            for (k, v) in b.w:
                if need.get(k, 0) < v:
                    need[k] = v
        for b in writes:
            for (k, v) in b.w:
                if need.get(k, 0) < v:
                    need[k] = v
            for (k, v) in b.r:
                if need.get(k, 0) < v:
                    need[k] = v
        seen = self.seen[eng]
        out = []
        for k, v in need.items():
            if seen.get(k, 0) < v:
                seen[k] = v
                out.append((k, v))
        return out

    def _commit(self, token, reads, writes):
        for b in writes:
            b.w = [token]
            b.r = []
        for b in reads:
            if b not in writes:
                b.r.append(token)
                if len(b.r) > 24:
                    m = {}
                    for (k, v) in b.r:
                        if m.get(k, 0) < v:
                            m[k] = v
                    b.r = list(m.items())

    def op(self, eng, fn, reads=(), writes=()):
        waits = self._deps(eng, reads, writes)
        self.cnt[eng] += 1
        token = (eng, self.cnt[eng])
        self._commit(token, reads, writes)
        self.q[eng].append((waits, fn, token))
        return token

    def dma(self, eng, fn, reads=(), writes=()):
        if eng == "pool":
            s = NDMASEM - 12 + self.dma_np % 12
            self.dma_np += 1
        else:
            s = self.dma_n % (NDMASEM - 12)
            self.dma_n += 1
        key = ("dma", s)
        waits = self._deps(eng, reads, writes)
        prev = self.dma_tot[s]
        if prev > 0 and self.seen[eng].get(key, 0) < prev:
            self.seen[eng][key] = prev
            waits.append((key, prev))
        self.dma_tot[s] = prev + 16
        token = (key, prev + 16)
        self._commit(token, reads, writes)
        self.q[eng].append((waits, fn, token))
        return token

    def barrier(self):
        for eng in ENGS:
            waits = []
            seen = self.seen[eng]
            for k in ENGS:
                v = self.cnt[k]
                if v > 0 and seen.get(k, 0) < v:
                    seen[k] = v
                    waits.append((k, v))
            for s in range(NDMASEM):
                v = self.dma_tot[s]
                key = ("dma", s)
                if v > 0 and seen.get(key, 0) < v:
                    seen[key] = v
                    waits.append((key, v))
            self.q[eng].append((waits, None, None))

    def emit(self, nc, sems, dsems):
        def semof(k):
            return dsems[k[1]] if isinstance(k, tuple) else sems[k]

        def run(eng):
            def body(h):
                for (waits, fn, token) in self.q[eng]:
                    for (k, v) in waits:
                        h.wait_ge(semof(k), v)
                    if fn is None:
                        continue
                    ins = fn(h)
                    k, v = token
                    ins.then_inc(semof(k), 16 if isinstance(k, tuple) else 1)
            return body

        with nc.Block() as block:
            block.tensor(run("pe"))
            block.scalar(run("act"))
            block.vector(run("dve"))
            block.gpsimd(run("pool"))
            block.sync(run("sp"))


class SemCtx:
    def __init__(self, nc):
        self.nc = nc
        self.stack = []

    def __enter__(self):
        sems = {}
        for e in ENGS:
            g = self.nc.semaphore("s_" + e)
            sems[e] = g.__enter__()
            self.stack.append(g)
        dsems = []
        for i in range(NDMASEM):
            g = self.nc.semaphore(f"d{i}")
            dsems.append(g.__enter__())
            self.stack.append(g)
        return sems, dsems

    def __exit__(self, *a):
        for g in reversed(self.stack):
            g.__exit__(None, None, None)
        return False


class Rot:
    def __init__(self, items):
        self.items = items
        self.i = 0

    def next(self):
        it = self.items[self.i % len(self.items)]
        self.i += 1
        return it


def build_program(nlayers=2, dbg=False, stop_after=None):
    nc = bass.Bass("TRN2", target_bir_lowering=False)

    def din(name, shape, dt=F32):
        return nc.dram_tensor(name, shape, dt, kind="ExternalInput").ap()

    xT_in = din("xT", [D, T])
    cT_in = din("cT", [128, 8, 2])
    w_mod = din("w_mod", [2, D, 6144])
    b_mod2 = din("b_mod2", [2, 2, 6144])
    w_in = din("w_in", [2, D, 2304])
    w_out = din("w_out", [2, D, D])
    w_up = din("w_up", [2, D, 5632])
    w_down = din("w_down", [2, 2816, D])
    lru_bd = din("lru_bd", [2, 12, 128, 128])
    smalls = din("smalls", [2, 128, NS])
    lamv = din("lamv", [2, 128, 4, 32])
    cmats = din("cmats", [7, 128, 128])
    rope = din("rope", [4, 128, T])
    out = nc.dram_tensor("out", [D, NLAT], F32, kind="ExternalOutput").ap()

    skind = "ExternalOutput" if dbg else "Internal"

    def dscr(name, shape, dt):
        return nc.dram_tensor(name, shape, dt, kind=skind).ap()

    xTs = dscr("xTs", [D, T], F32)
    qk = dscr("qk", [9, 128, T], BF16)
    vtok = dscr("vtok", [T, 768], BF16)
    cxs = dscr("cxs", [3, 128, T], F32)
    cgs = dscr("cgs", [3, 128, T], BF16)
    mix = dscr("mix", [D, T], BF16)
    h2s = dscr("h2s", [D, T + 4], BF16)
    modd = dscr("modd", [128, 96], F32) if dbg else None

    S = Sched()
    groups = [(0, 256, 1)] + [(256 + 512 * i, 512, 0) for i in range(8)]

    with (
        nc.sbuf_tensor("arena", [128, ARENA], U8) as arena,
        nc.sbuf_tensor("cst", [128, 4], F32) as cst,
        nc.sbuf_tensor("cmb", [128, 4, 128], BF16) as cmb,
        nc.sbuf_tensor("msk", [128, 2, 128], BF16) as msk,
        nc.sbuf_tensor("ident", [128, 128], F32) as ident,
        nc.sbuf_tensor("onesb", [128, 128], BF16) as onesb,
        nc.sbuf_tensor("smt", [128, NS], F32) as smt,
        nc.sbuf_tensor("ctile", [128, 8, 2], F32) as ctile,
        nc.sbuf_tensor("sc", [128, 8, 2], F32) as sc,
        nc.sbuf_tensor("modT", [128, 48, 2], F32) as modT,
        nc.sbuf_tensor("A1", [128, 8, 2], F32) as A1,
        nc.sbuf_tensor("A2", [128, 8, 2], F32) as A2,
        nc.sbuf_tensor("lamt", [128, 4, 32], F32) as lamt,
        nc.sbuf_tensor("sm2", [128, 64], F32) as sm2,
        nc.psum_tensor("ps", [128, 8, 512], F32) as ps,
        SemCtx(nc) as (sems, dsems),
    ):
        off = [0]

        def areset():
            off[0] = 0

        def alloc(shape, dt):
            esz = 2 if dt == BF16 else 4
            n = int(np.prod(shape))
            nb = (n * esz + 63) // 64 * 64
            assert off[0] + nb <= ARENA, (off[0], nb)
            a = arena[:, off[0]:off[0] + n * esz].bitcast(dt)
            off[0] += nb
            if len(shape) == 2:
                a = a.rearrange("p (a b) -> p a b", a=shape[0])
            elif len(shape) == 3:
                a = a.rearrange("p (a b c) -> p a b c", a=shape[0], b=shape[1])
            return a

        nb_ = [0]

        def B(name="b"):
            nb_[0] += 1
            return Buf(f"{name}{nb_[0]}")

        psb = [B("ps") for _ in range(8)]
        b_cst, b_cmb, b_msk, b_ident, b_ones, b_smt, b_ct, b_sc, b_modT, b_A, b_lam, b_sm2 = [B("c") for _ in range(12)]

        def psflat(b0, n):
            return ps[:, b0:b0 + (n + 511) // 512, :].rearrange("p b n -> p (b n)")[:, 0:n]

        def rev(ap2d):
            (pst, pn), (fs, fn_) = ap2d.ap
            from concourse.ap import AP
            return AP(ap2d.tensor, ap2d.offset + (fn_ - 1) * fs, [[pst, pn], [-fs, fn_]])

        S.op("pool", lambda g: g.memset(cst[:, 0:1], EPS), writes=[b_cst])
        S.op("pool", lambda g: g.memset(cst[:, 1:2], 1.0), writes=[b_cst])
        S.op("pool", lambda g: g.memset(cst[:, 2:3], 0.0), writes=[b_cst])
        S.op("pool", lambda g: g.memset(onesb[:], 1.0), writes=[b_ones])
        S.dma("pool", lambda g: g.dma_start(out=cmb[:], in_=cmats[0:4].rearrange("c p n -> p c n")), writes=[b_cmb])
        S.dma("pool", lambda g: g.dma_start(out=msk[:], in_=cmats[5:7].rearrange("c p n -> p c n")), writes=[b_msk])
        S.dma("sp", lambda g: g.dma_start(out=ident[:], in_=cmats[4]), writes=[b_ident])
        S.dma("sp", lambda g: g.dma_start(out=ctile[:], in_=cT_in), writes=[b_ct])
        S.op("act", lambda g: g.activation(out=sc[:], in_=ctile[:], func=AF.Silu), reads=[b_ct], writes=[b_sc])

        def run_layer(l):
            last = (l == nlayers - 1)
            ctx_out = not last
            lam_init = 0.8 - 0.6 * math.exp(-0.3 * l)
            xsrc = xT_in if l == 0 else xTs
            xsrc_k = xsrc.rearrange("(k p) t -> p k t", p=128)
            xTs_k = xTs.rearrange("(k p) t -> p k t", p=128)

            sh = {}
            def _ph():
                S.barrier()
                areset()
                S.dma("sp", lambda g, l=l: g.dma_start(out=smt[:], in_=smalls[l]), writes=[b_smt])
                S.dma("sp", lambda g, l=l: g.dma_start(out=lamt[:], in_=lamv[l]), writes=[b_lam])
                bm = alloc([6144], F32)
                modrow = alloc([6144], F32)
                wms = [alloc([8, 512], F32) for _ in range(2)]
                b_bm, b_modrow = B(), B()
                b_wm = [B(), B()]
                S.dma("sp", lambda g, l=l: g.dma_start(out=bm[0:2, :], in_=b_mod2[l]), writes=[b_bm])
                wmk = w_mod[l].rearrange("(k p) n -> p k n", p=128)
                for cg in range(12):
                    wm, bw = wms[cg % 2], b_wm[cg % 2]
                    S.dma("sp", lambda g, wm=wm, cg=cg: g.dma_start(out=wm, in_=wmk[:, :, cg * 512:(cg + 1) * 512]), writes=[bw])

                    def mm(g, wm=wm, cg=cg):
                        for k in range(8):
                            ins = g.matmul(ps[0:2, cg % 2, :], lhsT=sc[:, k, :], rhs=wm[:, k, :], start=(k == 0), stop=(k == 7))
                        return ins
                    S.op("pe", mm, reads=[bw, b_sc], writes=[psb[cg % 2]])
                    S.op("dve", lambda g, cg=cg: g.tensor_tensor(out=modrow[0:2, cg * 512:(cg + 1) * 512], in0=ps[0:2, cg % 2, :],
                                                                 in1=bm[0:2, cg * 512:(cg + 1) * 512], op=ALU.add),
                         reads=[psb[cg % 2], b_bm], writes=[b_modrow])

                def tr(g):
                    for j in range(48):
                        ins = g.transpose(ps[:, 2, 2 * j:2 * j + 2], modrow[0:2, j * 128:(j + 1) * 128], ident[0:2, 0:2])
                    return ins
                S.op("pe", tr, reads=[b_modrow, b_ident], writes=[psb[2]])
                S.op("dve", lambda g: g.tensor_copy(out=modT[:].rearrange("p a b -> p (a b)"), in_=ps[:, 2, 0:96]), reads=[psb[2]], writes=[b_modT])
                for v in range(2):
                    S.op("dve", lambda g, v=v: g.scalar_tensor_tensor(out=A1[:, :, v], in0=modT[:, 8:16, v], scalar=1.0, in1=smt[:, 0:8],
                                                                      op0=ALU.add, op1=ALU.mult), reads=[b_modT, b_smt], writes=[b_A])
                    S.op("dve", lambda g, v=v: g.scalar_tensor_tensor(out=A2[:, :, v], in0=modT[:, 32:40, v], scalar=1.0, in1=smt[:, 8:16],
                                                                      op0=ALU.add, op1=ALU.mult), reads=[b_modT, b_smt], writes=[b_A])
                if dbg and l == 0:
                    S.dma("pool", lambda g: g.dma_start(out=modd, in_=modT[:].rearrange("p a b -> p (a b)")), reads=[b_modT])
                S.op("act", lambda g: g.mul(out=sm2[:, 0:1], in_=smt[:, 20:21], mul=float(1.0 - lam_init)), reads=[b_smt], writes=[b_sm2])
                S.op("act", lambda g: g.activation(out=sm2[:, 2:8], in_=smt[:, 21:27], func=AF.Exp), reads=[b_smt], writes=[b_sm2])
                S.op("dve", lambda g: g.tensor_tensor(out=lamt[:, 0, :], in0=lamt[:, 0, :], in1=lamt[:, 1, :], op=ALU.mult), reads=[b_lam], writes=[b_lam])
                S.op("dve", lambda g: g.tensor_tensor(out=lamt[:, 2, :], in0=lamt[:, 2, :], in1=lamt[:, 3, :], op=ALU.mult), reads=[b_lam], writes=[b_lam])
                S.op("dve", lambda g: g.tensor_reduce(out=sm2[:, 20:21], in_=lamt[:, 0, :], axis=mybir.AxisListType.X, op=ALU.add), reads=[b_lam], writes=[b_sm2])
                S.op("dve", lambda g: g.tensor_reduce(out=sm2[:, 21:22], in_=lamt[:, 2, :], axis=mybir.AxisListType.X, op=ALU.add), reads=[b_lam], writes=[b_sm2])
                S.op("act", lambda g: g.activation(out=sm2[:, 22:24], in_=sm2[:, 20:22], func=AF.Exp), reads=[b_sm2], writes=[b_sm2])
                S.op("dve", lambda g: g.scalar_tensor_tensor(out=sm2[:, 1:2], in0=sm2[:, 23:24], scalar=float(-lam_init), in1=sm2[:, 22:23],
                                                             op0=ALU.add, op1=ALU.subtract), reads=[b_sm2], writes=[b_sm2])
                L_ = smt[:, 69:75]
                S.op("dve", lambda g: g.tensor_scalar_mul(out=sm2[:, 24:30], in0=L_, scalar1=-1.0), reads=[b_smt], writes=[b_sm2])
                S.op("dve", lambda g: g.tensor_tensor(out=sm2[:, 48:54], in0=sm2[:, 24:30], in1=L_, op=ALU.max), reads=[b_smt, b_sm2], writes=[b_sm2])
                S.op("act", lambda g: g.activation(out=sm2[:, 30:36], in_=sm2[:, 48:54], func=AF.Exp, scale=-1.0), reads=[b_sm2], writes=[b_sm2])
                S.op("dve", lambda g: g.tensor_scalar_add(out=sm2[:, 36:42], in0=sm2[:, 30:36], scalar1=1.0), reads=[b_sm2], writes=[b_sm2])
                S.op("act", lambda g: g.activation(out=sm2[:, 42:48], in_=sm2[:, 36:42], func=AF.Ln), reads=[b_sm2], writes=[b_sm2])
                S.op("dve", lambda g: g.tensor_scalar(out=sm2[:, 36:42], in0=sm2[:, 36:42], scalar1=-1.0, scalar2=1e-30, op0=ALU.add, op1=ALU.max),
                     reads=[b_sm2], writes=[b_sm2])
                S.op("dve", lambda g: g.reciprocal(out=sm2[:, 54:60], in_=sm2[:, 36:42]), reads=[b_sm2], writes=[b_sm2])
                S.op("dve", lambda g: g.tensor_tensor(out=sm2[:, 30:36], in0=sm2[:, 30:36], in1=sm2[:, 54:60], op=ALU.mult), reads=[b_sm2], writes=[b_sm2])
                S.op("dve", lambda g: g.tensor_tensor(out=sm2[:, 30:36], in0=sm2[:, 30:36], in1=sm2[:, 42:48], op=ALU.mult), reads=[b_sm2], writes=[b_sm2])
                S.op("dve", lambda g: g.tensor_scalar_max(out=sm2[:, 24:30], in0=sm2[:, 24:30], scalar1=0.0), reads=[b_sm2], writes=[b_sm2])
                S.op("dve", lambda g: g.tensor_tensor(out=sm2[:, 24:30], in0=sm2[:, 24:30], in1=sm2[:, 30:36], op=ALU.add), reads=[b_sm2], writes=[b_sm2])
                S.op("dve", lambda g: g.tensor_scalar_mul(out=sm2[:, 8:14], in0=sm2[:, 24:30], scalar1=-8.0), reads=[b_sm2], writes=[b_sm2])
                S.op("dve", lambda g: g.tensor_scalar_mul(out=sm2[:, 14:20], in0=sm2[:, 24:30], scalar1=-16.0), reads=[b_sm2], writes=[b_sm2])
                if stop_after == ("M", l):
                    return True

                return False
            if _ph():
                return True
            def _ph():
                S.barrier()
                areset()
                win = alloc([8, 2304], BF16)
                b_win = B()
                wik = w_in[l].rearrange("(k p) n -> p k n", p=128)
                for hh in range(2):
                    S.dma("pool", lambda g, hh=hh: g.dma_start(out=win[:, :, hh * 1152:(hh + 1) * 1152], in_=wik[:, :, hh * 1152:(hh + 1) * 1152]), writes=[b_win])
                xgs = [(alloc([8, 512], F32), B()) for _ in range(2)]
                rps = [(alloc([4, 512], F32), B()) for _ in range(2)]
                hTs = [(alloc([8, 512], BF16), B()) for _ in range(2)]
                sq, b_sq = alloc([8, 512], BF16), B()
                tmp8, b_tmp8 = alloc([8, 512], F32), B()
                rstd, b_rstd = alloc([512], F32), B()
                NSET = 3
                sets = [dict(qf=alloc([2, 512], F32), sqb=alloc([2, 512], BF16), rr=alloc([2, 512], F32), qn=alloc([2, 512], F32), qo=alloc([2, 512], BF16),
                             b={n: B() for n in ("qf", "sqb", "rr", "qn", "qo")}, pb=1 + 2 * i) for i in range(NSET)]
                vos = [(alloc([6, 128], BF16), B()) for _ in range(2)]
                for (vo, bvo) in vos:
                    S.op("pool", lambda g, vo=vo: g.memset(vo[:, :, 64:128], 1.0), writes=[bvo])
                ropek = rope.rearrange("r p t -> p r t")
                from concourse.ap import AP as _AP

                def bc3(ap2, nb):
                    (pst, pn), (fs, fn_) = ap2.ap
                    return _AP(ap2.tensor, ap2.offset, [[pst, pn], [0, nb], [fs, fn_]])

                batches = [(0, 2, 16, 0), (2, 2, 17, 0), (4, 2, 18, 1), (6, 1, 18, 1), (7, 2, 19, 1)]

                def prologue(gi):
                    t0, W, v = groups[gi]
                    xg, bxg = xgs[gi % 2]
                    rp, brp = rps[gi % 2]
                    hT, bhT = hTs[gi % 2]

                    def s0():
                        S.dma("sp", lambda g: g.dma_start(out=xg[:, :, 0:W], in_=xsrc_k[:, :, t0:t0 + W]), writes=[bxg])
                        S.dma("sp", lambda g: g.dma_start(out=rp[:, :, 0:W], in_=ropek[:, :, t0:t0 + W]), writes=[brp])

                    def s1():
                        S.op("act", lambda g: g.activation(out=sq[:, :, 0:W], in_=xg[:, :, 0:W], func=AF.Square), reads=[bxg], writes=[b_sq])

                    def s2():
                        def ssmm(g):
                            for k in range(8):
                                ins = g.matmul(ps[:, 0, 0:W], lhsT=onesb[:], rhs=sq[:, k, 0:W], start=(k == 0), stop=(k == 7))
                            return ins
                        S.op("pe", ssmm, reads=[b_sq, b_ones], writes=[psb[0]])

                    def s3():
                        S.op("act", lambda g: g.activation(out=rstd[:, 0:W], in_=ps[:, 0, 0:W], func=AF.Ln, scale=1.0 / D, bias=cst[:, 0:1]),
                             reads=[psb[0], b_cst], writes=[b_rstd])
                        S.op("act", lambda g: g.activation(out=rstd[:, 0:W], in_=rstd[:, 0:W], func=AF.Exp, scale=-0.5), reads=[b_rstd], writes=[b_rstd])

                    def s4():
                        S.op("dve", lambda g: g.tensor_tensor(out=tmp8[:, :, 0:W], in0=xg[:, :, 0:W], in1=bc3(rstd[:, 0:W], 8), op=ALU.mult),
                             reads=[bxg, b_rstd], writes=[b_tmp8])

                    def s5():
                        for k in range(8):
                            S.op("act", lambda g, k=k: g.activation(
                                out=hT[:, k, 0:W], in_=tmp8[:, k, 0:W], func=AF.Identity, scale=A1[:, k, v:v + 1], bias=modT[:, k, v:v + 1]),
                                reads=[b_tmp8, b_modT, b_A], writes=[bhT])
                    return [s0, s1, s2, s3, s4, s5]

                def proj(oc0, nb, pb0, hT, W):
                    def f(g):
                        for i in range(nb):
                            for k in range(8):
                                ins = g.matmul(ps[:, pb0 + i, 0:W], lhsT=win[:, k, (oc0 + i) * 128:(oc0 + i + 1) * 128], rhs=hT[:, k, 0:W], start=(k == 0), stop=(k == 7))
                        return ins
                    return f

                def qkbatch(gi, bi, st):
                    t0, W, v = groups[gi]
                    rp, brp = rps[gi % 2]
                    hT, bhT = hTs[gi % 2]
                    oc0, nb, gcol, mi = batches[bi]
                    bb = st["b"]
                    pb0 = st["pb"]
                    pbs = psb[pb0:pb0 + nb]
                    inv = 1.0 / 32 if mi == 0 else 1.0 / 64
                    rc, rsn = (0, 1) if mi == 0 else (2, 3)
                    qf, sqb, rr, qn, qo = (st[n][:, 0:nb, 0:W] for n in ("qf", "sqb", "rr", "qn", "qo"))
                    pv = ps[:, pb0:pb0 + nb, 0:W]

                    def s0():
                        S.op("pe", proj(oc0, nb, pb0, hT, W), reads=[b_win, bhT], writes=pbs)

                    def s1():
                        S.op("act", lambda g: g.activation(out=qf, in_=pv, func=AF.Identity), reads=pbs, writes=[bb["qf"]])
                        S.op("act", lambda g: g.activation(out=sqb, in_=pv, func=AF.Square), reads=pbs, writes=[bb["sqb"]])

                    def s2():
                        def smm(g):
                            for i in range(nb):
                                ins = g.matmul(ps[:, pb0 + i, 0:W], lhsT=cmb[:, mi, :], rhs=st["sqb"][:, i, 0:W], start=True, stop=True)
                            return ins
                        S.op("pe", smm, reads=[bb["sqb"], b_cmb], writes=pbs)

                    def s3():
                        S.op("act", lambda g: g.activation(out=rr, in_=pv, func=AF.Ln, scale=inv, bias=cst[:, 0:1]), reads=pbs + [b_cst], writes=[bb["rr"]])
                        S.op("act", lambda g: g.activation(out=rr, in_=rr, func=AF.Exp, scale=-0.5), reads=[bb["rr"]], writes=[bb["rr"]])

                    def s4():
                        S.op("dve", lambda g: g.scalar_tensor_tensor(out=qn, in0=qf, scalar=smt[:, gcol:gcol + 1], in1=rr, op0=ALU.mult, op1=ALU.mult),
                             reads=[bb["qf"], bb["rr"], b_smt], writes=[bb["qn"]])

                    def s5():
                        S.op("dve", lambda g: g.tensor_copy(out=sqb, in_=qn), reads=[bb["qn"]], writes=[bb["sqb"]])

                    def s6():
                        def rmm(g):
                            for i in range(nb):
                                ins = g.matmul(ps[:, pb0 + i, 0:W], lhsT=cmb[:, 2 + mi, :], rhs=st["sqb"][:, i, 0:W], start=True, stop=True)
                            return ins
                        S.op("pe", rmm, reads=[bb["sqb"], b_cmb], writes=pbs)
                        S.op("pool", lambda g: g.tensor_tensor(out=qf, in0=qn, in1=bc3(rp[:, rc, 0:W], nb), op=ALU.mult), reads=[bb["qn"], brp], writes=[bb["qf"]])

                    def s7():
                        S.op("dve", lambda g: g.tensor_tensor(out=rr, in0=pv, in1=bc3(rp[:, rsn, 0:W], nb), op=ALU.mult), reads=pbs + [brp], writes=[bb["rr"]])

                    def s8():
                        S.op("pool", lambda g: g.tensor_tensor(out=qo, in0=qf, in1=rr, op=ALU.add), reads=[bb["qf"], bb["rr"]], writes=[bb["qo"]])
                        S.dma("sp", lambda g: g.dma_start(out=qk[oc0:oc0 + nb].rearrange("c p t -> p c t")[:, :, t0:t0 + W], in_=qo), reads=[bb["qo"]])
                    return [s0, s1, s2, s3, s4, s5, s6, s7, s8]

                def cbatch(gi, which, st, c0, nb):
                    t0, W, v = groups[gi]
                    hT, bhT = hTs[gi % 2]
                    bb = st["b"]
                    pb0 = st["pb"]
                    pbs = psb[pb0:pb0 + nb]
                    pv = ps[:, pb0:pb0 + nb, 0:W]

                    def s0():
                        S.op("pe", proj((9 if which == 0 else 12) + c0, nb, pb0, hT, W), reads=[b_win, bhT], writes=pbs)

                    def s1():
                        if which == 0:
                            S.op("act", lambda g: g.activation(out=st["qn"][:, 0:nb, 0:W], in_=pv, func=AF.Identity), reads=pbs, writes=[bb["qn"]])
                            S.dma("sp", lambda g: g.dma_start(out=cxs[c0:c0 + nb].rearrange("c p t -> p c t")[:, :, t0:t0 + W], in_=st["qn"][:, 0:nb, 0:W]), reads=[bb["qn"]])
                        else:
                            S.op("act", lambda g: g.activation(out=st["qo"][:, 0:nb, 0:W], in_=pv, func=AF.Gelu_apprx_tanh), reads=pbs, writes=[bb["qo"]])
                            S.dma("sp", lambda g: g.dma_start(out=cgs[c0:c0 + nb].rearrange("c p t -> p c t")[:, :, t0:t0 + W], in_=st["qo"][:, 0:nb, 0:W]), reads=[bb["qo"]])
                    return [s0, s1]

                def vitem(gi):
                    t0, W, v = groups[gi]
                    hT, bhT = hTs[gi % 2]

                    def mk(tt):
                        def s():
                            def vmm(g):
                                for k in range(8):
                                    ins = g.matmul(ps[:, 7, 0:384], lhsT=hT[:, k, tt * 128:(tt + 1) * 128], rhs=win[:, k, 1920:2304], start=(k == 0), stop=(k == 7))
                                return ins
                            S.op("pe", vmm, reads=[b_win, bhT], writes=[psb[7]])
                            vo, bvo = vos[tt % 2]
                            S.op("dve", lambda g: g.tensor_copy(out=vo[:, :, 0:64], in_=ps[:, 7, 0:384].rearrange("p (h d) -> p h d", h=6)), reads=[psb[7]], writes=[bvo])
                            S.dma("sp", lambda g: g.dma_start(out=vtok[t0 + tt * 128:t0 + (tt + 1) * 128, :], in_=vo[:].rearrange("p h d -> p (h d)")), reads=[bvo])
                        return s
                    return [mk(tt) for tt in range(W // 128)]

                items = []
                pidx = {}
                nset = [0]

                def add(stages, res=None, deps=()):
                    items.append((stages, res, list(deps)))
                    return len(items) - 1

                pidx[0] = add(prologue(0))
                for gi in range(len(groups)):
                    for bi in range(len(batches)):
                        st = sets[nset[0] % NSET]
                        add(qkbatch(gi, bi, st), res=("set", nset[0] % NSET), deps=[pidx[gi]])
                        nset[0] += 1
                        if bi == 1 and gi + 1 < len(groups):
                            pidx[gi + 1] = add(prologue(gi + 1), res=("pro",))
                    for which in range(2):
                        for (c0, nb) in ((0, 2), (2, 1)):
                            st = sets[nset[0] % NSET]
                            add(cbatch(gi, which, st, c0, nb), res=("set", nset[0] % NSET), deps=[pidx[gi]])
                            nset[0] += 1
                    add(vitem(gi), res=("v",), deps=[pidx[gi]])
                SK = 3
                start, end_ = [], []
                resend = {}
                for i, (stages, res, deps) in enumerate(items):
                    s = 0 if i == 0 else start[i - 1] + SK
                    if res is not None and res in resend:
                        s = max(s, resend[res])
                    for dI in deps:
                        s = max(s, end_[dI])
                    start.append(s)
                    end_.append(s + len(stages))
                    if res is not None:
                        resend[res] = s + len(stages)
                tmax = max(end_)
                for t in range(tmax):
                    for i, (stages, res, deps) in enumerate(items):
                        k = t - start[i]
                        if 0 <= k < len(stages):
                            stages[k]()
                if stop_after == ("A", l):
                    return True
                return False
            if _ph():
                return True
            def _ph():
                S.barrier()
                areset()
                bdw, b_bdw = alloc([12, 128], BF16), B()
                S.dma("pool", lambda g, l=l: g.dma_start(out=bdw, in_=lru_bd[l].rearrange("c p n -> p c n")), writes=[b_bdw])
                xx, b_xx = alloc([T], F32), B()
                xc, b_xc = alloc([T], F32), B()
                xcb, b_xcb = alloc([T], BF16), B()
                rr_, b_r = alloc([T], F32), B()
                ii_, b_i = alloc([T], F32), B()
                aa_, b_a = alloc([T], F32), B()
                mm_, b_m = alloc([T], F32), B()
                hh_ = [alloc([T], F32), alloc([T], F32)]
                b_h = [B(), B()]
                gg_, b_g = alloc([T], BF16), B()
                yy_, b_y = alloc([T], BF16), B()
                segs = [(0, NCTX), (NCTX, T)]
                for cc in range(3):
                    S.dma("sp", lambda g, cc=cc: g.dma_start(out=xx, in_=cxs[cc]), writes=[b_xx])
                    S.dma("sp", lambda g, cc=cc: g.dma_start(out=gg_, in_=cgs[cc]), writes=[b_g])
                    for d in range(2):
                        wcol = lambda k, d=d, cc=cc: smt[:, 27 + d * 12 + k * 3 + cc:28 + d * 12 + k * 3 + cc]
                        bcol = smt[:, 51 + d * 3 + cc:52 + d * 3 + cc]
                        S.op("dve", lambda g, wcol=wcol, bcol=bcol: g.tensor_scalar(out=xc, in0=xx, scalar1=wcol(3), scalar2=bcol, op0=ALU.mult, op1=ALU.add),
                             reads=[b_xx, b_smt], writes=[b_xc])
                        for k in range(3):
                            s_ = 3 - k
                            for (a_, e_) in segs:
                                if d == 0:
                                    dst, src = (a_ + s_, e_), (a_, e_ - s_)
                                else:
                                    dst, src = (a_, e_ - s_), (a_ + s_, e_)
                                eng = "dve"
                                S.op(eng, lambda g, dst=dst, src=src, wcol=wcol, k=k: g.scalar_tensor_tensor(
                                    out=xc[:, dst[0]:dst[1]], in0=xx[:, src[0]:src[1]], scalar=wcol(k), in1=xc[:, dst[0]:dst[1]], op0=ALU.mult, op1=ALU.add),
                                    reads=[b_xx, b_xc, b_smt], writes=[b_xc])
                        S.op("pool", lambda g: g.tensor_copy(out=xcb, in_=xc), reads=[b_xc], writes=[b_xcb])
                        ia = (d * 2 + 0) * 3 + cc
                        ix = (d * 2 + 1) * 3 + cc
                        for sg0 in range(0, T, 2048):
                            sgw = min(2048, T - sg0)

                            def gmm(g, sg0=sg0, sgw=sgw, ia=ia, ix=ix):
                                for q0 in range(0, sgw, 512):
                                    w_ = min(512, sgw - q0)
                                    g.matmul(ps[:, q0 // 512, 0:w_], lhsT=bdw[:, ia, :], rhs=xcb[:, sg0 + q0:sg0 + q0 + w_], start=True, stop=True)
                                    ins = g.matmul(ps[:, 4 + q0 // 512, 0:w_], lhsT=bdw[:, ix, :], rhs=xcb[:, sg0 + q0:sg0 + q0 + w_], start=True, stop=True)
                                return ins
                            S.op("pe", gmm, reads=[b_xcb, b_bdw], writes=psb)
                            S.op("act", lambda g, sg0=sg0, sgw=sgw, d=d, cc=cc: g.activation(
                                out=rr_[:, sg0:sg0 + sgw], in_=psflat(0, sgw), func=AF.Sigmoid, bias=smt[:, 57 + d * 3 + cc:58 + d * 3 + cc]),
                                reads=psb[0:4] + [b_smt], writes=[b_r])
                            S.op("act", lambda g, sg0=sg0, sgw=sgw, d=d, cc=cc: g.activation(
                                out=ii_[:, sg0:sg0 + sgw], in_=psflat(4, sgw), func=AF.Sigmoid, bias=smt[:, 63 + d * 3 + cc:64 + d * 3 + cc]),
                                reads=psb[4:8] + [b_smt], writes=[b_i])
                        S.op("act", lambda g, d=d, cc=cc: g.activation(out=aa_, in_=rr_, func=AF.Exp, scale=sm2[:, 8 + d * 3 + cc:9 + d * 3 + cc]),
                             reads=[b_r, b_sm2], writes=[b_a])
                        S.op("act", lambda g, d=d, cc=cc: g.activation(out=mm_, in_=rr_, func=AF.Exp, scale=sm2[:, 14 + d * 3 + cc:15 + d * 3 + cc]),
                             reads=[b_r, b_sm2], writes=[b_m])
                        S.op("act", lambda g: g.activation(out=mm_, in_=mm_, func=AF.Sqrt, scale=-1.0, bias=cst[:, 1:2]), reads=[b_m, b_cst], writes=[b_m])
                        S.op("pool", lambda g: g.tensor_tensor(out=ii_, in0=ii_, in1=xc, op=ALU.mult), reads=[b_i, b_xc], writes=[b_i])
                        S.op("dve", lambda g: g.tensor_tensor(out=mm_, in0=mm_, in1=ii_, op=ALU.mult), reads=[b_m, b_i], writes=[b_m])
                        hd = hh_[d]
                        if d == 0:
                            S.op("dve", lambda g, hd=hd: g.tensor_tensor_scan(out=hd, data0=aa_, data1=mm_, initial=0.0, op0=ALU.mult, op1=ALU.add),
                                 reads=[b_a, b_m], writes=[b_h[d]])
                        else:
                            S.op("dve", lambda g, hd=hd: g.tensor_tensor_scan(out=rev(hd[:, 0:NCTX]), data0=rev(aa_[:, 0:NCTX]), data1=rev(mm_[:, 0:NCTX]),
                                                                              initial=0.0, op0=ALU.mult, op1=ALU.add), reads=[b_a, b_m], writes=[b_h[d]])
                            S.op("dve", lambda g, hd=hd: g.tensor_tensor_scan(out=rev(hd[:, NCTX:T]), data0=rev(aa_[:, NCTX:T]), data1=rev(mm_[:, NCTX:T]),
                                                                              initial=hd[:, 0:1], op0=ALU.mult, op1=ALU.add), reads=[b_a, b_m, b_h[d]], writes=[b_h[d]])
                    S.op("pool", lambda g: g.tensor_tensor(out=hh_[0], in0=hh_[0], in1=hh_[1], op=ALU.add), reads=[b_h[0], b_h[1]], writes=[b_h[0]])
                    S.op("dve", lambda g: g.tensor_tensor(out=yy_, in0=hh_[0], in1=gg_, op=ALU.mult), reads=[b_h[0], b_g], writes=[b_y])
                    S.dma("pool", lambda g, cc=cc: g.dma_start(out=mix[640 + cc * 128:640 + (cc + 1) * 128, :], in_=yy_), reads=[b_y])
                if stop_after == ("B3", l):
                    return True

                return False
            if _ph():
                return True
            def _ph():
                S.barrier()
                areset()
                Vt, b_Vt = alloc([34, 768], BF16), B()
                vtk = vtok.rearrange("(kt p) c -> p kt c", p=128)
                for q4 in range(0, 34, 9):
                    q5 = min(34, q4 + 9)
                    S.dma("sp", lambda g, q4=q4, q5=q5: g.dma_start(out=Vt[:, q4:q5, :], in_=vtk[:, q4:q5, :]), writes=[b_Vt])
                b1_mark = off[0]
                sh.update(Vt=Vt, b_Vt=b_Vt, b1_mark=b1_mark)
                KT, b_KT = alloc([2, T], BF16), B()
                S.dma("sp", lambda g: g.dma_start(out=KT, in_=qk[2:4].rearrange("c p t -> p c t")), writes=[b_KT])
                QTR = Rot([(alloc([2, 512], BF16), B()) for _ in range(2)])
                ER = [Rot([(alloc([2, 512], BF16), B()) for _ in range(2)]) for _ in range(2)]
                evR = Rot([dict(o=alloc([4, 512], F32), l=alloc([4, 512], F32), bo=B(), bl=B()) for _ in range(2)])
                finR = Rot([dict(oo=alloc([512], F32), osq=alloc([512], BF16), rr=alloc([512], F32), y=alloc([512], BF16),
                                 b={n: B() for n in ("oo", "osq", "rr", "y")}) for _ in range(4)])
                sbR = Rot([0, 1, 2])
                pending = []

                def flush():
                    while pending:
                        pending.pop(0)()
                for gi, (t0, W, v) in enumerate(groups):
                    if gi == 0 and not ctx_out:
                        continue
                    nkt = 2 if gi == 0 else 34
                    QT, bQT = QTR.next()
                    S.dma("sp", lambda g, QT=QT, t0=t0, W=W: g.dma_start(out=QT[:, :, 0:W], in_=qk[0:2].rearrange("c p t -> p c t")[:, :, t0:t0 + W]), writes=[bQT])
                    for c in range(2):
                        Ecur = [None] * 2
                        Eprev = [None] * 2
                        for kt in range(nkt + 1):
                            if kt == min(22, nkt):
                                flush()
                            if kt < nkt:
                                for p in range(2):
                                    E, bE = ER[p].next()
                                    Ecur[p] = (E, bE)

                                    def qk2(g, p=p, c=c, kt=kt, QT=QT, W=W):
                                        for j in (2 * p, 2 * p + 1):
                                            ins = g.matmul(ps[:, j, 0:W], lhsT=KT[32 * j:32 * j + 32, c, kt * 128:(kt + 1) * 128], rhs=QT[32 * j:32 * j + 32, c, 0:W],
                                                           start=True, stop=True, tile_position=(32 * j, 0))
                                        return ins
                                    S.op("pe", qk2, reads=[b_KT, bQT], writes=[psb[2 * p], psb[2 * p + 1]])
                                    S.op("act", lambda g, p=p, E=E, W=W: g.activation(out=E[:, :, 0:W], in_=ps[:, 2 * p:2 * p + 2, 0:W], func=AF.Exp, scale=SCALE_A),
                                         reads=[psb[2 * p], psb[2 * p + 1]], writes=[bE])
                            if kt >= 1:
                                for p in range(2):
                                    E, bE = Eprev[p]
                                    h = 2 * c + p

                                    def pv2(g, p=p, E=E, h=h, kt=kt, W=W, nkt=nkt):
                                        for m in range(2):
                                            ins = g.matmul(ps[:, 4 + 2 * p + m, 0:W], lhsT=Vt[:, kt - 1, h * 128:(h + 1) * 128], rhs=E[:, m, 0:W], start=(kt == 1), stop=(kt == nkt))
                                        return ins
                                    S.op("pe", pv2, reads=[bE, b_Vt], writes=[psb[4 + 2 * p], psb[5 + 2 * p]])
                            Eprev = list(Ecur)
                        ev = evR.next()
                        S.op("dve", lambda g, ev=ev, W=W: g.tensor_copy(out=ev["o"][0:64, :, 0:W], in_=ps[0:64, 4:8, 0:W]), reads=psb[4:8], writes=[ev["bo"]])
                        S.op("dve", lambda g, ev=ev, W=W: g.tensor_copy(out=ev["l"][0:64, :, 0:W], in_=ps[64:128, 4:8, 0:W]), reads=psb[4:8], writes=[ev["bl"]])
                        S.op("dve", lambda g, ev=ev, W=W: g.reciprocal(out=ev["l"][0:64, :, 0:W], in_=ev["l"][0:64, :, 0:W]), reads=[ev["bl"]], writes=[ev["bl"]])
                        S.op("pool", lambda g, ev=ev, W=W: g.tensor_tensor(out=ev["o"][0:64, :, 0:W], in0=ev["o"][0:64, :, 0:W], in1=ev["l"][0:64, :, 0:W], op=ALU.mult),
                             reads=[ev["bo"], ev["bl"]], writes=[ev["bo"]])
                        for hh in range(2):
                            h = 2 * c + hh
                            f = finR.next()
                            fb = f["b"]
                            S.op("dve", lambda g, f=f, ev=ev, hh=hh, W=W: g.scalar_tensor_tensor(
                                out=f["oo"][0:64, 0:W], in0=ev["o"][0:64, 2 * hh + 1, 0:W], scalar=sm2[0:64, 1:2], in1=ev["o"][0:64, 2 * hh, 0:W],
                                op0=ALU.mult, op1=ALU.add), reads=[ev["bo"], b_sm2], writes=[fb["oo"]])
                            S.op("pool", lambda g, f=f, W=W: g.tensor_tensor(out=f["osq"][0:64, 0:W], in0=f["oo"][0:64, 0:W], in1=f["oo"][0:64, 0:W], op=ALU.mult),
                                 reads=[fb["oo"]], writes=[fb["osq"]])

                            def late(f=f, fb=fb, h=h, t0=t0, W=W):
                                S.op("pe", lambda g: g.matmul(ps[0:64, 0, 0:W], lhsT=onesb[0:64, 0:64], rhs=f["osq"][0:64, 0:W], start=True, stop=True),
                                     reads=[fb["osq"], b_ones], writes=[psb[0]])
                                S.op("act", lambda g: g.activation(out=f["rr"][0:64, 0:W], in_=ps[0:64, 0, 0:W], func=AF.Ln, scale=1.0 / 64, bias=cst[0:64, 0:1]),
                                     reads=[psb[0], b_cst], writes=[fb["rr"]])
                                S.op("act", lambda g: g.activation(out=f["rr"][0:64, 0:W], in_=f["rr"][0:64, 0:W], func=AF.Exp, scale=-0.5),
                                     reads=[fb["rr"]], writes=[fb["rr"]])
                                S.op("dve", lambda g: g.scalar_tensor_tensor(out=f["y"][0:64, 0:W], in0=f["oo"][0:64, 0:W], scalar=sm2[0:64, 0:1], in1=f["rr"][0:64, 0:W],
                                                                             op0=ALU.mult, op1=ALU.mult), reads=[fb["oo"], fb["rr"], b_sm2], writes=[fb["y"]])
                                S.dma("pool", lambda g: g.dma_start(out=mix[h * 64:(h + 1) * 64, t0:t0 + W], in_=f["y"][0:64, 0:W]), reads=[fb["y"]])
                            pending.append(late)
                flush()
                if stop_after == ("B1", l):
                    return True
                return False
            if _ph():
                return True
            def _ph():
                Vt, b_Vt, b1_mark = sh["Vt"], sh["b_Vt"], sh["b1_mark"]
                S.barrier()
                WSZ = 2 * 22528 + 22528
                off[0] = ARENA - WSZ
                ws0 = dict(wg=alloc([8, 1408], BF16), wv=alloc([8, 1408], BF16), wd=alloc([11, D], BF16), b_wg=B(), b_wv=B(), b_wd=B())
                wuk = w_up[l].rearrange("(k p) n -> p k n", p=128)
                S.dma("pool", lambda g: g.dma_start(out=ws0["wg"], in_=wuk[:, :, 0:1408]), writes=[ws0["b_wg"]])
                S.dma("pool", lambda g: g.dma_start(out=ws0["wv"], in_=wuk[:, :, 2816:2816 + 1408]), writes=[ws0["b_wv"]])
                S.dma("pool", lambda g: g.dma_start(out=ws0["wd"], in_=w_down[l][0:1408, :].rearrange("(c p) n -> p c n", p=128)), writes=[ws0["b_wd"]])
                sh["ws0"] = ws0
                off[0] = b1_mark
                KB, b_KB = alloc([2, T], BF16), B()
                S.dma("sp", lambda g: g.dma_start(out=KB, in_=qk[7:9].rearrange("c p t -> p c t")), writes=[b_KB])
                QBs = [(alloc([3, 512], BF16), B()) for _ in range(2)]
                EBR = Rot([(alloc([640], BF16), B()) for _ in range(4)])
                fbR = Rot([dict(ls=alloc([512], F32), rl=alloc([512], F32), y=alloc([512], BF16), b={n: B() for n in ("ls", "rl", "y")}) for _ in range(3)])
                sbR = Rot([0, 2])
                abR = Rot([4, 5, 6, 7])
                assert off[0] <= ARENA - WSZ, off[0]
                units = []
                ng = 0
                for gi, (t0, W, v) in enumerate(groups):
                    if gi == 0 and not ctx_out:
                        continue
                    QB, bQB = QBs[ng % 2]
                    ng += 1
                    for hq in range(6):
                        ab = abR.next()
                        nqb = W // 128
                        for qb in range(nqb):
                            tt = t0 // 128 + qb
                            if gi == 0:
                                keys = [(0, None), (1, None)]
                            else:
                                n = tt - 2
                                keys = [(0, None), (1, None)]
                                if n > 0:
                                    keys.append((tt - 1, 0))
                                keys.append((tt, None))
                                if n < 31:
                                    keys.append((tt + 1, 1))
                            units.append(dict(gi=gi, t0=t0, W=W, hq=hq, qb=qb, keys=keys, ab=ab, QB=QB, bQB=bQB, first=(hq == 0 and qb == 0), lastq=(qb == nqb - 1)))

                def front(u):
                    t0, W, hq, qb, keys, QB, bQB = u["t0"], u["W"], u["hq"], u["qb"], u["keys"], u["QB"], u["bQB"]
                    c, half, kv = hq // 2, hq % 2, hq // 3
                    p0 = half * 64
                    if u["first"]:
                        S.dma("sp", lambda g: g.dma_start(out=QB[:, :, 0:W], in_=qk[4:7].rearrange("c p t -> p c t")[:, :, t0:t0 + W]), writes=[bQB])
                    nk = len(keys)
                    b0 = sbR.next()

                    def qkmm(g):
                        for i, (kt, m) in enumerate(keys):
                            ins = g.matmul(ps[:, b0 + i // 4, (i % 4) * 128:(i % 4 + 1) * 128], lhsT=KB[p0:p0 + 64, kv, kt * 128:(kt + 1) * 128],
                                           rhs=QB[p0:p0 + 64, c, qb * 128:(qb + 1) * 128], start=True, stop=True, tile_position=(p0, 0))
                        return ins
                    S.op("pe", qkmm, reads=[b_KB, bQB], writes=[psb[b0], psb[b0 + 1]])
                    E, bE = EBR.next()
                    u["E"], u["bE"] = E, bE
                    S.op("act", lambda g: g.activation(out=E[:, 0:nk * 128], in_=psflat(b0, nk * 128), func=AF.Exp, scale=SCALE_B),
                         reads=[psb[b0], psb[b0 + 1]], writes=[bE])
                    for i, (kt, m) in enumerate(keys):
                        if m is not None:
                            S.op("dve", lambda g, i=i, m=m: g.tensor_tensor(out=E[:, i * 128:(i + 1) * 128], in0=E[:, i * 128:(i + 1) * 128], in1=msk[:, m, :], op=ALU.mult),
                                 reads=[bE, b_msk], writes=[bE])

                def back(u):
                    t0, W, hq, qb, keys, ab = u["t0"], u["W"], u["hq"], u["qb"], u["keys"], u["ab"]
                    kv = hq // 3
                    E, bE = u["E"], u["bE"]

                    def pvmm(g):
                        for i, (kt, m) in enumerate(keys):
                            ins = g.matmul(ps[:, ab, qb * 128:(qb + 1) * 128], lhsT=Vt[:, kt, (4 + kv) * 128:(5 + kv) * 128], rhs=E[:, i * 128:(i + 1) * 128],
                                           start=(i == 0), stop=(i == len(keys) - 1))
                        return ins
                    S.op("pe", pvmm, reads=[bE, b_Vt], writes=[psb[ab]])
                    if u["lastq"]:
                        f = fbR.next()
                        fb = f["b"]
                        S.op("dve", lambda g: g.tensor_scalar_add(out=f["ls"][0:64, 0:W], in0=ps[64:128, ab, 0:W], scalar1=sm2[0:64, 2 + hq:3 + hq]),
                             reads=[psb[ab], b_sm2], writes=[fb["ls"]])
                        S.op("dve", lambda g: g.reciprocal(out=f["rl"][0:64, 0:W], in_=f["ls"][0:64, 0:W]), reads=[fb["ls"]], writes=[fb["rl"]])
                        S.op("dve", lambda g: g.tensor_tensor(out=f["y"][0:64, 0:W], in0=ps[0:64, ab, 0:W], in1=f["rl"][0:64, 0:W], op=ALU.mult),
                             reads=[psb[ab], fb["rl"]], writes=[fb["y"]])
                        S.dma("pool", lambda g: g.dma_start(out=mix[256 + hq * 64:256 + (hq + 1) * 64, t0:t0 + W], in_=f["y"][0:64, 0:W]), reads=[fb["y"]])

                LAG = 2
                for idx in range(len(units) + LAG):
                    if idx < len(units):
                        front(units[idx])
                    if idx >= LAG:
                        back(units[idx - LAG])
                if stop_after == ("B2", l):
                    return True
                return False
            if _ph():
                return True
            def _ph():
                S.barrier()
                areset()
                WSZ = 2 * 22528 + 22528
                wout, b_wout = alloc([8, D], BF16), B()
                S.dma("pool", lambda g: g.dma_start(out=wout, in_=w_out[l].rearrange("(k p) n -> p k n", p=128)), writes=[b_wout])
                mxs = [(alloc([8, 512], BF16), B()) for _ in range(2)]
                xgs = [(alloc([8, 512], F32), B()) for _ in range(2)]
                h2s_ = [(alloc([8, 514], BF16), B()) for _ in range(2)]
                sq, b_sq = alloc([8, 512], BF16), B()
                tmp8, b_tmp8 = alloc([8, 512], F32), B()
                rstd, b_rstd = alloc([512], F32), B()
                assert off[0] <= ARENA - WSZ
                for (h2, bh2) in h2s_:
                    S.op("pool", lambda g, h2=h2: g.memset(h2, 0.0), writes=[bh2])
                mixk = mix.rearrange("(k p) t -> p k t", p=128)
                h2sk = h2s.rearrange("(k p) t -> p k t", p=128)
                pbR = Rot([1, 2, 3, 4, 5, 6])
                from concourse.ap import AP as _AP

                def bc3(ap2, nb):
                    (pst, pn), (fs, fn_) = ap2.ap
                    return _AP(ap2.tensor, ap2.offset, [[pst, pn], [0, nb], [fs, fn_]])
                glist = [(gi, g_) for gi, g_ in enumerate(groups) if not (gi == 0 and not ctx_out)]

                def front(n):
                    gi, (t0, W, v) = glist[n]
                    mx, bmx = mxs[n % 2]
                    xg, bxg = xgs[n % 2]
                    for kh in range(2):
                        S.dma("sp", lambda g, kh=kh: g.dma_start(out=mx[:, 4 * kh:4 * kh + 4, 0:W], in_=mixk[:, 4 * kh:4 * kh + 4, t0:t0 + W]), writes=[bmx])
                    S.dma("sp", lambda g: g.dma_start(out=xg[:, :, 0:W], in_=xsrc_k[:, :, t0:t0 + W]), writes=[bxg])
                    for j in range(8):
                        pb = pbR.next()

                        def omm(g, j=j, pb=pb):
                            for c in range(8):
                                ins = g.matmul(ps[:, pb, 0:W], lhsT=wout[:, c, j * 128:(j + 1) * 128], rhs=mx[:, c, 0:W], start=(c == 0), stop=(c == 7))
                            return ins
                        S.op("pe", omm, reads=[b_wout, bmx], writes=[psb[pb]])
                        S.op("dve", lambda g, j=j, pb=pb: g.scalar_tensor_tensor(
                            out=xg[:, j, 0:W], in0=ps[:, pb, 0:W], scalar=modT[:, 16 + j, v:v + 1], in1=xg[:, j, 0:W], op0=ALU.mult, op1=ALU.add),
                            reads=[psb[pb], bxg, b_modT], writes=[bxg])
                    S.dma("pool", lambda g: g.dma_start(out=xTs_k[:, :, t0:t0 + W], in_=xg[:, :, 0:W]), reads=[bxg])

                def back(n):
                    gi, (t0, W, v) = glist[n]
                    xg, bxg = xgs[n % 2]
                    h2, bh2 = h2s_[n % 2]
                    S.op("act", lambda g: g.activation(out=sq[:, :, 0:W], in_=xg[:, :, 0:W], func=AF.Square), reads=[bxg], writes=[b_sq])

                    def ssmm2(g):
                        for k in range(8):
                            ins = g.matmul(ps[:, 0, 0:W], lhsT=onesb[:], rhs=sq[:, k, 0:W], start=(k == 0), stop=(k == 7))
                        return ins
                    S.op("pe", ssmm2, reads=[b_sq, b_ones], writes=[psb[0]])
                    S.op("act", lambda g: g.activation(out=rstd[:, 0:W], in_=ps[:, 0, 0:W], func=AF.Ln, scale=1.0 / D, bias=cst[:, 0:1]),
                         reads=[psb[0], b_cst], writes=[b_rstd])
                    S.op("act", lambda g: g.activation(out=rstd[:, 0:W], in_=rstd[:, 0:W], func=AF.Exp, scale=-0.5), reads=[b_rstd], writes=[b_rstd])
                    S.op("dve", lambda g: g.tensor_tensor(out=tmp8[:, :, 0:W], in0=xg[:, :, 0:W], in1=bc3(rstd[:, 0:W], 8), op=ALU.mult),
                         reads=[bxg, b_rstd], writes=[b_tmp8])
                    for k in range(8):
                        S.op("act", lambda g, k=k: g.activation(
                            out=h2[:, k, 1:1 + W], in_=tmp8[:, k, 0:W], func=AF.Identity, scale=A2[:, k, v:v + 1], bias=modT[:, 24 + k, v:v + 1]),
                            reads=[b_tmp8, b_modT, b_A], writes=[bh2])
                    if gi == 0:
                        S.dma("pool", lambda g: g.dma_start(out=h2sk[:, :, 0:258], in_=h2[:, :, 0:258]), reads=[bh2])
                    elif gi == 1:
                        S.dma("pool", lambda g: g.dma_start(out=h2sk[:, :, 258:258 + 513], in_=h2[:, :, 0:513]), reads=[bh2])
                    elif gi == 8:
                        S.dma("pool", lambda g: g.dma_start(out=h2sk[:, :, t0 + 3:t0 + 3 + 513], in_=h2[:, :, 1:514]), reads=[bh2])
                    else:
                        S.dma("pool", lambda g: g.dma_start(out=h2sk[:, :, t0 + 3:t0 + 3 + 512], in_=h2[:, :, 1:513]), reads=[bh2])

                for n in range(len(glist) + 1):
                    if n < len(glist):
                        front(n)
                    if n >= 1:
                        back(n - 1)
                if stop_after == ("C1", l):
                    return True
                return False
            if _ph():
                return True
            def _ph():
                h2sk = h2s.rearrange("(k p) t -> p k t", p=128)
                wins = []
                if ctx_out:
                    wins.append((0, 258, 0, 256, 1))
                for i in range(9):
                    wo = min(510, NLAT - 510 * i)
                    wins.append((258 + 510 * i, wo + 2, NCTX + 510 * i, wo, 0))
                S.barrier()
                areset()
                wsets = []
                wuk = w_up[l].rearrange("(k p) n -> p k n", p=128)
                wsets.append(sh["ws0"])
                for hf_ in range(1, 2):
                    ws = dict(wg=alloc([8, 1408], BF16), wv=alloc([8, 1408], BF16), wd=alloc([11, D], BF16), b_wg=B(), b_wv=B(), b_wd=B())
                    wsets.append(ws)
                    S.dma("pool", lambda g, hf_=hf_, ws=ws: g.dma_start(out=ws["wg"], in_=wuk[:, :, hf_ * 1408:(hf_ + 1) * 1408]), writes=[ws["b_wg"]])
                    S.dma("pool", lambda g, hf_=hf_, ws=ws: g.dma_start(out=ws["wv"], in_=wuk[:, :, 2816 + hf_ * 1408:2816 + (hf_ + 1) * 1408]), writes=[ws["b_wv"]])
                    S.dma("pool", lambda g, hf_=hf_, ws=ws: g.dma_start(out=ws["wd"], in_=w_down[l][hf_ * 1408:(hf_ + 1) * 1408, :].rearrange("(c p) n -> p c n", p=128)), writes=[ws["b_wd"]])
                hwR = Rot([(alloc([8, 512], BF16), B()) for _ in range(2)])
                xgR = Rot([(alloc([8, 512], F32), B()) for _ in range(1)])
                act, b_act = alloc([11, 512], BF16), B()
                cvR = Rot([(alloc([512], F32), B()) for _ in range(2)])
                sgR = Rot([(alloc([512], F32), B()) for _ in range(2)])
                gvR = Rot([(0, 1), (2, 3)])
                dbR = Rot([4, 5, 6, 7])
                bxw = [B() for _ in wins]
                assert off[0] <= ARENA - (2 * 22528 + 22528), off[0]
                for hf in range(2):
                    ws = wsets[hf]
                    wg, wv, wd, b_wg, b_wv, b_wd = ws["wg"], ws["wv"], ws["wd"], ws["b_wg"], ws["b_wv"], ws["b_wd"]
                    for wi, (cs, Wn, tk0, Wo, v) in enumerate(wins):
                        hw, bhw = hwR.next()
                        S.dma("sp", lambda g, hw=hw, cs=cs, Wn=Wn: g.dma_start(out=hw[:, :, 0:Wn], in_=h2sk[:, :, cs:cs + Wn]), writes=[bhw])
                        xg, bxg = xgR.next()
                        S.dma("sp", lambda g, xg=xg, tk0=tk0, Wo=Wo: g.dma_start(out=xg[:, :, 0:Wo], in_=xTs_k[:, :, tk0:tk0 + Wo]), reads=[bxw[wi]], writes=[bxg])
                        for ci in range(11):
                            c = hf * 11 + ci
                            gb, vb = gvR.next()

                            def umm(g, ci=ci, gb=gb, vb=vb, hw=hw, Wn=Wn, Wo=Wo, wg=wg, wv=wv):
                                for k in range(8):
                                    g.matmul(ps[:, gb, 0:Wn], lhsT=wg[:, k, ci * 128:(ci + 1) * 128], rhs=hw[:, k, 0:Wn], start=(k == 0), stop=(k == 7))
                                for k in range(8):
                                    ins = g.matmul(ps[:, vb, 0:Wo], lhsT=wv[:, k, ci * 128:(ci + 1) * 128], rhs=hw[:, k, 1:1 + Wo], start=(k == 0), stop=(k == 7))
                                return ins
                            S.op("pe", umm, reads=[b_wg, b_wv, bhw], writes=[psb[gb], psb[vb]])
                            cv, bcv = cvR.next()
                            sg, bsg = sgR.next()
                            w0 = smt[:, 75 + c:76 + c]
                            w1 = smt[:, 75 + 22 + c:76 + 22 + c]
                            w2 = smt[:, 75 + 44 + c:76 + 44 + c]
                            bb_ = smt[:, 141 + c:142 + c]
                            S.op("act", lambda g, cv=cv, gb=gb, Wo=Wo, w1=w1, bb_=bb_: g.activation(out=cv[:, 0:Wo], in_=ps[:, gb, 1:1 + Wo], func=AF.Identity, scale=w1, bias=bb_),
                                 reads=[psb[gb], b_smt], writes=[bcv])
                            S.op("dve", lambda g, cv=cv, gb=gb, Wo=Wo, w0=w0: g.scalar_tensor_tensor(out=cv[:, 0:Wo], in0=ps[:, gb, 0:Wo], scalar=w0, in1=cv[:, 0:Wo], op0=ALU.mult, op1=ALU.add),
                                 reads=[psb[gb], bcv, b_smt], writes=[bcv])
                            S.op("dve", lambda g, cv=cv, gb=gb, Wo=Wo, w2=w2: g.scalar_tensor_tensor(out=cv[:, 0:Wo], in0=ps[:, gb, 2:2 + Wo], scalar=w2, in1=cv[:, 0:Wo], op0=ALU.mult, op1=ALU.add),
                                 reads=[psb[gb], bcv, b_smt], writes=[bcv])
                            S.op("act", lambda g, cv=cv, sg=sg, Wo=Wo: g.activation(out=sg[:, 0:Wo], in_=cv[:, 0:Wo], func=AF.Silu), reads=[bcv], writes=[bsg])
                            S.op("dve", lambda g, sg=sg, vb=vb, ci=ci, Wo=Wo: g.tensor_tensor(out=act[:, ci, 0:Wo], in0=ps[:, vb, 0:Wo], in1=sg[:, 0:Wo], op=ALU.mult),
                                 reads=[psb[vb], bsg], writes=[b_act])
                        for j in range(8):
                            db = dbR.next()

                            def dmm(g, j=j, db=db, Wo=Wo, wd=wd):
                                for ci in range(11):
                                    ins = g.matmul(ps[:, db, 0:Wo], lhsT=wd[:, ci, j * 128:(j + 1) * 128], rhs=act[:, ci, 0:Wo], start=(ci == 0), stop=(ci == 10))
                                return ins
                            S.op("pe", dmm, reads=[b_wd, b_act], writes=[psb[db]])
                            S.op("dve", lambda g, j=j, db=db, xg=xg, Wo=Wo, v=v: g.scalar_tensor_tensor(
                                out=xg[:, j, 0:Wo], in0=ps[:, db, 0:Wo], scalar=modT[:, 40 + j, v:v + 1], in1=xg[:, j, 0:Wo], op0=ALU.mult, op1=ALU.add),
                                reads=[psb[db], bxg, b_modT], writes=[bxg])
                        if last and hf == 1:
                            outk = out.rearrange("(k p) t -> p k t", p=128)
                            S.dma("pool", lambda g, xg=xg, tk0=tk0, Wo=Wo: g.dma_start(out=outk[:, :, tk0 - NCTX:tk0 - NCTX + Wo], in_=xg[:, :, 0:Wo]), reads=[bxg])
                        else:
                            S.dma("pool", lambda g, xg=xg, tk0=tk0, Wo=Wo: g.dma_start(out=xTs_k[:, :, tk0:tk0 + Wo], in_=xg[:, :, 0:Wo]), reads=[bxg], writes=[bxw[wi]])
                    if stop_after == ("C2%d" % hf, l):
                        return True
                if stop_after is not None and stop_after[1] == l:
                    return True
                return False
            if _ph():
                return True
            return False

        for l in range(nlayers):
            if run_layer(l):
                break

        S.barrier()
        S.emit(nc, sems, dsems)
    return nc


def _rope_tables():
    pos = np.arange(NLAT)
    row = (pos // 64).astype(np.float32)
    col = (pos % 64).astype(np.float32)
    tabs = []
    for hd in (32, 64):
        quarter = hd // 4
        half = hd // 2
        inv_freq = (np.float32(10000.0) ** (-np.arange(quarter, dtype=np.float32) / np.float32(quarter))).astype(np.float32)
        ang = np.concatenate([row[:, None] * inv_freq[None, :], col[:, None] * inv_freq[None, :]], axis=-1).astype(np.float32)
        cos, sin = np.cos(ang).astype(np.float32), np.sin(ang).astype(np.float32)
        p = np.arange(128)
        d = p % hd
        j = d % half
        sign = np.where(d < half, -1.0, 1.0).astype(np.float32)
        C = np.ones((128, T), np.float32)
        Sg = np.zeros((128, T), np.float32)
        C[:, NCTX:] = cos[:, j].T
        Sg[:, NCTX:] = sin[:, j].T * sign[:, None]
        tabs += [C, Sg]
    return np.stack(tabs, 0)


def _const_mats():
    p = np.arange(128)
    m = np.zeros((7, 128, 128), np.float32)
    m[0] = (p[:, None] // 32 == p[None, :] // 32)
    m[1] = (p[:, None] // 64 == p[None, :] // 64)
    for idx, hd in ((2, 32), (3, 64)):
        perm = (p // hd) * hd + ((p % hd) + hd // 2) % hd
        m[idx] = (p[:, None] == perm[None, :])
    m[4] = np.eye(128, dtype=np.float32)
    m[5] = (p[:, None] >= p[None, :])
    m[6] = (p[:, None] <= p[None, :])
    return m


def _prep_shared(inp):
    f = lambda a: np.ascontiguousarray(np.asarray(a, dtype=np.float32))
    w_in = f(inp["w_in"])
    cols = np.concatenate([np.arange(0, 256), np.arange(256, 512), np.arange(768, 1152),
                           np.arange(1152, 1216), np.arange(1152, 1216), np.arange(1216, 1280), np.arange(1216, 1280),
                           np.arange(1408, 1792), np.arange(1792, 2176), np.arange(512, 768), np.arange(1280, 1408)])
    assert cols.size == 2304
    w_in_ext = np.ascontiguousarray(w_in[:, :, cols])
    b_mod2 = np.ascontiguousarray(np.repeat(f(inp["b_mod"])[:, None, :], 2, axis=1))
    wa, wx = f(inp["lru_wa"]), f(inp["lru_wx"])
    bd = np.zeros((2, 2, 2, 3, 128, 128), np.float32)
    for cc in range(3):
        for hb in range(2):
            bd[:, :, 0, cc, hb * 64:(hb + 1) * 64, hb * 64:(hb + 1) * 64] = wa[:, :, 2 * cc + hb]
            bd[:, :, 1, cc, hb * 64:(hb + 1) * 64, hb * 64:(hb + 1) * 64] = wx[:, :, 2 * cc + hb]
    bd = np.ascontiguousarray(bd.reshape(2, 12, 128, 128))
    p = np.arange(128)
    sm = np.zeros((2, 128, NS), np.float32)
    sm[:, :, 0:8] = f(inp["norm1_gain"]).reshape(2, 8, 128).transpose(0, 2, 1)
    sm[:, :, 8:16] = f(inp["norm2_gain"]).reshape(2, 8, 128).transpose(0, 2, 1)
    sm[:, :, 16] = f(inp["da_q_gain"])[:, p % 32]
    sm[:, :, 17] = f(inp["da_k_gain"])[:, p % 32]
    sm[:, :, 18] = f(inp["sw_q_gain"])[:, p % 64]
    sm[:, :, 19] = f(inp["sw_k_gain"])[:, p % 64]
    sm[:, :, 20] = f(inp["da_sub_gain"])[:, p % 64]
    sm[:, :, 21:27] = f(inp["sw_sink"])[:, None, :]
    cw = f(inp["lru_conv_w"]).reshape(2, 2, 4, 3, 128)
    sm[:, :, 27:51] = cw.transpose(0, 4, 1, 2, 3).reshape(2, 128, 24)
    for base, name in ((51, "lru_conv_b"), (57, "lru_ba"), (63, "lru_bx"), (69, "lru_lambda")):
        sm[:, :, base:base + 6] = f(inp[name]).reshape(2, 2, 3, 128).transpose(0, 3, 1, 2).reshape(2, 128, 6)
    fw = f(inp["ffn_conv_w"]).reshape(2, 3, 22, 128)
    sm[:, :, 75:141] = fw.transpose(0, 3, 1, 2).reshape(2, 128, 66)
    sm[:, :, 141:163] = f(inp["ffn_conv_b"]).reshape(2, 22, 128).transpose(0, 2, 1)
    lam = np.stack([f(inp["da_lam_q1"]), f(inp["da_lam_k1"]), f(inp["da_lam_q2"]), f(inp["da_lam_k2"])], axis=1)
    lamv = np.ascontiguousarray(np.broadcast_to(lam[:, None], (2, 128, 4, 32)))
    return {
        "w_mod": f(inp["w_mod"]), "b_mod2": b_mod2, "w_in": w_in_ext, "w_out": f(inp["w_out"]), "w_up": f(inp["w_up"]),
        "w_down": f(inp["w_down"]), "lru_bd": bd, "smalls": np.ascontiguousarray(sm), "lamv": lamv,
        "cmats": _const_mats(), "rope": _rope_tables(),
    }


def _prep_core(inp, b):
    x = np.asarray(inp["x"], dtype=np.float32)
    ctx = np.asarray(inp["ctx"], dtype=np.float32)
    c = np.asarray(inp["c"], dtype=np.float32)
    c_ctx = np.asarray(inp["c_ctx"], dtype=np.float32)
    xT = np.ascontiguousarray(np.concatenate([ctx[b].T, x[b].T], axis=1))
    cT = np.ascontiguousarray(np.stack([c[b].reshape(8, 128).T, c_ctx.reshape(8, 128).T], axis=-1))
    return {"xT": xT, "cT": cT}


_CACHE = {}


def kernel(**inputs):
    if "nc" not in _CACHE:
        _CACHE["nc"] = build_program()
    nc = _CACHE["nc"]
    shared = _prep_shared(inputs)
    n = 8
    in_maps = []
    for b in range(n):
        m = dict(shared)
        m.update(_prep_core(inputs, b))
        in_maps.append(m)
    res = run_bass_kernel_spmd(nc, in_maps, core_ids=list(range(n)))
    outs = [np.asarray(r["out"]).T for r in res.results]
    return np.ascontiguousarray(np.stack(outs, axis=0).astype(np.float32))
```

```python
import math
import numpy as np
import concourse.bass as bass
import concourse.mybir as mybir
from concourse.bass_utils import run_bass_kernel_spmd

F32 = mybir.dt.float32
BF16 = mybir.dt.bfloat16
U8 = mybir.dt.uint8
AF = mybir.ActivationFunctionType
ALU = mybir.AluOpType

ENGS = ("pe", "act", "dve", "pool", "sp")
NDMASEM = 44

D = 1024
NCTX = 256
NLAT = 4096
T = NCTX + NLAT
NS = 163
EPS = 1e-6
ARENA = 200 * 1024
SCALE_A = 32 ** -0.5
SCALE_B = 64 ** -0.5


class Buf:
    __slots__ = ("name", "w", "r")

    def __init__(self, name):
        self.name = name
        self.w = []
        self.r = []


class Sched:
    def __init__(self):
        self.q = {e: [] for e in ENGS}
        self.cnt = {e: 0 for e in ENGS}
        self.seen = {e: {} for e in ENGS}
        self.dma_n = 0
        self.dma_np = 0
        self.dma_tot = [0] * NDMASEM

    def bufs(self, name, n):
        return [Buf(f"{name}{i}") for i in range(n)]

    def _deps(self, eng, reads, writes):
        need = {}
        for b in reads:
            for (k, v) in b.w:
                if need.get(k, 0) < v:
                    need[k] = v
        for b in writes:
            for (k, v) in b.w:
                if need.get(k, 0) < v:
                    need[k] = v
            for (k, v) in b.r:
                if need.get(k, 0) < v:
                    need[k] = v
        seen = self.seen[eng]
        out = []
        for k, v in need.items():
            if seen.get(k, 0) < v:
                seen[k] = v
                out.append((k, v))
        return out

    def _commit(self, token, reads, writes):
        for b in writes:
            b.w = [token]
            b.r = []
        for b in reads:
            if b not in writes:
                b.r.append(token)
                if len(b.r) > 24:
                    m = {}
                    for (k, v) in b.r:
                        if m.get(k, 0) < v:
                            m[k] = v
                    b.r = list(m.items())

    def op(self, eng, fn, reads=(), writes=()):
        waits = self._deps(eng, reads, writes)
        self.cnt[eng] += 1
        token = (eng, self.cnt[eng])
        self._commit(token, reads, writes)
        self.q[eng].append((waits, fn, token))
        return token

    def dma(self, eng, fn, reads=(), writes=()):
        if eng == "pool":
            s = NDMASEM - 12 + self.dma_np % 12
            self.dma_np += 1
        else:
            s = self.dma_n % (NDMASEM - 12)
            self.dma_n += 1
        key = ("dma", s)
        waits = self._deps(eng, reads, writes)
        prev = self.dma_tot[s]
        if prev > 0 and self.seen[eng].get(key, 0) < prev:
            self.seen[eng][key] = prev
            waits.append((key, prev))
        self.dma_tot[s] = prev + 16
        token = (key, prev + 16)
        self._commit(token, reads, writes)
        self.q[eng].append((waits, fn, token))
        return token

    def barrier(self):
        for eng in ENGS:
            waits = []
            seen = self.seen[eng]
            for k in ENGS:
                v = self.cnt[k]
                if v > 0 and seen.get(k, 0) < v:
                    seen[k] = v
                    waits.append((k, v))
            for s in range(NDMASEM):
                v = self.dma_tot[s]
                key = ("dma", s)
                if v > 0 and seen.get(key, 0) < v:
                    seen[key] = v
                    waits.append((key, v))
            self.q[eng].append((waits, None, None))

    def emit(self, nc, sems, dsems):
        def semof(k):
            return dsems[k[1]] if isinstance(k, tuple) else sems[k]

        def run(eng):
            def body(h):
                for (waits, fn, token) in self.q[eng]:
                    for (k, v) in waits:
                        h.wait_ge(semof(k), v)
                    if fn is None:
                        continue
                    ins = fn(h)
                    k, v = token
                    ins.then_inc(semof(k), 16 if isinstance(k, tuple) else 1)
            return body

        with nc.Block() as block:
            block.tensor(run("pe"))
            block.scalar(run("act"))
            block.vector(run("dve"))
            block.gpsimd(run("pool"))
            block.sync(run("sp"))


class SemCtx:
    def __init__(self, nc):
        self.nc = nc
        self.stack = []

    def __enter__(self):
        sems = {}
        for e in ENGS:
            g = self.nc.semaphore("s_" + e)
            sems[e] = g.__enter__()
            self.stack.append(g)
        dsems = []
        for i in range(NDMASEM):
            g = self.nc.semaphore(f"d{i}")
            dsems.append(g.__enter__())
            self.stack.append(g)
        return sems, dsems

    def __exit__(self, *a):
        for g in reversed(self.stack):
            g.__exit__(None, None, None)
        return False


class Rot:
    def __init__(self, items):
        self.items = items
        self.i = 0

    def next(self):
        it = self.items[self.i % len(self.items)]
        self.i += 1
        return it


def build_program(nlayers=2, dbg=False, stop_after=None):
    nc = bass.Bass("TRN2", target_bir_lowering=False)

    def din(name, shape, dt=F32):
        return nc.dram_tensor(name, shape, dt, kind="ExternalInput").ap()

    xT_in = din("xT", [D, T])
    cT_in = din("cT", [128, 8, 2])
    w_mod = din("w_mod", [2, D, 6144])
    b_mod2 = din("b_mod2", [2, 2, 6144])
    w_in = din("w_in", [2, D, 2304])
    w_out = din("w_out", [2, D, D])
    w_up = din("w_up", [2, D, 5632])
    w_down = din("w_down", [2, 2816, D])
    lru_bd = din("lru_bd", [2, 12, 128, 128])
    smalls = din("smalls", [2, 128, NS])
    lamv = din("lamv", [2, 128, 4, 32])
    cmats = din("cmats", [7, 128, 128])
    rope = din("rope", [4, 128, T])
    out = nc.dram_tensor("out", [D, NLAT], F32, kind="ExternalOutput").ap()

    skind = "ExternalOutput" if dbg else "Internal"

    def dscr(name, shape, dt):
        return nc.dram_tensor(name, shape, dt, kind=skind).ap()

    xTs = dscr("xTs", [D, T], F32)
    qk = dscr("qk", [9, 128, T], BF16)
    vtok = dscr("vtok", [T, 768], BF16)
    cxs = dscr("cxs", [3, 128, T], F32)
    cgs = dscr("cgs", [3, 128, T], BF16)
    mix = dscr("mix", [D, T], BF16)
    h2s = dscr("h2s", [D, T + 4], BF16)
    modd = dscr("modd", [128, 96], F32) if dbg else None

    S = Sched()
    groups = [(0, 256, 1)] + [(256 + 512 * i, 512, 0) for i in range(8)]

    with (
        nc.sbuf_tensor("arena", [128, ARENA], U8) as arena,
        nc.sbuf_tensor("cst", [128, 4], F32) as cst,
        nc.sbuf_tensor("cmb", [128, 4, 128], BF16) as cmb,
        nc.sbuf_tensor("msk", [128, 2, 128], BF16) as msk,
        nc.sbuf_tensor("ident", [128, 128], F32) as ident,
        nc.sbuf_tensor("onesb", [128, 128], BF16) as onesb,
        nc.sbuf_tensor("smt", [128, NS], F32) as smt,
        nc.sbuf_tensor("ctile", [128, 8, 2], F32) as ctile,
        nc.sbuf_tensor("sc", [128, 8, 2], F32) as sc,
        nc.sbuf_tensor("modT", [128, 48, 2], F32) as modT,
        nc.sbuf_tensor("A1", [128, 8, 2], F32) as A1,
        nc.sbuf_tensor("A2", [128, 8, 2], F32) as A2,
        nc.sbuf_tensor("lamt", [128, 4, 32], F32) as lamt,
        nc.sbuf_tensor("sm2", [128, 64], F32) as sm2,
        nc.psum_tensor("ps", [128, 8, 512], F32) as ps,
        SemCtx(nc) as (sems, dsems),
    ):
        off = [0]

        def areset():
            off[0] = 0

        def alloc(shape, dt):
            esz = 2 if dt == BF16 else 4
            n = int(np.prod(shape))
            nb = (n * esz + 63) // 64 * 64
            assert off[0] + nb <= ARENA, (off[0], nb)
            a = arena[:, off[0]:off[0] + n * esz].bitcast(dt)
            off[0] += nb
            if len(shape) == 2:
                a = a.rearrange("p (a b) -> p a b", a=shape[0])
            elif len(shape) == 3:
                a = a.rearrange("p (a b c) -> p a b c", a=shape[0], b=shape[1])
            return a

        nb_ = [0]

        def B(name="b"):
            nb_[0] += 1
            return Buf(f"{name}{nb_[0]}")

        psb = [B("ps") for _ in range(8)]
        b_cst, b_cmb, b_msk, b_ident, b_ones, b_smt, b_ct, b_sc, b_modT, b_A, b_lam, b_sm2 = [B("c") for _ in range(12)]

        def psflat(b0, n):
            return ps[:, b0:b0 + (n + 511) // 512, :].rearrange("p b n -> p (b n)")[:, 0:n]

        def rev(ap2d):
            (pst, pn), (fs, fn_) = ap2d.ap
            from concourse.ap import AP
            return AP(ap2d.tensor, ap2d.offset + (fn_ - 1) * fs, [[pst, pn], [-fs, fn_]])

        S.op("pool", lambda g: g.memset(cst[:, 0:1], EPS), writes=[b_cst])
        S.op("pool", lambda g: g.memset(cst[:, 1:2], 1.0), writes=[b_cst])
        S.op("pool", lambda g: g.memset(cst[:, 2:3], 0.0), writes=[b_cst])
        S.op("pool", lambda g: g.memset(onesb[:], 1.0), writes=[b_ones])
        S.dma("pool", lambda g: g.dma_start(out=cmb[:], in_=cmats[0:4].rearrange("c p n -> p c n")), writes=[b_cmb])
        S.dma("pool", lambda g: g.dma_start(out=msk[:], in_=cmats[5:7].rearrange("c p n -> p c n")), writes=[b_msk])
        S.dma("sp", lambda g: g.dma_start(out=ident[:], in_=cmats[4]), writes=[b_ident])
        S.dma("sp", lambda g: g.dma_start(out=ctile[:], in_=cT_in), writes=[b_ct])
        S.op("act", lambda g: g.activation(out=sc[:], in_=ctile[:], func=AF.Silu), reads=[b_ct], writes=[b_sc])

        def run_layer(l):
            last = (l == nlayers - 1)
            ctx_out = not last
            lam_init = 0.8 - 0.6 * math.exp(-0.3 * l)
            xsrc = xT_in if l == 0 else xTs
            xsrc_k = xsrc.rearrange("(k p) t -> p k t", p=128)
            xTs_k = xTs.rearrange("(k p) t -> p k t", p=128)

            sh = {}
            def _ph():
                S.barrier()
                areset()
                S.dma("sp", lambda g, l=l: g.dma_start(out=smt[:], in_=smalls[l]), writes=[b_smt])
                S.dma("sp", lambda g, l=l: g.dma_start(out=lamt[:], in_=lamv[l]), writes=[b_lam])
                bm = alloc([6144], F32)
                modrow = alloc([6144], F32)
                wms = [alloc([8, 512], F32) for _ in range(2)]
                b_bm, b_modrow = B(), B()
                b_wm = [B(), B()]
                S.dma("sp", lambda g, l=l: g.dma_start(out=bm[0:2, :], in_=b_mod2[l]), writes=[b_bm])
                wmk = w_mod[l].rearrange("(k p) n -> p k n", p=128)
                for cg in range(12):
                    wm, bw = wms[cg % 2], b_wm[cg % 2]
                    S.dma("sp", lambda g, wm=wm, cg=cg: g.dma_start(out=wm, in_=wmk[:, :, cg * 512:(cg + 1) * 512]), writes=[bw])

                    def mm(g, wm=wm, cg=cg):
                        for k in range(8):
                            ins = g.matmul(ps[0:2, cg % 2, :], lhsT=sc[:, k, :], rhs=wm[:, k, :], start=(k == 0), stop=(k == 7))
                        return ins
                    S.op("pe", mm, reads=[bw, b_sc], writes=[psb[cg % 2]])
                    S.op("dve", lambda g, cg=cg: g.tensor_tensor(out=modrow[0:2, cg * 512:(cg + 1) * 512], in0=ps[0:2, cg % 2, :],
                                                                 in1=bm[0:2, cg * 512:(cg + 1) * 512], op=ALU.add),
                         reads=[psb[cg % 2], b_bm], writes=[b_modrow])

                def tr(g):
                    for j in range(48):
                        ins = g.transpose(ps[:, 2, 2 * j:2 * j + 2], modrow[0:2, j * 128:(j + 1) * 128], ident[0:2, 0:2])
                    return ins
                S.op("pe", tr, reads=[b_modrow, b_ident], writes=[psb[2]])
                S.op("dve", lambda g: g.tensor_copy(out=modT[:].rearrange("p a b -> p (a b)"), in_=ps[:, 2, 0:96]), reads=[psb[2]], writes=[b_modT])
                for v in range(2):
                    S.op("dve", lambda g, v=v: g.scalar_tensor_tensor(out=A1[:, :, v], in0=modT[:, 8:16, v], scalar=1.0, in1=smt[:, 0:8],
                                                                      op0=ALU.add, op1=ALU.mult), reads=[b_modT, b_smt], writes=[b_A])
                    S.op("dve", lambda g, v=v: g.scalar_tensor_tensor(out=A2[:, :, v], in0=modT[:, 32:40, v], scalar=1.0, in1=smt[:, 8:16],
                                                                      op0=ALU.add, op1=ALU.mult), reads=[b_modT, b_smt], writes=[b_A])
                if dbg and l == 0:
                    S.dma("pool", lambda g: g.dma_start(out=modd, in_=modT[:].rearrange("p a b -> p (a b)")), reads=[b_modT])
                S.op("act", lambda g: g.mul(out=sm2[:, 0:1], in_=smt[:, 20:21], mul=float(1.0 - lam_init)), reads=[b_smt], writes=[b_sm2])
                S.op("act", lambda g: g.activation(out=sm2[:, 2:8], in_=smt[:, 21:27], func=AF.Exp), reads=[b_smt], writes=[b_sm2])
                S.op("dve", lambda g: g.tensor_tensor(out=lamt[:, 0, :], in0=lamt[:, 0, :], in1=lamt[:, 1, :], op=ALU.mult), reads=[b_lam], writes=[b_lam])
                S.op("dve", lambda g: g.tensor_tensor(out=lamt[:, 2, :], in0=lamt[:, 2, :], in1=lamt[:, 3, :], op=ALU.mult), reads=[b_lam], writes=[b_lam])
                S.op("dve", lambda g: g.tensor_reduce(out=sm2[:, 20:21], in_=lamt[:, 0, :], axis=mybir.AxisListType.X, op=ALU.add), reads=[b_lam], writes=[b_sm2])
                S.op("dve", lambda g: g.tensor_reduce(out=sm2[:, 21:22], in_=lamt[:, 2, :], axis=mybir.AxisListType.X, op=ALU.add), reads=[b_lam], writes=[b_sm2])
                S.op("act", lambda g: g.activation(out=sm2[:, 22:24], in_=sm2[:, 20:22], func=AF.Exp), reads=[b_sm2], writes=[b_sm2])
                S.op("dve", lambda g: g.scalar_tensor_tensor(out=sm2[:, 1:2], in0=sm2[:, 23:24], scalar=float(-lam_init), in1=sm2[:, 22:23],
                                                             op0=ALU.add, op1=ALU.subtract), reads=[b_sm2], writes=[b_sm2])
                L_ = smt[:, 69:75]
                S.op("dve", lambda g: g.tensor_scalar_mul(out=sm2[:, 24:30], in0=L_, scalar1=-1.0), reads=[b_smt], writes=[b_sm2])
                S.op("dve", lambda g: g.tensor_tensor(out=sm2[:, 48:54], in0=sm2[:, 24:30], in1=L_, op=ALU.max), reads=[b_smt, b_sm2], writes=[b_sm2])
                S.op("act", lambda g: g.activation(out=sm2[:, 30:36], in_=sm2[:, 48:54], func=AF.Exp, scale=-1.0), reads=[b_sm2], writes=[b_sm2])
                S.op("dve", lambda g: g.tensor_scalar_add(out=sm2[:, 36:42], in0=sm2[:, 30:36], scalar1=1.0), reads=[b_sm2], writes=[b_sm2])
                S.op("act", lambda g: g.activation(out=sm2[:, 42:48], in_=sm2[:, 36:42], func=AF.Ln), reads=[b_sm2], writes=[b_sm2])
                S.op("dve", lambda g: g.tensor_scalar(out=sm2[:, 36:42], in0=sm2[:, 36:42], scalar1=-1.0, scalar2=1e-30, op0=ALU.add, op1=ALU.max),
                     reads=[b_sm2], writes=[b_sm2])
                S.op("dve", lambda g: g.reciprocal(out=sm2[:, 54:60], in_=sm2[:, 36:42]), reads=[b_sm2], writes=[b_sm2])
                S.op("dve", lambda g: g.tensor_tensor(out=sm2[:, 30:36], in0=sm2[:, 30:36], in1=sm2[:, 54:60], op=ALU.mult), reads=[b_sm2], writes=[b_sm2])
                S.op("dve", lambda g: g.tensor_tensor(out=sm2[:, 30:36], in0=sm2[:, 30:36], in1=sm2[:, 42:48], op=ALU.mult), reads=[b_sm2], writes=[b_sm2])
                S.op("dve", lambda g: g.tensor_scalar_max(out=sm2[:, 24:30], in0=sm2[:, 24:30], scalar1=0.0), reads=[b_sm2], writes=[b_sm2])
                S.op("dve", lambda g: g.tensor_tensor(out=sm2[:, 24:30], in0=sm2[:, 24:30], in1=sm2[:, 30:36], op=ALU.add), reads=[b_sm2], writes=[b_sm2])
                S.op("dve", lambda g: g.tensor_scalar_mul(out=sm2[:, 8:14], in0=sm2[:, 24:30], scalar1=-8.0), reads=[b_sm2], writes=[b_sm2])
                S.op("dve", lambda g: g.tensor_scalar_mul(out=sm2[:, 14:20], in0=sm2[:, 24:30], scalar1=-16.0), reads=[b_sm2], writes=[b_sm2])
                if stop_after == ("M", l):
                    return True

                return False
            if _ph():
                return True
            def _ph():
                S.barrier()
                areset()
                win = alloc([8, 2304], BF16)
                b_win = B()
                wik = w_in[l].rearrange("(k p) n -> p k n", p=128)
                for hh in range(2):
                    S.dma("pool", lambda g, hh=hh: g.dma_start(out=win[:, :, hh * 1152:(hh + 1) * 1152], in_=wik[:, :, hh * 1152:(hh + 1) * 1152]), writes=[b_win])
                xgs = [(alloc([8, 512], F32), B()) for _ in range(2)]
                rps = [(alloc([4, 512], F32), B()) for _ in range(2)]
                hTs = [(alloc([8, 512], BF16), B()) for _ in range(2)]
                sq, b_sq = alloc([8, 512], BF16), B()
                tmp8, b_tmp8 = alloc([8, 512], F32), B()
                rstd, b_rstd = alloc([512], F32), B()
                NSET = 3
                sets = [dict(qf=alloc([2, 512], F32), sqb=alloc([2, 512], BF16), rr=alloc([2, 512], F32), qn=alloc([2, 512], F32), qo=alloc([2, 512], BF16),
                             b={n: B() for n in ("qf", "sqb", "rr", "qn", "qo")}, pb=1 + 2 * i) for i in range(NSET)]
                vos = [(alloc([6, 128], BF16), B()) for _ in range(2)]
                for (vo, bvo) in vos:
                    S.op("pool", lambda g, vo=vo: g.memset(vo[:, :, 64:128], 1.0), writes=[bvo])
                ropek = rope.rearrange("r p t -> p r t")
                from concourse.ap import AP as _AP

                def bc3(ap2, nb):
                    (pst, pn), (fs, fn_) = ap2.ap
                    return _AP(ap2.tensor, ap2.offset, [[pst, pn], [0, nb], [fs, fn_]])

                batches = [(0, 2, 16, 0), (2, 2, 17, 0), (4, 2, 18, 1), (6, 1, 18, 1), (7, 2, 19, 1)]

                def prologue(gi):
                    t0, W, v = groups[gi]
                    xg, bxg = xgs[gi % 2]
                    rp, brp = rps[gi % 2]
                    hT, bhT = hTs[gi % 2]

                    def s0():
                        S.dma("sp", lambda g: g.dma_start(out=xg[:, :, 0:W], in_=xsrc_k[:, :, t0:t0 + W]), writes=[bxg])
                        S.dma("sp", lambda g: g.dma_start(out=rp[:, :, 0:W], in_=ropek[:, :, t0:t0 + W]), writes=[brp])

                    def s1():
                        S.op("act", lambda g: g.activation(out=sq[:, :, 0:W], in_=xg[:, :, 0:W], func=AF.Square), reads=[bxg], writes=[b_sq])

                    def s2():
                        def ssmm(g):
                            for k in range(8):
                                ins = g.matmul(ps[:, 0, 0:W], lhsT=onesb[:], rhs=sq[:, k, 0:W], start=(k == 0), stop=(k == 7))
                            return ins
                        S.op("pe", ssmm, reads=[b_sq, b_ones], writes=[psb[0]])

                    def s3():
                        S.op("act", lambda g: g.activation(out=rstd[:, 0:W], in_=ps[:, 0, 0:W], func=AF.Ln, scale=1.0 / D, bias=cst[:, 0:1]),
                             reads=[psb[0], b_cst], writes=[b_rstd])
                        S.op("act", lambda g: g.activation(out=rstd[:, 0:W], in_=rstd[:, 0:W], func=AF.Exp, scale=-0.5), reads=[b_rstd], writes=[b_rstd])

                    def s4():
                        S.op("dve", lambda g: g.tensor_tensor(out=tmp8[:, :, 0:W], in0=xg[:, :, 0:W], in1=bc3(rstd[:, 0:W], 8), op=ALU.mult),
                             reads=[bxg, b_rstd], writes=[b_tmp8])

                    def s5():
                        for k in range(8):
                            S.op("act", lambda g, k=k: g.activation(
                                out=hT[:, k, 0:W], in_=tmp8[:, k, 0:W], func=AF.Identity, scale=A1[:, k, v:v + 1], bias=modT[:, k, v:v + 1]),
                                reads=[b_tmp8, b_modT, b_A], writes=[bhT])
                    return [s0, s1, s2, s3, s4, s5]

                def proj(oc0, nb, pb0, hT, W):
                    def f(g):
                        for i in range(nb):
                            for k in range(8):
                                ins = g.matmul(ps[:, pb0 + i, 0:W], lhsT=win[:, k, (oc0 + i) * 128:(oc0 + i + 1) * 128], rhs=hT[:, k, 0:W], start=(k == 0), stop=(k == 7))
                        return ins
                    return f

                def qkbatch(gi, bi, st):
                    t0, W, v = groups[gi]
                    rp, brp = rps[gi % 2]
                    hT, bhT = hTs[gi % 2]
                    oc0, nb, gcol, mi = batches[bi]
                    bb = st["b"]
                    pb0 = st["pb"]
                    pbs = psb[pb0:pb0 + nb]
                    inv = 1.0 / 32 if mi == 0 else 1.0 / 64
                    rc, rsn = (0, 1) if mi == 0 else (2, 3)
                    qf, sqb, rr, qn, qo = (st[n][:, 0:nb, 0:W] for n in ("qf", "sqb", "rr", "qn", "qo"))
                    pv = ps[:, pb0:pb0 + nb, 0:W]

                    def s0():
                        S.op("pe", proj(oc0, nb, pb0, hT, W), reads=[b_win, bhT], writes=pbs)

                    def s1():
                        S.op("act", lambda g: g.activation(out=qf, in_=pv, func=AF.Identity), reads=pbs, writes=[bb["qf"]])
                        S.op("act", lambda g: g.activation(out=sqb, in_=pv, func=AF.Square), reads=pbs, writes=[bb["sqb"]])

                    def s2():
                        def smm(g):
                            for i in range(nb):
                                ins = g.matmul(ps[:, pb0 + i, 0:W], lhsT=cmb[:, mi, :], rhs=st["sqb"][:, i, 0:W], start=True, stop=True)
                            return ins
                        S.op("pe", smm, reads=[bb["sqb"], b_cmb], writes=pbs)

                    def s3():
                        S.op("act", lambda g: g.activation(out=rr, in_=pv, func=AF.Ln, scale=inv, bias=cst[:, 0:1]), reads=pbs + [b_cst], writes=[bb["rr"]])
                        S.op("act", lambda g: g.activation(out=rr, in_=rr, func=AF.Exp, scale=-0.5), reads=[bb["rr"]], writes=[bb["rr"]])

                    def s4():
                        S.op("dve", lambda g: g.scalar_tensor_tensor(out=qn, in0=qf, scalar=smt[:, gcol:gcol + 1], in1=rr, op0=ALU.mult, op1=ALU.mult),
                             reads=[bb["qf"], bb["rr"], b_smt], writes=[bb["qn"]])

                    def s5():
                        S.op("dve", lambda g: g.tensor_copy(out=sqb, in_=qn), reads=[bb["qn"]], writes=[bb["sqb"]])

                    def s6():
                        def rmm(g):
                            for i in range(nb):
                                ins = g.matmul(ps[:, pb0 + i, 0:W], lhsT=cmb[:, 2 + mi, :], rhs=st["sqb"][:, i, 0:W], start=True, stop=True)
                            return ins
                        S.op("pe", rmm, reads=[bb["sqb"], b_cmb], writes=pbs)
                        S.op("pool", lambda g: g.tensor_tensor(out=qf, in0=qn, in1=bc3(rp[:, rc, 0:W], nb), op=ALU.mult), reads=[bb["qn"], brp], writes=[bb["qf"]])

                    def s7():
                        S.op("dve", lambda g: g.tensor_tensor(out=rr, in0=pv, in1=bc3(rp[:, rsn, 0:W], nb), op=ALU.mult), reads=pbs + [brp], writes=[bb["rr"]])

                    def s8():
                        S.op("pool", lambda g: g.tensor_tensor(out=qo, in0=qf, in1=rr, op=ALU.add), reads=[bb["qf"], bb["rr"]], writes=[bb["qo"]])
                        S.dma("sp", lambda g: g.dma_start(out=qk[oc0:oc0 + nb].rearrange("c p t -> p c t")[:, :, t0:t0 + W], in_=qo), reads=[bb["qo"]])
                    return [s0, s1, s2, s3, s4, s5, s6, s7, s8]

                def cbatch(gi, which, st, c0, nb):
                    t0, W, v = groups[gi]
                    hT, bhT = hTs[gi % 2]
                    bb = st["b"]
                    pb0 = st["pb"]
                    pbs = psb[pb0:pb0 + nb]
                    pv = ps[:, pb0:pb0 + nb, 0:W]

                    def s0():
                        S.op("pe", proj((9 if which == 0 else 12) + c0, nb, pb0, hT, W), reads=[b_win, bhT], writes=pbs)

                    def s1():
                        if which == 0:
                            S.op("act", lambda g: g.activation(out=st["qn"][:, 0:nb, 0:W], in_=pv, func=AF.Identity), reads=pbs, writes=[bb["qn"]])
                            S.dma("sp", lambda g: g.dma_start(out=cxs[c0:c0 + nb].rearrange("c p t -> p c t")[:, :, t0:t0 + W], in_=st["qn"][:, 0:nb, 0:W]), reads=[bb["qn"]])
                        else:
                            S.op("act", lambda g: g.activation(out=st["qo"][:, 0:nb, 0:W], in_=pv, func=AF.Gelu_apprx_tanh), reads=pbs, writes=[bb["qo"]])
                            S.dma("sp", lambda g: g.dma_start(out=cgs[c0:c0 + nb].rearrange("c p t -> p c t")[:, :, t0:t0 + W], in_=st["qo"][:, 0:nb, 0:W]), reads=[bb["qo"]])
                    return [s0, s1]

                def vitem(gi):
                    t0, W, v = groups[gi]
                    hT, bhT = hTs[gi % 2]

                    def mk(tt):
                        def s():
                            def vmm(g):
                                for k in range(8):
                                    ins = g.matmul(ps[:, 7, 0:384], lhsT=hT[:, k, tt * 128:(tt + 1) * 128], rhs=win[:, k, 1920:2304], start=(k == 0), stop=(k == 7))
                                return ins
                            S.op("pe", vmm, reads=[b_win, bhT], writes=[psb[7]])
                            vo, bvo = vos[tt % 2]
                            S.op("dve", lambda g: g.tensor_copy(out=vo[:, :, 0:64], in_=ps[:, 7, 0:384].rearrange("p (h d) -> p h d", h=6)), reads=[psb[7]], writes=[bvo])
                            S.dma("sp", lambda g: g.dma_start(out=vtok[t0 + tt * 128:t0 + (tt + 1) * 128, :], in_=vo[:].rearrange("p h d -> p (h d)")), reads=[bvo])
                        return s
                    return [mk(tt) for tt in range(W // 128)]

                items = []
                pidx = {}
                nset = [0]

                def add(stages, res=None, deps=()):
                    items.append((stages, res, list(deps)))
                    return len(items) - 1

                pidx[0] = add(prologue(0))
                for gi in range(len(groups)):
                    for bi in range(len(batches)):
                        st = sets[nset[0] % NSET]
                        add(qkbatch(gi, bi, st), res=("set", nset[0] % NSET), deps=[pidx[gi]])
                        nset[0] += 1
                        if bi == 1 and gi + 1 < len(groups):
                            pidx[gi + 1] = add(prologue(gi + 1), res=("pro",))
                    for which in range(2):
                        for (c0, nb) in ((0, 2), (2, 1)):
                            st = sets[nset[0] % NSET]
                            add(cbatch(gi, which, st, c0, nb), res=("set", nset[0] % NSET), deps=[pidx[gi]])
                            nset[0] += 1
                    add(vitem(gi), res=("v",), deps=[pidx[gi]])
                SK = 3
                start, end_ = [], []
                resend = {}
                for i, (stages, res, deps) in enumerate(items):
                    s = 0 if i == 0 else start[i - 1] + SK
                    if res is not None and res in resend:
                        s = max(s, resend[res])
                    for dI in deps:
                        s = max(s, end_[dI])
                    start.append(s)
                    end_.append(s + len(stages))
                    if res is not None:
                        resend[res] = s + len(stages)
                tmax = max(end_)
                for t in range(tmax):
                    for i, (stages, res, deps) in enumerate(items):
                        k = t - start[i]
                        if 0 <= k < len(stages):
                            stages[k]()
                if stop_after == ("A", l):
                    return True
                return False
            if _ph():
                return True
            def _ph():
                S.barrier()
                areset()
                bdw, b_bdw = alloc([12, 128], BF16), B()
                S.dma("pool", lambda g: g.dma_start(out=bdw, in_=lru_bd[l].rearrange("c p n -> p c n")), writes=[b_bdw])
                xxs = [(alloc([T], F32), B()) for _ in range(2)]
                xcs = [(alloc([T], F32), B()) for _ in range(2)]
                xcbs = [(alloc([T], BF16), B()) for _ in range(2)]
                rr_, b_r = alloc([T], F32), B()
                ii_, b_i = alloc([T], F32), B()
                aa_, b_a = alloc([T], F32), B()
                mm_, b_m = alloc([T], F32), B()
                hh_ = [alloc([T], F32), alloc([T], F32)]
                b_h = [B(), B()]
                gg_, b_g = alloc([T], BF16), B()
                segs = [(0, NCTX), (NCTX, T)]
                its = [(cc, d) for cc in range(3) for d in range(2)]

                def conv(n):
                    cc, d = its[n]
                    xx, b_xx = xxs[cc % 2]
                    xc, b_xc = xcs[n % 2]
                    xcb, b_xcb = xcbs[n % 2]
                    if d == 0:
                        S.dma("sp", lambda g: g.dma_start(out=xx, in_=cxs[cc]), writes=[b_xx])
                    wcol = lambda k: smt[:, 27 + d * 12 + k * 3 + cc:28 + d * 12 + k * 3 + cc]
                    bcol = smt[:, 51 + d * 3 + cc:52 + d * 3 + cc]
                    S.op("dve", lambda g: g.tensor_scalar(out=xc, in0=xx, scalar1=wcol(3), scalar2=bcol, op0=ALU.mult, op1=ALU.add),
                         reads=[b_xx, b_smt], writes=[b_xc])
                    for k in range(3):
                        s_ = 3 - k
                        for (a_, e_) in segs:
                            if d == 0:
                                dst, src = (a_ + s_, e_), (a_, e_ - s_)
                            else:
                                dst, src = (a_, e_ - s_), (a_ + s_, e_)
                            S.op("dve", lambda g, dst=dst, src=src, k=k: g.scalar_tensor_tensor(
                                out=xc[:, dst[0]:dst[1]], in0=xx[:, src[0]:src[1]], scalar=wcol(k), in1=xc[:, dst[0]:dst[1]], op0=ALU.mult, op1=ALU.add),
                                reads=[b_xx, b_xc, b_smt], writes=[b_xc])
                    S.op("act", lambda g: g.activation(out=xcb, in_=xc, func=AF.Identity), reads=[b_xc], writes=[b_xcb])

                def gates(n):
                    cc, d = its[n]
                    xc, b_xc = xcs[n % 2]
                    xcb, b_xcb = xcbs[n % 2]
                    ia = (d * 2 + 0) * 3 + cc
                    ix = (d * 2 + 1) * 3 + cc
                    for sg0 in range(0, T, 2048):
                        sgw = min(2048, T - sg0)

                        def gmm(g, sg0=sg0, sgw=sgw):
                            for q0 in range(0, sgw, 512):
                                w_ = min(512, sgw - q0)
                                g.matmul(ps[:, q0 // 512, 0:w_], lhsT=bdw[:, ia, :], rhs=xcb[:, sg0 + q0:sg0 + q0 + w_], start=True, stop=True)
                                ins = g.matmul(ps[:, 4 + q0 // 512, 0:w_], lhsT=bdw[:, ix, :], rhs=xcb[:, sg0 + q0:sg0 + q0 + w_], start=True, stop=True)
                            return ins
                        S.op("pe", gmm, reads=[b_xcb, b_bdw], writes=psb)
                        S.op("act", lambda g, sg0=sg0, sgw=sgw: g.activation(
                            out=rr_[:, sg0:sg0 + sgw], in_=psflat(0, sgw), func=AF.Sigmoid, bias=smt[:, 57 + d * 3 + cc:58 + d * 3 + cc]),
                            reads=psb[0:4] + [b_smt], writes=[b_r])
                        S.op("act", lambda g, sg0=sg0, sgw=sgw: g.activation(
                            out=ii_[:, sg0:sg0 + sgw], in_=psflat(4, sgw), func=AF.Sigmoid, bias=smt[:, 63 + d * 3 + cc:64 + d * 3 + cc]),
                            reads=psb[4:8] + [b_smt], writes=[b_i])
                    S.op("act", lambda g: g.activation(out=aa_, in_=rr_, func=AF.Exp, scale=sm2[:, 8 + d * 3 + cc:9 + d * 3 + cc]),
                         reads=[b_r, b_sm2], writes=[b_a])
                    S.op("act", lambda g: g.activation(out=mm_, in_=rr_, func=AF.Exp, scale=sm2[:, 14 + d * 3 + cc:15 + d * 3 + cc]),
                         reads=[b_r, b_sm2], writes=[b_m])
                    S.op("act", lambda g: g.activation(out=mm_, in_=mm_, func=AF.Sqrt, scale=-1.0, bias=cst[:, 1:2]), reads=[b_m, b_cst], writes=[b_m])
                    S.op("pool", lambda g: g.tensor_tensor(out=ii_, in0=ii_, in1=xc, op=ALU.mult), reads=[b_i, b_xc], writes=[b_i])

                def scan(n):
                    cc, d = its[n]
                    S.op("dve", lambda g: g.tensor_tensor(out=mm_, in0=mm_, in1=ii_, op=ALU.mult), reads=[b_m, b_i], writes=[b_m])
                    hd = hh_[d]
                    if d == 0:
                        S.op("dve", lambda g: g.tensor_tensor_scan(out=hd, data0=aa_, data1=mm_, initial=0.0, op0=ALU.mult, op1=ALU.add),
                             reads=[b_a, b_m], writes=[b_h[d]])
                    else:
                        S.op("dve", lambda g: g.tensor_tensor_scan(out=rev(hd[:, 0:NCTX]), data0=rev(aa_[:, 0:NCTX]), data1=rev(mm_[:, 0:NCTX]),
                                                                   initial=0.0, op0=ALU.mult, op1=ALU.add), reads=[b_a, b_m], writes=[b_h[d]])
                        S.op("dve", lambda g: g.tensor_tensor_scan(out=rev(hd[:, NCTX:T]), data0=rev(aa_[:, NCTX:T]), data1=rev(mm_[:, NCTX:T]),
                                                                   initial=hd[:, 0:1], op0=ALU.mult, op1=ALU.add), reads=[b_a, b_m, b_h[d]], writes=[b_h[d]])
                        S.dma("sp", lambda g: g.dma_start(out=gg_, in_=cgs[cc]), writes=[b_g])
                        S.op("pool", lambda g: g.tensor_tensor(out=hh_[0], in0=hh_[0], in1=hh_[1], op=ALU.add), reads=[b_h[0], b_h[1]], writes=[b_h[0]])
                        yy_, b_y = xcbs[n % 2]
                        S.op("pool", lambda g: g.tensor_tensor(out=yy_, in0=hh_[0], in1=gg_, op=ALU.mult), reads=[b_h[0], b_g], writes=[b_y])
                        S.dma("pool", lambda g: g.dma_start(out=mix[640 + cc * 128:640 + (cc + 1) * 128, :], in_=yy_), reads=[b_y])

                conv(0)
                for n in range(len(its)):
                    gates(n)
                    if n + 1 < len(its):
                        conv(n + 1)
                    scan(n)
                if stop_after == ("B3", l):
                    return True
                return False
            if _ph():
                return True
            def _ph():
                S.barrier()
                areset()
                Vt, b_Vt = alloc([34, 768], BF16), B()
                vtk = vtok.rearrange("(kt p) c -> p kt c", p=128)
                for q4 in range(0, 34, 9):
                    q5 = min(34, q4 + 9)
                    S.dma("sp", lambda g, q4=q4, q5=q5: g.dma_start(out=Vt[:, q4:q5, :], in_=vtk[:, q4:q5, :]), writes=[b_Vt])
                b1_mark = off[0]
                sh.update(Vt=Vt, b_Vt=b_Vt, b1_mark=b1_mark)
                KT, b_KT = alloc([2, T], BF16), B()
                S.dma("sp", lambda g: g.dma_start(out=KT, in_=qk[2:4].rearrange("c p t -> p c t")), writes=[b_KT])
                QTR = Rot([(alloc([2, 512], BF16), B()) for _ in range(2)])
                ER = [Rot([(alloc([2, 512], BF16), B()) for _ in range(2)]) for _ in range(2)]
                evR = Rot([dict(o=alloc([4, 512], F32), l=alloc([4, 512], F32), bo=B(), bl=B()) for _ in range(2)])
                finR = Rot([dict(oo=alloc([512], F32), osq=alloc([512], BF16), rr=alloc([512], F32), y=alloc([512], BF16),
                                 b={n: B() for n in ("oo", "osq", "rr", "y")}) for _ in range(4)])
                sbR = Rot([0, 1, 2])
                pending = []

                def flush():
                    while pending:
                        pending.pop(0)()
                for gi, (t0, W, v) in enumerate(groups):
                    if gi == 0 and not ctx_out:
                        continue
                    nkt = 2 if gi == 0 else 34
                    QT, bQT = QTR.next()
                    S.dma("sp", lambda g, QT=QT, t0=t0, W=W: g.dma_start(out=QT[:, :, 0:W], in_=qk[0:2].rearrange("c p t -> p c t")[:, :, t0:t0 + W]), writes=[bQT])
                    for c in range(2):
                        Ecur = [None] * 2
                        Eprev = [None] * 2
                        for kt in range(nkt + 1):
                            if kt == min(22, nkt):
                                flush()
                            if kt < nkt:
                                for p in range(2):
                                    E, bE = ER[p].next()
                                    Ecur[p] = (E, bE)

                                    def qk2(g, p=p, c=c, kt=kt, QT=QT, W=W):
                                        for j in (2 * p, 2 * p + 1):
                                            ins = g.matmul(ps[:, j, 0:W], lhsT=KT[32 * j:32 * j + 32, c, kt * 128:(kt + 1) * 128], rhs=QT[32 * j:32 * j + 32, c, 0:W],
                                                           start=True, stop=True, tile_position=(32 * j, 0))
                                        return ins
                                    S.op("pe", qk2, reads=[b_KT, bQT], writes=[psb[2 * p], psb[2 * p + 1]])
                                    S.op("act", lambda g, p=p, E=E, W=W: g.activation(out=E[:, :, 0:W], in_=ps[:, 2 * p:2 * p + 2, 0:W], func=AF.Exp, scale=SCALE_A),
                                         reads=[psb[2 * p], psb[2 * p + 1]], writes=[bE])
                            if kt >= 1:
                                for p in range(2):
                                    E, bE = Eprev[p]
                                    h = 2 * c + p

                                    def pv2(g, p=p, E=E, h=h, kt=kt, W=W, nkt=nkt):
                                        for m in range(2):
                                            ins = g.matmul(ps[:, 4 + 2 * p + m, 0:W], lhsT=Vt[:, kt - 1, h * 128:(h + 1) * 128], rhs=E[:, m, 0:W], start=(kt == 1), stop=(kt == nkt))
                                        return ins
                                    S.op("pe", pv2, reads=[bE, b_Vt], writes=[psb[4 + 2 * p], psb[5 + 2 * p]])
                            Eprev = list(Ecur)
                        ev = evR.next()
                        S.op("dve", lambda g, ev=ev, W=W: g.tensor_copy(out=ev["o"][0:64, :, 0:W], in_=ps[0:64, 4:8, 0:W]), reads=psb[4:8], writes=[ev["bo"]])
                        S.op("dve", lambda g, ev=ev, W=W: g.tensor_copy(out=ev["l"][0:64, :, 0:W], in_=ps[64:128, 4:8, 0:W]), reads=psb[4:8], writes=[ev["bl"]])
                        S.op("dve", lambda g, ev=ev, W=W: g.reciprocal(out=ev["l"][0:64, :, 0:W], in_=ev["l"][0:64, :, 0:W]), reads=[ev["bl"]], writes=[ev["bl"]])
                        S.op("pool", lambda g, ev=ev, W=W: g.tensor_tensor(out=ev["o"][0:64, :, 0:W], in0=ev["o"][0:64, :, 0:W], in1=ev["l"][0:64, :, 0:W], op=ALU.mult),
                             reads=[ev["bo"], ev["bl"]], writes=[ev["bo"]])
                        for hh in range(2):
                            h = 2 * c + hh
                            f = finR.next()
                            fb = f["b"]
                            S.op("dve", lambda g, f=f, ev=ev, hh=hh, W=W: g.scalar_tensor_tensor(
                                out=f["oo"][0:64, 0:W], in0=ev["o"][0:64, 2 * hh + 1, 0:W], scalar=sm2[0:64, 1:2], in1=ev["o"][0:64, 2 * hh, 0:W],
                                op0=ALU.mult, op1=ALU.add), reads=[ev["bo"], b_sm2], writes=[fb["oo"]])
                            S.op("pool", lambda g, f=f, W=W: g.tensor_tensor(out=f["osq"][0:64, 0:W], in0=f["oo"][0:64, 0:W], in1=f["oo"][0:64, 0:W], op=ALU.mult),
                                 reads=[fb["oo"]], writes=[fb["osq"]])

                            def late(f=f, fb=fb, h=h, t0=t0, W=W):
                                S.op("pe", lambda g: g.matmul(ps[0:64, 0, 0:W], lhsT=onesb[0:64, 0:64], rhs=f["osq"][0:64, 0:W], start=True, stop=True),
                                     reads=[fb["osq"], b_ones], writes=[psb[0]])
                                S.op("act", lambda g: g.activation(out=f["rr"][0:64, 0:W], in_=ps[0:64, 0, 0:W], func=AF.Ln, scale=1.0 / 64, bias=cst[0:64, 0:1]),
                                     reads=[psb[0], b_cst], writes=[fb["rr"]])
                                S.op("act", lambda g: g.activation(out=f["rr"][0:64, 0:W], in_=f["rr"][0:64, 0:W], func=AF.Exp, scale=-0.5),
                                     reads=[fb["rr"]], writes=[fb["rr"]])
                                S.op("dve", lambda g: g.scalar_tensor_tensor(out=f["y"][0:64, 0:W], in0=f["oo"][0:64, 0:W], scalar=sm2[0:64, 0:1], in1=f["rr"][0:64, 0:W],
                                                                             op0=ALU.mult, op1=ALU.mult), reads=[fb["oo"], fb["rr"], b_sm2], writes=[fb["y"]])
                                S.dma("pool", lambda g: g.dma_start(out=mix[h * 64:(h + 1) * 64, t0:t0 + W], in_=f["y"][0:64, 0:W]), reads=[fb["y"]])
                            pending.append(late)
                flush()
                if stop_after == ("B1", l):
                    return True
                return False
            if _ph():
                return True
            def _ph():
                Vt, b_Vt, b1_mark = sh["Vt"], sh["b_Vt"], sh["b1_mark"]
                S.barrier()
                WSZ = 2 * 22528 + 22528
                off[0] = ARENA - WSZ
                ws0 = dict(wg=alloc([8, 1408], BF16), wv=alloc([8, 1408], BF16), wd=alloc([11, D], BF16), b_wg=B(), b_wv=B(), b_wd=B())
                wuk = w_up[l].rearrange("(k p) n -> p k n", p=128)
                S.dma("pool", lambda g: g.dma_start(out=ws0["wg"], in_=wuk[:, :, 0:1408]), writes=[ws0["b_wg"]])
                S.dma("pool", lambda g: g.dma_start(out=ws0["wv"], in_=wuk[:, :, 2816:2816 + 1408]), writes=[ws0["b_wv"]])
                S.dma("pool", lambda g: g.dma_start(out=ws0["wd"], in_=w_down[l][0:1408, :].rearrange("(c p) n -> p c n", p=128)), writes=[ws0["b_wd"]])
                sh["ws0"] = ws0
                off[0] = b1_mark
                KB, b_KB = alloc([2, T], BF16), B()
                S.dma("sp", lambda g: g.dma_start(out=KB, in_=qk[7:9].rearrange("c p t -> p c t")), writes=[b_KB])
                QBs = [(alloc([3, 512], BF16), B()) for _ in range(2)]
                EBR = Rot([(alloc([640], BF16), B()) for _ in range(4)])
                fbR = Rot([dict(ls=alloc([512], F32), rl=alloc([512], F32), y=alloc([512], BF16), b={n: B() for n in ("ls", "rl", "y")}) for _ in range(3)])
                sbR = Rot([0, 2])
                abR = Rot([4, 5, 6, 7])
                assert off[0] <= ARENA - WSZ, off[0]
                units = []
                ng = 0
                for gi, (t0, W, v) in enumerate(groups):
                    if gi == 0 and not ctx_out:
                        continue
                    QB, bQB = QBs[ng % 2]
                    ng += 1
                    for hq in range(6):
                        ab = abR.next()
                        nqb = W // 128
                        for qb in range(nqb):
                            tt = t0 // 128 + qb
                            if gi == 0:
                                keys = [(0, None), (1, None)]
                            else:
                                n = tt - 2
                                keys = [(0, None), (1, None)]
                                if n > 0:
                                    keys.append((tt - 1, 0))
                                keys.append((tt, None))
                                if n < 31:
                                    keys.append((tt + 1, 1))
                            units.append(dict(gi=gi, t0=t0, W=W, hq=hq, qb=qb, keys=keys, ab=ab, QB=QB, bQB=bQB, first=(hq == 0 and qb == 0), lastq=(qb == nqb - 1)))

                def front(u):
                    t0, W, hq, qb, keys, QB, bQB = u["t0"], u["W"], u["hq"], u["qb"], u["keys"], u["QB"], u["bQB"]
                    c, half, kv = hq // 2, hq % 2, hq // 3
                    p0 = half * 64
                    if u["first"]:
                        S.dma("sp", lambda g: g.dma_start(out=QB[:, :, 0:W], in_=qk[4:7].rearrange("c p t -> p c t")[:, :, t0:t0 + W]), writes=[bQB])
                    nk = len(keys)
                    b0 = sbR.next()

                    def qkmm(g):
                        for i, (kt, m) in enumerate(keys):
                            ins = g.matmul(ps[:, b0 + i // 4, (i % 4) * 128:(i % 4 + 1) * 128], lhsT=KB[p0:p0 + 64, kv, kt * 128:(kt + 1) * 128],
                                           rhs=QB[p0:p0 + 64, c, qb * 128:(qb + 1) * 128], start=True, stop=True, tile_position=(p0, 0))
                        return ins
                    S.op("pe", qkmm, reads=[b_KB, bQB], writes=[psb[b0], psb[b0 + 1]])
                    E, bE = EBR.next()
                    u["E"], u["bE"] = E, bE
                    S.op("act", lambda g: g.activation(out=E[:, 0:nk * 128], in_=psflat(b0, nk * 128), func=AF.Exp, scale=SCALE_B),
                         reads=[psb[b0], psb[b0 + 1]], writes=[bE])
                    for i, (kt, m) in enumerate(keys):
                        if m is not None:
                            S.op("dve", lambda g, i=i, m=m: g.tensor_tensor(out=E[:, i * 128:(i + 1) * 128], in0=E[:, i * 128:(i + 1) * 128], in1=msk[:, m, :], op=ALU.mult),
                                 reads=[bE, b_msk], writes=[bE])

                def back(u):
                    t0, W, hq, qb, keys, ab = u["t0"], u["W"], u["hq"], u["qb"], u["keys"], u["ab"]
                    kv = hq // 3
                    E, bE = u["E"], u["bE"]

                    def pvmm(g):
                        for i, (kt, m) in enumerate(keys):
                            ins = g.matmul(ps[:, ab, qb * 128:(qb + 1) * 128], lhsT=Vt[:, kt, (4 + kv) * 128:(5 + kv) * 128], rhs=E[:, i * 128:(i + 1) * 128],
                                           start=(i == 0), stop=(i == len(keys) - 1))
                        return ins
                    S.op("pe", pvmm, reads=[bE, b_Vt], writes=[psb[ab]])
                    if u["lastq"]:
                        f = fbR.next()
                        fb = f["b"]
                        S.op("dve", lambda g: g.tensor_scalar_add(out=f["ls"][0:64, 0:W], in0=ps[64:128, ab, 0:W], scalar1=sm2[0:64, 2 + hq:3 + hq]),
                             reads=[psb[ab], b_sm2], writes=[fb["ls"]])
                        S.op("dve", lambda g: g.reciprocal(out=f["rl"][0:64, 0:W], in_=f["ls"][0:64, 0:W]), reads=[fb["ls"]], writes=[fb["rl"]])
                        S.op("dve", lambda g: g.tensor_tensor(out=f["y"][0:64, 0:W], in0=ps[0:64, ab, 0:W], in1=f["rl"][0:64, 0:W], op=ALU.mult),
                             reads=[psb[ab], fb["rl"]], writes=[fb["y"]])
                        S.dma("pool", lambda g: g.dma_start(out=mix[256 + hq * 64:256 + (hq + 1) * 64, t0:t0 + W], in_=f["y"][0:64, 0:W]), reads=[fb["y"]])

                LAG = 2
                for idx in range(len(units) + LAG):
                    if idx < len(units):
                        front(units[idx])
                    if idx >= LAG:
                        back(units[idx - LAG])
                if stop_after == ("B2", l):
                    return True
                return False
            if _ph():
                return True
            def _ph():
                S.barrier()
                areset()
                WSZ = 2 * 22528 + 22528
                wout, b_wout = alloc([8, D], BF16), B()
                S.dma("pool", lambda g: g.dma_start(out=wout, in_=w_out[l].rearrange("(k p) n -> p k n", p=128)), writes=[b_wout])
                mxs = [(alloc([8, 512], BF16), B()) for _ in range(2)]
                xgs = [(alloc([8, 512], F32), B()) for _ in range(2)]
                h2s_ = [(alloc([8, 514], BF16), B()) for _ in range(2)]
                sq, b_sq = alloc([8, 512], BF16), B()
                tmp8, b_tmp8 = alloc([8, 512], F32), B()
                rstd, b_rstd = alloc([512], F32), B()
                assert off[0] <= ARENA - WSZ
                for (h2, bh2) in h2s_:
                    S.op("pool", lambda g, h2=h2: g.memset(h2, 0.0), writes=[bh2])
                mixk = mix.rearrange("(k p) t -> p k t", p=128)
                h2sk = h2s.rearrange("(k p) t -> p k t", p=128)
                pbR = Rot([1, 2, 3, 4, 5, 6])
                from concourse.ap import AP as _AP

                def bc3(ap2, nb):
                    (pst, pn), (fs, fn_) = ap2.ap
                    return _AP(ap2.tensor, ap2.offset, [[pst, pn], [0, nb], [fs, fn_]])
                glist = [(gi, g_) for gi, g_ in enumerate(groups) if not (gi == 0 and not ctx_out)]

                def front(n):
                    gi, (t0, W, v) = glist[n]
                    mx, bmx = mxs[n % 2]
                    xg, bxg = xgs[n % 2]
                    for kh in range(2):
                        S.dma("sp", lambda g, kh=kh: g.dma_start(out=mx[:, 4 * kh:4 * kh + 4, 0:W], in_=mixk[:, 4 * kh:4 * kh + 4, t0:t0 + W]), writes=[bmx])
                    S.dma("sp", lambda g: g.dma_start(out=xg[:, :, 0:W], in_=xsrc_k[:, :, t0:t0 + W]), writes=[bxg])
                    for j in range(8):
                        pb = pbR.next()

                        def omm(g, j=j, pb=pb):
                            for c in range(8):
                                ins = g.matmul(ps[:, pb, 0:W], lhsT=wout[:, c, j * 128:(j + 1) * 128], rhs=mx[:, c, 0:W], start=(c == 0), stop=(c == 7))
                            return ins
                        S.op("pe", omm, reads=[b_wout, bmx], writes=[psb[pb]])
                        S.op("dve", lambda g, j=j, pb=pb: g.scalar_tensor_tensor(
                            out=xg[:, j, 0:W], in0=ps[:, pb, 0:W], scalar=modT[:, 16 + j, v:v + 1], in1=xg[:, j, 0:W], op0=ALU.mult, op1=ALU.add),
                            reads=[psb[pb], bxg, b_modT], writes=[bxg])
                    S.dma("pool", lambda g: g.dma_start(out=xTs_k[:, :, t0:t0 + W], in_=xg[:, :, 0:W]), reads=[bxg])

                def back(n):
                    gi, (t0, W, v) = glist[n]
                    xg, bxg = xgs[n % 2]
                    h2, bh2 = h2s_[n % 2]
                    S.op("act", lambda g: g.activation(out=sq[:, :, 0:W], in_=xg[:, :, 0:W], func=AF.Square), reads=[bxg], writes=[b_sq])

                    def ssmm2(g):
                        for k in range(8):
                            ins = g.matmul(ps[:, 0, 0:W], lhsT=onesb[:], rhs=sq[:, k, 0:W], start=(k == 0), stop=(k == 7))
                        return ins
                    S.op("pe", ssmm2, reads=[b_sq, b_ones], writes=[psb[0]])
                    S.op("act", lambda g: g.activation(out=rstd[:, 0:W], in_=ps[:, 0, 0:W], func=AF.Ln, scale=1.0 / D, bias=cst[:, 0:1]),
                         reads=[psb[0], b_cst], writes=[b_rstd])
                    S.op("act", lambda g: g.activation(out=rstd[:, 0:W], in_=rstd[:, 0:W], func=AF.Exp, scale=-0.5), reads=[b_rstd], writes=[b_rstd])
                    S.op("dve", lambda g: g.tensor_tensor(out=tmp8[:, :, 0:W], in0=xg[:, :, 0:W], in1=bc3(rstd[:, 0:W], 8), op=ALU.mult),
                         reads=[bxg, b_rstd], writes=[b_tmp8])
                    for k in range(8):
                        S.op("act", lambda g, k=k: g.activation(
                            out=h2[:, k, 1:1 + W], in_=tmp8[:, k, 0:W], func=AF.Identity, scale=A2[:, k, v:v + 1], bias=modT[:, 24 + k, v:v + 1]),
                            reads=[b_tmp8, b_modT, b_A], writes=[bh2])
                    if gi == 0:
                        S.dma("pool", lambda g: g.dma_start(out=h2sk[:, :, 0:258], in_=h2[:, :, 0:258]), reads=[bh2])
                    elif gi == 1:
                        S.dma("pool", lambda g: g.dma_start(out=h2sk[:, :, 258:258 + 513], in_=h2[:, :, 0:513]), reads=[bh2])
                    elif gi == 8:
                        S.dma("pool", lambda g: g.dma_start(out=h2sk[:, :, t0 + 3:t0 + 3 + 513], in_=h2[:, :, 1:514]), reads=[bh2])
                    else:
                        S.dma("pool", lambda g: g.dma_start(out=h2sk[:, :, t0 + 3:t0 + 3 + 512], in_=h2[:, :, 1:513]), reads=[bh2])

                for n in range(len(glist) + 1):
                    if n < len(glist):
                        front(n)
                    if n >= 1:
                        back(n - 1)
                if stop_after == ("C1", l):
                    return True
                return False
            if _ph():
                return True
            def _ph():
                h2sk = h2s.rearrange("(k p) t -> p k t", p=128)
                wins = []
                if ctx_out:
                    wins.append((0, 258, 0, 256, 1))
                for i in range(9):
                    wo = min(510, NLAT - 510 * i)
                    wins.append((258 + 510 * i, wo + 2, NCTX + 510 * i, wo, 0))
                S.barrier()
                areset()
                wsets = []
                wuk = w_up[l].rearrange("(k p) n -> p k n", p=128)
                wsets.append(sh["ws0"])
                for hf_ in range(1, 2):
                    ws = dict(wg=alloc([8, 1408], BF16), wv=alloc([8, 1408], BF16), wd=alloc([11, D], BF16), b_wg=B(), b_wv=B(), b_wd=B())
                    wsets.append(ws)
                    S.dma("pool", lambda g, hf_=hf_, ws=ws: g.dma_start(out=ws["wg"], in_=wuk[:, :, hf_ * 1408:(hf_ + 1) * 1408]), writes=[ws["b_wg"]])
                    S.dma("pool", lambda g, hf_=hf_, ws=ws: g.dma_start(out=ws["wv"], in_=wuk[:, :, 2816 + hf_ * 1408:2816 + (hf_ + 1) * 1408]), writes=[ws["b_wv"]])
                    S.dma("pool", lambda g, hf_=hf_, ws=ws: g.dma_start(out=ws["wd"], in_=w_down[l][hf_ * 1408:(hf_ + 1) * 1408, :].rearrange("(c p) n -> p c n", p=128)), writes=[ws["b_wd"]])
                hwR = Rot([(alloc([8, 512], BF16), B()) for _ in range(2)])
                xgR = Rot([(alloc([8, 512], F32), B()) for _ in range(1)])
                act, b_act = alloc([11, 512], BF16), B()
                cvR = Rot([(alloc([512], F32), B()) for _ in range(2)])
                sgR = Rot([(alloc([512], F32), B()) for _ in range(2)])
                gvR = Rot([(0, 1), (2, 3)])
                dbR = Rot([4, 5, 6, 7])
                bxw = [B() for _ in wins]
                assert off[0] <= ARENA - (2 * 22528 + 22528), off[0]
                for hf in range(2):
                    ws = wsets[hf]
                    wg, wv, wd, b_wg, b_wv, b_wd = ws["wg"], ws["wv"], ws["wd"], ws["b_wg"], ws["b_wv"], ws["b_wd"]
                    for wi, (cs, Wn, tk0, Wo, v) in enumerate(wins):
                        hw, bhw = hwR.next()
                        S.dma("sp", lambda g, hw=hw, cs=cs, Wn=Wn: g.dma_start(out=hw[:, :, 0:Wn], in_=h2sk[:, :, cs:cs + Wn]), writes=[bhw])
                        xg, bxg = xgR.next()
                        S.dma("sp", lambda g, xg=xg, tk0=tk0, Wo=Wo: g.dma_start(out=xg[:, :, 0:Wo], in_=xTs_k[:, :, tk0:tk0 + Wo]), reads=[bxw[wi]], writes=[bxg])
                        for ci in range(11):
                            c = hf * 11 + ci
                            gb, vb = gvR.next()

                            def umm(g, ci=ci, gb=gb, vb=vb, hw=hw, Wn=Wn, Wo=Wo, wg=wg, wv=wv):
                                for k in range(8):
                                    g.matmul(ps[:, gb, 0:Wn], lhsT=wg[:, k, ci * 128:(ci + 1) * 128], rhs=hw[:, k, 0:Wn], start=(k == 0), stop=(k == 7))
                                for k in range(8):
                                    ins = g.matmul(ps[:, vb, 0:Wo], lhsT=wv[:, k, ci * 128:(ci + 1) * 128], rhs=hw[:, k, 1:1 + Wo], start=(k == 0), stop=(k == 7))
                                return ins
                            S.op("pe", umm, reads=[b_wg, b_wv, bhw], writes=[psb[gb], psb[vb]])
                            cv, bcv = cvR.next()
                            sg, bsg = sgR.next()
                            w0 = smt[:, 75 + c:76 + c]
                            w1 = smt[:, 75 + 22 + c:76 + 22 + c]
                            w2 = smt[:, 75 + 44 + c:76 + 44 + c]
                            bb_ = smt[:, 141 + c:142 + c]
                            S.op("act", lambda g, cv=cv, gb=gb, Wo=Wo, w1=w1, bb_=bb_: g.activation(out=cv[:, 0:Wo], in_=ps[:, gb, 1:1 + Wo], func=AF.Identity, scale=w1, bias=bb_),
                                 reads=[psb[gb], b_smt], writes=[bcv])
                            S.op("dve", lambda g, cv=cv, gb=gb, Wo=Wo, w0=w0: g.scalar_tensor_tensor(out=cv[:, 0:Wo], in0=ps[:, gb, 0:Wo], scalar=w0, in1=cv[:, 0:Wo], op0=ALU.mult, op1=ALU.add),
                                 reads=[psb[gb], bcv, b_smt], writes=[bcv])
                            S.op("dve", lambda g, cv=cv, gb=gb, Wo=Wo, w2=w2: g.scalar_tensor_tensor(out=cv[:, 0:Wo], in0=ps[:, gb, 2:2 + Wo], scalar=w2, in1=cv[:, 0:Wo], op0=ALU.mult, op1=ALU.add),
                                 reads=[psb[gb], bcv, b_smt], writes=[bcv])
                            S.op("act", lambda g, cv=cv, sg=sg, Wo=Wo: g.activation(out=sg[:, 0:Wo], in_=cv[:, 0:Wo], func=AF.Silu), reads=[bcv], writes=[bsg])
                            S.op("dve", lambda g, sg=sg, vb=vb, ci=ci, Wo=Wo: g.tensor_tensor(out=act[:, ci, 0:Wo], in0=ps[:, vb, 0:Wo], in1=sg[:, 0:Wo], op=ALU.mult),
                                 reads=[psb[vb], bsg], writes=[b_act])
                        for j in range(8):
                            db = dbR.next()

                            def dmm(g, j=j, db=db, Wo=Wo, wd=wd):
                                for ci in range(11):
                                    ins = g.matmul(ps[:, db, 0:Wo], lhsT=wd[:, ci, j * 128:(j + 1) * 128], rhs=act[:, ci, 0:Wo], start=(ci == 0), stop=(ci == 10))
                                return ins
                            S.op("pe", dmm, reads=[b_wd, b_act], writes=[psb[db]])
                            S.op("dve", lambda g, j=j, db=db, xg=xg, Wo=Wo, v=v: g.scalar_tensor_tensor(
                                out=xg[:, j, 0:Wo], in0=ps[:, db, 0:Wo], scalar=modT[:, 40 + j, v:v + 1], in1=xg[:, j, 0:Wo], op0=ALU.mult, op1=ALU.add),
                                reads=[psb[db], bxg, b_modT], writes=[bxg])
                        if last and hf == 1:
                            outk = out.rearrange("(k p) t -> p k t", p=128)
                            S.dma("pool", lambda g, xg=xg, tk0=tk0, Wo=Wo: g.dma_start(out=outk[:, :, tk0 - NCTX:tk0 - NCTX + Wo], in_=xg[:, :, 0:Wo]), reads=[bxg])
                        else:
                            S.dma("pool", lambda g, xg=xg, tk0=tk0, Wo=Wo: g.dma_start(out=xTs_k[:, :, tk0:tk0 + Wo], in_=xg[:, :, 0:Wo]), reads=[bxg], writes=[bxw[wi]])
                    if stop_after == ("C2%d" % hf, l):
                        return True
                if stop_after is not None and stop_after[1] == l:
                    return True
                return False
            if _ph():
                return True
            return False

        for l in range(nlayers):
            if run_layer(l):
                break

        S.barrier()
        S.emit(nc, sems, dsems)
    return nc


def _rope_tables():
    pos = np.arange(NLAT)
    row = (pos // 64).astype(np.float32)
    col = (pos % 64).astype(np.float32)
    tabs = []
    for hd in (32, 64):
        quarter = hd // 4
        half = hd // 2
        inv_freq = (np.float32(10000.0) ** (-np.arange(quarter, dtype=np.float32) / np.float32(quarter))).astype(np.float32)
        ang = np.concatenate([row[:, None] * inv_freq[None, :], col[:, None] * inv_freq[None, :]], axis=-1).astype(np.float32)
        cos, sin = np.cos(ang).astype(np.float32), np.sin(ang).astype(np.float32)
        p = np.arange(128)
        d = p % hd
        j = d % half
        sign = np.where(d < half, -1.0, 1.0).astype(np.float32)
        C = np.ones((128, T), np.float32)
        Sg = np.zeros((128, T), np.float32)
        C[:, NCTX:] = cos[:, j].T
        Sg[:, NCTX:] = sin[:, j].T * sign[:, None]
        tabs += [C, Sg]
    return np.stack(tabs, 0)


def _const_mats():
    p = np.arange(128)
    m = np.zeros((7, 128, 128), np.float32)
    m[0] = (p[:, None] // 32 == p[None, :] // 32)
    m[1] = (p[:, None] // 64 == p[None, :] // 64)
    for idx, hd in ((2, 32), (3, 64)):
        perm = (p // hd) * hd + ((p % hd) + hd // 2) % hd
        m[idx] = (p[:, None] == perm[None, :])
    m[4] = np.eye(128, dtype=np.float32)
    m[5] = (p[:, None] >= p[None, :])
    m[6] = (p[:, None] <= p[None, :])
    return m


def _prep_shared(inp):
    f = lambda a: np.ascontiguousarray(np.asarray(a, dtype=np.float32))
    w_in = f(inp["w_in"])
    cols = np.concatenate([np.arange(0, 256), np.arange(256, 512), np.arange(768, 1152),
                           np.arange(1152, 1216), np.arange(1152, 1216), np.arange(1216, 1280), np.arange(1216, 1280),
                           np.arange(1408, 1792), np.arange(1792, 2176), np.arange(512, 768), np.arange(1280, 1408)])
    assert cols.size == 2304
    w_in_ext = np.ascontiguousarray(w_in[:, :, cols])
    b_mod2 = np.ascontiguousarray(np.repeat(f(inp["b_mod"])[:, None, :], 2, axis=1))
    wa, wx = f(inp["lru_wa"]), f(inp["lru_wx"])
    bd = np.zeros((2, 2, 2, 3, 128, 128), np.float32)
    for cc in range(3):
        for hb in range(2):
            bd[:, :, 0, cc, hb * 64:(hb + 1) * 64, hb * 64:(hb + 1) * 64] = wa[:, :, 2 * cc + hb]
            bd[:, :, 1, cc, hb * 64:(hb + 1) * 64, hb * 64:(hb + 1) * 64] = wx[:, :, 2 * cc + hb]
    bd = np.ascontiguousarray(bd.reshape(2, 12, 128, 128))
    p = np.arange(128)
    sm = np.zeros((2, 128, NS), np.float32)
    sm[:, :, 0:8] = f(inp["norm1_gain"]).reshape(2, 8, 128).transpose(0, 2, 1)
    sm[:, :, 8:16] = f(inp["norm2_gain"]).reshape(2, 8, 128).transpose(0, 2, 1)
    sm[:, :, 16] = f(inp["da_q_gain"])[:, p % 32]
    sm[:, :, 17] = f(inp["da_k_gain"])[:, p % 32]
    sm[:, :, 18] = f(inp["sw_q_gain"])[:, p % 64]
    sm[:, :, 19] = f(inp["sw_k_gain"])[:, p % 64]
    sm[:, :, 20] = f(inp["da_sub_gain"])[:, p % 64]
    sm[:, :, 21:27] = f(inp["sw_sink"])[:, None, :]
    cw = f(inp["lru_conv_w"]).reshape(2, 2, 4, 3, 128)
    sm[:, :, 27:51] = cw.transpose(0, 4, 1, 2, 3).reshape(2, 128, 24)
    for base, name in ((51, "lru_conv_b"), (57, "lru_ba"), (63, "lru_bx"), (69, "lru_lambda")):
        sm[:, :, base:base + 6] = f(inp[name]).reshape(2, 2, 3, 128).transpose(0, 3, 1, 2).reshape(2, 128, 6)
    fw = f(inp["ffn_conv_w"]).reshape(2, 3, 22, 128)
    sm[:, :, 75:141] = fw.transpose(0, 3, 1, 2).reshape(2, 128, 66)
    sm[:, :, 141:163] = f(inp["ffn_conv_b"]).reshape(2, 22, 128).transpose(0, 2, 1)
    lam = np.stack([f(inp["da_lam_q1"]), f(inp["da_lam_k1"]), f(inp["da_lam_q2"]), f(inp["da_lam_k2"])], axis=1)
    lamv = np.ascontiguousarray(np.broadcast_to(lam[:, None], (2, 128, 4, 32)))
    return {
        "w_mod": f(inp["w_mod"]), "b_mod2": b_mod2, "w_in": w_in_ext, "w_out": f(inp["w_out"]), "w_up": f(inp["w_up"]),
        "w_down": f(inp["w_down"]), "lru_bd": bd, "smalls": np.ascontiguousarray(sm), "lamv": lamv,
        "cmats": _const_mats(), "rope": _rope_tables(),
    }


def _prep_core(inp, b):
    x = np.asarray(inp["x"], dtype=np.float32)
    ctx = np.asarray(inp["ctx"], dtype=np.float32)
    c = np.asarray(inp["c"], dtype=np.float32)
    c_ctx = np.asarray(inp["c_ctx"], dtype=np.float32)
    xT = np.ascontiguousarray(np.concatenate([ctx[b].T, x[b].T], axis=1))
    cT = np.ascontiguousarray(np.stack([c[b].reshape(8, 128).T, c_ctx.reshape(8, 128).T], axis=-1))
    return {"xT": xT, "cT": cT}


_CACHE = {}


def kernel(**inputs):
    if "nc" not in _CACHE:
        _CACHE["nc"] = build_program()
    nc = _CACHE["nc"]
    shared = _prep_shared(inputs)
    n = 8
    in_maps = []
    for b in range(n):
        m = dict(shared)
        m.update(_prep_core(inputs, b))
        in_maps.append(m)
    res = run_bass_kernel_spmd(nc, in_maps, core_ids=list(range(n)))
    outs = [np.asarray(r["out"]).T for r in res.results]
    return np.ascontiguousarray(np.stack(outs, axis=0).astype(np.float32))
```

```python
import math
import numpy as np
import concourse.bass as bass
import concourse.mybir as mybir
from concourse.bass_utils import run_bass_kernel_spmd

F32 = mybir.dt.float32
BF16 = mybir.dt.bfloat16
U8 = mybir.dt.uint8
AF = mybir.ActivationFunctionType
ALU = mybir.AluOpType

ENGS = ("pe", "act", "dve", "pool", "sp")
NDMASEM = 44

D = 1024
NCTX = 256
NLAT = 4096
T = NCTX + NLAT
NS = 163
EPS = 1e-6
ARENA = 200 * 1024
SCALE_A = 32 ** -0.5
SCALE_B = 64 ** -0.5


class Buf:
    __slots__ = ("name", "w", "r")

    def __init__(self, name):
        self.name = name
        self.w = []
        self.r = []


class Sched:
    def __init__(self):
        self.q = {e: [] for e in ENGS}
        self.cnt = {e: 0 for e in ENGS}
        self.seen = {e: {} for e in ENGS}
        self.dma_n = 0
        self.dma_np = 0
        self.dma_tot = [0] * NDMASEM

    def bufs(self, name, n):
        return [Buf(f"{name}{i}") for i in range(n)]

    def _deps(self, eng, reads, writes):
        need = {}
        for b in reads:
            for (k, v) in b.w:
                if need.get(k, 0) < v:
                    need[k] = v
        for b in writes:
            for (k, v) in b.w:
                if need.get(k, 0) < v:
                    need[k] = v
            for (k, v) in b.r:
                if need.get(k, 0) < v:
                    need[k] = v
        seen = self.seen[eng]
        out = []
        for k, v in need.items():
            if seen.get(k, 0) < v:
                seen[k] = v
                out.append((k, v))
        return out

    def _commit(self, token, reads, writes):
        for b in writes:
            b.w = [token]
            b.r = []
        for b in reads:
            if b not in writes:
                b.r.append(token)
                if len(b.r) > 24:
                    m = {}
                    for (k, v) in b.r:
                        if m.get(k, 0) < v:
                            m[k] = v
                    b.r = list(m.items())

    def op(self, eng, fn, reads=(), writes=()):
        waits = self._deps(eng, reads, writes)
        self.cnt[eng] += 1
        token = (eng, self.cnt[eng])
        self._commit(token, reads, writes)
        self.q[eng].append((waits, fn, token))
        return token

    def dma(self, eng, fn, reads=(), writes=()):
        if eng == "pool":
            s = NDMASEM - 12 + self.dma_np % 12
            self.dma_np += 1
        else:
            s = self.dma_n % (NDMASEM - 12)
            self.dma_n += 1
        key = ("dma", s)
        waits = self._deps(eng, reads, writes)
        prev = self.dma_tot[s]
        if prev > 0 and self.seen[eng].get(key, 0) < prev:
            self.seen[eng][key] = prev
            waits.append((key, prev))
        self.dma_tot[s] = prev + 16
        token = (key, prev + 16)
        self._commit(token, reads, writes)
        self.q[eng].append((waits, fn, token))
        return token

    def barrier(self):
        for eng in ENGS:
            waits = []
            seen = self.seen[eng]
            for k in ENGS:
                v = self.cnt[k]
                if v > 0 and seen.get(k, 0) < v:
                    seen[k] = v
                    waits.append((k, v))
            for s in range(NDMASEM):
                v = self.dma_tot[s]
                key = ("dma", s)
                if v > 0 and seen.get(key, 0) < v:
                    seen[key] = v
                    waits.append((key, v))
            self.q[eng].append((waits, None, None))

    def emit(self, nc, sems, dsems):
        def semof(k):
            return dsems[k[1]] if isinstance(k, tuple) else sems[k]

        def run(eng):
            def body(h):
                for (waits, fn, token) in self.q[eng]:
                    for (k, v) in waits:
                        h.wait_ge(semof(k), v)
                    if fn is None:
                        continue
                    ins = fn(h)
                    k, v = token
                    ins.then_inc(semof(k), 16 if isinstance(k, tuple) else 1)
            return body

        with nc.Block() as block:
            block.tensor(run("pe"))
            block.scalar(run("act"))
            block.vector(run("dve"))
            block.gpsimd(run("pool"))
            block.sync(run("sp"))


class SemCtx:
    def __init__(self, nc):
        self.nc = nc
        self.stack = []

    def __enter__(self):
        sems = {}
        for e in ENGS:
            g = self.nc.semaphore("s_" + e)
            sems[e] = g.__enter__()
            self.stack.append(g)
        dsems = []
        for i in range(NDMASEM):
            g = self.nc.semaphore(f"d{i}")
            dsems.append(g.__enter__())
            self.stack.append(g)
        return sems, dsems

    def __exit__(self, *a):
        for g in reversed(self.stack):
            g.__exit__(None, None, None)
        return False


class Rot:
    def __init__(self, items):
        self.items = items
        self.i = 0

    def next(self):
        it = self.items[self.i % len(self.items)]
        self.i += 1
        return it


def build_program(nlayers=2, dbg=False, stop_after=None):
    nc = bass.Bass("TRN2", target_bir_lowering=False)

    def din(name, shape, dt=F32):
        return nc.dram_tensor(name, shape, dt, kind="ExternalInput").ap()

    xT_in = din("xT", [D, T])
    cT_in = din("cT", [128, 8, 2])
    w_mod = din("w_mod", [2, D, 6144])
    b_mod2 = din("b_mod2", [2, 2, 6144])
    w_in = din("w_in", [2, D, 2304])
    w_out = din("w_out", [2, D, D])
    w_up = din("w_up", [2, D, 5632])
    w_down = din("w_down", [2, 2816, D])
    lru_bd = din("lru_bd", [2, 12, 128, 128])
    smalls = din("smalls", [2, 128, NS])
    lamv = din("lamv", [2, 128, 4, 32])
    cmats = din("cmats", [7, 128, 128])
    rope = din("rope", [4, 128, T])
    out = nc.dram_tensor("out", [D, NLAT], F32, kind="ExternalOutput").ap()

    skind = "ExternalOutput" if dbg else "Internal"

    def dscr(name, shape, dt):
        return nc.dram_tensor(name, shape, dt, kind=skind).ap()

    xTs = dscr("xTs", [D, T], F32)
    qk = dscr("qk", [9, 128, T], BF16)
    vtok = dscr("vtok", [T, 768], BF16)
    cxs = dscr("cxs", [3, 128, T], F32)
    cgs = dscr("cgs", [3, 128, T], BF16)
    mix = dscr("mix", [D, T], BF16)
    h2s = dscr("h2s", [D, T + 4], BF16)
    modd = dscr("modd", [128, 96], F32) if dbg else None

    S = Sched()
    groups = [(0, 256, 1)] + [(256 + 512 * i, 512, 0) for i in range(8)]

    with (
        nc.sbuf_tensor("arena", [128, ARENA], U8) as arena,
        nc.sbuf_tensor("cst", [128, 4], F32) as cst,
        nc.sbuf_tensor("cmb", [128, 4, 128], BF16) as cmb,
        nc.sbuf_tensor("msk", [128, 2, 128], BF16) as msk,
        nc.sbuf_tensor("ident", [128, 128], F32) as ident,
        nc.sbuf_tensor("onesb", [128, 128], BF16) as onesb,
        nc.sbuf_tensor("smt", [128, NS], F32) as smt,
        nc.sbuf_tensor("ctile", [128, 8, 2], F32) as ctile,
        nc.sbuf_tensor("sc", [128, 8, 2], F32) as sc,
        nc.sbuf_tensor("modT", [128, 48, 2], F32) as modT,
        nc.sbuf_tensor("A1", [128, 8, 2], F32) as A1,
        nc.sbuf_tensor("A2", [128, 8, 2], F32) as A2,
        nc.sbuf_tensor("lamt", [128, 4, 32], F32) as lamt,
        nc.sbuf_tensor("sm2", [128, 64], F32) as sm2,
        nc.psum_tensor("ps", [128, 8, 512], F32) as ps,
        SemCtx(nc) as (sems, dsems),
    ):
        off = [0]

        def areset():
            off[0] = 0

        def alloc(shape, dt):
            esz = 2 if dt == BF16 else 4
            n = int(np.prod(shape))
            nb = (n * esz + 63) // 64 * 64
            assert off[0] + nb <= ARENA, (off[0], nb)
            a = arena[:, off[0]:off[0] + n * esz].bitcast(dt)
            off[0] += nb
            if len(shape) == 2:
                a = a.rearrange("p (a b) -> p a b", a=shape[0])
            elif len(shape) == 3:
                a = a.rearrange("p (a b c) -> p a b c", a=shape[0], b=shape[1])
            return a

        nb_ = [0]

        def B(name="b"):
            nb_[0] += 1
            return Buf(f"{name}{nb_[0]}")

        psb = [B("ps") for _ in range(8)]
        b_cst, b_cmb, b_msk, b_ident, b_ones, b_smt, b_ct, b_sc, b_modT, b_A, b_lam, b_sm2 = [B("c") for _ in range(12)]

        def psflat(b0, n):
            return ps[:, b0:b0 + (n + 511) // 512, :].rearrange("p b n -> p (b n)")[:, 0:n]

        def rev(ap2d):
            (pst, pn), (fs, fn_) = ap2d.ap
            from concourse.ap import AP
            return AP(ap2d.tensor, ap2d.offset + (fn_ - 1) * fs, [[pst, pn], [-fs, fn_]])

        S.op("pool", lambda g: g.memset(cst[:, 0:1], EPS), writes=[b_cst])
        S.op("pool", lambda g: g.memset(cst[:, 1:2], 1.0), writes=[b_cst])
        S.op("pool", lambda g: g.memset(cst[:, 2:3], 0.0), writes=[b_cst])
        S.op("pool", lambda g: g.memset(onesb[:], 1.0), writes=[b_ones])
        S.dma("pool", lambda g: g.dma_start(out=cmb[:], in_=cmats[0:4].rearrange("c p n -> p c n")), writes=[b_cmb])
        S.dma("pool", lambda g: g.dma_start(out=msk[:], in_=cmats[5:7].rearrange("c p n -> p c n")), writes=[b_msk])
        S.dma("sp", lambda g: g.dma_start(out=ident[:], in_=cmats[4]), writes=[b_ident])
        S.dma("sp", lambda g: g.dma_start(out=ctile[:], in_=cT_in), writes=[b_ct])
        S.op("act", lambda g: g.activation(out=sc[:], in_=ctile[:], func=AF.Silu), reads=[b_ct], writes=[b_sc])

        def run_layer(l):
            last = (l == nlayers - 1)
            ctx_out = not last
            lam_init = 0.8 - 0.6 * math.exp(-0.3 * l)
            xsrc = xT_in if l == 0 else xTs
            xsrc_k = xsrc.rearrange("(k p) t -> p k t", p=128)
            xTs_k = xTs.rearrange("(k p) t -> p k t", p=128)

            sh = {}
            def _ph():
                S.barrier()
                areset()
                S.dma("sp", lambda g, l=l: g.dma_start(out=smt[:], in_=smalls[l]), writes=[b_smt])
                S.dma("sp", lambda g, l=l: g.dma_start(out=lamt[:], in_=lamv[l]), writes=[b_lam])
                bm = alloc([6144], F32)
                modrow = alloc([6144], F32)
                wms = [alloc([8, 512], F32) for _ in range(2)]
                b_bm, b_modrow = B(), B()
                b_wm = [B(), B()]
                S.dma("sp", lambda g, l=l: g.dma_start(out=bm[0:2, :], in_=b_mod2[l]), writes=[b_bm])
                wmk = w_mod[l].rearrange("(k p) n -> p k n", p=128)
                for cg in range(12):
                    wm, bw = wms[cg % 2], b_wm[cg % 2]
                    S.dma("sp", lambda g, wm=wm, cg=cg: g.dma_start(out=wm, in_=wmk[:, :, cg * 512:(cg + 1) * 512]), writes=[bw])

                    def mm(g, wm=wm, cg=cg):
                        for k in range(8):
                            ins = g.matmul(ps[0:2, cg % 2, :], lhsT=sc[:, k, :], rhs=wm[:, k, :], start=(k == 0), stop=(k == 7))
                        return ins
                    S.op("pe", mm, reads=[bw, b_sc], writes=[psb[cg % 2]])
                    S.op("dve", lambda g, cg=cg: g.tensor_tensor(out=modrow[0:2, cg * 512:(cg + 1) * 512], in0=ps[0:2, cg % 2, :],
                                                                 in1=bm[0:2, cg * 512:(cg + 1) * 512], op=ALU.add),
                         reads=[psb[cg % 2], b_bm], writes=[b_modrow])

                def tr(g):
                    for j in range(48):
                        ins = g.transpose(ps[:, 2, 2 * j:2 * j + 2], modrow[0:2, j * 128:(j + 1) * 128], ident[0:2, 0:2])
                    return ins
                S.op("pe", tr, reads=[b_modrow, b_ident], writes=[psb[2]])
                S.op("dve", lambda g: g.tensor_copy(out=modT[:].rearrange("p a b -> p (a b)"), in_=ps[:, 2, 0:96]), reads=[psb[2]], writes=[b_modT])
                for v in range(2):
                    S.op("dve", lambda g, v=v: g.scalar_tensor_tensor(out=A1[:, :, v], in0=modT[:, 8:16, v], scalar=1.0, in1=smt[:, 0:8],
                                                                      op0=ALU.add, op1=ALU.mult), reads=[b_modT, b_smt], writes=[b_A])
                    S.op("dve", lambda g, v=v: g.scalar_tensor_tensor(out=A2[:, :, v], in0=modT[:, 32:40, v], scalar=1.0, in1=smt[:, 8:16],
                                                                      op0=ALU.add, op1=ALU.mult), reads=[b_modT, b_smt], writes=[b_A])
                if dbg and l == 0:
                    S.dma("pool", lambda g: g.dma_start(out=modd, in_=modT[:].rearrange("p a b -> p (a b)")), reads=[b_modT])
                S.op("act", lambda g: g.mul(out=sm2[:, 0:1], in_=smt[:, 20:21], mul=float(1.0 - lam_init)), reads=[b_smt], writes=[b_sm2])
                S.op("act", lambda g: g.activation(out=sm2[:, 2:8], in_=smt[:, 21:27], func=AF.Exp), reads=[b_smt], writes=[b_sm2])
                S.op("dve", lambda g: g.tensor_tensor(out=lamt[:, 0, :], in0=lamt[:, 0, :], in1=lamt[:, 1, :], op=ALU.mult), reads=[b_lam], writes=[b_lam])
                S.op("dve", lambda g: g.tensor_tensor(out=lamt[:, 2, :], in0=lamt[:, 2, :], in1=lamt[:, 3, :], op=ALU.mult), reads=[b_lam], writes=[b_lam])
                S.op("dve", lambda g: g.tensor_reduce(out=sm2[:, 20:21], in_=lamt[:, 0, :], axis=mybir.AxisListType.X, op=ALU.add), reads=[b_lam], writes=[b_sm2])
                S.op("dve", lambda g: g.tensor_reduce(out=sm2[:, 21:22], in_=lamt[:, 2, :], axis=mybir.AxisListType.X, op=ALU.add), reads=[b_lam], writes=[b_sm2])
                S.op("act", lambda g: g.activation(out=sm2[:, 22:24], in_=sm2[:, 20:22], func=AF.Exp), reads=[b_sm2], writes=[b_sm2])
                S.op("dve", lambda g: g.scalar_tensor_tensor(out=sm2[:, 1:2], in0=sm2[:, 23:24], scalar=float(-lam_init), in1=sm2[:, 22:23],
                                                             op0=ALU.add, op1=ALU.subtract), reads=[b_sm2], writes=[b_sm2])
                L_ = smt[:, 69:75]
                S.op("dve", lambda g: g.tensor_scalar_mul(out=sm2[:, 24:30], in0=L_, scalar1=-1.0), reads=[b_smt], writes=[b_sm2])
                S.op("dve", lambda g: g.tensor_tensor(out=sm2[:, 48:54], in0=sm2[:, 24:30], in1=L_, op=ALU.max), reads=[b_smt, b_sm2], writes=[b_sm2])
                S.op("act", lambda g: g.activation(out=sm2[:, 30:36], in_=sm2[:, 48:54], func=AF.Exp, scale=-1.0), reads=[b_sm2], writes=[b_sm2])
                S.op("dve", lambda g: g.tensor_scalar_add(out=sm2[:, 36:42], in0=sm2[:, 30:36], scalar1=1.0), reads=[b_sm2], writes=[b_sm2])
                S.op("act", lambda g: g.activation(out=sm2[:, 42:48], in_=sm2[:, 36:42], func=AF.Ln), reads=[b_sm2], writes=[b_sm2])
                S.op("dve", lambda g: g.tensor_scalar(out=sm2[:, 36:42], in0=sm2[:, 36:42], scalar1=-1.0, scalar2=1e-30, op0=ALU.add, op1=ALU.max),
                     reads=[b_sm2], writes=[b_sm2])
                S.op("dve", lambda g: g.reciprocal(out=sm2[:, 54:60], in_=sm2[:, 36:42]), reads=[b_sm2], writes=[b_sm2])
                S.op("dve", lambda g: g.tensor_tensor(out=sm2[:, 30:36], in0=sm2[:, 30:36], in1=sm2[:, 54:60], op=ALU.mult), reads=[b_sm2], writes=[b_sm2])
                S.op("dve", lambda g: g.tensor_tensor(out=sm2[:, 30:36], in0=sm2[:, 30:36], in1=sm2[:, 42:48], op=ALU.mult), reads=[b_sm2], writes=[b_sm2])
                S.op("dve", lambda g: g.tensor_scalar_max(out=sm2[:, 24:30], in0=sm2[:, 24:30], scalar1=0.0), reads=[b_sm2], writes=[b_sm2])
                S.op("dve", lambda g: g.tensor_tensor(out=sm2[:, 24:30], in0=sm2[:, 24:30], in1=sm2[:, 30:36], op=ALU.add), reads=[b_sm2], writes=[b_sm2])
                S.op("dve", lambda g: g.tensor_scalar_mul(out=sm2[:, 8:14], in0=sm2[:, 24:30], scalar1=-8.0), reads=[b_sm2], writes=[b_sm2])
                S.op("dve", lambda g: g.tensor_scalar_mul(out=sm2[:, 14:20], in0=sm2[:, 24:30], scalar1=-16.0), reads=[b_sm2], writes=[b_sm2])
                if stop_after == ("M", l):
                    return True

                return False
            if _ph():
                return True
            def _ph():
                S.barrier()
                areset()
                win = alloc([8, 2304], BF16)
                b_win = B()
                wik = w_in[l].rearrange("(k p) n -> p k n", p=128)
                for hh in range(2):
                    S.dma("pool", lambda g, hh=hh: g.dma_start(out=win[:, :, hh * 1152:(hh + 1) * 1152], in_=wik[:, :, hh * 1152:(hh + 1) * 1152]), writes=[b_win])
                xgs = [(alloc([8, 512], F32), B()) for _ in range(2)]
                rps = [(alloc([4, 512], F32), B()) for _ in range(2)]
                hTs = [(alloc([8, 512], BF16), B()) for _ in range(2)]
                sq, b_sq = alloc([8, 512], BF16), B()
                tmp8, b_tmp8 = alloc([8, 512], F32), B()
                rstd, b_rstd = alloc([512], F32), B()
                NSET = 3
                sets = [dict(qf=alloc([2, 512], F32), sqb=alloc([2, 512], BF16), rr=alloc([2, 512], F32), qn=alloc([2, 512], F32), qo=alloc([2, 512], BF16),
                             b={n: B() for n in ("qf", "sqb", "rr", "qn", "qo")}, pb=1 + 2 * i) for i in range(NSET)]
                vos = [(alloc([6, 128], BF16), B()) for _ in range(2)]
                for (vo, bvo) in vos:
                    S.op("pool", lambda g, vo=vo: g.memset(vo[:, :, 64:128], 1.0), writes=[bvo])
                ropek = rope.rearrange("r p t -> p r t")
                from concourse.ap import AP as _AP

                def bc3(ap2, nb):
                    (pst, pn), (fs, fn_) = ap2.ap
                    return _AP(ap2.tensor, ap2.offset, [[pst, pn], [0, nb], [fs, fn_]])

                batches = [(0, 2, 16, 0), (2, 2, 17, 0), (4, 2, 18, 1), (6, 1, 18, 1), (7, 2, 19, 1)]

                def prologue(gi):
                    t0, W, v = groups[gi]
                    xg, bxg = xgs[gi % 2]
                    rp, brp = rps[gi % 2]
                    hT, bhT = hTs[gi % 2]

                    def s0():
                        S.dma("sp", lambda g: g.dma_start(out=xg[:, :, 0:W], in_=xsrc_k[:, :, t0:t0 + W]), writes=[bxg])
                        S.dma("sp", lambda g: g.dma_start(out=rp[:, :, 0:W], in_=ropek[:, :, t0:t0 + W]), writes=[brp])

                    def s1():
                        S.op("act", lambda g: g.activation(out=sq[:, :, 0:W], in_=xg[:, :, 0:W], func=AF.Square), reads=[bxg], writes=[b_sq])

                    def s2():
                        def ssmm(g):
                            for k in range(8):
                                ins = g.matmul(ps[:, 0, 0:W], lhsT=onesb[:], rhs=sq[:, k, 0:W], start=(k == 0), stop=(k == 7))
                            return ins
                        S.op("pe", ssmm, reads=[b_sq, b_ones], writes=[psb[0]])

                    def s3():
                        S.op("act", lambda g: g.activation(out=rstd[:, 0:W], in_=ps[:, 0, 0:W], func=AF.Ln, scale=1.0 / D, bias=cst[:, 0:1]),
                             reads=[psb[0], b_cst], writes=[b_rstd])
                        S.op("act", lambda g: g.activation(out=rstd[:, 0:W], in_=rstd[:, 0:W], func=AF.Exp, scale=-0.5), reads=[b_rstd], writes=[b_rstd])

                    def s4():
                        S.op("dve", lambda g: g.tensor_tensor(out=tmp8[:, :, 0:W], in0=xg[:, :, 0:W], in1=bc3(rstd[:, 0:W], 8), op=ALU.mult),
                             reads=[bxg, b_rstd], writes=[b_tmp8])

                    def s5():
                        for k in range(8):
                            S.op("act", lambda g, k=k: g.activation(
                                out=hT[:, k, 0:W], in_=tmp8[:, k, 0:W], func=AF.Identity, scale=A1[:, k, v:v + 1], bias=modT[:, k, v:v + 1]),
                                reads=[b_tmp8, b_modT, b_A], writes=[bhT])
                    return [s0, s1, s2, s3, s4, s5]

                def proj(oc0, nb, pb0, hT, W):
                    def f(g):
                        for i in range(nb):
                            for k in range(8):
                                ins = g.matmul(ps[:, pb0 + i, 0:W], lhsT=win[:, k, (oc0 + i) * 128:(oc0 + i + 1) * 128], rhs=hT[:, k, 0:W], start=(k == 0), stop=(k == 7))
                        return ins
                    return f

                def qkbatch(gi, bi, st):
                    t0, W, v = groups[gi]
                    rp, brp = rps[gi % 2]
                    hT, bhT = hTs[gi % 2]
                    oc0, nb, gcol, mi = batches[bi]
                    bb = st["b"]
                    pb0 = st["pb"]
                    pbs = psb[pb0:pb0 + nb]
                    inv = 1.0 / 32 if mi == 0 else 1.0 / 64
                    rc, rsn = (0, 1) if mi == 0 else (2, 3)
                    qf, sqb, rr, qn, qo = (st[n][:, 0:nb, 0:W] for n in ("qf", "sqb", "rr", "qn", "qo"))
                    pv = ps[:, pb0:pb0 + nb, 0:W]

                    def s0():
                        S.op("pe", proj(oc0, nb, pb0, hT, W), reads=[b_win, bhT], writes=pbs)

                    def s1():
                        S.op("act", lambda g: g.activation(out=qf, in_=pv, func=AF.Identity), reads=pbs, writes=[bb["qf"]])
                        S.op("act", lambda g: g.activation(out=sqb, in_=pv, func=AF.Square), reads=pbs, writes=[bb["sqb"]])

                    def s2():
                        def smm(g):
                            for i in range(nb):
                                ins = g.matmul(ps[:, pb0 + i, 0:W], lhsT=cmb[:, mi, :], rhs=st["sqb"][:, i, 0:W], start=True, stop=True)
                            return ins
                        S.op("pe", smm, reads=[bb["sqb"], b_cmb], writes=pbs)

                    def s3():
                        S.op("act", lambda g: g.activation(out=rr, in_=pv, func=AF.Ln, scale=inv, bias=cst[:, 0:1]), reads=pbs + [b_cst], writes=[bb["rr"]])
                        S.op("act", lambda g: g.activation(out=rr, in_=rr, func=AF.Exp, scale=-0.5), reads=[bb["rr"]], writes=[bb["rr"]])

                    def s4():
                        S.op("dve", lambda g: g.scalar_tensor_tensor(out=qn, in0=qf, scalar=smt[:, gcol:gcol + 1], in1=rr, op0=ALU.mult, op1=ALU.mult),
                             reads=[bb["qf"], bb["rr"], b_smt], writes=[bb["qn"]])

                    def s5():
                        S.op("dve", lambda g: g.tensor_copy(out=sqb, in_=qn), reads=[bb["qn"]], writes=[bb["sqb"]])

                    def s6():
                        def rmm(g):
                            for i in range(nb):
                                ins = g.matmul(ps[:, pb0 + i, 0:W], lhsT=cmb[:, 2 + mi, :], rhs=st["sqb"][:, i, 0:W], start=True, stop=True)
                            return ins
                        S.op("pe", rmm, reads=[bb["sqb"], b_cmb], writes=pbs)
                        S.op("pool", lambda g: g.tensor_tensor(out=qf, in0=qn, in1=bc3(rp[:, rc, 0:W], nb), op=ALU.mult), reads=[bb["qn"], brp], writes=[bb["qf"]])

                    def s7():
                        S.op("dve", lambda g: g.tensor_tensor(out=rr, in0=pv, in1=bc3(rp[:, rsn, 0:W], nb), op=ALU.mult), reads=pbs + [brp], writes=[bb["rr"]])

                    def s8():
                        S.op("pool", lambda g: g.tensor_tensor(out=qo, in0=qf, in1=rr, op=ALU.add), reads=[bb["qf"], bb["rr"]], writes=[bb["qo"]])
                        S.dma("sp", lambda g: g.dma_start(out=qk[oc0:oc0 + nb].rearrange("c p t -> p c t")[:, :, t0:t0 + W], in_=qo), reads=[bb["qo"]])
                    return [s0, s1, s2, s3, s4, s5, s6, s7, s8]

                def cbatch(gi, which, st, c0, nb):
                    t0, W, v = groups[gi]
                    hT, bhT = hTs[gi % 2]
                    bb = st["b"]
                    pb0 = st["pb"]
                    pbs = psb[pb0:pb0 + nb]
                    pv = ps[:, pb0:pb0 + nb, 0:W]

                    def s0():
                        S.op("pe", proj((9 if which == 0 else 12) + c0, nb, pb0, hT, W), reads=[b_win, bhT], writes=pbs)

                    def s1():
                        if which == 0:
                            S.op("act", lambda g: g.activation(out=st["qn"][:, 0:nb, 0:W], in_=pv, func=AF.Identity), reads=pbs, writes=[bb["qn"]])
                            S.dma("sp", lambda g: g.dma_start(out=cxs[c0:c0 + nb].rearrange("c p t -> p c t")[:, :, t0:t0 + W], in_=st["qn"][:, 0:nb, 0:W]), reads=[bb["qn"]])
                        else:
                            S.op("act", lambda g: g.activation(out=st["qo"][:, 0:nb, 0:W], in_=pv, func=AF.Gelu_apprx_tanh), reads=pbs, writes=[bb["qo"]])
                            S.dma("sp", lambda g: g.dma_start(out=cgs[c0:c0 + nb].rearrange("c p t -> p c t")[:, :, t0:t0 + W], in_=st["qo"][:, 0:nb, 0:W]), reads=[bb["qo"]])
                    return [s0, s1]

                def vitem(gi):
                    t0, W, v = groups[gi]
                    hT, bhT = hTs[gi % 2]

                    def mk(tt):
                        def s():
                            def vmm(g):
                                for k in range(8):
                                    ins = g.matmul(ps[:, 7, 0:384], lhsT=hT[:, k, tt * 128:(tt + 1) * 128], rhs=win[:, k, 1920:2304], start=(k == 0), stop=(k == 7))
                                return ins
                            S.op("pe", vmm, reads=[b_win, bhT], writes=[psb[7]])
                            vo, bvo = vos[tt % 2]
                            S.op("dve", lambda g: g.tensor_copy(out=vo[:, :, 0:64], in_=ps[:, 7, 0:384].rearrange("p (h d) -> p h d", h=6)), reads=[psb[7]], writes=[bvo])
                            S.dma("sp", lambda g: g.dma_start(out=vtok[t0 + tt * 128:t0 + (tt + 1) * 128, :], in_=vo[:].rearrange("p h d -> p (h d)")), reads=[bvo])
                        return s
                    return [mk(tt) for tt in range(W // 128)]

                items = []
                pidx = {}
                nset = [0]

                def add(stages, res=None, deps=()):
                    items.append((stages, res, list(deps)))
                    return len(items) - 1

                pidx[0] = add(prologue(0))
                for gi in range(len(groups)):
                    for bi in range(len(batches)):
                        st = sets[nset[0] % NSET]
                        add(qkbatch(gi, bi, st), res=("set", nset[0] % NSET), deps=[pidx[gi]])
                        nset[0] += 1
                        if bi == 1 and gi + 1 < len(groups):
                            pidx[gi + 1] = add(prologue(gi + 1), res=("pro",))
                    for which in range(2):
                        for (c0, nb) in ((0, 2), (2, 1)):
                            st = sets[nset[0] % NSET]
                            add(cbatch(gi, which, st, c0, nb), res=("set", nset[0] % NSET), deps=[pidx[gi]])
                            nset[0] += 1
                    add(vitem(gi), res=("v",), deps=[pidx[gi]])
                SK = 2
                start, end_ = [], []
                resend = {}
                for i, (stages, res, deps) in enumerate(items):
                    s = 0 if i == 0 else start[i - 1] + SK
                    if res is not None and res in resend:
                        s = max(s, resend[res])
                    for dI in deps:
                        s = max(s, end_[dI])
                    start.append(s)
                    end_.append(s + len(stages))
                    if res is not None:
                        resend[res] = s + len(stages)
                tmax = max(end_)
                for t in range(tmax):
                    for i, (stages, res, deps) in enumerate(items):
                        k = t - start[i]
                        if 0 <= k < len(stages):
                            stages[k]()
                if stop_after == ("A", l):
                    return True
                return False
            if _ph():
                return True
            def _ph():
                S.barrier()
                areset()
                bdw, b_bdw = alloc([12, 128], BF16), B()
                S.dma("pool", lambda g: g.dma_start(out=bdw, in_=lru_bd[l].rearrange("c p n -> p c n")), writes=[b_bdw])
                xxs = [(alloc([T], F32), B()) for _ in range(2)]
                xcs = [(alloc([T], F32), B()) for _ in range(2)]
                xcbs = [(alloc([T], BF16), B()) for _ in range(2)]
                rr_, b_r = alloc([T], F32), B()
                ii_, b_i = alloc([T], F32), B()
                aa_, b_a = alloc([T], F32), B()
                mm_, b_m = alloc([T], F32), B()
                hh_ = [alloc([T], F32), alloc([T], F32)]
                b_h = [B(), B()]
                gg_, b_g = alloc([T], BF16), B()
                segs = [(0, NCTX), (NCTX, T)]
                its = [(cc, d) for cc in range(3) for d in range(2)]

                def conv(n):
                    cc, d = its[n]
                    xx, b_xx = xxs[cc % 2]
                    xc, b_xc = xcs[n % 2]
                    xcb, b_xcb = xcbs[n % 2]
                    if d == 0:
                        S.dma("sp", lambda g: g.dma_start(out=xx, in_=cxs[cc]), writes=[b_xx])
                    wcol = lambda k: smt[:, 27 + d * 12 + k * 3 + cc:28 + d * 12 + k * 3 + cc]
                    bcol = smt[:, 51 + d * 3 + cc:52 + d * 3 + cc]
                    S.op("dve", lambda g: g.tensor_scalar(out=xc, in0=xx, scalar1=wcol(3), scalar2=bcol, op0=ALU.mult, op1=ALU.add),
                         reads=[b_xx, b_smt], writes=[b_xc])
                    for k in range(3):
                        s_ = 3 - k
                        for (a_, e_) in segs:
                            if d == 0:
                                dst, src = (a_ + s_, e_), (a_, e_ - s_)
                            else:
                                dst, src = (a_, e_ - s_), (a_ + s_, e_)
                            S.op("dve", lambda g, dst=dst, src=src, k=k: g.scalar_tensor_tensor(
                                out=xc[:, dst[0]:dst[1]], in0=xx[:, src[0]:src[1]], scalar=wcol(k), in1=xc[:, dst[0]:dst[1]], op0=ALU.mult, op1=ALU.add),
                                reads=[b_xx, b_xc, b_smt], writes=[b_xc])
                    S.op("act", lambda g: g.activation(out=xcb, in_=xc, func=AF.Identity), reads=[b_xc], writes=[b_xcb])

                def gates(n):
                    cc, d = its[n]
                    xc, b_xc = xcs[n % 2]
                    xcb, b_xcb = xcbs[n % 2]
                    ia = (d * 2 + 0) * 3 + cc
                    ix = (d * 2 + 1) * 3 + cc
                    for sg0 in range(0, T, 2048):
                        sgw = min(2048, T - sg0)

                        def gmm(g, sg0=sg0, sgw=sgw):
                            for q0 in range(0, sgw, 512):
                                w_ = min(512, sgw - q0)
                                g.matmul(ps[:, q0 // 512, 0:w_], lhsT=bdw[:, ia, :], rhs=xcb[:, sg0 + q0:sg0 + q0 + w_], start=True, stop=True)
                                ins = g.matmul(ps[:, 4 + q0 // 512, 0:w_], lhsT=bdw[:, ix, :], rhs=xcb[:, sg0 + q0:sg0 + q0 + w_], start=True, stop=True)
                            return ins
                        S.op("pe", gmm, reads=[b_xcb, b_bdw], writes=psb)
                        S.op("act", lambda g, sg0=sg0, sgw=sgw: g.activation(
                            out=rr_[:, sg0:sg0 + sgw], in_=psflat(0, sgw), func=AF.Sigmoid, bias=smt[:, 57 + d * 3 + cc:58 + d * 3 + cc]),
                            reads=psb[0:4] + [b_smt], writes=[b_r])
                        S.op("act", lambda g, sg0=sg0, sgw=sgw: g.activation(
                            out=ii_[:, sg0:sg0 + sgw], in_=psflat(4, sgw), func=AF.Sigmoid, bias=smt[:, 63 + d * 3 + cc:64 + d * 3 + cc]),
                            reads=psb[4:8] + [b_smt], writes=[b_i])
                    S.op("act", lambda g: g.activation(out=aa_, in_=rr_, func=AF.Exp, scale=sm2[:, 8 + d * 3 + cc:9 + d * 3 + cc]),
                         reads=[b_r, b_sm2], writes=[b_a])
                    S.op("act", lambda g: g.activation(out=mm_, in_=rr_, func=AF.Exp, scale=sm2[:, 14 + d * 3 + cc:15 + d * 3 + cc]),
                         reads=[b_r, b_sm2], writes=[b_m])
                    S.op("act", lambda g: g.activation(out=mm_, in_=mm_, func=AF.Sqrt, scale=-1.0, bias=cst[:, 1:2]), reads=[b_m, b_cst], writes=[b_m])
                    S.op("pool", lambda g: g.tensor_tensor(out=ii_, in0=ii_, in1=xc, op=ALU.mult), reads=[b_i, b_xc], writes=[b_i])

                def scan(n):
                    cc, d = its[n]
                    S.op("dve", lambda g: g.tensor_tensor(out=mm_, in0=mm_, in1=ii_, op=ALU.mult), reads=[b_m, b_i], writes=[b_m])
                    hd = hh_[d]
                    if d == 0:
                        S.op("dve", lambda g: g.tensor_tensor_scan(out=hd, data0=aa_, data1=mm_, initial=0.0, op0=ALU.mult, op1=ALU.add),
                             reads=[b_a, b_m], writes=[b_h[d]])
                    else:
                        S.op("dve", lambda g: g.tensor_tensor_scan(out=rev(hd[:, 0:NCTX]), data0=rev(aa_[:, 0:NCTX]), data1=rev(mm_[:, 0:NCTX]),
                                                                   initial=0.0, op0=ALU.mult, op1=ALU.add), reads=[b_a, b_m], writes=[b_h[d]])
                        S.op("dve", lambda g: g.tensor_tensor_scan(out=rev(hd[:, NCTX:T]), data0=rev(aa_[:, NCTX:T]), data1=rev(mm_[:, NCTX:T]),
                                                                   initial=hd[:, 0:1], op0=ALU.mult, op1=ALU.add), reads=[b_a, b_m, b_h[d]], writes=[b_h[d]])
                        S.dma("sp", lambda g: g.dma_start(out=gg_, in_=cgs[cc]), writes=[b_g])
                        S.op("pool", lambda g: g.tensor_tensor(out=hh_[0], in0=hh_[0], in1=hh_[1], op=ALU.add), reads=[b_h[0], b_h[1]], writes=[b_h[0]])
                        yy_, b_y = xcbs[n % 2]
                        S.op("pool", lambda g: g.tensor_tensor(out=yy_, in0=hh_[0], in1=gg_, op=ALU.mult), reads=[b_h[0], b_g], writes=[b_y])
                        S.dma("pool", lambda g: g.dma_start(out=mix[640 + cc * 128:640 + (cc + 1) * 128, :], in_=yy_), reads=[b_y])

                conv(0)
                for n in range(len(its)):
                    gates(n)
                    if n + 1 < len(its):
                        conv(n + 1)
                    scan(n)
                if stop_after == ("B3", l):
                    return True
                return False
            if _ph():
                return True
            def _ph():
                S.barrier()
                areset()
                Vt, b_Vt = alloc([34, 768], BF16), B()
                vtk = vtok.rearrange("(kt p) c -> p kt c", p=128)
                for q4 in range(0, 34, 9):
                    q5 = min(34, q4 + 9)
                    S.dma("sp", lambda g, q4=q4, q5=q5: g.dma_start(out=Vt[:, q4:q5, :], in_=vtk[:, q4:q5, :]), writes=[b_Vt])
                b1_mark = off[0]
                sh.update(Vt=Vt, b_Vt=b_Vt, b1_mark=b1_mark)
                KT, b_KT = alloc([2, T], BF16), B()
                S.dma("sp", lambda g: g.dma_start(out=KT, in_=qk[2:4].rearrange("c p t -> p c t")), writes=[b_KT])
                QTR = Rot([(alloc([2, 512], BF16), B()) for _ in range(2)])
                ER = [Rot([(alloc([2, 512], BF16), B()) for _ in range(2)]) for _ in range(2)]
                evR = Rot([dict(o=alloc([4, 512], F32), l=alloc([4, 512], F32), bo=B(), bl=B()) for _ in range(2)])
                finR = Rot([dict(oo=alloc([512], F32), osq=alloc([512], BF16), rr=alloc([512], F32), y=alloc([512], BF16),
                                 b={n: B() for n in ("oo", "osq", "rr", "y")}) for _ in range(4)])
                sbR = Rot([0, 1, 2])
                pending = []

                def flush():
                    while pending:
                        pending.pop(0)()
                for gi, (t0, W, v) in enumerate(groups):
                    if gi == 0 and not ctx_out:
                        continue
                    nkt = 2 if gi == 0 else 34
                    QT, bQT = QTR.next()
                    S.dma("sp", lambda g, QT=QT, t0=t0, W=W: g.dma_start(out=QT[:, :, 0:W], in_=qk[0:2].rearrange("c p t -> p c t")[:, :, t0:t0 + W]), writes=[bQT])
                    for c in range(2):
                        Ecur = [None] * 2
                        Eprev = [None] * 2
                        for kt in range(nkt + 1):
                            if kt == min(22, nkt):
                                flush()
                            if kt < nkt:
                                for p in range(2):
                                    E, bE = ER[p].next()
                                    Ecur[p] = (E, bE)

                                    def qk2(g, p=p, c=c, kt=kt, QT=QT, W=W):
                                        for j in (2 * p, 2 * p + 1):
                                            ins = g.matmul(ps[:, j, 0:W], lhsT=KT[32 * j:32 * j + 32, c, kt * 128:(kt + 1) * 128], rhs=QT[32 * j:32 * j + 32, c, 0:W],
                                                           start=True, stop=True, tile_position=(32 * j, 0))
                                        return ins
                                    S.op("pe", qk2, reads=[b_KT, bQT], writes=[psb[2 * p], psb[2 * p + 1]])
                                    S.op("act", lambda g, p=p, E=E, W=W: g.activation(out=E[:, :, 0:W], in_=ps[:, 2 * p:2 * p + 2, 0:W], func=AF.Exp, scale=SCALE_A),
                                         reads=[psb[2 * p], psb[2 * p + 1]], writes=[bE])
                            if kt >= 1:
                                for p in range(2):
                                    E, bE = Eprev[p]
                                    h = 2 * c + p

                                    def pv2(g, p=p, E=E, h=h, kt=kt, W=W, nkt=nkt):
                                        for m in range(2):
                                            ins = g.matmul(ps[:, 4 + 2 * p + m, 0:W], lhsT=Vt[:, kt - 1, h * 128:(h + 1) * 128], rhs=E[:, m, 0:W], start=(kt == 1), stop=(kt == nkt))
                                        return ins
                                    S.op("pe", pv2, reads=[bE, b_Vt], writes=[psb[4 + 2 * p], psb[5 + 2 * p]])
                            Eprev = list(Ecur)
                        ev = evR.next()
                        S.op("dve", lambda g, ev=ev, W=W: g.tensor_copy(out=ev["o"][0:64, :, 0:W], in_=ps[0:64, 4:8, 0:W]), reads=psb[4:8], writes=[ev["bo"]])
                        S.op("dve", lambda g, ev=ev, W=W: g.tensor_copy(out=ev["l"][0:64, :, 0:W], in_=ps[64:128, 4:8, 0:W]), reads=psb[4:8], writes=[ev["bl"]])
                        S.op("dve", lambda g, ev=ev, W=W: g.reciprocal(out=ev["l"][0:64, :, 0:W], in_=ev["l"][0:64, :, 0:W]), reads=[ev["bl"]], writes=[ev["bl"]])
                        S.op("pool", lambda g, ev=ev, W=W: g.tensor_tensor(out=ev["o"][0:64, :, 0:W], in0=ev["o"][0:64, :, 0:W], in1=ev["l"][0:64, :, 0:W], op=ALU.mult),
                             reads=[ev["bo"], ev["bl"]], writes=[ev["bo"]])
                        for hh in range(2):
                            h = 2 * c + hh
                            f = finR.next()
                            fb = f["b"]
                            S.op("dve", lambda g, f=f, ev=ev, hh=hh, W=W: g.scalar_tensor_tensor(
                                out=f["oo"][0:64, 0:W], in0=ev["o"][0:64, 2 * hh + 1, 0:W], scalar=sm2[0:64, 1:2], in1=ev["o"][0:64, 2 * hh, 0:W],
                                op0=ALU.mult, op1=ALU.add), reads=[ev["bo"], b_sm2], writes=[fb["oo"]])
                            S.op("pool", lambda g, f=f, W=W: g.tensor_tensor(out=f["osq"][0:64, 0:W], in0=f["oo"][0:64, 0:W], in1=f["oo"][0:64, 0:W], op=ALU.mult),
                                 reads=[fb["oo"]], writes=[fb["osq"]])

                            def late(f=f, fb=fb, h=h, t0=t0, W=W):
                                S.op("pe", lambda g: g.matmul(ps[0:64, 0, 0:W], lhsT=onesb[0:64, 0:64], rhs=f["osq"][0:64, 0:W], start=True, stop=True),
                                     reads=[fb["osq"], b_ones], writes=[psb[0]])
                                S.op("act", lambda g: g.activation(out=f["rr"][0:64, 0:W], in_=ps[0:64, 0, 0:W], func=AF.Ln, scale=1.0 / 64, bias=cst[0:64, 0:1]),
                                     reads=[psb[0], b_cst], writes=[fb["rr"]])
                                S.op("act", lambda g: g.activation(out=f["rr"][0:64, 0:W], in_=f["rr"][0:64, 0:W], func=AF.Exp, scale=-0.5),
                                     reads=[fb["rr"]], writes=[fb["rr"]])
                                S.op("dve", lambda g: g.scalar_tensor_tensor(out=f["y"][0:64, 0:W], in0=f["oo"][0:64, 0:W], scalar=sm2[0:64, 0:1], in1=f["rr"][0:64, 0:W],
                                                                             op0=ALU.mult, op1=ALU.mult), reads=[fb["oo"], fb["rr"], b_sm2], writes=[fb["y"]])
                                S.dma("pool", lambda g: g.dma_start(out=mix[h * 64:(h + 1) * 64, t0:t0 + W], in_=f["y"][0:64, 0:W]), reads=[fb["y"]])
                            pending.append(late)
                flush()
                if stop_after == ("B1", l):
                    return True
                return False
            if _ph():
                return True
            def _ph():
                Vt, b_Vt, b1_mark = sh["Vt"], sh["b_Vt"], sh["b1_mark"]
                S.barrier()
                WSZ = 2 * 22528 + 22528
                off[0] = ARENA - WSZ
                ws0 = dict(wg=alloc([8, 1408], BF16), wv=alloc([8, 1408], BF16), wd=alloc([11, D], BF16), b_wg=B(), b_wv=B(), b_wd=B())
                wuk = w_up[l].rearrange("(k p) n -> p k n", p=128)
                S.dma("pool", lambda g: g.dma_start(out=ws0["wg"], in_=wuk[:, :, 0:1408]), writes=[ws0["b_wg"]])
                S.dma("pool", lambda g: g.dma_start(out=ws0["wv"], in_=wuk[:, :, 2816:2816 + 1408]), writes=[ws0["b_wv"]])
                S.dma("pool", lambda g: g.dma_start(out=ws0["wd"], in_=w_down[l][0:1408, :].rearrange("(c p) n -> p c n", p=128)), writes=[ws0["b_wd"]])
                sh["ws0"] = ws0
                off[0] = b1_mark
                KB, b_KB = alloc([2, T], BF16), B()
                S.dma("sp", lambda g: g.dma_start(out=KB, in_=qk[7:9].rearrange("c p t -> p c t")), writes=[b_KB])
                QBs = [(alloc([3, 512], BF16), B()) for _ in range(2)]
                EBR = Rot([(alloc([640], BF16), B()) for _ in range(6)])
                fbR = Rot([dict(ls=alloc([512], F32), rl=alloc([512], F32), y=alloc([512], BF16), b={n: B() for n in ("ls", "rl", "y")}) for _ in range(3)])
                sbR = Rot([0, 2, 4])
                abR = Rot([6, 7])
                assert off[0] <= ARENA - WSZ, off[0]
                units = []
                ng = 0
                for gi, (t0, W, v) in enumerate(groups):
                    if gi == 0 and not ctx_out:
                        continue
                    QB, bQB = QBs[ng % 2]
                    ng += 1
                    for hq in range(6):
                        ab = abR.next()
                        nqb = W // 128
                        for qb in range(nqb):
                            tt = t0 // 128 + qb
                            if gi == 0:
                                keys = [(0, None), (1, None)]
                            else:
                                n = tt - 2
                                keys = [(0, None), (1, None)]
                                if n > 0:
                                    keys.append((tt - 1, 0))
                                keys.append((tt, None))
                                if n < 31:
                                    keys.append((tt + 1, 1))
                            units.append(dict(gi=gi, t0=t0, W=W, hq=hq, qb=qb, keys=keys, ab=ab, QB=QB, bQB=bQB, first=(hq == 0 and qb == 0), lastq=(qb == nqb - 1)))

                def front(u):
                    t0, W, hq, qb, keys, QB, bQB = u["t0"], u["W"], u["hq"], u["qb"], u["keys"], u["QB"], u["bQB"]
                    c, half, kv = hq // 2, hq % 2, hq // 3
                    p0 = half * 64
                    if u["first"]:
                        S.dma("sp", lambda g: g.dma_start(out=QB[:, :, 0:W], in_=qk[4:7].rearrange("c p t -> p c t")[:, :, t0:t0 + W]), writes=[bQB])
                    nk = len(keys)
                    b0 = sbR.next()

                    def qkmm(g):
                        for i, (kt, m) in enumerate(keys):
                            ins = g.matmul(ps[:, b0 + i // 4, (i % 4) * 128:(i % 4 + 1) * 128], lhsT=KB[p0:p0 + 64, kv, kt * 128:(kt + 1) * 128],
                                           rhs=QB[p0:p0 + 64, c, qb * 128:(qb + 1) * 128], start=True, stop=True, tile_position=(p0, 0))
                        return ins
                    S.op("pe", qkmm, reads=[b_KB, bQB], writes=[psb[b0], psb[b0 + 1]])
                    E, bE = EBR.next()
                    u["E"], u["bE"] = E, bE
                    S.op("act", lambda g: g.activation(out=E[:, 0:nk * 128], in_=psflat(b0, nk * 128), func=AF.Exp, scale=SCALE_B),
                         reads=[psb[b0], psb[b0 + 1]], writes=[bE])
                    for i, (kt, m) in enumerate(keys):
                        if m is not None:
                            S.op("dve", lambda g, i=i, m=m: g.tensor_tensor(out=E[:, i * 128:(i + 1) * 128], in0=E[:, i * 128:(i + 1) * 128], in1=msk[:, m, :], op=ALU.mult),
                                 reads=[bE, b_msk], writes=[bE])

                def back(u):
                    t0, W, hq, qb, keys, ab = u["t0"], u["W"], u["hq"], u["qb"], u["keys"], u["ab"]
                    kv = hq // 3
                    E, bE = u["E"], u["bE"]

                    def pvmm(g):
                        for i, (kt, m) in enumerate(keys):
                            ins = g.matmul(ps[:, ab, qb * 128:(qb + 1) * 128], lhsT=Vt[:, kt, (4 + kv) * 128:(5 + kv) * 128], rhs=E[:, i * 128:(i + 1) * 128],
                                           start=(i == 0), stop=(i == len(keys) - 1))
                        return ins
                    S.op("pe", pvmm, reads=[bE, b_Vt], writes=[psb[ab]])
                    if u["lastq"]:
                        f = fbR.next()
                        fb = f["b"]
                        S.op("dve", lambda g: g.tensor_scalar_add(out=f["ls"][0:64, 0:W], in0=ps[64:128, ab, 0:W], scalar1=sm2[0:64, 2 + hq:3 + hq]),
                             reads=[psb[ab], b_sm2], writes=[fb["ls"]])
                        S.op("dve", lambda g: g.reciprocal(out=f["rl"][0:64, 0:W], in_=f["ls"][0:64, 0:W]), reads=[fb["ls"]], writes=[fb["rl"]])
                        S.op("dve", lambda g: g.tensor_tensor(out=f["y"][0:64, 0:W], in0=ps[0:64, ab, 0:W], in1=f["rl"][0:64, 0:W], op=ALU.mult),
                             reads=[psb[ab], fb["rl"]], writes=[fb["y"]])
                        S.dma("pool", lambda g: g.dma_start(out=mix[256 + hq * 64:256 + (hq + 1) * 64, t0:t0 + W], in_=f["y"][0:64, 0:W]), reads=[fb["y"]])

                LAG = 3
                for idx in range(len(units) + LAG):
                    if idx < len(units):
                        front(units[idx])
                    if idx >= LAG:
                        back(units[idx - LAG])
                if stop_after == ("B2", l):
                    return True
                return False
            if _ph():
                return True
            def _ph():
                S.barrier()
                areset()
                WSZ = 2 * 22528 + 22528
                wout, b_wout = alloc([8, D], BF16), B()
                S.dma("pool", lambda g: g.dma_start(out=wout, in_=w_out[l].rearrange("(k p) n -> p k n", p=128)), writes=[b_wout])
                mxs = [(alloc([8, 512], BF16), B()) for _ in range(2)]
                xgs = [(alloc([8, 512], F32), B()) for _ in range(2)]
                h2s_ = [(alloc([8, 514], BF16), B()) for _ in range(2)]
                sq, b_sq = alloc([8, 512], BF16), B()
                tmp8, b_tmp8 = alloc([8, 512], F32), B()
                rstd, b_rstd = alloc([512], F32), B()
                assert off[0] <= ARENA - WSZ
                for (h2, bh2) in h2s_:
                    S.op("pool", lambda g, h2=h2: g.memset(h2, 0.0), writes=[bh2])
                mixk = mix.rearrange("(k p) t -> p k t", p=128)
                h2sk = h2s.rearrange("(k p) t -> p k t", p=128)
                pbR = Rot([1, 2, 3, 4, 5, 6])
                from concourse.ap import AP as _AP

                def bc3(ap2, nb):
                    (pst, pn), (fs, fn_) = ap2.ap
                    return _AP(ap2.tensor, ap2.offset, [[pst, pn], [0, nb], [fs, fn_]])
                glist = [(gi, g_) for gi, g_ in enumerate(groups) if not (gi == 0 and not ctx_out)]

                def front(n):
                    gi, (t0, W, v) = glist[n]
                    mx, bmx = mxs[n % 2]
                    xg, bxg = xgs[n % 2]
                    for kh in range(2):
                        S.dma("sp", lambda g, kh=kh: g.dma_start(out=mx[:, 4 * kh:4 * kh + 4, 0:W], in_=mixk[:, 4 * kh:4 * kh + 4, t0:t0 + W]), writes=[bmx])
                    S.dma("sp", lambda g: g.dma_start(out=xg[:, :, 0:W], in_=xsrc_k[:, :, t0:t0 + W]), writes=[bxg])
                    for j in range(8):
                        pb = pbR.next()

                        def omm(g, j=j, pb=pb):
                            for c in range(8):
                                ins = g.matmul(ps[:, pb, 0:W], lhsT=wout[:, c, j * 128:(j + 1) * 128], rhs=mx[:, c, 0:W], start=(c == 0), stop=(c == 7))
                            return ins
                        S.op("pe", omm, reads=[b_wout, bmx], writes=[psb[pb]])
                        S.op("dve", lambda g, j=j, pb=pb: g.scalar_tensor_tensor(
                            out=xg[:, j, 0:W], in0=ps[:, pb, 0:W], scalar=modT[:, 16 + j, v:v + 1], in1=xg[:, j, 0:W], op0=ALU.mult, op1=ALU.add),
                            reads=[psb[pb], bxg, b_modT], writes=[bxg])
                    S.dma("pool", lambda g: g.dma_start(out=xTs_k[:, :, t0:t0 + W], in_=xg[:, :, 0:W]), reads=[bxg])

                def back(n):
                    gi, (t0, W, v) = glist[n]
                    xg, bxg = xgs[n % 2]
                    h2, bh2 = h2s_[n % 2]
                    S.op("act", lambda g: g.activation(out=sq[:, :, 0:W], in_=xg[:, :, 0:W], func=AF.Square), reads=[bxg], writes=[b_sq])

                    def ssmm2(g):
                        for k in range(8):
                            ins = g.matmul(ps[:, 0, 0:W], lhsT=onesb[:], rhs=sq[:, k, 0:W], start=(k == 0), stop=(k == 7))
                        return ins
                    S.op("pe", ssmm2, reads=[b_sq, b_ones], writes=[psb[0]])
                    S.op("act", lambda g: g.activation(out=rstd[:, 0:W], in_=ps[:, 0, 0:W], func=AF.Ln, scale=1.0 / D, bias=cst[:, 0:1]),
                         reads=[psb[0], b_cst], writes=[b_rstd])
                    S.op("act", lambda g: g.activation(out=rstd[:, 0:W], in_=rstd[:, 0:W], func=AF.Exp, scale=-0.5), reads=[b_rstd], writes=[b_rstd])
                    S.op("dve", lambda g: g.tensor_tensor(out=tmp8[:, :, 0:W], in0=xg[:, :, 0:W], in1=bc3(rstd[:, 0:W], 8), op=ALU.mult),
                         reads=[bxg, b_rstd], writes=[b_tmp8])
                    for k in range(8):
                        S.op("act", lambda g, k=k: g.activation(
                            out=h2[:, k, 1:1 + W], in_=tmp8[:, k, 0:W], func=AF.Identity, scale=A2[:, k, v:v + 1], bias=modT[:, 24 + k, v:v + 1]),
                            reads=[b_tmp8, b_modT, b_A], writes=[bh2])
                    if gi == 0:
                        S.dma("pool", lambda g: g.dma_start(out=h2sk[:, :, 0:258], in_=h2[:, :, 0:258]), reads=[bh2])
                    elif gi == 1:
                        S.dma("pool", lambda g: g.dma_start(out=h2sk[:, :, 258:258 + 513], in_=h2[:, :, 0:513]), reads=[bh2])
                    elif gi == 8:
                        S.dma("pool", lambda g: g.dma_start(out=h2sk[:, :, t0 + 3:t0 + 3 + 513], in_=h2[:, :, 1:514]), reads=[bh2])
                    else:
                        S.dma("pool", lambda g: g.dma_start(out=h2sk[:, :, t0 + 3:t0 + 3 + 512], in_=h2[:, :, 1:513]), reads=[bh2])

                for n in range(len(glist) + 1):
                    if n < len(glist):
                        front(n)
                    if n >= 1:
                        back(n - 1)
                if stop_after == ("C1", l):
                    return True
                return False
            if _ph():
                return True
            def _ph():
                h2sk = h2s.rearrange("(k p) t -> p k t", p=128)
                wins = []
                if ctx_out:
                    wins.append((0, 258, 0, 256, 1))
                for i in range(9):
                    wo = min(510, NLAT - 510 * i)
                    wins.append((258 + 510 * i, wo + 2, NCTX + 510 * i, wo, 0))
                S.barrier()
                areset()
                wsets = []
                wuk = w_up[l].rearrange("(k p) n -> p k n", p=128)
                wsets.append(sh["ws0"])
                for hf_ in range(1, 2):
                    ws = dict(wg=alloc([8, 1408], BF16), wv=alloc([8, 1408], BF16), wd=alloc([11, D], BF16), b_wg=B(), b_wv=B(), b_wd=B())
                    wsets.append(ws)
                    S.dma("pool", lambda g, hf_=hf_, ws=ws: g.dma_start(out=ws["wg"], in_=wuk[:, :, hf_ * 1408:(hf_ + 1) * 1408]), writes=[ws["b_wg"]])
                    S.dma("pool", lambda g, hf_=hf_, ws=ws: g.dma_start(out=ws["wv"], in_=wuk[:, :, 2816 + hf_ * 1408:2816 + (hf_ + 1) * 1408]), writes=[ws["b_wv"]])
                    S.dma("pool", lambda g, hf_=hf_, ws=ws: g.dma_start(out=ws["wd"], in_=w_down[l][hf_ * 1408:(hf_ + 1) * 1408, :].rearrange("(c p) n -> p c n", p=128)), writes=[ws["b_wd"]])
                hwR = Rot([(alloc([8, 512], BF16), B()) for _ in range(2)])
                xgR = Rot([(alloc([8, 512], F32), B()) for _ in range(1)])
                act, b_act = alloc([11, 512], BF16), B()
                cvR = Rot([(alloc([512], F32), B()) for _ in range(3)])
                sgR = Rot([(alloc([512], F32), B()) for _ in range(3)])
                gvR = Rot([(0, 1), (2, 3), (4, 5)])
                dbR = Rot([6, 7])
                bxw = [B() for _ in wins]
                assert off[0] <= ARENA - (2 * 22528 + 22528), off[0]
                for hf in range(2):
                    ws = wsets[hf]
                    wg, wv, wd, b_wg, b_wv, b_wd = ws["wg"], ws["wv"], ws["wd"], ws["b_wg"], ws["b_wv"], ws["b_wd"]
                    for wi, (cs, Wn, tk0, Wo, v) in enumerate(wins):
                        hw, bhw = hwR.next()
                        S.dma("sp", lambda g, hw=hw, cs=cs, Wn=Wn: g.dma_start(out=hw[:, :, 0:Wn], in_=h2sk[:, :, cs:cs + Wn]), writes=[bhw])
                        xg, bxg = xgR.next()
                        S.dma("sp", lambda g, xg=xg, tk0=tk0, Wo=Wo: g.dma_start(out=xg[:, :, 0:Wo], in_=xTs_k[:, :, tk0:tk0 + Wo]), reads=[bxw[wi]], writes=[bxg])
                        for ci in range(11):
                            c = hf * 11 + ci
                            gb, vb = gvR.next()

                            def umm(g, ci=ci, gb=gb, vb=vb, hw=hw, Wn=Wn, Wo=Wo, wg=wg, wv=wv):
                                for k in range(8):
                                    g.matmul(ps[:, gb, 0:Wn], lhsT=wg[:, k, ci * 128:(ci + 1) * 128], rhs=hw[:, k, 0:Wn], start=(k == 0), stop=(k == 7))
                                for k in range(8):
                                    ins = g.matmul(ps[:, vb, 0:Wo], lhsT=wv[:, k, ci * 128:(ci + 1) * 128], rhs=hw[:, k, 1:1 + Wo], start=(k == 0), stop=(k == 7))
                                return ins
                            S.op("pe", umm, reads=[b_wg, b_wv, bhw], writes=[psb[gb], psb[vb]])
                            cv, bcv = cvR.next()
                            sg, bsg = sgR.next()
                            w0 = smt[:, 75 + c:76 + c]
                            w1 = smt[:, 75 + 22 + c:76 + 22 + c]
                            w2 = smt[:, 75 + 44 + c:76 + 44 + c]
                            bb_ = smt[:, 141 + c:142 + c]
                            S.op("act", lambda g, cv=cv, gb=gb, Wo=Wo, w1=w1, bb_=bb_: g.activation(out=cv[:, 0:Wo], in_=ps[:, gb, 1:1 + Wo], func=AF.Identity, scale=w1, bias=bb_),
                                 reads=[psb[gb], b_smt], writes=[bcv])
                            S.op("dve", lambda g, cv=cv, gb=gb, Wo=Wo, w0=w0: g.scalar_tensor_tensor(out=cv[:, 0:Wo], in0=ps[:, gb, 0:Wo], scalar=w0, in1=cv[:, 0:Wo], op0=ALU.mult, op1=ALU.add),
                                 reads=[psb[gb], bcv, b_smt], writes=[bcv])
                            S.op("dve", lambda g, cv=cv, gb=gb, Wo=Wo, w2=w2: g.scalar_tensor_tensor(out=cv[:, 0:Wo], in0=ps[:, gb, 2:2 + Wo], scalar=w2, in1=cv[:, 0:Wo], op0=ALU.mult, op1=ALU.add),
                                 reads=[psb[gb], bcv, b_smt], writes=[bcv])
                            S.op("act", lambda g, cv=cv, sg=sg, Wo=Wo: g.activation(out=sg[:, 0:Wo], in_=cv[:, 0:Wo], func=AF.Silu), reads=[bcv], writes=[bsg])
                            S.op("dve", lambda g, sg=sg, vb=vb, ci=ci, Wo=Wo: g.tensor_tensor(out=act[:, ci, 0:Wo], in0=ps[:, vb, 0:Wo], in1=sg[:, 0:Wo], op=ALU.mult),
                                 reads=[psb[vb], bsg], writes=[b_act])
                        for j in range(8):
                            db = dbR.next()

                            def dmm(g, j=j, db=db, Wo=Wo, wd=wd):
                                for ci in range(11):
                                    ins = g.matmul(ps[:, db, 0:Wo], lhsT=wd[:, ci, j * 128:(j + 1) * 128], rhs=act[:, ci, 0:Wo], start=(ci == 0), stop=(ci == 10))
                                return ins
                            S.op("pe", dmm, reads=[b_wd, b_act], writes=[psb[db]])
                            S.op("dve", lambda g, j=j, db=db, xg=xg, Wo=Wo, v=v: g.scalar_tensor_tensor(
                                out=xg[:, j, 0:Wo], in0=ps[:, db, 0:Wo], scalar=modT[:, 40 + j, v:v + 1], in1=xg[:, j, 0:Wo], op0=ALU.mult, op1=ALU.add),
                                reads=[psb[db], bxg, b_modT], writes=[bxg])
                        if last and hf == 1:
                            outk = out.rearrange("(k p) t -> p k t", p=128)
                            S.dma("pool", lambda g, xg=xg, tk0=tk0, Wo=Wo: g.dma_start(out=outk[:, :, tk0 - NCTX:tk0 - NCTX + Wo], in_=xg[:, :, 0:Wo]), reads=[bxg])
                        else:
                            S.dma("pool", lambda g, xg=xg, tk0=tk0, Wo=Wo: g.dma_start(out=xTs_k[:, :, tk0:tk0 + Wo], in_=xg[:, :, 0:Wo]), reads=[bxg], writes=[bxw[wi]])
                    if stop_after == ("C2%d" % hf, l):
                        return True
                if stop_after is not None and stop_after[1] == l:
                    return True
                return False
            if _ph():
                return True
            return False

        for l in range(nlayers):
            if run_layer(l):
                break

        S.barrier()
        S.emit(nc, sems, dsems)
    return nc


def _rope_tables():
    pos = np.arange(NLAT)
    row = (pos // 64).astype(np.float32)
    col = (pos % 64).astype(np.float32)
    tabs = []
    for hd in (32, 64):
        quarter = hd // 4
        half = hd // 2
        inv_freq = (np.float32(10000.0) ** (-np.arange(quarter, dtype=np.float32) / np.float32(quarter))).astype(np.float32)
        ang = np.concatenate([row[:, None] * inv_freq[None, :], col[:, None] * inv_freq[None, :]], axis=-1).astype(np.float32)
        cos, sin = np.cos(ang).astype(np.float32), np.sin(ang).astype(np.float32)
        p = np.arange(128)
        d = p % hd
        j = d % half
        sign = np.where(d < half, -1.0, 1.0).astype(np.float32)
        C = np.ones((128, T), np.float32)
        Sg = np.zeros((128, T), np.float32)
        C[:, NCTX:] = cos[:, j].T
        Sg[:, NCTX:] = sin[:, j].T * sign[:, None]
        tabs += [C, Sg]
    return np.stack(tabs, 0)


def _const_mats():
    p = np.arange(128)
    m = np.zeros((7, 128, 128), np.float32)
    m[0] = (p[:, None] // 32 == p[None, :] // 32)
    m[1] = (p[:, None] // 64 == p[None, :] // 64)
    for idx, hd in ((2, 32), (3, 64)):
        perm = (p // hd) * hd + ((p % hd) + hd // 2) % hd
        m[idx] = (p[:, None] == perm[None, :])
    m[4] = np.eye(128, dtype=np.float32)
    m[5] = (p[:, None] >= p[None, :])
    m[6] = (p[:, None] <= p[None, :])
    return m


def _prep_shared(inp):
    f = lambda a: np.ascontiguousarray(np.asarray(a, dtype=np.float32))
    w_in = f(inp["w_in"])
    cols = np.concatenate([np.arange(0, 256), np.arange(256, 512), np.arange(768, 1152),
                           np.arange(1152, 1216), np.arange(1152, 1216), np.arange(1216, 1280), np.arange(1216, 1280),
                           np.arange(1408, 1792), np.arange(1792, 2176), np.arange(512, 768), np.arange(1280, 1408)])
    assert cols.size == 2304
    w_in_ext = np.ascontiguousarray(w_in[:, :, cols])
    b_mod2 = np.ascontiguousarray(np.repeat(f(inp["b_mod"])[:, None, :], 2, axis=1))
    wa, wx = f(inp["lru_wa"]), f(inp["lru_wx"])
    bd = np.zeros((2, 2, 2, 3, 128, 128), np.float32)
    for cc in range(3):
        for hb in range(2):
            bd[:, :, 0, cc, hb * 64:(hb + 1) * 64, hb * 64:(hb + 1) * 64] = wa[:, :, 2 * cc + hb]
            bd[:, :, 1, cc, hb * 64:(hb + 1) * 64, hb * 64:(hb + 1) * 64] = wx[:, :, 2 * cc + hb]
    bd = np.ascontiguousarray(bd.reshape(2, 12, 128, 128))
    p = np.arange(128)
    sm = np.zeros((2, 128, NS), np.float32)
    sm[:, :, 0:8] = f(inp["norm1_gain"]).reshape(2, 8, 128).transpose(0, 2, 1)
    sm[:, :, 8:16] = f(inp["norm2_gain"]).reshape(2, 8, 128).transpose(0, 2, 1)
    sm[:, :, 16] = f(inp["da_q_gain"])[:, p % 32]
    sm[:, :, 17] = f(inp["da_k_gain"])[:, p % 32]
    sm[:, :, 18] = f(inp["sw_q_gain"])[:, p % 64]
    sm[:, :, 19] = f(inp["sw_k_gain"])[:, p % 64]
    sm[:, :, 20] = f(inp["da_sub_gain"])[:, p % 64]
    sm[:, :, 21:27] = f(inp["sw_sink"])[:, None, :]
    cw = f(inp["lru_conv_w"]).reshape(2, 2, 4, 3, 128)
    sm[:, :, 27:51] = cw.transpose(0, 4, 1, 2, 3).reshape(2, 128, 24)
    for base, name in ((51, "lru_conv_b"), (57, "lru_ba"), (63, "lru_bx"), (69, "lru_lambda")):
        sm[:, :, base:base + 6] = f(inp[name]).reshape(2, 2, 3, 128).transpose(0, 3, 1, 2).reshape(2, 128, 6)
    fw = f(inp["ffn_conv_w"]).reshape(2, 3, 22, 128)
    sm[:, :, 75:141] = fw.transpose(0, 3, 1, 2).reshape(2, 128, 66)
    sm[:, :, 141:163] = f(inp["ffn_conv_b"]).reshape(2, 22, 128).transpose(0, 2, 1)
    lam = np.stack([f(inp["da_lam_q1"]), f(inp["da_lam_k1"]), f(inp["da_lam_q2"]), f(inp["da_lam_k2"])], axis=1)
    lamv = np.ascontiguousarray(np.broadcast_to(lam[:, None], (2, 128, 4, 32)))
    return {
        "w_mod": f(inp["w_mod"]), "b_mod2": b_mod2, "w_in": w_in_ext, "w_out": f(inp["w_out"]), "w_up": f(inp["w_up"]),
        "w_down": f(inp["w_down"]), "lru_bd": bd, "smalls": np.ascontiguousarray(sm), "lamv": lamv,
        "cmats": _const_mats(), "rope": _rope_tables(),
    }


def _prep_core(inp, b):
    x = np.asarray(inp["x"], dtype=np.float32)
    ctx = np.asarray(inp["ctx"], dtype=np.float32)
    c = np.asarray(inp["c"], dtype=np.float32)
    c_ctx = np.asarray(inp["c_ctx"], dtype=np.float32)
    xT = np.ascontiguousarray(np.concatenate([ctx[b].T, x[b].T], axis=1))
    cT = np.ascontiguousarray(np.stack([c[b].reshape(8, 128).T, c_ctx.reshape(8, 128).T], axis=-1))
    return {"xT": xT, "cT": cT}


_CACHE = {}


def kernel(**inputs):
    if "nc" not in _CACHE:
        _CACHE["nc"] = build_program()
    nc = _CACHE["nc"]
    shared = _prep_shared(inputs)
    n = 8
    in_maps = []
    for b in range(n):
        m = dict(shared)
        m.update(_prep_core(inputs, b))
        in_maps.append(m)
    res = run_bass_kernel_spmd(nc, in_maps, core_ids=list(range(n)))
    outs = [np.asarray(r["out"]).T for r in res.results]
    return np.ascontiguousarray(np.stack(outs, axis=0).astype(np.float32))
```

```python
import math
import numpy as np
import concourse.bass as bass
import concourse.mybir as mybir
from concourse.bass_utils import run_bass_kernel_spmd

F32 = mybir.dt.float32
BF16 = mybir.dt.bfloat16
U8 = mybir.dt.uint8
AF = mybir.ActivationFunctionType
ALU = mybir.AluOpType

ENGS = ("pe", "act", "dve", "pool", "sp")
NDMASEM = 44

D = 1024
NCTX = 256
NLAT = 4096
T = NCTX + NLAT
NS = 163
EPS = 1e-6
ARENA = 200 * 1024
SCALE_A = 32 ** -0.5
SCALE_B = 64 ** -0.5


class Buf:
    __slots__ = ("name", "w", "r")

    def __init__(self, name):
        self.name = name
        self.w = []
        self.r = []


class Sched:
    def __init__(self):
        self.q = {e: [] for e in ENGS}
        self.cnt = {e: 0 for e in ENGS}
        self.seen = {e: {} for e in ENGS}
        self.dma_n = 0
        self.dma_np = 0
        self.dma_tot = [0] * NDMASEM

    def bufs(self, name, n):
        return [Buf(f"{name}{i}") for i in range(n)]

    def _deps(self, eng, reads, writes):
        need = {}
        for b in reads:
            for (k, v) in b.w:
                if need.get(k, 0) < v:
                    need[k] = v
        for b in writes:
            for (k, v) in b.w:
                if need.get(k, 0) < v:
                    need[k] = v
            for (k, v) in b.r:
                if need.get(k, 0) < v:
                    need[k] = v
        seen = self.seen[eng]
        out = []
        for k, v in need.items():
            if seen.get(k, 0) < v:
                seen[k] = v
                out.append((k, v))
        return out

    def _commit(self, token, reads, writes):
        for b in writes:
            b.w = [token]
            b.r = []
        for b in reads:
            if b not in writes:
                b.r.append(token)
                if len(b.r) > 24:
                    m = {}
                    for (k, v) in b.r:
                        if m.get(k, 0) < v:
                            m[k] = v
                    b.r = list(m.items())

    def op(self, eng, fn, reads=(), writes=()):
        waits = self._deps(eng, reads, writes)
        self.cnt[eng] += 1
        token = (eng, self.cnt[eng])
        self._commit(token, reads, writes)
        self.q[eng].append((waits, fn, token))
        return token

    def dma(self, eng, fn, reads=(), writes=()):
        if eng == "pool":
            s = NDMASEM - 12 + self.dma_np % 12
            self.dma_np += 1
        else:
            s = self.dma_n % (NDMASEM - 12)
            self.dma_n += 1
        key = ("dma", s)
        waits = self._deps(eng, reads, writes)
        prev = self.dma_tot[s]
        if prev > 0 and self.seen[eng].get(key, 0) < prev:
            self.seen[eng][key] = prev
            waits.append((key, prev))
        self.dma_tot[s] = prev + 16
        token = (key, prev + 16)
        self._commit(token, reads, writes)
        self.q[eng].append((waits, fn, token))
        return token

    def barrier(self):
        for eng in ENGS:
            waits = []
            seen = self.seen[eng]
            for k in ENGS:
                v = self.cnt[k]
                if v > 0 and seen.get(k, 0) < v:
                    seen[k] = v
                    waits.append((k, v))
            for s in range(NDMASEM):
                v = self.dma_tot[s]
                key = ("dma", s)
                if v > 0 and seen.get(key, 0) < v:
                    seen[key] = v
                    waits.append((key, v))
            self.q[eng].append((waits, None, None))

    def emit(self, nc, sems, dsems):
        def semof(k):
            return dsems[k[1]] if isinstance(k, tuple) else sems[k]

        def run(eng):
            def body(h):
                for (waits, fn, token) in self.q[eng]:
                    for (k, v) in waits:
                        h.wait_ge(semof(k), v)
                    if fn is None:
                        continue
                    ins = fn(h)
                    k, v = token
                    ins.then_inc(semof(k), 16 if isinstance(k, tuple) else 1)
            return body

        with nc.Block() as block:
            block.tensor(run("pe"))
            block.scalar(run("act"))
            block.vector(run("dve"))
            block.gpsimd(run("pool"))
            block.sync(run("sp"))


class SemCtx:
    def __init__(self, nc):
        self.nc = nc
        self.stack = []

    def __enter__(self):
        sems = {}
        for e in ENGS:
            g = self.nc.semaphore("s_" + e)
            sems[e] = g.__enter__()
            self.stack.append(g)
        dsems = []
        for i in range(NDMASEM):
            g = self.nc.semaphore(f"d{i}")
            dsems.append(g.__enter__())
            self.stack.append(g)
        return sems, dsems

    def __exit__(self, *a):
        for g in reversed(self.stack):
            g.__exit__(None, None, None)
        return False


class Rot:
    def __init__(self, items):
        self.items = items
        self.i = 0

    def next(self):
        it = self.items[self.i % len(self.items)]
        self.i += 1
        return it


def build_program(nlayers=2, dbg=False, stop_after=None):
    nc = bass.Bass("TRN2", target_bir_lowering=False)

    def din(name, shape, dt=F32):
        return nc.dram_tensor(name, shape, dt, kind="ExternalInput").ap()

    xT_in = din("xT", [D, T])
    cT_in = din("cT", [128, 8, 2])
    w_mod = din("w_mod", [2, D, 6144])
    b_mod2 = din("b_mod2", [2, 2, 6144])
    w_in = din("w_in", [2, D, 2304])
    w_out = din("w_out", [2, D, D])
    w_up = din("w_up", [2, D, 5632])
    w_down = din("w_down", [2, 2816, D])
    lru_bd = din("lru_bd", [2, 12, 128, 128])
    smalls = din("smalls", [2, 128, NS])
    lamv = din("lamv", [2, 128, 4, 32])
    cmats = din("cmats", [7, 128, 128])
    rope = din("rope", [4, 128, T])
    out = nc.dram_tensor("out", [D, NLAT], F32, kind="ExternalOutput").ap()

    skind = "ExternalOutput" if dbg else "Internal"

    def dscr(name, shape, dt):
        return nc.dram_tensor(name, shape, dt, kind=skind).ap()

    xTs = dscr("xTs", [D, T], F32)
    qk = dscr("qk", [9, 128, T], BF16)
    vtok = dscr("vtok", [T, 768], BF16)
    cxs = dscr("cxs", [3, 128, T], F32)
    cgs = dscr("cgs", [3, 128, T], BF16)
    mix = dscr("mix", [D, T], BF16)
    h2s = dscr("h2s", [D, T + 4], BF16)
    modd = dscr("modd", [128, 96], F32) if dbg else None

    S = Sched()
    groups = [(0, 256, 1)] + [(256 + 512 * i, 512, 0) for i in range(8)]

    with (
        nc.sbuf_tensor("arena", [128, ARENA], U8) as arena,
        nc.sbuf_tensor("cst", [128, 4], F32) as cst,
        nc.sbuf_tensor("cmb", [128, 4, 128], BF16) as cmb,
        nc.sbuf_tensor("msk", [128, 2, 128], BF16) as msk,
        nc.sbuf_tensor("ident", [128, 128], F32) as ident,
        nc.sbuf_tensor("onesb", [128, 128], BF16) as onesb,
        nc.sbuf_tensor("smt", [128, NS], F32) as smt,
        nc.sbuf_tensor("ctile", [128, 8, 2], F32) as ctile,
        nc.sbuf_tensor("sc", [128, 8, 2], F32) as sc,
        nc.sbuf_tensor("modT", [128, 48, 2], F32) as modT,
        nc.sbuf_tensor("A1", [128, 8, 2], F32) as A1,
        nc.sbuf_tensor("A2", [128, 8, 2], F32) as A2,
        nc.sbuf_tensor("lamt", [128, 4, 32], F32) as lamt,
        nc.sbuf_tensor("sm2", [128, 64], F32) as sm2,
        nc.psum_tensor("ps", [128, 8, 512], F32) as ps,
        SemCtx(nc) as (sems, dsems),
    ):
        off = [0]

        def areset():
            off[0] = 0

        def alloc(shape, dt):
            esz = 2 if dt == BF16 else 4
            n = int(np.prod(shape))
            nb = (n * esz + 63) // 64 * 64
            assert off[0] + nb <= ARENA, (off[0], nb)
            a = arena[:, off[0]:off[0] + n * esz].bitcast(dt)
            off[0] += nb
            if len(shape) == 2:
                a = a.rearrange("p (a b) -> p a b", a=shape[0])
            elif len(shape) == 3:
                a = a.rearrange("p (a b c) -> p a b c", a=shape[0], b=shape[1])
            return a

        nb_ = [0]

        def B(name="b"):
            nb_[0] += 1
            return Buf(f"{name}{nb_[0]}")

        psb = [B("ps") for _ in range(8)]
        b_cst, b_cmb, b_msk, b_ident, b_ones, b_smt, b_ct, b_sc, b_modT, b_A, b_lam, b_sm2 = [B("c") for _ in range(12)]

        def psflat(b0, n):
            return ps[:, b0:b0 + (n + 511) // 512, :].rearrange("p b n -> p (b n)")[:, 0:n]

        def rev(ap2d):
            (pst, pn), (fs, fn_) = ap2d.ap
            from concourse.ap import AP
            return AP(ap2d.tensor, ap2d.offset + (fn_ - 1) * fs, [[pst, pn], [-fs, fn_]])

        S.op("pool", lambda g: g.memset(cst[:, 0:1], EPS), writes=[b_cst])
        S.op("pool", lambda g: g.memset(cst[:, 1:2], 1.0), writes=[b_cst])
        S.op("pool", lambda g: g.memset(cst[:, 2:3], 0.0), writes=[b_cst])
        S.op("pool", lambda g: g.memset(onesb[:], 1.0), writes=[b_ones])
        S.dma("pool", lambda g: g.dma_start(out=cmb[:], in_=cmats[0:4].rearrange("c p n -> p c n")), writes=[b_cmb])
        S.dma("pool", lambda g: g.dma_start(out=msk[:], in_=cmats[5:7].rearrange("c p n -> p c n")), writes=[b_msk])
        S.dma("sp", lambda g: g.dma_start(out=ident[:], in_=cmats[4]), writes=[b_ident])
        S.dma("sp", lambda g: g.dma_start(out=ctile[:], in_=cT_in), writes=[b_ct])
        S.op("act", lambda g: g.activation(out=sc[:], in_=ctile[:], func=AF.Silu), reads=[b_ct], writes=[b_sc])

        def run_layer(l):
            last = (l == nlayers - 1)
            ctx_out = not last
            lam_init = 0.8 - 0.6 * math.exp(-0.3 * l)
            xsrc = xT_in if l == 0 else xTs
            xsrc_k = xsrc.rearrange("(k p) t -> p k t", p=128)
            xTs_k = xTs.rearrange("(k p) t -> p k t", p=128)

            sh = {}
            def _ph():
                S.barrier()
                areset()
                S.dma("sp", lambda g, l=l: g.dma_start(out=smt[:], in_=smalls[l]), writes=[b_smt])
                S.dma("sp", lambda g, l=l: g.dma_start(out=lamt[:], in_=lamv[l]), writes=[b_lam])
                bm = alloc([6144], F32)
                modrow = alloc([6144], F32)
                wms = [alloc([8, 512], F32) for _ in range(2)]
                b_bm, b_modrow = B(), B()
                b_wm = [B(), B()]
                S.dma("sp", lambda g, l=l: g.dma_start(out=bm[0:2, :], in_=b_mod2[l]), writes=[b_bm])
                wmk = w_mod[l].rearrange("(k p) n -> p k n", p=128)
                for cg in range(12):
                    wm, bw = wms[cg % 2], b_wm[cg % 2]
                    S.dma("sp", lambda g, wm=wm, cg=cg: g.dma_start(out=wm, in_=wmk[:, :, cg * 512:(cg + 1) * 512]), writes=[bw])

                    def mm(g, wm=wm, cg=cg):
                        for k in range(8):
                            ins = g.matmul(ps[0:2, cg % 2, :], lhsT=sc[:, k, :], rhs=wm[:, k, :], start=(k == 0), stop=(k == 7))
                        return ins
                    S.op("pe", mm, reads=[bw, b_sc], writes=[psb[cg % 2]])
                    S.op("dve", lambda g, cg=cg: g.tensor_tensor(out=modrow[0:2, cg * 512:(cg + 1) * 512], in0=ps[0:2, cg % 2, :],
                                                                 in1=bm[0:2, cg * 512:(cg + 1) * 512], op=ALU.add),
                         reads=[psb[cg % 2], b_bm], writes=[b_modrow])

                def tr(g):
                    for j in range(48):
                        ins = g.transpose(ps[:, 2, 2 * j:2 * j + 2], modrow[0:2, j * 128:(j + 1) * 128], ident[0:2, 0:2])
                    return ins
                S.op("pe", tr, reads=[b_modrow, b_ident], writes=[psb[2]])
                S.op("dve", lambda g: g.tensor_copy(out=modT[:].rearrange("p a b -> p (a b)"), in_=ps[:, 2, 0:96]), reads=[psb[2]], writes=[b_modT])
                for v in range(2):
                    S.op("dve", lambda g, v=v: g.scalar_tensor_tensor(out=A1[:, :, v], in0=modT[:, 8:16, v], scalar=1.0, in1=smt[:, 0:8],
                                                                      op0=ALU.add, op1=ALU.mult), reads=[b_modT, b_smt], writes=[b_A])
                    S.op("dve", lambda g, v=v: g.scalar_tensor_tensor(out=A2[:, :, v], in0=modT[:, 32:40, v], scalar=1.0, in1=smt[:, 8:16],
                                                                      op0=ALU.add, op1=ALU.mult), reads=[b_modT, b_smt], writes=[b_A])
                if dbg and l == 0:
                    S.dma("pool", lambda g: g.dma_start(out=modd, in_=modT[:].rearrange("p a b -> p (a b)")), reads=[b_modT])
                S.op("act", lambda g: g.mul(out=sm2[:, 0:1], in_=smt[:, 20:21], mul=float(1.0 - lam_init)), reads=[b_smt], writes=[b_sm2])
                S.op("act", lambda g: g.activation(out=sm2[:, 2:8], in_=smt[:, 21:27], func=AF.Exp), reads=[b_smt], writes=[b_sm2])
                S.op("dve", lambda g: g.tensor_tensor(out=lamt[:, 0, :], in0=lamt[:, 0, :], in1=lamt[:, 1, :], op=ALU.mult), reads=[b_lam], writes=[b_lam])
                S.op("dve", lambda g: g.tensor_tensor(out=lamt[:, 2, :], in0=lamt[:, 2, :], in1=lamt[:, 3, :], op=ALU.mult), reads=[b_lam], writes=[b_lam])
                S.op("dve", lambda g: g.tensor_reduce(out=sm2[:, 20:21], in_=lamt[:, 0, :], axis=mybir.AxisListType.X, op=ALU.add), reads=[b_lam], writes=[b_sm2])
                S.op("dve", lambda g: g.tensor_reduce(out=sm2[:, 21:22], in_=lamt[:, 2, :], axis=mybir.AxisListType.X, op=ALU.add), reads=[b_lam], writes=[b_sm2])
                S.op("act", lambda g: g.activation(out=sm2[:, 22:24], in_=sm2[:, 20:22], func=AF.Exp), reads=[b_sm2], writes=[b_sm2])
                S.op("dve", lambda g: g.scalar_tensor_tensor(out=sm2[:, 1:2], in0=sm2[:, 23:24], scalar=float(-lam_init), in1=sm2[:, 22:23],
                                                             op0=ALU.add, op1=ALU.subtract), reads=[b_sm2], writes=[b_sm2])
                L_ = smt[:, 69:75]
                S.op("dve", lambda g: g.tensor_scalar_mul(out=sm2[:, 24:30], in0=L_, scalar1=-1.0), reads=[b_smt], writes=[b_sm2])
                S.op("dve", lambda g: g.tensor_tensor(out=sm2[:, 48:54], in0=sm2[:, 24:30], in1=L_, op=ALU.max), reads=[b_smt, b_sm2], writes=[b_sm2])
                S.op("act", lambda g: g.activation(out=sm2[:, 30:36], in_=sm2[:, 48:54], func=AF.Exp, scale=-1.0), reads=[b_sm2], writes=[b_sm2])
                S.op("dve", lambda g: g.tensor_scalar_add(out=sm2[:, 36:42], in0=sm2[:, 30:36], scalar1=1.0), reads=[b_sm2], writes=[b_sm2])
                S.op("act", lambda g: g.activation(out=sm2[:, 42:48], in_=sm2[:, 36:42], func=AF.Ln), reads=[b_sm2], writes=[b_sm2])
                S.op("dve", lambda g: g.tensor_scalar(out=sm2[:, 36:42], in0=sm2[:, 36:42], scalar1=-1.0, scalar2=1e-30, op0=ALU.add, op1=ALU.max),
                     reads=[b_sm2], writes=[b_sm2])
                S.op("dve", lambda g: g.reciprocal(out=sm2[:, 54:60], in_=sm2[:, 36:42]), reads=[b_sm2], writes=[b_sm2])
                S.op("dve", lambda g: g.tensor_tensor(out=sm2[:, 30:36], in0=sm2[:, 30:36], in1=sm2[:, 54:60], op=ALU.mult), reads=[b_sm2], writes=[b_sm2])
                S.op("dve", lambda g: g.tensor_tensor(out=sm2[:, 30:36], in0=sm2[:, 30:36], in1=sm2[:, 42:48], op=ALU.mult), reads=[b_sm2], writes=[b_sm2])
                S.op("dve", lambda g: g.tensor_scalar_max(out=sm2[:, 24:30], in0=sm2[:, 24:30], scalar1=0.0), reads=[b_sm2], writes=[b_sm2])
                S.op("dve", lambda g: g.tensor_tensor(out=sm2[:, 24:30], in0=sm2[:, 24:30], in1=sm2[:, 30:36], op=ALU.add), reads=[b_sm2], writes=[b_sm2])
                S.op("dve", lambda g: g.tensor_scalar_mul(out=sm2[:, 8:14], in0=sm2[:, 24:30], scalar1=-8.0), reads=[b_sm2], writes=[b_sm2])
                S.op("dve", lambda g: g.tensor_scalar_mul(out=sm2[:, 14:20], in0=sm2[:, 24:30], scalar1=-16.0), reads=[b_sm2], writes=[b_sm2])
                if stop_after == ("M", l):
                    return True

                return False
            if _ph():
                return True
            def _ph():
                S.barrier()
                areset()
                win = alloc([8, 2304], BF16)
                b_win = B()
                wik = w_in[l].rearrange("(k p) n -> p k n", p=128)
                for hh in range(2):
                    S.dma("pool", lambda g, hh=hh: g.dma_start(out=win[:, :, hh * 1152:(hh + 1) * 1152], in_=wik[:, :, hh * 1152:(hh + 1) * 1152]), writes=[b_win])
                xgs = [(alloc([8, 512], F32), B()) for _ in range(2)]
                rps = [(alloc([4, 512], F32), B()) for _ in range(2)]
                hTs = [(alloc([8, 512], BF16), B()) for _ in range(2)]
                sq, b_sq = alloc([8, 512], BF16), B()
                tmp8, b_tmp8 = alloc([8, 512], F32), B()
                rstd, b_rstd = alloc([512], F32), B()
                NSET = 3
                sets = [dict(qf=alloc([2, 512], F32), sqb=alloc([2, 512], BF16), rr=alloc([2, 512], F32), qn=alloc([2, 512], F32), qo=alloc([2, 512], BF16),
                             b={n: B() for n in ("qf", "sqb", "rr", "qn", "qo")}, pb=1 + 2 * i) for i in range(NSET)]
                vos = [(alloc([6, 128], BF16), B()) for _ in range(2)]
                for (vo, bvo) in vos:
                    S.op("pool", lambda g, vo=vo: g.memset(vo[:, :, 64:128], 1.0), writes=[bvo])
                ropek = rope.rearrange("r p t -> p r t")
                from concourse.ap import AP as _AP

                def bc3(ap2, nb):
                    (pst, pn), (fs, fn_) = ap2.ap
                    return _AP(ap2.tensor, ap2.offset, [[pst, pn], [0, nb], [fs, fn_]])

                batches = [(0, 2, 16, 0), (2, 2, 17, 0), (4, 2, 18, 1), (6, 1, 18, 1), (7, 2, 19, 1)]

                def prologue(gi):
                    t0, W, v = groups[gi]
                    xg, bxg = xgs[gi % 2]
                    rp, brp = rps[gi % 2]
                    hT, bhT = hTs[gi % 2]

                    def s0():
                        S.dma("sp", lambda g: g.dma_start(out=xg[:, :, 0:W], in_=xsrc_k[:, :, t0:t0 + W]), writes=[bxg])
                        S.dma("sp", lambda g: g.dma_start(out=rp[:, :, 0:W], in_=ropek[:, :, t0:t0 + W]), writes=[brp])

                    def s1():
                        S.op("act", lambda g: g.activation(out=sq[:, :, 0:W], in_=xg[:, :, 0:W], func=AF.Square), reads=[bxg], writes=[b_sq])

                    def s2():
                        def ssmm(g):
                            for k in range(8):
                                ins = g.matmul(ps[:, 0, 0:W], lhsT=onesb[:], rhs=sq[:, k, 0:W], start=(k == 0), stop=(k == 7))
                            return ins
                        S.op("pe", ssmm, reads=[b_sq, b_ones], writes=[psb[0]])

                    def s3():
                        S.op("act", lambda g: g.activation(out=rstd[:, 0:W], in_=ps[:, 0, 0:W], func=AF.Ln, scale=1.0 / D, bias=cst[:, 0:1]),
                             reads=[psb[0], b_cst], writes=[b_rstd])
                        S.op("act", lambda g: g.activation(out=rstd[:, 0:W], in_=rstd[:, 0:W], func=AF.Exp, scale=-0.5), reads=[b_rstd], writes=[b_rstd])

                    def s4():
                        S.op("dve", lambda g: g.tensor_tensor(out=tmp8[:, :, 0:W], in0=xg[:, :, 0:W], in1=bc3(rstd[:, 0:W], 8), op=ALU.mult),
                             reads=[bxg, b_rstd], writes=[b_tmp8])

                    def s5():
                        for k in range(8):
                            S.op("act", lambda g, k=k: g.activation(
                                out=hT[:, k, 0:W], in_=tmp8[:, k, 0:W], func=AF.Identity, scale=A1[:, k, v:v + 1], bias=modT[:, k, v:v + 1]),
                                reads=[b_tmp8, b_modT, b_A], writes=[bhT])
                    return [s0, s1, s2, s3, s4, s5]

                def proj(oc0, nb, pb0, hT, W):
                    def f(g):
                        for i in range(nb):
                            for k in range(8):
                                ins = g.matmul(ps[:, pb0 + i, 0:W], lhsT=win[:, k, (oc0 + i) * 128:(oc0 + i + 1) * 128], rhs=hT[:, k, 0:W], start=(k == 0), stop=(k == 7))
                        return ins
                    return f

                def qkbatch(gi, bi, st):
                    t0, W, v = groups[gi]
                    rp, brp = rps[gi % 2]
                    hT, bhT = hTs[gi % 2]
                    oc0, nb, gcol, mi = batches[bi]
                    bb = st["b"]
                    pb0 = st["pb"]
                    pbs = psb[pb0:pb0 + nb]
                    inv = 1.0 / 32 if mi == 0 else 1.0 / 64
                    rc, rsn = (0, 1) if mi == 0 else (2, 3)
                    qf, sqb, rr, qn, qo = (st[n][:, 0:nb, 0:W] for n in ("qf", "sqb", "rr", "qn", "qo"))
                    pv = ps[:, pb0:pb0 + nb, 0:W]

                    def s0():
                        S.op("pe", proj(oc0, nb, pb0, hT, W), reads=[b_win, bhT], writes=pbs)

                    def s1():
                        S.op("act", lambda g: g.activation(out=qf, in_=pv, func=AF.Identity), reads=pbs, writes=[bb["qf"]])
                        S.op("act", lambda g: g.activation(out=sqb, in_=pv, func=AF.Square), reads=pbs, writes=[bb["sqb"]])

                    def s2():
                        def smm(g):
                            for i in range(nb):
                                ins = g.matmul(ps[:, pb0 + i, 0:W], lhsT=cmb[:, mi, :], rhs=st["sqb"][:, i, 0:W], start=True, stop=True)
                            return ins
                        S.op("pe", smm, reads=[bb["sqb"], b_cmb], writes=pbs)

                    def s3():
                        S.op("act", lambda g: g.activation(out=rr, in_=pv, func=AF.Ln, scale=inv, bias=cst[:, 0:1]), reads=pbs + [b_cst], writes=[bb["rr"]])
                        S.op("act", lambda g: g.activation(out=rr, in_=rr, func=AF.Exp, scale=-0.5), reads=[bb["rr"]], writes=[bb["rr"]])

                    def s4():
                        S.op("dve", lambda g: g.scalar_tensor_tensor(out=qn, in0=qf, scalar=smt[:, gcol:gcol + 1], in1=rr, op0=ALU.mult, op1=ALU.mult),
                             reads=[bb["qf"], bb["rr"], b_smt], writes=[bb["qn"]])

                    def s5():
                        S.op("dve", lambda g: g.tensor_copy(out=sqb, in_=qn), reads=[bb["qn"]], writes=[bb["sqb"]])

                    def s6():
                        def rmm(g):
                            for i in range(nb):
                                ins = g.matmul(ps[:, pb0 + i, 0:W], lhsT=cmb[:, 2 + mi, :], rhs=st["sqb"][:, i, 0:W], start=True, stop=True)
                            return ins
                        S.op("pe", rmm, reads=[bb["sqb"], b_cmb], writes=pbs)
                        S.op("pool", lambda g: g.tensor_tensor(out=qf, in0=qn, in1=bc3(rp[:, rc, 0:W], nb), op=ALU.mult), reads=[bb["qn"], brp], writes=[bb["qf"]])

                    def s7():
                        S.op("dve", lambda g: g.tensor_tensor(out=rr, in0=pv, in1=bc3(rp[:, rsn, 0:W], nb), op=ALU.mult), reads=pbs + [brp], writes=[bb["rr"]])

                    def s8():
                        S.op("pool", lambda g: g.tensor_tensor(out=qo, in0=qf, in1=rr, op=ALU.add), reads=[bb["qf"], bb["rr"]], writes=[bb["qo"]])
                        S.dma("sp", lambda g: g.dma_start(out=qk[oc0:oc0 + nb].rearrange("c p t -> p c t")[:, :, t0:t0 + W], in_=qo), reads=[bb["qo"]])
                    return [s0, s1, s2, s3, s4, s5, s6, s7, s8]

                def cbatch(gi, which, st, c0, nb):
                    t0, W, v = groups[gi]
                    hT, bhT = hTs[gi % 2]
                    bb = st["b"]
                    pb0 = st["pb"]
                    pbs = psb[pb0:pb0 + nb]
                    pv = ps[:, pb0:pb0 + nb, 0:W]

                    def s0():
                        S.op("pe", proj((9 if which == 0 else 12) + c0, nb, pb0, hT, W), reads=[b_win, bhT], writes=pbs)

                    def s1():
                        if which == 0:
                            S.op("act", lambda g: g.activation(out=st["qn"][:, 0:nb, 0:W], in_=pv, func=AF.Identity), reads=pbs, writes=[bb["qn"]])
                            S.dma("sp", lambda g: g.dma_start(out=cxs[c0:c0 + nb].rearrange("c p t -> p c t")[:, :, t0:t0 + W], in_=st["qn"][:, 0:nb, 0:W]), reads=[bb["qn"]])
                        else:
                            S.op("act", lambda g: g.activation(out=st["qo"][:, 0:nb, 0:W], in_=pv, func=AF.Gelu_apprx_tanh), reads=pbs, writes=[bb["qo"]])
                            S.dma("sp", lambda g: g.dma_start(out=cgs[c0:c0 + nb].rearrange("c p t -> p c t")[:, :, t0:t0 + W], in_=st["qo"][:, 0:nb, 0:W]), reads=[bb["qo"]])
                    return [s0, s1]

                def vitem(gi):
                    t0, W, v = groups[gi]
                    hT, bhT = hTs[gi % 2]

                    def mk(tt):
                        def s():
                            def vmm(g):
                                for k in range(8):
                                    ins = g.matmul(ps[:, 7, 0:384], lhsT=hT[:, k, tt * 128:(tt + 1) * 128], rhs=win[:, k, 1920:2304], start=(k == 0), stop=(k == 7))
                                return ins
                            S.op("pe", vmm, reads=[b_win, bhT], writes=[psb[7]])
                            vo, bvo = vos[tt % 2]
                            S.op("dve", lambda g: g.tensor_copy(out=vo[:, :, 0:64], in_=ps[:, 7, 0:384].rearrange("p (h d) -> p h d", h=6)), reads=[psb[7]], writes=[bvo])
                            S.dma("sp", lambda g: g.dma_start(out=vtok[t0 + tt * 128:t0 + (tt + 1) * 128, :], in_=vo[:].rearrange("p h d -> p (h d)")), reads=[bvo])
                        return s
                    return [mk(tt) for tt in range(W // 128)]

                items = []
                pidx = {}
                nset = [0]

                def add(stages, res=None, deps=()):
                    items.append((stages, res, list(deps)))
                    return len(items) - 1

                pidx[0] = add(prologue(0))
                for gi in range(len(groups)):
                    for bi in range(len(batches)):
                        st = sets[nset[0] % NSET]
                        add(qkbatch(gi, bi, st), res=("set", nset[0] % NSET), deps=[pidx[gi]])
                        nset[0] += 1
                        if bi == 1 and gi + 1 < len(groups):
                            pidx[gi + 1] = add(prologue(gi + 1), res=("pro",))
                    for which in range(2):
                        for (c0, nb) in ((0, 2), (2, 1)):
                            st = sets[nset[0] % NSET]
                            add(cbatch(gi, which, st, c0, nb), res=("set", nset[0] % NSET), deps=[pidx[gi]])
                            nset[0] += 1
                    add(vitem(gi), res=("v",), deps=[pidx[gi]])
                SK = 2
                start, end_ = [], []
                resend = {}
                for i, (stages, res, deps) in enumerate(items):
                    s = 0 if i == 0 else start[i - 1] + SK
                    if res is not None and res in resend:
                        s = max(s, resend[res])
                    for dI in deps:
                        s = max(s, end_[dI])
                    start.append(s)
                    end_.append(s + len(stages))
                    if res is not None:
                        resend[res] = s + len(stages)
                tmax = max(end_)
                for t in range(tmax):
                    for i, (stages, res, deps) in enumerate(items):
                        k = t - start[i]
                        if 0 <= k < len(stages):
                            stages[k]()
                if stop_after == ("A", l):
                    return True
                return False
            if _ph():
                return True
            def _ph():
                S.barrier()
                areset()
                bdw, b_bdw = alloc([12, 128], BF16), B()
                S.dma("pool", lambda g: g.dma_start(out=bdw, in_=lru_bd[l].rearrange("c p n -> p c n")), writes=[b_bdw])
                xxs = [(alloc([T], F32), B()) for _ in range(2)]
                xcs = [(alloc([T], F32), B()) for _ in range(2)]
                xcbs = [(alloc([T], BF16), B()) for _ in range(2)]
                rr_, b_r = alloc([T], F32), B()
                ii_, b_i = alloc([T], F32), B()
                aa_, b_a = alloc([T], F32), B()
                mm_, b_m = alloc([T], F32), B()
                hh_ = [alloc([T], F32), alloc([T], F32)]
                b_h = [B(), B()]
                gg_, b_g = alloc([T], BF16), B()
                segs = [(0, NCTX), (NCTX, T)]
                its = [(cc, d) for cc in range(3) for d in range(2)]

                def conv(n):
                    cc, d = its[n]
                    xx, b_xx = xxs[cc % 2]
                    xc, b_xc = xcs[n % 2]
                    xcb, b_xcb = xcbs[n % 2]
                    if d == 0:
                        S.dma("sp", lambda g: g.dma_start(out=xx, in_=cxs[cc]), writes=[b_xx])
                    wcol = lambda k: smt[:, 27 + d * 12 + k * 3 + cc:28 + d * 12 + k * 3 + cc]
                    bcol = smt[:, 51 + d * 3 + cc:52 + d * 3 + cc]
                    S.op("dve", lambda g: g.tensor_scalar(out=xc, in0=xx, scalar1=wcol(3), scalar2=bcol, op0=ALU.mult, op1=ALU.add),
                         reads=[b_xx, b_smt], writes=[b_xc])
                    for k in range(3):
                        s_ = 3 - k
                        for (a_, e_) in segs:
                            if d == 0:
                                dst, src = (a_ + s_, e_), (a_, e_ - s_)
                            else:
                                dst, src = (a_, e_ - s_), (a_ + s_, e_)
                            S.op("dve", lambda g, dst=dst, src=src, k=k: g.scalar_tensor_tensor(
                                out=xc[:, dst[0]:dst[1]], in0=xx[:, src[0]:src[1]], scalar=wcol(k), in1=xc[:, dst[0]:dst[1]], op0=ALU.mult, op1=ALU.add),
                                reads=[b_xx, b_xc, b_smt], writes=[b_xc])
                    S.op("act", lambda g: g.activation(out=xcb, in_=xc, func=AF.Identity), reads=[b_xc], writes=[b_xcb])

                def gates(n):
                    cc, d = its[n]
                    xc, b_xc = xcs[n % 2]
                    xcb, b_xcb = xcbs[n % 2]
                    ia = (d * 2 + 0) * 3 + cc
                    ix = (d * 2 + 1) * 3 + cc
                    for sg0 in range(0, T, 2048):
                        sgw = min(2048, T - sg0)

                        def gmm(g, sg0=sg0, sgw=sgw):
                            for q0 in range(0, sgw, 512):
                                w_ = min(512, sgw - q0)
                                g.matmul(ps[:, q0 // 512, 0:w_], lhsT=bdw[:, ia, :], rhs=xcb[:, sg0 + q0:sg0 + q0 + w_], start=True, stop=True)
                                ins = g.matmul(ps[:, 4 + q0 // 512, 0:w_], lhsT=bdw[:, ix, :], rhs=xcb[:, sg0 + q0:sg0 + q0 + w_], start=True, stop=True)
                            return ins
                        S.op("pe", gmm, reads=[b_xcb, b_bdw], writes=psb)
                        S.op("act", lambda g, sg0=sg0, sgw=sgw: g.activation(
                            out=rr_[:, sg0:sg0 + sgw], in_=psflat(0, sgw), func=AF.Sigmoid, bias=smt[:, 57 + d * 3 + cc:58 + d * 3 + cc]),
                            reads=psb[0:4] + [b_smt], writes=[b_r])
                        S.op("act", lambda g, sg0=sg0, sgw=sgw: g.activation(
                            out=ii_[:, sg0:sg0 + sgw], in_=psflat(4, sgw), func=AF.Sigmoid, bias=smt[:, 63 + d * 3 + cc:64 + d * 3 + cc]),
                            reads=psb[4:8] + [b_smt], writes=[b_i])
                    S.op("act", lambda g: g.activation(out=aa_, in_=rr_, func=AF.Exp, scale=sm2[:, 8 + d * 3 + cc:9 + d * 3 + cc]),
                         reads=[b_r, b_sm2], writes=[b_a])
                    S.op("act", lambda g: g.activation(out=mm_, in_=rr_, func=AF.Exp, scale=sm2[:, 14 + d * 3 + cc:15 + d * 3 + cc]),
                         reads=[b_r, b_sm2], writes=[b_m])
                    S.op("act", lambda g: g.activation(out=mm_, in_=mm_, func=AF.Sqrt, scale=-1.0, bias=cst[:, 1:2]), reads=[b_m, b_cst], writes=[b_m])
                    S.op("pool", lambda g: g.tensor_tensor(out=ii_, in0=ii_, in1=xc, op=ALU.mult), reads=[b_i, b_xc], writes=[b_i])

                def scan(n):
                    cc, d = its[n]
                    S.op("dve", lambda g: g.tensor_tensor(out=mm_, in0=mm_, in1=ii_, op=ALU.mult), reads=[b_m, b_i], writes=[b_m])
                    hd = hh_[d]
                    if d == 0:
                        S.op("dve", lambda g: g.tensor_tensor_scan(out=hd, data0=aa_, data1=mm_, initial=0.0, op0=ALU.mult, op1=ALU.add),
                             reads=[b_a, b_m], writes=[b_h[d]])
                    else:
                        S.op("dve", lambda g: g.tensor_tensor_scan(out=rev(hd[:, 0:NCTX]), data0=rev(aa_[:, 0:NCTX]), data1=rev(mm_[:, 0:NCTX]),
                                                                   initial=0.0, op0=ALU.mult, op1=ALU.add), reads=[b_a, b_m], writes=[b_h[d]])
                        S.op("dve", lambda g: g.tensor_tensor_scan(out=rev(hd[:, NCTX:T]), data0=rev(aa_[:, NCTX:T]), data1=rev(mm_[:, NCTX:T]),
                                                                   initial=hd[:, 0:1], op0=ALU.mult, op1=ALU.add), reads=[b_a, b_m, b_h[d]], writes=[b_h[d]])
                        S.dma("sp", lambda g: g.dma_start(out=gg_, in_=cgs[cc]), writes=[b_g])
                        S.op("pool", lambda g: g.tensor_tensor(out=hh_[0], in0=hh_[0], in1=hh_[1], op=ALU.add), reads=[b_h[0], b_h[1]], writes=[b_h[0]])
                        yy_, b_y = xcbs[n % 2]
                        S.op("pool", lambda g: g.tensor_tensor(out=yy_, in0=hh_[0], in1=gg_, op=ALU.mult), reads=[b_h[0], b_g], writes=[b_y])
                        S.dma("pool", lambda g: g.dma_start(out=mix[640 + cc * 128:640 + (cc + 1) * 128, :], in_=yy_), reads=[b_y])

                conv(0)
                for n in range(len(its)):
                    gates(n)
                    if n + 1 < len(its):
                        conv(n + 1)
                    scan(n)
                if stop_after == ("B3", l):
                    return True
                return False
            if _ph():
                return True
            def _ph():
                S.barrier()
                areset()
                Vt, b_Vt = alloc([34, 768], BF16), B()
                vtk = vtok.rearrange("(kt p) c -> p kt c", p=128)
                for q4 in range(0, 34, 9):
                    q5 = min(34, q4 + 9)
                    S.dma("sp", lambda g, q4=q4, q5=q5: g.dma_start(out=Vt[:, q4:q5, :], in_=vtk[:, q4:q5, :]), writes=[b_Vt])
                b1_mark = off[0]
                sh.update(Vt=Vt, b_Vt=b_Vt, b1_mark=b1_mark)
                KT, b_KT = alloc([2, T], BF16), B()
                S.dma("sp", lambda g: g.dma_start(out=KT, in_=qk[2:4].rearrange("c p t -> p c t")), writes=[b_KT])
                QTR = Rot([(alloc([2, 512], BF16), B()) for _ in range(2)])
                ER = [Rot([(alloc([2, 512], BF16), B()) for _ in range(2)]) for _ in range(2)]
                evR = Rot([dict(o=alloc([4, 512], F32), l=alloc([4, 512], F32), bo=B(), bl=B()) for _ in range(2)])
                finR = Rot([dict(oo=alloc([512], F32), osq=alloc([512], BF16), rr=alloc([512], F32), y=alloc([512], BF16),
                                 b={n: B() for n in ("oo", "osq", "rr", "y")}) for _ in range(4)])
                sbR = Rot([0, 1, 2])
                pending = []

                def flush():
                    while pending:
                        pending.pop(0)()
                for gi, (t0, W, v) in enumerate(groups):
                    if gi == 0 and not ctx_out:
                        continue
                    nkt = 2 if gi == 0 else 34
                    QT, bQT = QTR.next()
                    S.dma("sp", lambda g, QT=QT, t0=t0, W=W: g.dma_start(out=QT[:, :, 0:W], in_=qk[0:2].rearrange("c p t -> p c t")[:, :, t0:t0 + W]), writes=[bQT])
                    for c in range(2):
                        Ecur = [None] * 2
                        Eprev = [None] * 2
                        for kt in range(nkt + 1):
                            if kt == min(22, nkt):
                                flush()
                            if kt < nkt:
                                for p in range(2):
                                    E, bE = ER[p].next()
                                    Ecur[p] = (E, bE)

                                    def qk2(g, p=p, c=c, kt=kt, QT=QT, W=W):
                                        for j in (2 * p, 2 * p + 1):
                                            ins = g.matmul(ps[:, j, 0:W], lhsT=KT[32 * j:32 * j + 32, c, kt * 128:(kt + 1) * 128], rhs=QT[32 * j:32 * j + 32, c, 0:W],
                                                           start=True, stop=True, tile_position=(32 * j, 0))
                                        return ins
                                    S.op("pe", qk2, reads=[b_KT, bQT], writes=[psb[2 * p], psb[2 * p + 1]])
                                    S.op("act", lambda g, p=p, E=E, W=W: g.activation(out=E[:, :, 0:W], in_=ps[:, 2 * p:2 * p + 2, 0:W], func=AF.Exp, scale=SCALE_A),
                                         reads=[psb[2 * p], psb[2 * p + 1]], writes=[bE])
                            if kt >= 1:
                                for p in range(2):
                                    E, bE = Eprev[p]
                                    h = 2 * c + p

                                    def pv2(g, p=p, E=E, h=h, kt=kt, W=W, nkt=nkt):
                                        for m in range(2):
                                            ins = g.matmul(ps[:, 4 + 2 * p + m, 0:W], lhsT=Vt[:, kt - 1, h * 128:(h + 1) * 128], rhs=E[:, m, 0:W], start=(kt == 1), stop=(kt == nkt))
                                        return ins
                                    S.op("pe", pv2, reads=[bE, b_Vt], writes=[psb[4 + 2 * p], psb[5 + 2 * p]])
                            Eprev = list(Ecur)
                        ev = evR.next()
                        S.op("dve", lambda g, ev=ev, W=W: g.tensor_copy(out=ev["o"][0:64, :, 0:W], in_=ps[0:64, 4:8, 0:W]), reads=psb[4:8], writes=[ev["bo"]])
                        S.op("dve", lambda g, ev=ev, W=W: g.tensor_copy(out=ev["l"][0:64, :, 0:W], in_=ps[64:128, 4:8, 0:W]), reads=psb[4:8], writes=[ev["bl"]])
                        S.op("dve", lambda g, ev=ev, W=W: g.reciprocal(out=ev["l"][0:64, :, 0:W], in_=ev["l"][0:64, :, 0:W]), reads=[ev["bl"]], writes=[ev["bl"]])
                        S.op("pool", lambda g, ev=ev, W=W: g.tensor_tensor(out=ev["o"][0:64, :, 0:W], in0=ev["o"][0:64, :, 0:W], in1=ev["l"][0:64, :, 0:W], op=ALU.mult),
                             reads=[ev["bo"], ev["bl"]], writes=[ev["bo"]])
                        for hh in range(2):
                            h = 2 * c + hh
                            f = finR.next()
                            fb = f["b"]
                            S.op("dve", lambda g, f=f, ev=ev, hh=hh, W=W: g.scalar_tensor_tensor(
                                out=f["oo"][0:64, 0:W], in0=ev["o"][0:64, 2 * hh + 1, 0:W], scalar=sm2[0:64, 1:2], in1=ev["o"][0:64, 2 * hh, 0:W],
                                op0=ALU.mult, op1=ALU.add), reads=[ev["bo"], b_sm2], writes=[fb["oo"]])
                            S.op("pool", lambda g, f=f, W=W: g.tensor_tensor(out=f["osq"][0:64, 0:W], in0=f["oo"][0:64, 0:W], in1=f["oo"][0:64, 0:W], op=ALU.mult),
                                 reads=[fb["oo"]], writes=[fb["osq"]])

                            def late(f=f, fb=fb, h=h, t0=t0, W=W):
                                S.op("pe", lambda g: g.matmul(ps[0:64, 0, 0:W], lhsT=onesb[0:64, 0:64], rhs=f["osq"][0:64, 0:W], start=True, stop=True),
                                     reads=[fb["osq"], b_ones], writes=[psb[0]])
                                S.op("act", lambda g: g.activation(out=f["rr"][0:64, 0:W], in_=ps[0:64, 0, 0:W], func=AF.Ln, scale=1.0 / 64, bias=cst[0:64, 0:1]),
                                     reads=[psb[0], b_cst], writes=[fb["rr"]])
                                S.op("act", lambda g: g.activation(out=f["rr"][0:64, 0:W], in_=f["rr"][0:64, 0:W], func=AF.Exp, scale=-0.5),
                                     reads=[fb["rr"]], writes=[fb["rr"]])
                                S.op("dve", lambda g: g.scalar_tensor_tensor(out=f["y"][0:64, 0:W], in0=f["oo"][0:64, 0:W], scalar=sm2[0:64, 0:1], in1=f["rr"][0:64, 0:W],
                                                                             op0=ALU.mult, op1=ALU.mult), reads=[fb["oo"], fb["rr"], b_sm2], writes=[fb["y"]])
                                S.dma("pool", lambda g: g.dma_start(out=mix[h * 64:(h + 1) * 64, t0:t0 + W], in_=f["y"][0:64, 0:W]), reads=[fb["y"]])
                            pending.append(late)
                flush()
                if stop_after == ("B1", l):
                    return True
                return False
            if _ph():
                return True
            def _ph():
                Vt, b_Vt, b1_mark = sh["Vt"], sh["b_Vt"], sh["b1_mark"]
                S.barrier()
                WSZ = 2 * 22528 + 22528
                off[0] = ARENA - WSZ
                ws0 = dict(wg=alloc([8, 1408], BF16), wv=alloc([8, 1408], BF16), wd=alloc([11, D], BF16), b_wg=B(), b_wv=B(), b_wd=B())
                wuk = w_up[l].rearrange("(k p) n -> p k n", p=128)
                S.dma("pool", lambda g: g.dma_start(out=ws0["wg"], in_=wuk[:, :, 0:1408]), writes=[ws0["b_wg"]])
                S.dma("pool", lambda g: g.dma_start(out=ws0["wv"], in_=wuk[:, :, 2816:2816 + 1408]), writes=[ws0["b_wv"]])
                S.dma("pool", lambda g: g.dma_start(out=ws0["wd"], in_=w_down[l][0:1408, :].rearrange("(c p) n -> p c n", p=128)), writes=[ws0["b_wd"]])
                sh["ws0"] = ws0
                off[0] = b1_mark
                KB, b_KB = alloc([2, T], BF16), B()
                S.dma("sp", lambda g: g.dma_start(out=KB, in_=qk[7:9].rearrange("c p t -> p c t")), writes=[b_KB])
                QBs = [(alloc([3, 512], BF16), B()) for _ in range(2)]
                EBR = Rot([(alloc([640], BF16), B()) for _ in range(6)])
                fbR = Rot([dict(ls=alloc([512], F32), rl=alloc([512], F32), y=alloc([512], BF16), b={n: B() for n in ("ls", "rl", "y")}) for _ in range(3)])
                sbR = Rot([0, 2, 4])
                abR = Rot([6, 7])
                assert off[0] <= ARENA - WSZ, off[0]
                units = []
                ng = 0
                for gi, (t0, W, v) in enumerate(groups):
                    if gi == 0 and not ctx_out:
                        continue
                    QB, bQB = QBs[ng % 2]
                    ng += 1
                    for hq in range(6):
                        ab = abR.next()
                        nqb = W // 128
                        for qb in range(nqb):
                            tt = t0 // 128 + qb
                            if gi == 0:
                                keys = [(0, None), (1, None)]
                            else:
                                n = tt - 2
                                keys = [(0, None), (1, None)]
                                if n > 0:
                                    keys.append((tt - 1, 0))
                                keys.append((tt, None))
                                if n < 31:
                                    keys.append((tt + 1, 1))
                            units.append(dict(gi=gi, t0=t0, W=W, hq=hq, qb=qb, keys=keys, ab=ab, QB=QB, bQB=bQB, first=(hq == 0 and qb == 0), lastq=(qb == nqb - 1)))

                def front(u):
                    t0, W, hq, qb, keys, QB, bQB = u["t0"], u["W"], u["hq"], u["qb"], u["keys"], u["QB"], u["bQB"]
                    c, half, kv = hq // 2, hq % 2, hq // 3
                    p0 = half * 64
                    if u["first"]:
                        S.dma("sp", lambda g: g.dma_start(out=QB[:, :, 0:W], in_=qk[4:7].rearrange("c p t -> p c t")[:, :, t0:t0 + W]), writes=[bQB])
                    nk = len(keys)
                    b0 = sbR.next()

                    def qkmm(g):
                        for i, (kt, m) in enumerate(keys):
                            ins = g.matmul(ps[:, b0 + i // 4, (i % 4) * 128:(i % 4 + 1) * 128], lhsT=KB[p0:p0 + 64, kv, kt * 128:(kt + 1) * 128],
                                           rhs=QB[p0:p0 + 64, c, qb * 128:(qb + 1) * 128], start=True, stop=True, tile_position=(p0, 0))
                        return ins
                    S.op("pe", qkmm, reads=[b_KB, bQB], writes=[psb[b0], psb[b0 + 1]])
                    E, bE = EBR.next()
                    u["E"], u["bE"] = E, bE
                    S.op("act", lambda g: g.activation(out=E[:, 0:nk * 128], in_=psflat(b0, nk * 128), func=AF.Exp, scale=SCALE_B),
                         reads=[psb[b0], psb[b0 + 1]], writes=[bE])
                    for i, (kt, m) in enumerate(keys):
                        if m is not None:
                            S.op("dve", lambda g, i=i, m=m: g.tensor_tensor(out=E[:, i * 128:(i + 1) * 128], in0=E[:, i * 128:(i + 1) * 128], in1=msk[:, m, :], op=ALU.mult),
                                 reads=[bE, b_msk], writes=[bE])

                def back(u):
                    t0, W, hq, qb, keys, ab = u["t0"], u["W"], u["hq"], u["qb"], u["keys"], u["ab"]
                    kv = hq // 3
                    E, bE = u["E"], u["bE"]

                    def pvmm(g):
                        for i, (kt, m) in enumerate(keys):
                            ins = g.matmul(ps[:, ab, qb * 128:(qb + 1) * 128], lhsT=Vt[:, kt, (4 + kv) * 128:(5 + kv) * 128], rhs=E[:, i * 128:(i + 1) * 128],
                                           start=(i == 0), stop=(i == len(keys) - 1))
                        return ins
                    S.op("pe", pvmm, reads=[bE, b_Vt], writes=[psb[ab]])
                    if u["lastq"]:
                        f = fbR.next()
                        fb = f["b"]
                        S.op("dve", lambda g: g.tensor_scalar_add(out=f["ls"][0:64, 0:W], in0=ps[64:128, ab, 0:W], scalar1=sm2[0:64, 2 + hq:3 + hq]),
                             reads=[psb[ab], b_sm2], writes=[fb["ls"]])
                        S.op("dve", lambda g: g.reciprocal(out=f["rl"][0:64, 0:W], in_=f["ls"][0:64, 0:W]), reads=[fb["ls"]], writes=[fb["rl"]])
                        S.op("dve", lambda g: g.tensor_tensor(out=f["y"][0:64, 0:W], in0=ps[0:64, ab, 0:W], in1=f["rl"][0:64, 0:W], op=ALU.mult),
                             reads=[psb[ab], fb["rl"]], writes=[fb["y"]])
                        S.dma("pool", lambda g: g.dma_start(out=mix[256 + hq * 64:256 + (hq + 1) * 64, t0:t0 + W], in_=f["y"][0:64, 0:W]), reads=[fb["y"]])

                LAG = 3
                for idx in range(len(units) + LAG):
                    if idx < len(units):
                        front(units[idx])
                    if idx >= LAG:
                        back(units[idx - LAG])
                if stop_after == ("B2", l):
                    return True
                return False
            if _ph():
                return True
            def _ph():
                S.barrier()
                areset()
                WSZ = 2 * 22528 + 22528
                wout, b_wout = alloc([8, D], BF16), B()
                S.dma("pool", lambda g: g.dma_start(out=wout, in_=w_out[l].rearrange("(k p) n -> p k n", p=128)), writes=[b_wout])
                mxs = [(alloc([8, 512], BF16), B()) for _ in range(2)]
                xgs = [(alloc([8, 512], F32), B()) for _ in range(2)]
                h2s_ = [(alloc([8, 514], BF16), B()) for _ in range(2)]
                sq, b_sq = alloc([8, 512], BF16), B()
                tmp8, b_tmp8 = alloc([8, 512], F32), B()
                rstd, b_rstd = alloc([512], F32), B()
                assert off[0] <= ARENA - WSZ
                for (h2, bh2) in h2s_:
                    S.op("pool", lambda g, h2=h2: g.memset(h2, 0.0), writes=[bh2])
                mixk = mix.rearrange("(k p) t -> p k t", p=128)
                h2sk = h2s.rearrange("(k p) t -> p k t", p=128)
                pbR = Rot([1, 2, 3, 4, 5, 6])
                from concourse.ap import AP as _AP

                def bc3(ap2, nb):
                    (pst, pn), (fs, fn_) = ap2.ap
                    return _AP(ap2.tensor, ap2.offset, [[pst, pn], [0, nb], [fs, fn_]])
                glist = [(gi, g_) for gi, g_ in enumerate(groups) if not (gi == 0 and not ctx_out)]

                def front(n):
                    gi, (t0, W, v) = glist[n]
                    mx, bmx = mxs[n % 2]
                    xg, bxg = xgs[n % 2]
                    for kh in range(2):
                        S.dma("sp", lambda g, kh=kh: g.dma_start(out=mx[:, 4 * kh:4 * kh + 4, 0:W], in_=mixk[:, 4 * kh:4 * kh + 4, t0:t0 + W]), writes=[bmx])
                    S.dma("sp", lambda g: g.dma_start(out=xg[:, :, 0:W], in_=xsrc_k[:, :, t0:t0 + W]), writes=[bxg])
                    for j in range(8):
                        pb = pbR.next()

                        def omm(g, j=j, pb=pb):
                            for c in range(8):
                                ins = g.matmul(ps[:, pb, 0:W], lhsT=wout[:, c, j * 128:(j + 1) * 128], rhs=mx[:, c, 0:W], start=(c == 0), stop=(c == 7))
                            return ins
                        S.op("pe", omm, reads=[b_wout, bmx], writes=[psb[pb]])
                        S.op("dve", lambda g, j=j, pb=pb: g.scalar_tensor_tensor(
                            out=xg[:, j, 0:W], in0=ps[:, pb, 0:W], scalar=modT[:, 16 + j, v:v + 1], in1=xg[:, j, 0:W], op0=ALU.mult, op1=ALU.add),
                            reads=[psb[pb], bxg, b_modT], writes=[bxg])
                    S.dma("pool", lambda g: g.dma_start(out=xTs_k[:, :, t0:t0 + W], in_=xg[:, :, 0:W]), reads=[bxg])

                def back(n):
                    gi, (t0, W, v) = glist[n]
                    xg, bxg = xgs[n % 2]
                    h2, bh2 = h2s_[n % 2]
                    S.op("act", lambda g: g.activation(out=sq[:, :, 0:W], in_=xg[:, :, 0:W], func=AF.Square), reads=[bxg], writes=[b_sq])

                    def ssmm2(g):
                        for k in range(8):
                            ins = g.matmul(ps[:, 0, 0:W], lhsT=onesb[:], rhs=sq[:, k, 0:W], start=(k == 0), stop=(k == 7))
                        return ins
                    S.op("pe", ssmm2, reads=[b_sq, b_ones], writes=[psb[0]])
                    S.op("act", lambda g: g.activation(out=rstd[:, 0:W], in_=ps[:, 0, 0:W], func=AF.Ln, scale=1.0 / D, bias=cst[:, 0:1]),
                         reads=[psb[0], b_cst], writes=[b_rstd])
                    S.op("act", lambda g: g.activation(out=rstd[:, 0:W], in_=rstd[:, 0:W], func=AF.Exp, scale=-0.5), reads=[b_rstd], writes=[b_rstd])
                    S.op("dve", lambda g: g.tensor_tensor(out=tmp8[:, :, 0:W], in0=xg[:, :, 0:W], in1=bc3(rstd[:, 0:W], 8), op=ALU.mult),
                         reads=[bxg, b_rstd], writes=[b_tmp8])
                    for k in range(8):
                        S.op("act", lambda g, k=k: g.activation(
                            out=h2[:, k, 1:1 + W], in_=tmp8[:, k, 0:W], func=AF.Identity, scale=A2[:, k, v:v + 1], bias=modT[:, 24 + k, v:v + 1]),
                            reads=[b_tmp8, b_modT, b_A], writes=[bh2])
                    if gi == 0:
                        S.dma("pool", lambda g: g.dma_start(out=h2sk[:, :, 0:258], in_=h2[:, :, 0:258]), reads=[bh2])
                    elif gi == 1:
                        S.dma("pool", lambda g: g.dma_start(out=h2sk[:, :, 258:258 + 513], in_=h2[:, :, 0:513]), reads=[bh2])
                    elif gi == 8:
                        S.dma("pool", lambda g: g.dma_start(out=h2sk[:, :, t0 + 3:t0 + 3 + 513], in_=h2[:, :, 1:514]), reads=[bh2])
                    else:
                        S.dma("pool", lambda g: g.dma_start(out=h2sk[:, :, t0 + 3:t0 + 3 + 512], in_=h2[:, :, 1:513]), reads=[bh2])

                for n in range(len(glist) + 1):
                    if n < len(glist):
                        front(n)
                    if n >= 1:
                        back(n - 1)
                if stop_after == ("C1", l):
                    return True
                return False
            if _ph():
                return True
            def _ph():
                h2sk = h2s.rearrange("(k p) t -> p k t", p=128)
                wins = []
                if ctx_out:
                    wins.append((0, 258, 0, 256, 1))
                for i in range(9):
                    wo = min(510, NLAT - 510 * i)
                    wins.append((258 + 510 * i, wo + 2, NCTX + 510 * i, wo, 0))
                S.barrier()
                areset()
                wsets = []
                wuk = w_up[l].rearrange("(k p) n -> p k n", p=128)
                wsets.append(sh["ws0"])
                for hf_ in range(1, 2):
                    ws = dict(wg=alloc([8, 1408], BF16), wv=alloc([8, 1408], BF16), wd=alloc([11, D], BF16), b_wg=B(), b_wv=B(), b_wd=B())
                    wsets.append(ws)
                    S.dma("pool", lambda g, hf_=hf_, ws=ws: g.dma_start(out=ws["wg"], in_=wuk[:, :, hf_ * 1408:(hf_ + 1) * 1408]), writes=[ws["b_wg"]])
                    S.dma("pool", lambda g, hf_=hf_, ws=ws: g.dma_start(out=ws["wv"], in_=wuk[:, :, 2816 + hf_ * 1408:2816 + (hf_ + 1) * 1408]), writes=[ws["b_wv"]])
                    S.dma("pool", lambda g, hf_=hf_, ws=ws: g.dma_start(out=ws["wd"], in_=w_down[l][hf_ * 1408:(hf_ + 1) * 1408, :].rearrange("(c p) n -> p c n", p=128)), writes=[ws["b_wd"]])
                hwR = Rot([(alloc([8, 512], BF16), B()) for _ in range(2)])
                xgR = Rot([(alloc([8, 512], F32), B()) for _ in range(1)])
                act, b_act = alloc([11, 512], BF16), B()
                cvR = Rot([(alloc([512], F32), B()) for _ in range(3)])
                sgR = Rot([(alloc([512], F32), B()) for _ in range(3)])
                gvR = Rot([(0, 1), (2, 3), (4, 5)])
                dbR = Rot([6, 7])
                bxw = [B() for _ in wins]
                assert off[0] <= ARENA - (2 * 22528 + 22528), off[0]
                acts = [(act, b_act), (alloc([11, 512], BF16), B())]
                hws = hwR.items
                outk = out.rearrange("(k p) t -> p k t", p=128)
                seq = [(hf, wi) for hf in range(2) for wi in range(len(wins))]
                if stop_after == ("C20", l):
                    seq = [(0, wi) for wi in range(len(wins))]

                def load_h(n):
                    hf, wi = seq[n]
                    cs, Wn, tk0, Wo, v = wins[wi]
                    hw, bhw = hws[n % 2]
                    S.dma("sp", lambda g: g.dma_start(out=hw[:, :, 0:Wn], in_=h2sk[:, :, cs:cs + Wn]), writes=[bhw])

                def up_chunk(n, ci):
                    hf, wi = seq[n]
                    cs, Wn, tk0, Wo, v = wins[wi]
                    ws = wsets[hf]
                    wg, wv = ws["wg"], ws["wv"]
                    hw, bhw = hws[n % 2]
                    act_, b_act_ = acts[n % 2]
                    c = hf * 11 + ci
                    gb, vb = gvR.next()

                    def umm(g):
                        for k in range(8):
                            g.matmul(ps[:, gb, 0:Wn], lhsT=wg[:, k, ci * 128:(ci + 1) * 128], rhs=hw[:, k, 0:Wn], start=(k == 0), stop=(k == 7))
                        for k in range(8):
                            ins = g.matmul(ps[:, vb, 0:Wo], lhsT=wv[:, k, ci * 128:(ci + 1) * 128], rhs=hw[:, k, 1:1 + Wo], start=(k == 0), stop=(k == 7))
                        return ins
                    S.op("pe", umm, reads=[ws["b_wg"], ws["b_wv"], bhw], writes=[psb[gb], psb[vb]])
                    cv, bcv = cvR.next()
                    sg, bsg = sgR.next()
                    w0 = smt[:, 75 + c:76 + c]
                    w1 = smt[:, 75 + 22 + c:76 + 22 + c]
                    w2 = smt[:, 75 + 44 + c:76 + 44 + c]
                    bb_ = smt[:, 141 + c:142 + c]
                    S.op("act", lambda g: g.activation(out=cv[:, 0:Wo], in_=ps[:, gb, 1:1 + Wo], func=AF.Identity, scale=w1, bias=bb_), reads=[psb[gb], b_smt], writes=[bcv])
                    S.op("dve", lambda g: g.scalar_tensor_tensor(out=cv[:, 0:Wo], in0=ps[:, gb, 0:Wo], scalar=w0, in1=cv[:, 0:Wo], op0=ALU.mult, op1=ALU.add),
                         reads=[psb[gb], bcv, b_smt], writes=[bcv])
                    S.op("dve", lambda g: g.scalar_tensor_tensor(out=cv[:, 0:Wo], in0=ps[:, gb, 2:2 + Wo], scalar=w2, in1=cv[:, 0:Wo], op0=ALU.mult, op1=ALU.add),
                         reads=[psb[gb], bcv, b_smt], writes=[bcv])
                    S.op("act", lambda g: g.activation(out=sg[:, 0:Wo], in_=cv[:, 0:Wo], func=AF.Silu), reads=[bcv], writes=[bsg])
                    S.op("dve", lambda g: g.tensor_tensor(out=act_[:, ci, 0:Wo], in0=ps[:, vb, 0:Wo], in1=sg[:, 0:Wo], op=ALU.mult), reads=[psb[vb], bsg], writes=[b_act_])

                def down(n):
                    hf, wi = seq[n]
                    cs, Wn, tk0, Wo, v = wins[wi]
                    wd, b_wd = wsets[hf]["wd"], wsets[hf]["b_wd"]
                    act_, b_act_ = acts[n % 2]
                    xg, bxg = xgR.items[0]
                    for j in range(8):
                        db = dbR.next()

                        def dmm(g, j=j, db=db):
                            for ci in range(11):
                                ins = g.matmul(ps[:, db, 0:Wo], lhsT=wd[:, ci, j * 128:(j + 1) * 128], rhs=act_[:, ci, 0:Wo], start=(ci == 0), stop=(ci == 10))
                            return ins
                        S.op("pe", dmm, reads=[b_wd, b_act_], writes=[psb[db]])
                        S.op("dve", lambda g, j=j, db=db: g.scalar_tensor_tensor(
                            out=xg[:, j, 0:Wo], in0=ps[:, db, 0:Wo], scalar=modT[:, 40 + j, v:v + 1], in1=xg[:, j, 0:Wo], op0=ALU.mult, op1=ALU.add),
                            reads=[psb[db], bxg, b_modT], writes=[bxg])
                    if last and hf == 1:
                        S.dma("pool", lambda g: g.dma_start(out=outk[:, :, tk0 - NCTX:tk0 - NCTX + Wo], in_=xg[:, :, 0:Wo]), reads=[bxg])
                    else:
                        S.dma("pool", lambda g: g.dma_start(out=xTs_k[:, :, tk0:tk0 + Wo], in_=xg[:, :, 0:Wo]), reads=[bxg], writes=[bxw[wi]])

                def load_x(n):
                    hf, wi = seq[n]
                    cs, Wn, tk0, Wo, v = wins[wi]
                    xg, bxg = xgR.items[0]
                    S.dma("sp", lambda g: g.dma_start(out=xg[:, :, 0:Wo], in_=xTs_k[:, :, tk0:tk0 + Wo]), reads=[bxw[wi]], writes=[bxg])

                NPRE = 2
                load_h(0)
                load_x(0)
                for ci in range(11):
                    up_chunk(0, ci)
                for n in range(len(seq)):
                    if n + 1 < len(seq):
                        load_h(n + 1)
                        for ci in range(NPRE):
                            up_chunk(n + 1, ci)
                    down(n)
                    if n + 1 < len(seq):
                        load_x(n + 1)
                        for ci in range(NPRE, 11):
                            up_chunk(n + 1, ci)
                if stop_after in (("C20", l), ("C21", l)):
                    return True
                if stop_after is not None and stop_after[1] == l:
                    return True
                return False
            if _ph():
                return True
            return False

        for l in range(nlayers):
            if run_layer(l):
                break

        S.barrier()
        S.emit(nc, sems, dsems)
    return nc


def _rope_tables():
    pos = np.arange(NLAT)
    row = (pos // 64).astype(np.float32)
    col = (pos % 64).astype(np.float32)
    tabs = []
    for hd in (32, 64):
        quarter = hd // 4
        half = hd // 2
        inv_freq = (np.float32(10000.0) ** (-np.arange(quarter, dtype=np.float32) / np.float32(quarter))).astype(np.float32)
        ang = np.concatenate([row[:, None] * inv_freq[None, :], col[:, None] * inv_freq[None, :]], axis=-1).astype(np.float32)
        cos, sin = np.cos(ang).astype(np.float32), np.sin(ang).astype(np.float32)
        p = np.arange(128)
        d = p % hd
        j = d % half
        sign = np.where(d < half, -1.0, 1.0).astype(np.float32)
        C = np.ones((128, T), np.float32)
        Sg = np.zeros((128, T), np.float32)
        C[:, NCTX:] = cos[:, j].T
        Sg[:, NCTX:] = sin[:, j].T * sign[:, None]
        tabs += [C, Sg]
    return np.stack(tabs, 0)


def _const_mats():
    p = np.arange(128)
    m = np.zeros((7, 128, 128), np.float32)
    m[0] = (p[:, None] // 32 == p[None, :] // 32)
    m[1] = (p[:, None] // 64 == p[None, :] // 64)
    for idx, hd in ((2, 32), (3, 64)):
        perm = (p // hd) * hd + ((p % hd) + hd // 2) % hd
        m[idx] = (p[:, None] == perm[None, :])
    m[4] = np.eye(128, dtype=np.float32)
    m[5] = (p[:, None] >= p[None, :])
    m[6] = (p[:, None] <= p[None, :])
    return m


def _prep_shared(inp):
    f = lambda a: np.ascontiguousarray(np.asarray(a, dtype=np.float32))
    w_in = f(inp["w_in"])
    cols = np.concatenate([np.arange(0, 256), np.arange(256, 512), np.arange(768, 1152),
                           np.arange(1152, 1216), np.arange(1152, 1216), np.arange(1216, 1280), np.arange(1216, 1280),
                           np.arange(1408, 1792), np.arange(1792, 2176), np.arange(512, 768), np.arange(1280, 1408)])
    assert cols.size == 2304
    w_in_ext = np.ascontiguousarray(w_in[:, :, cols])
    b_mod2 = np.ascontiguousarray(np.repeat(f(inp["b_mod"])[:, None, :], 2, axis=1))
    wa, wx = f(inp["lru_wa"]), f(inp["lru_wx"])
    bd = np.zeros((2, 2, 2, 3, 128, 128), np.float32)
    for cc in range(3):
        for hb in range(2):
            bd[:, :, 0, cc, hb * 64:(hb + 1) * 64, hb * 64:(hb + 1) * 64] = wa[:, :, 2 * cc + hb]
            bd[:, :, 1, cc, hb * 64:(hb + 1) * 64, hb * 64:(hb + 1) * 64] = wx[:, :, 2 * cc + hb]
    bd = np.ascontiguousarray(bd.reshape(2, 12, 128, 128))
    p = np.arange(128)
    sm = np.zeros((2, 128, NS), np.float32)
    sm[:, :, 0:8] = f(inp["norm1_gain"]).reshape(2, 8, 128).transpose(0, 2, 1)
    sm[:, :, 8:16] = f(inp["norm2_gain"]).reshape(2, 8, 128).transpose(0, 2, 1)
    sm[:, :, 16] = f(inp["da_q_gain"])[:, p % 32]
    sm[:, :, 17] = f(inp["da_k_gain"])[:, p % 32]
    sm[:, :, 18] = f(inp["sw_q_gain"])[:, p % 64]
    sm[:, :, 19] = f(inp["sw_k_gain"])[:, p % 64]
    sm[:, :, 20] = f(inp["da_sub_gain"])[:, p % 64]
    sm[:, :, 21:27] = f(inp["sw_sink"])[:, None, :]
    cw = f(inp["lru_conv_w"]).reshape(2, 2, 4, 3, 128)
    sm[:, :, 27:51] = cw.transpose(0, 4, 1, 2, 3).reshape(2, 128, 24)
    for base, name in ((51, "lru_conv_b"), (57, "lru_ba"), (63, "lru_bx"), (69, "lru_lambda")):
        sm[:, :, base:base + 6] = f(inp[name]).reshape(2, 2, 3, 128).transpose(0, 3, 1, 2).reshape(2, 128, 6)
    fw = f(inp["ffn_conv_w"]).reshape(2, 3, 22, 128)
    sm[:, :, 75:141] = fw.transpose(0, 3, 1, 2).reshape(2, 128, 66)
    sm[:, :, 141:163] = f(inp["ffn_conv_b"]).reshape(2, 22, 128).transpose(0, 2, 1)
    lam = np.stack([f(inp["da_lam_q1"]), f(inp["da_lam_k1"]), f(inp["da_lam_q2"]), f(inp["da_lam_k2"])], axis=1)
    lamv = np.ascontiguousarray(np.broadcast_to(lam[:, None], (2, 128, 4, 32)))
    return {
        "w_mod": f(inp["w_mod"]), "b_mod2": b_mod2, "w_in": w_in_ext, "w_out": f(inp["w_out"]), "w_up": f(inp["w_up"]),
        "w_down": f(inp["w_down"]), "lru_bd": bd, "smalls": np.ascontiguousarray(sm), "lamv": lamv,
        "cmats": _const_mats(), "rope": _rope_tables(),
    }


def _prep_core(inp, b):
    x = np.asarray(inp["x"], dtype=np.float32)
    ctx = np.asarray(inp["ctx"], dtype=np.float32)
    c = np.asarray(inp["c"], dtype=np.float32)
    c_ctx = np.asarray(inp["c_ctx"], dtype=np.float32)
    xT = np.ascontiguousarray(np.concatenate([ctx[b].T, x[b].T], axis=1))
    cT = np.ascontiguousarray(np.stack([c[b].reshape(8, 128).T, c_ctx.reshape(8, 128).T], axis=-1))
    return {"xT": xT, "cT": cT}


_CACHE = {}


def kernel(**inputs):
    if "nc" not in _CACHE:
        _CACHE["nc"] = build_program()
    nc = _CACHE["nc"]
    shared = _prep_shared(inputs)
    n = 8
    in_maps = []
    for b in range(n):
        m = dict(shared)
        m.update(_prep_core(inputs, b))
        in_maps.append(m)
    res = run_bass_kernel_spmd(nc, in_maps, core_ids=list(range(n)))
    outs = [np.asarray(r["out"]).T for r in res.results]
    return np.ascontiguousarray(np.stack(outs, axis=0).astype(np.float32))
```

```python
import math
import numpy as np
import concourse.bass as bass
import concourse.mybir as mybir
from concourse.bass_utils import run_bass_kernel_spmd

F32 = mybir.dt.float32
BF16 = mybir.dt.bfloat16
U8 = mybir.dt.uint8
AF = mybir.ActivationFunctionType
ALU = mybir.AluOpType

ENGS = ("pe", "act", "dve", "pool", "sp")
NDMASEM = 44

D = 1024
NCTX = 256
NLAT = 4096
T = NCTX + NLAT
NS = 163
EPS = 1e-6
ARENA = 200 * 1024
SCALE_A = 32 ** -0.5
SCALE_B = 64 ** -0.5


class Buf:
    __slots__ = ("name", "w", "r")

    def __init__(self, name):
        self.name = name
        self.w = []
        self.r = []


class Sched:
    def __init__(self):
        self.q = {e: [] for e in ENGS}
        self.cnt = {e: 0 for e in ENGS}
        self.seen = {e: {} for e in ENGS}
        self.dma_n = 0
        self.dma_np = 0
        self.dma_tot = [0] * NDMASEM

    def bufs(self, name, n):
        return [Buf(f"{name}{i}") for i in range(n)]

    def _deps(self, eng, reads, writes):
        need = {}
        for b in reads:
            for (k, v) in b.w:
                if need.get(k, 0) < v:
                    need[k] = v
        for b in writes:
            for (k, v) in b.w:
                if need.get(k, 0) < v:
                    need[k] = v
            for (k, v) in b.r:
                if need.get(k, 0) < v:
                    need[k] = v
        seen = self.seen[eng]
        out = []
        for k, v in need.items():
            if seen.get(k, 0) < v:
                seen[k] = v
                out.append((k, v))
        return out

    def _commit(self, token, reads, writes, acc=False):
        for b in writes:
            if acc:
                b.w.append(token)
            else:
                b.w = [token]
                b.r = []
        for b in reads:
            if b not in writes:
                b.r.append(token)
                if len(b.r) > 24:
                    m = {}
                    for (k, v) in b.r:
                        if m.get(k, 0) < v:
                            m[k] = v
                    b.r = list(m.items())

    def op(self, eng, fn, reads=(), writes=()):
        waits = self._deps(eng, reads, writes)
        self.cnt[eng] += 1
        token = (eng, self.cnt[eng])
        self._commit(token, reads, writes)
        self.q[eng].append((waits, fn, token))
        return token

    def dma(self, eng, fn, reads=(), writes=(), acc=False):
        if eng == "pool":
            s = NDMASEM - 12 + self.dma_np % 12
            self.dma_np += 1
        else:
            s = self.dma_n % (NDMASEM - 12)
            self.dma_n += 1
        key = ("dma", s)
        waits = self._deps(eng, reads, writes)
        prev = self.dma_tot[s]
        if prev > 0 and self.seen[eng].get(key, 0) < prev:
            self.seen[eng][key] = prev
            waits.append((key, prev))
        self.dma_tot[s] = prev + 16
        token = (key, prev + 16)
        self._commit(token, reads, writes, acc)
        self.q[eng].append((waits, fn, token))
        return token

    def barrier(self):
        for eng in ENGS:
            waits = []
            seen = self.seen[eng]
            for k in ENGS:
                v = self.cnt[k]
                if v > 0 and seen.get(k, 0) < v:
                    seen[k] = v
                    waits.append((k, v))
            for s in range(NDMASEM):
                v = self.dma_tot[s]
                key = ("dma", s)
                if v > 0 and seen.get(key, 0) < v:
                    seen[key] = v
                    waits.append((key, v))
            self.q[eng].append((waits, None, None))

    def emit(self, nc, sems, dsems):
        def semof(k):
            return dsems[k[1]] if isinstance(k, tuple) else sems[k]

        def run(eng):
            def body(h):
                for (waits, fn, token) in self.q[eng]:
                    for (k, v) in waits:
                        h.wait_ge(semof(k), v)
                    if fn is None:
                        continue
                    ins = fn(h)
                    k, v = token
                    ins.then_inc(semof(k), 16 if isinstance(k, tuple) else 1)
            return body

        with nc.Block() as block:
            block.tensor(run("pe"))
            block.scalar(run("act"))
            block.vector(run("dve"))
            block.gpsimd(run("pool"))
            block.sync(run("sp"))


class SemCtx:
    def __init__(self, nc):
        self.nc = nc
        self.stack = []

    def __enter__(self):
        sems = {}
        for e in ENGS:
            g = self.nc.semaphore("s_" + e)
            sems[e] = g.__enter__()
            self.stack.append(g)
        dsems = []
        for i in range(NDMASEM):
            g = self.nc.semaphore(f"d{i}")
            dsems.append(g.__enter__())
            self.stack.append(g)
        return sems, dsems

    def __exit__(self, *a):
        for g in reversed(self.stack):
            g.__exit__(None, None, None)
        return False


class Rot:
    def __init__(self, items):
        self.items = items
        self.i = 0

    def next(self):
        it = self.items[self.i % len(self.items)]
        self.i += 1
        return it


def build_program(nlayers=2, dbg=False, stop_after=None):
    nc = bass.Bass("TRN2", target_bir_lowering=False)

    def din(name, shape, dt=F32):
        return nc.dram_tensor(name, shape, dt, kind="ExternalInput").ap()

    xT_in = din("xT", [D, T])
    cT_in = din("cT", [128, 8, 2])
    w_mod = din("w_mod", [2, D, 6144])
    b_mod2 = din("b_mod2", [2, 2, 6144])
    w_in = din("w_in", [2, D, 2304])
    w_out = din("w_out", [2, D, D])
    w_up = din("w_up", [2, D, 5632])
    w_down = din("w_down", [2, 2816, D])
    lru_bd = din("lru_bd", [2, 12, 128, 128])
    smalls = din("smalls", [2, 128, NS])
    lamv = din("lamv", [2, 128, 4, 32])
    cmats = din("cmats", [7, 128, 128])
    rope = din("rope", [4, 128, T])
    out = nc.dram_tensor("out", [D, NLAT], F32, kind="ExternalOutput").ap()

    skind = "ExternalOutput" if dbg else "Internal"

    def dscr(name, shape, dt):
        return nc.dram_tensor(name, shape, dt, kind=skind).ap()

    xTs = dscr("xTs", [D, T], F32)
    qk = dscr("qk", [9, 128, T], BF16)
    vtok = dscr("vtok", [T, 768], BF16)
    cxs = dscr("cxs", [3, 128, T], F32)
    cgs = dscr("cgs", [3, 128, T], BF16)
    mix = dscr("mix", [D, T], BF16)
    h2s = dscr("h2s", [D, T + 4], BF16)
    modd = dscr("modd", [128, 96], F32) if dbg else None

    S = Sched()
    groups = [(0, 256, 1)] + [(256 + 512 * i, 512, 0) for i in range(8)]

    with (
        nc.sbuf_tensor("arena", [128, ARENA], U8) as arena,
        nc.sbuf_tensor("cst", [128, 4], F32) as cst,
        nc.sbuf_tensor("cmb", [128, 4, 128], BF16) as cmb,
        nc.sbuf_tensor("msk", [128, 2, 128], BF16) as msk,
        nc.sbuf_tensor("ident", [128, 128], F32) as ident,
        nc.sbuf_tensor("onesb", [128, 128], BF16) as onesb,
        nc.sbuf_tensor("smt", [128, NS], F32) as smt,
        nc.sbuf_tensor("ctile", [128, 8, 2], F32) as ctile,
        nc.sbuf_tensor("sc", [128, 8, 2], F32) as sc,
        nc.sbuf_tensor("modT", [128, 48, 2], F32) as modT,
        nc.sbuf_tensor("A1", [128, 8, 2], F32) as A1,
        nc.sbuf_tensor("A2", [128, 8, 2], F32) as A2,
        nc.sbuf_tensor("lamt", [128, 4, 32], F32) as lamt,
        nc.sbuf_tensor("sm2", [128, 64], F32) as sm2,
        nc.psum_tensor("ps", [128, 8, 512], F32) as ps,
        SemCtx(nc) as (sems, dsems),
    ):
        off = [0]

        def areset():
            off[0] = 0

        def alloc(shape, dt):
            esz = 2 if dt == BF16 else 4
            n = int(np.prod(shape))
            nb = (n * esz + 63) // 64 * 64
            assert off[0] + nb <= ARENA, (off[0], nb)
            a = arena[:, off[0]:off[0] + n * esz].bitcast(dt)
            off[0] += nb
            if len(shape) == 2:
                a = a.rearrange("p (a b) -> p a b", a=shape[0])
            elif len(shape) == 3:
                a = a.rearrange("p (a b c) -> p a b c", a=shape[0], b=shape[1])
            return a

        nb_ = [0]

        def B(name="b"):
            nb_[0] += 1
            return Buf(f"{name}{nb_[0]}")

        psb = [B("ps") for _ in range(8)]
        b_cst, b_cmb, b_msk, b_ident, b_ones, b_smt, b_ct, b_sc, b_modT, b_A, b_lam, b_sm2 = [B("c") for _ in range(12)]

        def psflat(b0, n):
            return ps[:, b0:b0 + (n + 511) // 512, :].rearrange("p b n -> p (b n)")[:, 0:n]

        def rev(ap2d):
            (pst, pn), (fs, fn_) = ap2d.ap
            from concourse.ap import AP
            return AP(ap2d.tensor, ap2d.offset + (fn_ - 1) * fs, [[pst, pn], [-fs, fn_]])

        S.op("pool", lambda g: g.memset(cst[:, 0:1], EPS), writes=[b_cst])
        S.op("pool", lambda g: g.memset(cst[:, 1:2], 1.0), writes=[b_cst])
        S.op("pool", lambda g: g.memset(cst[:, 2:3], 0.0), writes=[b_cst])
        S.op("pool", lambda g: g.memset(onesb[:], 1.0), writes=[b_ones])
        S.dma("pool", lambda g: g.dma_start(out=cmb[:], in_=cmats[0:4].rearrange("c p n -> p c n")), writes=[b_cmb])
        S.dma("pool", lambda g: g.dma_start(out=msk[:], in_=cmats[5:7].rearrange("c p n -> p c n")), writes=[b_msk])
        S.dma("sp", lambda g: g.dma_start(out=ident[:], in_=cmats[4]), writes=[b_ident])
        S.dma("sp", lambda g: g.dma_start(out=ctile[:], in_=cT_in), writes=[b_ct])
        S.op("act", lambda g: g.activation(out=sc[:], in_=ctile[:], func=AF.Silu), reads=[b_ct], writes=[b_sc])

        def run_layer(l):
            last = (l == nlayers - 1)
            ctx_out = not last
            lam_init = 0.8 - 0.6 * math.exp(-0.3 * l)
            xsrc = xT_in if l == 0 else xTs
            xsrc_k = xsrc.rearrange("(k p) t -> p k t", p=128)
            xTs_k = xTs.rearrange("(k p) t -> p k t", p=128)

            sh = {}
            def _ph():
                S.barrier()
                areset()
                S.dma("sp", lambda g, l=l: g.dma_start(out=smt[:], in_=smalls[l]), writes=[b_smt])
                S.dma("sp", lambda g, l=l: g.dma_start(out=lamt[:], in_=lamv[l]), writes=[b_lam])
                bm = alloc([6144], F32)
                modrow = alloc([6144], F32)
                wms = [alloc([8, 512], F32) for _ in range(2)]
                b_bm, b_modrow = B(), B()
                b_wm = [B(), B()]
                S.dma("sp", lambda g, l=l: g.dma_start(out=bm[0:2, :], in_=b_mod2[l]), writes=[b_bm])
                wmk = w_mod[l].rearrange("(k p) n -> p k n", p=128)
                for cg in range(12):
                    wm, bw = wms[cg % 2], b_wm[cg % 2]
                    S.dma("sp", lambda g, wm=wm, cg=cg: g.dma_start(out=wm, in_=wmk[:, :, cg * 512:(cg + 1) * 512]), writes=[bw])

                    def mm(g, wm=wm, cg=cg):
                        for k in range(8):
                            ins = g.matmul(ps[0:2, cg % 2, :], lhsT=sc[:, k, :], rhs=wm[:, k, :], start=(k == 0), stop=(k == 7))
                        return ins
                    S.op("pe", mm, reads=[bw, b_sc], writes=[psb[cg % 2]])
                    S.op("dve", lambda g, cg=cg: g.tensor_tensor(out=modrow[0:2, cg * 512:(cg + 1) * 512], in0=ps[0:2, cg % 2, :],
                                                                 in1=bm[0:2, cg * 512:(cg + 1) * 512], op=ALU.add),
                         reads=[psb[cg % 2], b_bm], writes=[b_modrow])

                def tr(g):
                    for j in range(48):
                        ins = g.transpose(ps[:, 2, 2 * j:2 * j + 2], modrow[0:2, j * 128:(j + 1) * 128], ident[0:2, 0:2])
                    return ins
                S.op("pe", tr, reads=[b_modrow, b_ident], writes=[psb[2]])
                S.op("dve", lambda g: g.tensor_copy(out=modT[:].rearrange("p a b -> p (a b)"), in_=ps[:, 2, 0:96]), reads=[psb[2]], writes=[b_modT])
                for v in range(2):
                    S.op("dve", lambda g, v=v: g.scalar_tensor_tensor(out=A1[:, :, v], in0=modT[:, 8:16, v], scalar=1.0, in1=smt[:, 0:8],
                                                                      op0=ALU.add, op1=ALU.mult), reads=[b_modT, b_smt], writes=[b_A])
                    S.op("dve", lambda g, v=v: g.scalar_tensor_tensor(out=A2[:, :, v], in0=modT[:, 32:40, v], scalar=1.0, in1=smt[:, 8:16],
                                                                      op0=ALU.add, op1=ALU.mult), reads=[b_modT, b_smt], writes=[b_A])
                if dbg and l == 0:
                    S.dma("pool", lambda g: g.dma_start(out=modd, in_=modT[:].rearrange("p a b -> p (a b)")), reads=[b_modT])
                S.op("act", lambda g: g.mul(out=sm2[:, 0:1], in_=smt[:, 20:21], mul=float(1.0 - lam_init)), reads=[b_smt], writes=[b_sm2])
                S.op("act", lambda g: g.activation(out=sm2[:, 2:8], in_=smt[:, 21:27], func=AF.Exp), reads=[b_smt], writes=[b_sm2])
                S.op("dve", lambda g: g.tensor_tensor(out=lamt[:, 0, :], in0=lamt[:, 0, :], in1=lamt[:, 1, :], op=ALU.mult), reads=[b_lam], writes=[b_lam])
                S.op("dve", lambda g: g.tensor_tensor(out=lamt[:, 2, :], in0=lamt[:, 2, :], in1=lamt[:, 3, :], op=ALU.mult), reads=[b_lam], writes=[b_lam])
                S.op("dve", lambda g: g.tensor_reduce(out=sm2[:, 20:21], in_=lamt[:, 0, :], axis=mybir.AxisListType.X, op=ALU.add), reads=[b_lam], writes=[b_sm2])
                S.op("dve", lambda g: g.tensor_reduce(out=sm2[:, 21:22], in_=lamt[:, 2, :], axis=mybir.AxisListType.X, op=ALU.add), reads=[b_lam], writes=[b_sm2])
                S.op("act", lambda g: g.activation(out=sm2[:, 22:24], in_=sm2[:, 20:22], func=AF.Exp), reads=[b_sm2], writes=[b_sm2])
                S.op("dve", lambda g: g.scalar_tensor_tensor(out=sm2[:, 1:2], in0=sm2[:, 23:24], scalar=float(-lam_init), in1=sm2[:, 22:23],
                                                             op0=ALU.add, op1=ALU.subtract), reads=[b_sm2], writes=[b_sm2])
                L_ = smt[:, 69:75]
                S.op("dve", lambda g: g.tensor_scalar_mul(out=sm2[:, 24:30], in0=L_, scalar1=-1.0), reads=[b_smt], writes=[b_sm2])
                S.op("dve", lambda g: g.tensor_tensor(out=sm2[:, 48:54], in0=sm2[:, 24:30], in1=L_, op=ALU.max), reads=[b_smt, b_sm2], writes=[b_sm2])
                S.op("act", lambda g: g.activation(out=sm2[:, 30:36], in_=sm2[:, 48:54], func=AF.Exp, scale=-1.0), reads=[b_sm2], writes=[b_sm2])
                S.op("dve", lambda g: g.tensor_scalar_add(out=sm2[:, 36:42], in0=sm2[:, 30:36], scalar1=1.0), reads=[b_sm2], writes=[b_sm2])
                S.op("act", lambda g: g.activation(out=sm2[:, 42:48], in_=sm2[:, 36:42], func=AF.Ln), reads=[b_sm2], writes=[b_sm2])
                S.op("dve", lambda g: g.tensor_scalar(out=sm2[:, 36:42], in0=sm2[:, 36:42], scalar1=-1.0, scalar2=1e-30, op0=ALU.add, op1=ALU.max),
                     reads=[b_sm2], writes=[b_sm2])
                S.op("dve", lambda g: g.reciprocal(out=sm2[:, 54:60], in_=sm2[:, 36:42]), reads=[b_sm2], writes=[b_sm2])
                S.op("dve", lambda g: g.tensor_tensor(out=sm2[:, 30:36], in0=sm2[:, 30:36], in1=sm2[:, 54:60], op=ALU.mult), reads=[b_sm2], writes=[b_sm2])
                S.op("dve", lambda g: g.tensor_tensor(out=sm2[:, 30:36], in0=sm2[:, 30:36], in1=sm2[:, 42:48], op=ALU.mult), reads=[b_sm2], writes=[b_sm2])
                S.op("dve", lambda g: g.tensor_scalar_max(out=sm2[:, 24:30], in0=sm2[:, 24:30], scalar1=0.0), reads=[b_sm2], writes=[b_sm2])
                S.op("dve", lambda g: g.tensor_tensor(out=sm2[:, 24:30], in0=sm2[:, 24:30], in1=sm2[:, 30:36], op=ALU.add), reads=[b_sm2], writes=[b_sm2])
                S.op("dve", lambda g: g.tensor_scalar_mul(out=sm2[:, 8:14], in0=sm2[:, 24:30], scalar1=-8.0), reads=[b_sm2], writes=[b_sm2])
                S.op("dve", lambda g: g.tensor_scalar_mul(out=sm2[:, 14:20], in0=sm2[:, 24:30], scalar1=-16.0), reads=[b_sm2], writes=[b_sm2])
                if stop_after == ("M", l):
                    return True

                return False
            if _ph():
                return True
            def _ph():
                S.barrier()
                areset()
                win = alloc([8, 2304], BF16)
                b_win = B()
                wik = w_in[l].rearrange("(k p) n -> p k n", p=128)
                for hh in range(2):
                    S.dma("pool", lambda g, hh=hh: g.dma_start(out=win[:, :, hh * 1152:(hh + 1) * 1152], in_=wik[:, :, hh * 1152:(hh + 1) * 1152]), writes=[b_win], acc=(hh > 0))
                xgs = [(alloc([8, 512], F32), B()) for _ in range(2)]
                rps = [(alloc([4, 512], F32), B()) for _ in range(2)]
                hTs = [(alloc([8, 512], BF16), B()) for _ in range(2)]
                sq, b_sq = alloc([8, 512], BF16), B()
                tmp8, b_tmp8 = alloc([8, 512], F32), B()
                rstd, b_rstd = alloc([512], F32), B()
                NSET = 3
                sets = [dict(qf=alloc([2, 512], F32), sqb=alloc([2, 512], BF16), rr=alloc([2, 512], F32), qn=alloc([2, 512], F32), qo=alloc([2, 512], BF16),
                             b={n: B() for n in ("qf", "sqb", "rr", "qn", "qo")}, pb=1 + 2 * i) for i in range(NSET)]
                vos = [(alloc([6, 128], BF16), B()) for _ in range(2)]
                for (vo, bvo) in vos:
                    S.op("pool", lambda g, vo=vo: g.memset(vo[:, :, 64:128], 1.0), writes=[bvo])
                ropek = rope.rearrange("r p t -> p r t")
                from concourse.ap import AP as _AP

                def bc3(ap2, nb):
                    (pst, pn), (fs, fn_) = ap2.ap
                    return _AP(ap2.tensor, ap2.offset, [[pst, pn], [0, nb], [fs, fn_]])

                batches = [(0, 2, 16, 0), (2, 2, 17, 0), (4, 2, 18, 1), (6, 1, 18, 1), (7, 2, 19, 1)]

                def prologue(gi):
                    t0, W, v = groups[gi]
                    xg, bxg = xgs[gi % 2]
                    rp, brp = rps[gi % 2]
                    hT, bhT = hTs[gi % 2]

                    def s0():
                        S.dma("sp", lambda g: g.dma_start(out=xg[:, :, 0:W], in_=xsrc_k[:, :, t0:t0 + W]), writes=[bxg])
                        S.dma("sp", lambda g: g.dma_start(out=rp[:, :, 0:W], in_=ropek[:, :, t0:t0 + W]), writes=[brp])

                    def s1():
                        S.op("act", lambda g: g.activation(out=sq[:, :, 0:W], in_=xg[:, :, 0:W], func=AF.Square), reads=[bxg], writes=[b_sq])

                    def s2():
                        def ssmm(g):
                            for k in range(8):
                                ins = g.matmul(ps[:, 0, 0:W], lhsT=onesb[:], rhs=sq[:, k, 0:W], start=(k == 0), stop=(k == 7))
                            return ins
                        S.op("pe", ssmm, reads=[b_sq, b_ones], writes=[psb[0]])

                    def s3():
                        S.op("act", lambda g: g.activation(out=rstd[:, 0:W], in_=ps[:, 0, 0:W], func=AF.Ln, scale=1.0 / D, bias=cst[:, 0:1]),
                             reads=[psb[0], b_cst], writes=[b_rstd])
                        S.op("act", lambda g: g.activation(out=rstd[:, 0:W], in_=rstd[:, 0:W], func=AF.Exp, scale=-0.5), reads=[b_rstd], writes=[b_rstd])

                    def s4():
                        S.op("dve", lambda g: g.tensor_tensor(out=tmp8[:, :, 0:W], in0=xg[:, :, 0:W], in1=bc3(rstd[:, 0:W], 8), op=ALU.mult),
                             reads=[bxg, b_rstd], writes=[b_tmp8])

                    def s5():
                        for k in range(8):
                            S.op("act", lambda g, k=k: g.activation(
                                out=hT[:, k, 0:W], in_=tmp8[:, k, 0:W], func=AF.Identity, scale=A1[:, k, v:v + 1], bias=modT[:, k, v:v + 1]),
                                reads=[b_tmp8, b_modT, b_A], writes=[bhT])
                    return [s0, s1, s2, s3, s4, s5]

                def proj(oc0, nb, pb0, hT, W):
                    def f(g):
                        for i in range(nb):
                            for k in range(8):
                                ins = g.matmul(ps[:, pb0 + i, 0:W], lhsT=win[:, k, (oc0 + i) * 128:(oc0 + i + 1) * 128], rhs=hT[:, k, 0:W], start=(k == 0), stop=(k == 7))
                        return ins
                    return f

                def qkbatch(gi, bi, st):
                    t0, W, v = groups[gi]
                    rp, brp = rps[gi % 2]
                    hT, bhT = hTs[gi % 2]
                    oc0, nb, gcol, mi = batches[bi]
                    bb = st["b"]
                    pb0 = st["pb"]
                    pbs = psb[pb0:pb0 + nb]
                    inv = 1.0 / 32 if mi == 0 else 1.0 / 64
                    rc, rsn = (0, 1) if mi == 0 else (2, 3)
                    qf, sqb, rr, qn, qo = (st[n][:, 0:nb, 0:W] for n in ("qf", "sqb", "rr", "qn", "qo"))
                    pv = ps[:, pb0:pb0 + nb, 0:W]

                    def s0():
                        S.op("pe", proj(oc0, nb, pb0, hT, W), reads=[b_win, bhT], writes=pbs)

                    def s1():
                        S.op("act", lambda g: g.activation(out=qf, in_=pv, func=AF.Identity), reads=pbs, writes=[bb["qf"]])
                        S.op("act", lambda g: g.activation(out=sqb, in_=pv, func=AF.Square), reads=pbs, writes=[bb["sqb"]])

                    def s2():
                        def smm(g):
                            for i in range(nb):
                                ins = g.matmul(ps[:, pb0 + i, 0:W], lhsT=cmb[:, mi, :], rhs=st["sqb"][:, i, 0:W], start=True, stop=True)
                            return ins
                        S.op("pe", smm, reads=[bb["sqb"], b_cmb], writes=pbs)

                    def s3():
                        S.op("act", lambda g: g.activation(out=rr, in_=pv, func=AF.Ln, scale=inv, bias=cst[:, 0:1]), reads=pbs + [b_cst], writes=[bb["rr"]])
                        S.op("act", lambda g: g.activation(out=rr, in_=rr, func=AF.Exp, scale=-0.5), reads=[bb["rr"]], writes=[bb["rr"]])

                    def s4():
                        S.op("dve", lambda g: g.scalar_tensor_tensor(out=qn, in0=qf, scalar=smt[:, gcol:gcol + 1], in1=rr, op0=ALU.mult, op1=ALU.mult),
                             reads=[bb["qf"], bb["rr"], b_smt], writes=[bb["qn"]])

                    def s5():
                        S.op("dve", lambda g: g.tensor_copy(out=sqb, in_=qn), reads=[bb["qn"]], writes=[bb["sqb"]])

                    def s6():
                        def rmm(g):
                            for i in range(nb):
                                ins = g.matmul(ps[:, pb0 + i, 0:W], lhsT=cmb[:, 2 + mi, :], rhs=st["sqb"][:, i, 0:W], start=True, stop=True)
                            return ins
                        S.op("pe", rmm, reads=[bb["sqb"], b_cmb], writes=pbs)
                        S.op("pool", lambda g: g.tensor_tensor(out=qf, in0=qn, in1=bc3(rp[:, rc, 0:W], nb), op=ALU.mult), reads=[bb["qn"], brp], writes=[bb["qf"]])

                    def s7():
                        S.op("dve", lambda g: g.tensor_tensor(out=rr, in0=pv, in1=bc3(rp[:, rsn, 0:W], nb), op=ALU.mult), reads=pbs + [brp], writes=[bb["rr"]])

                    def s8():
                        S.op("pool", lambda g: g.tensor_tensor(out=qo, in0=qf, in1=rr, op=ALU.add), reads=[bb["qf"], bb["rr"]], writes=[bb["qo"]])
                        S.dma("sp", lambda g: g.dma_start(out=qk[oc0:oc0 + nb].rearrange("c p t -> p c t")[:, :, t0:t0 + W], in_=qo), reads=[bb["qo"]])
                    return [s0, s1, s2, s3, s4, s5, s6, s7, s8]

                def cbatch(gi, which, st, c0, nb):
                    t0, W, v = groups[gi]
                    hT, bhT = hTs[gi % 2]
                    bb = st["b"]
                    pb0 = st["pb"]
                    pbs = psb[pb0:pb0 + nb]
                    pv = ps[:, pb0:pb0 + nb, 0:W]

                    def s0():
                        S.op("pe", proj((9 if which == 0 else 12) + c0, nb, pb0, hT, W), reads=[b_win, bhT], writes=pbs)

                    def s1():
                        if which == 0:
                            S.op("act", lambda g: g.activation(out=st["qn"][:, 0:nb, 0:W], in_=pv, func=AF.Identity), reads=pbs, writes=[bb["qn"]])
                            S.dma("sp", lambda g: g.dma_start(out=cxs[c0:c0 + nb].rearrange("c p t -> p c t")[:, :, t0:t0 + W], in_=st["qn"][:, 0:nb, 0:W]), reads=[bb["qn"]])
                        else:
                            S.op("act", lambda g: g.activation(out=st["qo"][:, 0:nb, 0:W], in_=pv, func=AF.Gelu_apprx_tanh), reads=pbs, writes=[bb["qo"]])
                            S.dma("sp", lambda g: g.dma_start(out=cgs[c0:c0 + nb].rearrange("c p t -> p c t")[:, :, t0:t0 + W], in_=st["qo"][:, 0:nb, 0:W]), reads=[bb["qo"]])
                    return [s0, s1]

                def vitem(gi):
                    t0, W, v = groups[gi]
                    hT, bhT = hTs[gi % 2]

                    def mk(tt):
                        def s():
                            def vmm(g):
                                for k in range(8):
                                    ins = g.matmul(ps[:, 7, 0:384], lhsT=hT[:, k, tt * 128:(tt + 1) * 128], rhs=win[:, k, 1920:2304], start=(k == 0), stop=(k == 7))
                                return ins
                            S.op("pe", vmm, reads=[b_win, bhT], writes=[psb[7]])
                            vo, bvo = vos[tt % 2]
                            S.op("dve", lambda g: g.tensor_copy(out=vo[:, :, 0:64], in_=ps[:, 7, 0:384].rearrange("p (h d) -> p h d", h=6)), reads=[psb[7]], writes=[bvo])
                            S.dma("sp", lambda g: g.dma_start(out=vtok[t0 + tt * 128:t0 + (tt + 1) * 128, :], in_=vo[:].rearrange("p h d -> p (h d)")), reads=[bvo])
                        return s
                    return [mk(tt) for tt in range(W // 128)]

                items = []
                pidx = {}
                nset = [0]

                def add(stages, res=None, deps=()):
                    items.append((stages, res, list(deps)))
                    return len(items) - 1

                pidx[0] = add(prologue(0))
                for gi in range(len(groups)):
                    for bi in range(len(batches)):
                        st = sets[nset[0] % NSET]
                        add(qkbatch(gi, bi, st), res=("set", nset[0] % NSET), deps=[pidx[gi]])
                        nset[0] += 1
                        if bi == 1 and gi + 1 < len(groups):
                            pidx[gi + 1] = add(prologue(gi + 1), res=("pro",))
                    for which in range(2):
                        for (c0, nb) in ((0, 2), (2, 1)):
                            st = sets[nset[0] % NSET]
                            add(cbatch(gi, which, st, c0, nb), res=("set", nset[0] % NSET), deps=[pidx[gi]])
                            nset[0] += 1
                    add(vitem(gi), res=("v",), deps=[pidx[gi]])
                SK = 2
                start, end_ = [], []
                resend = {}
                for i, (stages, res, deps) in enumerate(items):
                    s = 0 if i == 0 else start[i - 1] + SK
                    if res is not None and res in resend:
                        s = max(s, resend[res])
                    for dI in deps:
                        s = max(s, end_[dI])
                    start.append(s)
                    end_.append(s + len(stages))
                    if res is not None:
                        resend[res] = s + len(stages)
                tmax = max(end_)
                for t in range(tmax):
                    for i, (stages, res, deps) in enumerate(items):
                        k = t - start[i]
                        if 0 <= k < len(stages):
                            stages[k]()
                if stop_after == ("A", l):
                    return True
                return False
            if _ph():
                return True
            def _ph():
                S.barrier()
                areset()
                bdw, b_bdw = alloc([12, 128], BF16), B()
                S.dma("pool", lambda g: g.dma_start(out=bdw, in_=lru_bd[l].rearrange("c p n -> p c n")), writes=[b_bdw])
                xxs = [(alloc([T], F32), B()) for _ in range(2)]
                xcs = [(alloc([T], F32), B()) for _ in range(2)]
                xcbs = [(alloc([T], BF16), B()) for _ in range(2)]
                rr_, b_r = alloc([T], F32), B()
                ii_, b_i = alloc([T], F32), B()
                aa_, b_a = alloc([T], F32), B()
                mm_, b_m = alloc([T], F32), B()
                hh_ = [alloc([T], F32), alloc([T], F32)]
                b_h = [B(), B()]
                gg_, b_g = alloc([T], BF16), B()
                segs = [(0, NCTX), (NCTX, T)]
                its = [(cc, d) for cc in range(3) for d in range(2)]

                def conv(n):
                    cc, d = its[n]
                    xx, b_xx = xxs[cc % 2]
                    xc, b_xc = xcs[n % 2]
                    xcb, b_xcb = xcbs[n % 2]
                    if d == 0:
                        S.dma("sp", lambda g: g.dma_start(out=xx, in_=cxs[cc]), writes=[b_xx])
                    wcol = lambda k: smt[:, 27 + d * 12 + k * 3 + cc:28 + d * 12 + k * 3 + cc]
                    bcol = smt[:, 51 + d * 3 + cc:52 + d * 3 + cc]
                    S.op("dve", lambda g: g.tensor_scalar(out=xc, in0=xx, scalar1=wcol(3), scalar2=bcol, op0=ALU.mult, op1=ALU.add),
                         reads=[b_xx, b_smt], writes=[b_xc])
                    for k in range(3):
                        s_ = 3 - k
                        for (a_, e_) in segs:
                            if d == 0:
                                dst, src = (a_ + s_, e_), (a_, e_ - s_)
                            else:
                                dst, src = (a_, e_ - s_), (a_ + s_, e_)
                            S.op("dve", lambda g, dst=dst, src=src, k=k: g.scalar_tensor_tensor(
                                out=xc[:, dst[0]:dst[1]], in0=xx[:, src[0]:src[1]], scalar=wcol(k), in1=xc[:, dst[0]:dst[1]], op0=ALU.mult, op1=ALU.add),
                                reads=[b_xx, b_xc, b_smt], writes=[b_xc])
                    S.op("act", lambda g: g.activation(out=xcb, in_=xc, func=AF.Identity), reads=[b_xc], writes=[b_xcb])

                def gates(n):
                    cc, d = its[n]
                    xc, b_xc = xcs[n % 2]
                    xcb, b_xcb = xcbs[n % 2]
                    ia = (d * 2 + 0) * 3 + cc
                    ix = (d * 2 + 1) * 3 + cc
                    for sg0 in range(0, T, 2048):
                        sgw = min(2048, T - sg0)

                        def gmm(g, sg0=sg0, sgw=sgw):
                            for q0 in range(0, sgw, 512):
                                w_ = min(512, sgw - q0)
                                g.matmul(ps[:, q0 // 512, 0:w_], lhsT=bdw[:, ia, :], rhs=xcb[:, sg0 + q0:sg0 + q0 + w_], start=True, stop=True)
                                ins = g.matmul(ps[:, 4 + q0 // 512, 0:w_], lhsT=bdw[:, ix, :], rhs=xcb[:, sg0 + q0:sg0 + q0 + w_], start=True, stop=True)
                            return ins
                        S.op("pe", gmm, reads=[b_xcb, b_bdw], writes=psb)
                        S.op("act", lambda g, sg0=sg0, sgw=sgw: g.activation(
                            out=rr_[:, sg0:sg0 + sgw], in_=psflat(0, sgw), func=AF.Sigmoid, bias=smt[:, 57 + d * 3 + cc:58 + d * 3 + cc]),
                            reads=psb[0:4] + [b_smt], writes=[b_r])
                        S.op("act", lambda g, sg0=sg0, sgw=sgw: g.activation(
                            out=ii_[:, sg0:sg0 + sgw], in_=psflat(4, sgw), func=AF.Sigmoid, bias=smt[:, 63 + d * 3 + cc:64 + d * 3 + cc]),
                            reads=psb[4:8] + [b_smt], writes=[b_i])
                    S.op("act", lambda g: g.activation(out=aa_, in_=rr_, func=AF.Exp, scale=sm2[:, 8 + d * 3 + cc:9 + d * 3 + cc]),
                         reads=[b_r, b_sm2], writes=[b_a])
                    S.op("act", lambda g: g.activation(out=mm_, in_=rr_, func=AF.Exp, scale=sm2[:, 14 + d * 3 + cc:15 + d * 3 + cc]),
                         reads=[b_r, b_sm2], writes=[b_m])
                    S.op("act", lambda g: g.activation(out=mm_, in_=mm_, func=AF.Sqrt, scale=-1.0, bias=cst[:, 1:2]), reads=[b_m, b_cst], writes=[b_m])
                    S.op("pool", lambda g: g.tensor_tensor(out=ii_, in0=ii_, in1=xc, op=ALU.mult), reads=[b_i, b_xc], writes=[b_i])

                def scan(n):
                    cc, d = its[n]
                    S.op("dve", lambda g: g.tensor_tensor(out=mm_, in0=mm_, in1=ii_, op=ALU.mult), reads=[b_m, b_i], writes=[b_m])
                    hd = hh_[d]
                    if d == 0:
                        S.op("dve", lambda g: g.tensor_tensor_scan(out=hd, data0=aa_, data1=mm_, initial=0.0, op0=ALU.mult, op1=ALU.add),
                             reads=[b_a, b_m], writes=[b_h[d]])
                    else:
                        S.op("dve", lambda g: g.tensor_tensor_scan(out=rev(hd[:, 0:NCTX]), data0=rev(aa_[:, 0:NCTX]), data1=rev(mm_[:, 0:NCTX]),
                                                                   initial=0.0, op0=ALU.mult, op1=ALU.add), reads=[b_a, b_m], writes=[b_h[d]])
                        S.op("dve", lambda g: g.tensor_tensor_scan(out=rev(hd[:, NCTX:T]), data0=rev(aa_[:, NCTX:T]), data1=rev(mm_[:, NCTX:T]),
                                                                   initial=hd[:, 0:1], op0=ALU.mult, op1=ALU.add), reads=[b_a, b_m, b_h[d]], writes=[b_h[d]])
                        S.dma("sp", lambda g: g.dma_start(out=gg_, in_=cgs[cc]), writes=[b_g])
                        S.op("pool", lambda g: g.tensor_tensor(out=hh_[0], in0=hh_[0], in1=hh_[1], op=ALU.add), reads=[b_h[0], b_h[1]], writes=[b_h[0]])
                        yy_, b_y = xcbs[n % 2]
                        S.op("pool", lambda g: g.tensor_tensor(out=yy_, in0=hh_[0], in1=gg_, op=ALU.mult), reads=[b_h[0], b_g], writes=[b_y])
                        S.dma("pool", lambda g: g.dma_start(out=mix[640 + cc * 128:640 + (cc + 1) * 128, :], in_=yy_), reads=[b_y])

                conv(0)
                for n in range(len(its)):
                    gates(n)
                    if n + 1 < len(its):
                        conv(n + 1)
                    scan(n)
                if stop_after == ("B3", l):
                    return True
                return False
            if _ph():
                return True
            def _ph():
                S.barrier()
                areset()
                Vt, b_Vt = alloc([34, 768], BF16), B()
                vtk = vtok.rearrange("(kt p) c -> p kt c", p=128)
                KT_pre, b_KT_pre = None, None
                for q4 in range(0, 34, 9):
                    q5 = min(34, q4 + 9)
                    S.dma("sp", lambda g, q4=q4, q5=q5: g.dma_start(out=Vt[:, q4:q5, :], in_=vtk[:, q4:q5, :]), writes=[b_Vt], acc=(q4 > 0))
                b1_mark = off[0]
                sh.update(Vt=Vt, b_Vt=b_Vt, b1_mark=b1_mark)
                KT, b_KT = alloc([2, T], BF16), B()
                S.dma("sp", lambda g: g.dma_start(out=KT, in_=qk[2:4].rearrange("c p t -> p c t")), writes=[b_KT])
                QTR = Rot([(alloc([2, 512], BF16), B()) for _ in range(2)])
                ER = [Rot([(alloc([2, 512], BF16), B()) for _ in range(2)]) for _ in range(2)]
                evR = Rot([dict(o=alloc([4, 512], F32), l=alloc([4, 512], F32), bo=B(), bl=B()) for _ in range(2)])
                finR = Rot([dict(oo=alloc([512], F32), osq=alloc([512], BF16), rr=alloc([512], F32), y=alloc([512], BF16),
                                 b={n: B() for n in ("oo", "osq", "rr", "y")}) for _ in range(4)])
                sbR = Rot([0, 1, 2])
                pending = []

                def flush():
                    while pending:
                        pending.pop(0)()
                for gi, (t0, W, v) in enumerate(groups):
                    if gi == 0 and not ctx_out:
                        continue
                    nkt = 2 if gi == 0 else 34
                    QT, bQT = QTR.next()
                    S.dma("sp", lambda g, QT=QT, t0=t0, W=W: g.dma_start(out=QT[:, :, 0:W], in_=qk[0:2].rearrange("c p t -> p c t")[:, :, t0:t0 + W]), writes=[bQT])
                    for c in range(2):
                        Ecur = [None] * 2
                        Eprev = [None] * 2
                        for kt in range(nkt + 1):
                            if kt == min(22, nkt):
                                flush()
                            if kt < nkt:
                                for p in range(2):
                                    E, bE = ER[p].next()
                                    Ecur[p] = (E, bE)

                                    def qk2(g, p=p, c=c, kt=kt, QT=QT, W=W):
                                        for j in (2 * p, 2 * p + 1):
                                            ins = g.matmul(ps[:, j, 0:W], lhsT=KT[32 * j:32 * j + 32, c, kt * 128:(kt + 1) * 128], rhs=QT[32 * j:32 * j + 32, c, 0:W],
                                                           start=True, stop=True, tile_position=(32 * j, 0))
                                        return ins
                                    S.op("pe", qk2, reads=[b_KT, bQT], writes=[psb[2 * p], psb[2 * p + 1]])
                                    S.op("act", lambda g, p=p, E=E, W=W: g.activation(out=E[:, :, 0:W], in_=ps[:, 2 * p:2 * p + 2, 0:W], func=AF.Exp, scale=SCALE_A),
                                         reads=[psb[2 * p], psb[2 * p + 1]], writes=[bE])
                            if kt >= 1:
                                for p in range(2):
                                    E, bE = Eprev[p]
                                    h = 2 * c + p

                                    def pv2(g, p=p, E=E, h=h, kt=kt, W=W, nkt=nkt):
                                        for m in range(2):
                                            ins = g.matmul(ps[:, 4 + 2 * p + m, 0:W], lhsT=Vt[:, kt - 1, h * 128:(h + 1) * 128], rhs=E[:, m, 0:W], start=(kt == 1), stop=(kt == nkt))
                                        return ins
                                    S.op("pe", pv2, reads=[bE, b_Vt], writes=[psb[4 + 2 * p], psb[5 + 2 * p]])
                            Eprev = list(Ecur)
                        ev = evR.next()
                        S.op("dve", lambda g, ev=ev, W=W: g.tensor_copy(out=ev["o"][0:64, :, 0:W], in_=ps[0:64, 4:8, 0:W]), reads=psb[4:8], writes=[ev["bo"]])
                        S.op("dve", lambda g, ev=ev, W=W: g.tensor_copy(out=ev["l"][0:64, :, 0:W], in_=ps[64:128, 4:8, 0:W]), reads=psb[4:8], writes=[ev["bl"]])
                        S.op("dve", lambda g, ev=ev, W=W: g.reciprocal(out=ev["l"][0:64, :, 0:W], in_=ev["l"][0:64, :, 0:W]), reads=[ev["bl"]], writes=[ev["bl"]])
                        S.op("pool", lambda g, ev=ev, W=W: g.tensor_tensor(out=ev["o"][0:64, :, 0:W], in0=ev["o"][0:64, :, 0:W], in1=ev["l"][0:64, :, 0:W], op=ALU.mult),
                             reads=[ev["bo"], ev["bl"]], writes=[ev["bo"]])
                        for hh in range(2):
                            h = 2 * c + hh
                            f = finR.next()
                            fb = f["b"]
                            S.op("dve", lambda g, f=f, ev=ev, hh=hh, W=W: g.scalar_tensor_tensor(
                                out=f["oo"][0:64, 0:W], in0=ev["o"][0:64, 2 * hh + 1, 0:W], scalar=sm2[0:64, 1:2], in1=ev["o"][0:64, 2 * hh, 0:W],
                                op0=ALU.mult, op1=ALU.add), reads=[ev["bo"], b_sm2], writes=[fb["oo"]])
                            S.op("pool", lambda g, f=f, W=W: g.tensor_tensor(out=f["osq"][0:64, 0:W], in0=f["oo"][0:64, 0:W], in1=f["oo"][0:64, 0:W], op=ALU.mult),
                                 reads=[fb["oo"]], writes=[fb["osq"]])

                            def late(f=f, fb=fb, h=h, t0=t0, W=W):
                                S.op("pe", lambda g: g.matmul(ps[0:64, 0, 0:W], lhsT=onesb[0:64, 0:64], rhs=f["osq"][0:64, 0:W], start=True, stop=True),
                                     reads=[fb["osq"], b_ones], writes=[psb[0]])
                                S.op("act", lambda g: g.activation(out=f["rr"][0:64, 0:W], in_=ps[0:64, 0, 0:W], func=AF.Ln, scale=1.0 / 64, bias=cst[0:64, 0:1]),
                                     reads=[psb[0], b_cst], writes=[fb["rr"]])
                                S.op("act", lambda g: g.activation(out=f["rr"][0:64, 0:W], in_=f["rr"][0:64, 0:W], func=AF.Exp, scale=-0.5),
                                     reads=[fb["rr"]], writes=[fb["rr"]])
                                S.op("dve", lambda g: g.scalar_tensor_tensor(out=f["y"][0:64, 0:W], in0=f["oo"][0:64, 0:W], scalar=sm2[0:64, 0:1], in1=f["rr"][0:64, 0:W],
                                                                             op0=ALU.mult, op1=ALU.mult), reads=[fb["oo"], fb["rr"], b_sm2], writes=[fb["y"]])
                                S.dma("pool", lambda g: g.dma_start(out=mix[h * 64:(h + 1) * 64, t0:t0 + W], in_=f["y"][0:64, 0:W]), reads=[fb["y"]])
                            pending.append(late)
                flush()
                if stop_after == ("B1", l):
                    return True
                return False
            if _ph():
                return True
            def _ph():
                Vt, b_Vt, b1_mark = sh["Vt"], sh["b_Vt"], sh["b1_mark"]
                S.barrier()
                WSZ = 2 * 22528 + 22528
                off[0] = ARENA - WSZ
                ws0 = dict(wg=alloc([8, 1408], BF16), wv=alloc([8, 1408], BF16), wd=alloc([11, D], BF16), b_wg=B(), b_wv=B(), b_wd=B())
                wuk = w_up[l].rearrange("(k p) n -> p k n", p=128)
                S.dma("pool", lambda g: g.dma_start(out=ws0["wg"], in_=wuk[:, :, 0:1408]), writes=[ws0["b_wg"]])
                S.dma("pool", lambda g: g.dma_start(out=ws0["wv"], in_=wuk[:, :, 2816:2816 + 1408]), writes=[ws0["b_wv"]])
                S.dma("pool", lambda g: g.dma_start(out=ws0["wd"], in_=w_down[l][0:1408, :].rearrange("(c p) n -> p c n", p=128)), writes=[ws0["b_wd"]])
                sh["ws0"] = ws0
                off[0] = b1_mark
                KB, b_KB = alloc([2, T], BF16), B()
                S.dma("sp", lambda g: g.dma_start(out=KB, in_=qk[7:9].rearrange("c p t -> p c t")), writes=[b_KB])
                QBs = [(alloc([3, 512], BF16), B()) for _ in range(2)]
                EBR = Rot([(alloc([640], BF16), B()) for _ in range(6)])
                fbR = Rot([dict(ls=alloc([512], F32), rl=alloc([512], F32), y=alloc([512], BF16), b={n: B() for n in ("ls", "rl", "y")}) for _ in range(3)])
                sbR = Rot([0, 2, 4])
                abR = Rot([6, 7])
                assert off[0] <= ARENA - WSZ, off[0]
                units = []
                ng = 0
                for gi, (t0, W, v) in enumerate(groups):
                    if gi == 0 and not ctx_out:
                        continue
                    QB, bQB = QBs[ng % 2]
                    ng += 1
                    for hq in range(6):
                        ab = abR.next()
                        nqb = W // 128
                        for qb in range(nqb):
                            tt = t0 // 128 + qb
                            if gi == 0:
                                keys = [(0, None), (1, None)]
                            else:
                                n = tt - 2
                                keys = [(0, None), (1, None)]
                                if n > 0:
                                    keys.append((tt - 1, 0))
                                keys.append((tt, None))
                                if n < 31:
                                    keys.append((tt + 1, 1))
                            units.append(dict(gi=gi, t0=t0, W=W, hq=hq, qb=qb, keys=keys, ab=ab, QB=QB, bQB=bQB, first=(hq == 0 and qb == 0), lastq=(qb == nqb - 1)))

                def front(u):
                    t0, W, hq, qb, keys, QB, bQB = u["t0"], u["W"], u["hq"], u["qb"], u["keys"], u["QB"], u["bQB"]
                    c, half, kv = hq // 2, hq % 2, hq // 3
                    p0 = half * 64
                    if u["first"]:
                        S.dma("sp", lambda g: g.dma_start(out=QB[:, :, 0:W], in_=qk[4:7].rearrange("c p t -> p c t")[:, :, t0:t0 + W]), writes=[bQB])
                    nk = len(keys)
                    b0 = sbR.next()

                    def qkmm(g):
                        for i, (kt, m) in enumerate(keys):
                            ins = g.matmul(ps[:, b0 + i // 4, (i % 4) * 128:(i % 4 + 1) * 128], lhsT=KB[p0:p0 + 64, kv, kt * 128:(kt + 1) * 128],
                                           rhs=QB[p0:p0 + 64, c, qb * 128:(qb + 1) * 128], start=True, stop=True, tile_position=(p0, 0))
                        return ins
                    S.op("pe", qkmm, reads=[b_KB, bQB], writes=[psb[b0], psb[b0 + 1]])
                    E, bE = EBR.next()
                    u["E"], u["bE"] = E, bE
                    S.op("act", lambda g: g.activation(out=E[:, 0:nk * 128], in_=psflat(b0, nk * 128), func=AF.Exp, scale=SCALE_B),
                         reads=[psb[b0], psb[b0 + 1]], writes=[bE])
                    for i, (kt, m) in enumerate(keys):
                        if m is not None:
                            S.op("dve", lambda g, i=i, m=m: g.tensor_tensor(out=E[:, i * 128:(i + 1) * 128], in0=E[:, i * 128:(i + 1) * 128], in1=msk[:, m, :], op=ALU.mult),
                                 reads=[bE, b_msk], writes=[bE])

                def back(u):
                    t0, W, hq, qb, keys, ab = u["t0"], u["W"], u["hq"], u["qb"], u["keys"], u["ab"]
                    kv = hq // 3
                    E, bE = u["E"], u["bE"]

                    def pvmm(g):
                        for i, (kt, m) in enumerate(keys):
                            ins = g.matmul(ps[:, ab, qb * 128:(qb + 1) * 128], lhsT=Vt[:, kt, (4 + kv) * 128:(5 + kv) * 128], rhs=E[:, i * 128:(i + 1) * 128],
                                           start=(i == 0), stop=(i == len(keys) - 1))
                        return ins
                    S.op("pe", pvmm, reads=[bE, b_Vt], writes=[psb[ab]])
                    if u["lastq"]:
                        f = fbR.next()
                        fb = f["b"]
                        S.op("dve", lambda g: g.tensor_scalar_add(out=f["ls"][0:64, 0:W], in0=ps[64:128, ab, 0:W], scalar1=sm2[0:64, 2 + hq:3 + hq]),
                             reads=[psb[ab], b_sm2], writes=[fb["ls"]])
                        S.op("dve", lambda g: g.reciprocal(out=f["rl"][0:64, 0:W], in_=f["ls"][0:64, 0:W]), reads=[fb["ls"]], writes=[fb["rl"]])
                        S.op("dve", lambda g: g.tensor_tensor(out=f["y"][0:64, 0:W], in0=ps[0:64, ab, 0:W], in1=f["rl"][0:64, 0:W], op=ALU.mult),
                             reads=[psb[ab], fb["rl"]], writes=[fb["y"]])
                        S.dma("pool", lambda g: g.dma_start(out=mix[256 + hq * 64:256 + (hq + 1) * 64, t0:t0 + W], in_=f["y"][0:64, 0:W]), reads=[fb["y"]])

                LAG = 3
                for idx in range(len(units) + LAG):
                    if idx < len(units):
                        front(units[idx])
                    if idx >= LAG:
                        back(units[idx - LAG])
                if stop_after == ("B2", l):
                    return True
                return False
            if _ph():
                return True
            def _ph():
                S.barrier()
                areset()
                WSZ = 2 * 22528 + 22528
                wout, b_wout = alloc([8, D], BF16), B()
                S.dma("pool", lambda g: g.dma_start(out=wout, in_=w_out[l].rearrange("(k p) n -> p k n", p=128)), writes=[b_wout])
                mxs = [(alloc([8, 512], BF16), B()) for _ in range(2)]
                xgs = [(alloc([8, 512], F32), B()) for _ in range(2)]
                h2s_ = [(alloc([8, 514], BF16), B()) for _ in range(2)]
                sq, b_sq = alloc([8, 512], BF16), B()
                tmp8, b_tmp8 = alloc([8, 512], F32), B()
                rstd, b_rstd = alloc([512], F32), B()
                assert off[0] <= ARENA - WSZ
                for (h2, bh2) in h2s_:
                    S.op("pool", lambda g, h2=h2: g.memset(h2, 0.0), writes=[bh2])
                mixk = mix.rearrange("(k p) t -> p k t", p=128)
                h2sk = h2s.rearrange("(k p) t -> p k t", p=128)
                pbR = Rot([1, 2, 3, 4, 5, 6])
                from concourse.ap import AP as _AP

                def bc3(ap2, nb):
                    (pst, pn), (fs, fn_) = ap2.ap
                    return _AP(ap2.tensor, ap2.offset, [[pst, pn], [0, nb], [fs, fn_]])
                glist = [(gi, g_) for gi, g_ in enumerate(groups) if not (gi == 0 and not ctx_out)]

                def front(n):
                    gi, (t0, W, v) = glist[n]
                    mx, bmx = mxs[n % 2]
                    xg, bxg = xgs[n % 2]
                    for kh in range(2):
                        S.dma("sp", lambda g, kh=kh: g.dma_start(out=mx[:, 4 * kh:4 * kh + 4, 0:W], in_=mixk[:, 4 * kh:4 * kh + 4, t0:t0 + W]), writes=[bmx], acc=(kh > 0))
                    S.dma("sp", lambda g: g.dma_start(out=xg[:, :, 0:W], in_=xsrc_k[:, :, t0:t0 + W]), writes=[bxg])
                    for j in range(8):
                        pb = pbR.next()

                        def omm(g, j=j, pb=pb):
                            for c in range(8):
                                ins = g.matmul(ps[:, pb, 0:W], lhsT=wout[:, c, j * 128:(j + 1) * 128], rhs=mx[:, c, 0:W], start=(c == 0), stop=(c == 7))
                            return ins
                        S.op("pe", omm, reads=[b_wout, bmx], writes=[psb[pb]])
                        S.op("dve", lambda g, j=j, pb=pb: g.scalar_tensor_tensor(
                            out=xg[:, j, 0:W], in0=ps[:, pb, 0:W], scalar=modT[:, 16 + j, v:v + 1], in1=xg[:, j, 0:W], op0=ALU.mult, op1=ALU.add),
                            reads=[psb[pb], bxg, b_modT], writes=[bxg])
                    S.dma("pool", lambda g: g.dma_start(out=xTs_k[:, :, t0:t0 + W], in_=xg[:, :, 0:W]), reads=[bxg])

                def back(n):
                    gi, (t0, W, v) = glist[n]
                    xg, bxg = xgs[n % 2]
                    h2, bh2 = h2s_[n % 2]
                    S.op("act", lambda g: g.activation(out=sq[:, :, 0:W], in_=xg[:, :, 0:W], func=AF.Square), reads=[bxg], writes=[b_sq])

                    def ssmm2(g):
                        for k in range(8):
                            ins = g.matmul(ps[:, 0, 0:W], lhsT=onesb[:], rhs=sq[:, k, 0:W], start=(k == 0), stop=(k == 7))
                        return ins
                    S.op("pe", ssmm2, reads=[b_sq, b_ones], writes=[psb[0]])
                    S.op("act", lambda g: g.activation(out=rstd[:, 0:W], in_=ps[:, 0, 0:W], func=AF.Ln, scale=1.0 / D, bias=cst[:, 0:1]),
                         reads=[psb[0], b_cst], writes=[b_rstd])
                    S.op("act", lambda g: g.activation(out=rstd[:, 0:W], in_=rstd[:, 0:W], func=AF.Exp, scale=-0.5), reads=[b_rstd], writes=[b_rstd])
                    S.op("dve", lambda g: g.tensor_tensor(out=tmp8[:, :, 0:W], in0=xg[:, :, 0:W], in1=bc3(rstd[:, 0:W], 8), op=ALU.mult),
                         reads=[bxg, b_rstd], writes=[b_tmp8])
                    for k in range(8):
                        S.op("act", lambda g, k=k: g.activation(
                            out=h2[:, k, 1:1 + W], in_=tmp8[:, k, 0:W], func=AF.Identity, scale=A2[:, k, v:v + 1], bias=modT[:, 24 + k, v:v + 1]),
                            reads=[b_tmp8, b_modT, b_A], writes=[bh2])
                    if gi == 0:
                        S.dma("pool", lambda g: g.dma_start(out=h2sk[:, :, 0:258], in_=h2[:, :, 0:258]), reads=[bh2])
                    elif gi == 1:
                        S.dma("pool", lambda g: g.dma_start(out=h2sk[:, :, 258:258 + 513], in_=h2[:, :, 0:513]), reads=[bh2])
                    elif gi == 8:
                        S.dma("pool", lambda g: g.dma_start(out=h2sk[:, :, t0 + 3:t0 + 3 + 513], in_=h2[:, :, 1:514]), reads=[bh2])
                    else:
                        S.dma("pool", lambda g: g.dma_start(out=h2sk[:, :, t0 + 3:t0 + 3 + 512], in_=h2[:, :, 1:513]), reads=[bh2])

                for n in range(len(glist) + 1):
                    if n < len(glist):
                        front(n)
                    if n >= 1:
                        back(n - 1)
                if stop_after == ("C1", l):
                    return True
                return False
            if _ph():
                return True
            def _ph():
                h2sk = h2s.rearrange("(k p) t -> p k t", p=128)
                wins = []
                if ctx_out:
                    wins.append((0, 258, 0, 256, 1))
                for i in range(9):
                    wo = min(510, NLAT - 510 * i)
                    wins.append((258 + 510 * i, wo + 2, NCTX + 510 * i, wo, 0))
                S.barrier()
                areset()
                wsets = []
                wuk = w_up[l].rearrange("(k p) n -> p k n", p=128)
                wsets.append(sh["ws0"])
                for hf_ in range(1, 2):
                    ws = dict(wg=alloc([8, 1408], BF16), wv=alloc([8, 1408], BF16), wd=alloc([11, D], BF16), b_wg=B(), b_wv=B(), b_wd=B())
                    wsets.append(ws)
                    S.dma("pool", lambda g, hf_=hf_, ws=ws: g.dma_start(out=ws["wg"], in_=wuk[:, :, hf_ * 1408:(hf_ + 1) * 1408]), writes=[ws["b_wg"]])
                    S.dma("pool", lambda g, hf_=hf_, ws=ws: g.dma_start(out=ws["wv"], in_=wuk[:, :, 2816 + hf_ * 1408:2816 + (hf_ + 1) * 1408]), writes=[ws["b_wv"]])
                    S.dma("pool", lambda g, hf_=hf_, ws=ws: g.dma_start(out=ws["wd"], in_=w_down[l][hf_ * 1408:(hf_ + 1) * 1408, :].rearrange("(c p) n -> p c n", p=128)), writes=[ws["b_wd"]])
                hwR = Rot([(alloc([8, 512], BF16), B()) for _ in range(2)])
                xgR = Rot([(alloc([8, 512], F32), B()) for _ in range(1)])
                act, b_act = alloc([11, 512], BF16), B()
                cvR = Rot([(alloc([512], F32), B()) for _ in range(3)])
                sgR = Rot([(alloc([512], F32), B()) for _ in range(3)])
                gvR = Rot([(0, 1), (2, 3), (4, 5)])
                dbR = Rot([6, 7])
                bxw = [B() for _ in wins]
                assert off[0] <= ARENA - (2 * 22528 + 22528), off[0]
                acts = [(act, b_act), (alloc([11, 512], BF16), B())]
                hws = hwR.items
                outk = out.rearrange("(k p) t -> p k t", p=128)
                seq = [(hf, wi) for hf in range(2) for wi in range(len(wins))]
                if stop_after == ("C20", l):
                    seq = [(0, wi) for wi in range(len(wins))]

                def load_h(n):
                    hf, wi = seq[n]
                    cs, Wn, tk0, Wo, v = wins[wi]
                    hw, bhw = hws[n % 2]
                    S.dma("sp", lambda g: g.dma_start(out=hw[:, :, 0:Wn], in_=h2sk[:, :, cs:cs + Wn]), writes=[bhw])

                def up_chunk(n, ci):
                    hf, wi = seq[n]
                    cs, Wn, tk0, Wo, v = wins[wi]
                    ws = wsets[hf]
                    wg, wv = ws["wg"], ws["wv"]
                    hw, bhw = hws[n % 2]
                    act_, b_act_ = acts[n % 2]
                    c = hf * 11 + ci
                    gb, vb = gvR.next()

                    def umm(g):
                        for k in range(8):
                            g.matmul(ps[:, gb, 0:Wn], lhsT=wg[:, k, ci * 128:(ci + 1) * 128], rhs=hw[:, k, 0:Wn], start=(k == 0), stop=(k == 7))
                        for k in range(8):
                            ins = g.matmul(ps[:, vb, 0:Wo], lhsT=wv[:, k, ci * 128:(ci + 1) * 128], rhs=hw[:, k, 1:1 + Wo], start=(k == 0), stop=(k == 7))
                        return ins
                    S.op("pe", umm, reads=[ws["b_wg"], ws["b_wv"], bhw], writes=[psb[gb], psb[vb]])
                    cv, bcv = cvR.next()
                    sg, bsg = sgR.next()
                    w0 = smt[:, 75 + c:76 + c]
                    w1 = smt[:, 75 + 22 + c:76 + 22 + c]
                    w2 = smt[:, 75 + 44 + c:76 + 44 + c]
                    bb_ = smt[:, 141 + c:142 + c]
                    S.op("act", lambda g: g.activation(out=cv[:, 0:Wo], in_=ps[:, gb, 1:1 + Wo], func=AF.Identity, scale=w1, bias=bb_), reads=[psb[gb], b_smt], writes=[bcv])
                    S.op("dve", lambda g: g.scalar_tensor_tensor(out=cv[:, 0:Wo], in0=ps[:, gb, 0:Wo], scalar=w0, in1=cv[:, 0:Wo], op0=ALU.mult, op1=ALU.add),
                         reads=[psb[gb], bcv, b_smt], writes=[bcv])
                    S.op("dve", lambda g: g.scalar_tensor_tensor(out=cv[:, 0:Wo], in0=ps[:, gb, 2:2 + Wo], scalar=w2, in1=cv[:, 0:Wo], op0=ALU.mult, op1=ALU.add),
                         reads=[psb[gb], bcv, b_smt], writes=[bcv])
                    S.op("act", lambda g: g.activation(out=sg[:, 0:Wo], in_=cv[:, 0:Wo], func=AF.Silu), reads=[bcv], writes=[bsg])
                    S.op("dve", lambda g: g.tensor_tensor(out=act_[:, ci, 0:Wo], in0=ps[:, vb, 0:Wo], in1=sg[:, 0:Wo], op=ALU.mult), reads=[psb[vb], bsg], writes=[b_act_])

                def down(n):
                    hf, wi = seq[n]
                    cs, Wn, tk0, Wo, v = wins[wi]
                    wd, b_wd = wsets[hf]["wd"], wsets[hf]["b_wd"]
                    act_, b_act_ = acts[n % 2]
                    xg, bxg = xgR.items[0]
                    for j in range(8):
                        db = dbR.next()

                        def dmm(g, j=j, db=db):
                            for ci in range(11):
                                ins = g.matmul(ps[:, db, 0:Wo], lhsT=wd[:, ci, j * 128:(j + 1) * 128], rhs=act_[:, ci, 0:Wo], start=(ci == 0), stop=(ci == 10))
                            return ins
                        S.op("pe", dmm, reads=[b_wd, b_act_], writes=[psb[db]])
                        S.op("dve", lambda g, j=j, db=db: g.scalar_tensor_tensor(
                            out=xg[:, j, 0:Wo], in0=ps[:, db, 0:Wo], scalar=modT[:, 40 + j, v:v + 1], in1=xg[:, j, 0:Wo], op0=ALU.mult, op1=ALU.add),
                            reads=[psb[db], bxg, b_modT], writes=[bxg])
                    if last and hf == 1:
                        S.dma("pool", lambda g: g.dma_start(out=outk[:, :, tk0 - NCTX:tk0 - NCTX + Wo], in_=xg[:, :, 0:Wo]), reads=[bxg])
                    else:
                        S.dma("pool", lambda g: g.dma_start(out=xTs_k[:, :, tk0:tk0 + Wo], in_=xg[:, :, 0:Wo]), reads=[bxg], writes=[bxw[wi]])

                def load_x(n):
                    hf, wi = seq[n]
                    cs, Wn, tk0, Wo, v = wins[wi]
                    xg, bxg = xgR.items[0]
                    S.dma("sp", lambda g: g.dma_start(out=xg[:, :, 0:Wo], in_=xTs_k[:, :, tk0:tk0 + Wo]), reads=[bxw[wi]], writes=[bxg])

                NPRE = 2
                load_h(0)
                load_x(0)
                for ci in range(11):
                    up_chunk(0, ci)
                for n in range(len(seq)):
                    if n + 1 < len(seq):
                        load_h(n + 1)
                        for ci in range(NPRE):
                            up_chunk(n + 1, ci)
                    down(n)
                    if n + 1 < len(seq):
                        load_x(n + 1)
                        for ci in range(NPRE, 11):
                            up_chunk(n + 1, ci)
                if stop_after in (("C20", l), ("C21", l)):
                    return True
                if stop_after is not None and stop_after[1] == l:
                    return True
                return False
            if _ph():
                return True
            return False

        for l in range(nlayers):
            if run_layer(l):
                break

        S.barrier()
        S.emit(nc, sems, dsems)
    return nc


def _rope_tables():
    pos = np.arange(NLAT)
    row = (pos // 64).astype(np.float32)
    col = (pos % 64).astype(np.float32)
    tabs = []
    for hd in (32, 64):
        quarter = hd // 4
        half = hd // 2
        inv_freq = (np.float32(10000.0) ** (-np.arange(quarter, dtype=np.float32) / np.float32(quarter))).astype(np.float32)
        ang = np.concatenate([row[:, None] * inv_freq[None, :], col[:, None] * inv_freq[None, :]], axis=-1).astype(np.float32)
        cos, sin = np.cos(ang).astype(np.float32), np.sin(ang).astype(np.float32)
        p = np.arange(128)
        d = p % hd
        j = d % half
        sign = np.where(d < half, -1.0, 1.0).astype(np.float32)
        C = np.ones((128, T), np.float32)
        Sg = np.zeros((128, T), np.float32)
        C[:, NCTX:] = cos[:, j].T
        Sg[:, NCTX:] = sin[:, j].T * sign[:, None]
        tabs += [C, Sg]
    return np.stack(tabs, 0)


def _const_mats():
    p = np.arange(128)
    m = np.zeros((7, 128, 128), np.float32)
    m[0] = (p[:, None] // 32 == p[None, :] // 32)
    m[1] = (p[:, None] // 64 == p[None, :] // 64)
    for idx, hd in ((2, 32), (3, 64)):
        perm = (p // hd) * hd + ((p % hd) + hd // 2) % hd
        m[idx] = (p[:, None] == perm[None, :])
    m[4] = np.eye(128, dtype=np.float32)
    m[5] = (p[:, None] >= p[None, :])
    m[6] = (p[:, None] <= p[None, :])
    return m


def _prep_shared(inp):
    f = lambda a: np.ascontiguousarray(np.asarray(a, dtype=np.float32))
    w_in = f(inp["w_in"])
    cols = np.concatenate([np.arange(0, 256), np.arange(256, 512), np.arange(768, 1152),
                           np.arange(1152, 1216), np.arange(1152, 1216), np.arange(1216, 1280), np.arange(1216, 1280),
                           np.arange(1408, 1792), np.arange(1792, 2176), np.arange(512, 768), np.arange(1280, 1408)])
    assert cols.size == 2304
    w_in_ext = np.ascontiguousarray(w_in[:, :, cols])
    b_mod2 = np.ascontiguousarray(np.repeat(f(inp["b_mod"])[:, None, :], 2, axis=1))
    wa, wx = f(inp["lru_wa"]), f(inp["lru_wx"])
    bd = np.zeros((2, 2, 2, 3, 128, 128), np.float32)
    for cc in range(3):
        for hb in range(2):
            bd[:, :, 0, cc, hb * 64:(hb + 1) * 64, hb * 64:(hb + 1) * 64] = wa[:, :, 2 * cc + hb]
            bd[:, :, 1, cc, hb * 64:(hb + 1) * 64, hb * 64:(hb + 1) * 64] = wx[:, :, 2 * cc + hb]
    bd = np.ascontiguousarray(bd.reshape(2, 12, 128, 128))
    p = np.arange(128)
    sm = np.zeros((2, 128, NS), np.float32)
    sm[:, :, 0:8] = f(inp["norm1_gain"]).reshape(2, 8, 128).transpose(0, 2, 1)
    sm[:, :, 8:16] = f(inp["norm2_gain"]).reshape(2, 8, 128).transpose(0, 2, 1)
    sm[:, :, 16] = f(inp["da_q_gain"])[:, p % 32]
    sm[:, :, 17] = f(inp["da_k_gain"])[:, p % 32]
    sm[:, :, 18] = f(inp["sw_q_gain"])[:, p % 64]
    sm[:, :, 19] = f(inp["sw_k_gain"])[:, p % 64]
    sm[:, :, 20] = f(inp["da_sub_gain"])[:, p % 64]
    sm[:, :, 21:27] = f(inp["sw_sink"])[:, None, :]
    cw = f(inp["lru_conv_w"]).reshape(2, 2, 4, 3, 128)
    sm[:, :, 27:51] = cw.transpose(0, 4, 1, 2, 3).reshape(2, 128, 24)
    for base, name in ((51, "lru_conv_b"), (57, "lru_ba"), (63, "lru_bx"), (69, "lru_lambda")):
        sm[:, :, base:base + 6] = f(inp[name]).reshape(2, 2, 3, 128).transpose(0, 3, 1, 2).reshape(2, 128, 6)
    fw = f(inp["ffn_conv_w"]).reshape(2, 3, 22, 128)
    sm[:, :, 75:141] = fw.transpose(0, 3, 1, 2).reshape(2, 128, 66)
    sm[:, :, 141:163] = f(inp["ffn_conv_b"]).reshape(2, 22, 128).transpose(0, 2, 1)
    lam = np.stack([f(inp["da_lam_q1"]), f(inp["da_lam_k1"]), f(inp["da_lam_q2"]), f(inp["da_lam_k2"])], axis=1)
    lamv = np.ascontiguousarray(np.broadcast_to(lam[:, None], (2, 128, 4, 32)))
    return {
        "w_mod": f(inp["w_mod"]), "b_mod2": b_mod2, "w_in": w_in_ext, "w_out": f(inp["w_out"]), "w_up": f(inp["w_up"]),
        "w_down": f(inp["w_down"]), "lru_bd": bd, "smalls": np.ascontiguousarray(sm), "lamv": lamv,
        "cmats": _const_mats(), "rope": _rope_tables(),
    }


def _prep_core(inp, b):
    x = np.asarray(inp["x"], dtype=np.float32)
    ctx = np.asarray(inp["ctx"], dtype=np.float32)
    c = np.asarray(inp["c"], dtype=np.float32)
    c_ctx = np.asarray(inp["c_ctx"], dtype=np.float32)
    xT = np.ascontiguousarray(np.concatenate([ctx[b].T, x[b].T], axis=1))
    cT = np.ascontiguousarray(np.stack([c[b].reshape(8, 128).T, c_ctx.reshape(8, 128).T], axis=-1))
    return {"xT": xT, "cT": cT}


_CACHE = {}


def kernel(**inputs):
    if "nc" not in _CACHE:
        _CACHE["nc"] = build_program()
    nc = _CACHE["nc"]
    shared = _prep_shared(inputs)
    n = 8
    in_maps = []
    for b in range(n):
        m = dict(shared)
        m.update(_prep_core(inputs, b))
        in_maps.append(m)
    res = run_bass_kernel_spmd(nc, in_maps, core_ids=list(range(n)))
    outs = [np.asarray(r["out"]).T for r in res.results]
    return np.ascontiguousarray(np.stack(outs, axis=0).astype(np.float32))
```

```python
import math
import numpy as np
import concourse.bass as bass
import concourse.mybir as mybir
from concourse.bass_utils import run_bass_kernel_spmd

F32 = mybir.dt.float32
BF16 = mybir.dt.bfloat16
U8 = mybir.dt.uint8
AF = mybir.ActivationFunctionType
ALU = mybir.AluOpType

ENGS = ("pe", "act", "dve", "pool", "sp")
NDMASEM = 44

D = 1024
NCTX = 256
NLAT = 4096
T = NCTX + NLAT
NS = 163
EPS = 1e-6
ARENA = 200 * 1024
SCALE_A = 32 ** -0.5
SCALE_B = 64 ** -0.5


class Buf:
    __slots__ = ("name", "w", "r")

    def __init__(self, name):
        self.name = name
        self.w = []
        self.r = []


class Sched:
    def __init__(self):
        self.q = {e: [] for e in ENGS}
        self.cnt = {e: 0 for e in ENGS}
        self.seen = {e: {} for e in ENGS}
        self.dma_n = 0
        self.dma_np = 0
        self.dma_tot = [0] * NDMASEM

    def bufs(self, name, n):
        return [Buf(f"{name}{i}") for i in range(n)]

    def _deps(self, eng, reads, writes):
        need = {}
        for b in reads:
            for (k, v) in b.w:
                if need.get(k, 0) < v:
                    need[k] = v
        for b in writes:
            for (k, v) in b.w:
                if need.get(k, 0) < v:
                    need[k] = v
            for (k, v) in b.r:
                if need.get(k, 0) < v:
                    need[k] = v
        seen = self.seen[eng]
        out = []
        for k, v in need.items():
            if seen.get(k, 0) < v:
                seen[k] = v
                out.append((k, v))
        return out

    def _commit(self, token, reads, writes, acc=False):
        for b in writes:
            if acc:
                b.w.append(token)
            else:
                b.w = [token]
                b.r = []
        for b in reads:
            if b not in writes:
                b.r.append(token)
                if len(b.r) > 24:
                    m = {}
                    for (k, v) in b.r:
                        if m.get(k, 0) < v:
                            m[k] = v
                    b.r = list(m.items())

    def op(self, eng, fn, reads=(), writes=()):
        waits = self._deps(eng, reads, writes)
        self.cnt[eng] += 1
        token = (eng, self.cnt[eng])
        self._commit(token, reads, writes)
        self.q[eng].append((waits, fn, token))
        return token

    def dma(self, eng, fn, reads=(), writes=(), acc=False):
        if eng == "pool":
            s = NDMASEM - 12 + self.dma_np % 12
            self.dma_np += 1
        else:
            s = self.dma_n % (NDMASEM - 12)
            self.dma_n += 1
        key = ("dma", s)
        waits = self._deps(eng, reads, writes)
        prev = self.dma_tot[s]
        if prev > 0 and self.seen[eng].get(key, 0) < prev:
            self.seen[eng][key] = prev
            waits.append((key, prev))
        self.dma_tot[s] = prev + 16
        token = (key, prev + 16)
        self._commit(token, reads, writes, acc)
        self.q[eng].append((waits, fn, token))
        return token

    def barrier(self):
        for eng in ENGS:
            waits = []
            seen = self.seen[eng]
            for k in ENGS:
                v = self.cnt[k]
                if v > 0 and seen.get(k, 0) < v:
                    seen[k] = v
                    waits.append((k, v))
            for s in range(NDMASEM):
                v = self.dma_tot[s]
                key = ("dma", s)
                if v > 0 and seen.get(key, 0) < v:
                    seen[key] = v
                    waits.append((key, v))
            self.q[eng].append((waits, None, None))

    def emit(self, nc, sems, dsems):
        def semof(k):
            return dsems[k[1]] if isinstance(k, tuple) else sems[k]

        def run(eng):
            def body(h):
                for (waits, fn, token) in self.q[eng]:
                    for (k, v) in waits:
                        h.wait_ge(semof(k), v)
                    if fn is None:
                        continue
                    ins = fn(h)
                    k, v = token
                    ins.then_inc(semof(k), 16 if isinstance(k, tuple) else 1)
            return body

        with nc.Block() as block:
            block.tensor(run("pe"))
            block.scalar(run("act"))
            block.vector(run("dve"))
            block.gpsimd(run("pool"))
            block.sync(run("sp"))


class SemCtx:
    def __init__(self, nc):
        self.nc = nc
        self.stack = []

    def __enter__(self):
        sems = {}
        for e in ENGS:
            g = self.nc.semaphore("s_" + e)
            sems[e] = g.__enter__()
            self.stack.append(g)
        dsems = []
        for i in range(NDMASEM):
            g = self.nc.semaphore(f"d{i}")
            dsems.append(g.__enter__())
            self.stack.append(g)
        return sems, dsems

    def __exit__(self, *a):
        for g in reversed(self.stack):
            g.__exit__(None, None, None)
        return False


class Rot:
    def __init__(self, items):
        self.items = items
        self.i = 0

    def next(self):
        it = self.items[self.i % len(self.items)]
        self.i += 1
        return it


def build_program(nlayers=2, dbg=False, stop_after=None):
    nc = bass.Bass("TRN2", target_bir_lowering=False)

    def din(name, shape, dt=F32):
        return nc.dram_tensor(name, shape, dt, kind="ExternalInput").ap()

    xT_in = din("xT", [D, T])
    cT_in = din("cT", [128, 8, 2])
    w_mod = din("w_mod", [2, D, 6144])
    b_mod2 = din("b_mod2", [2, 2, 6144])
    w_in = din("w_in", [2, D, 2304])
    w_out = din("w_out", [2, D, D])
    w_up = din("w_up", [2, D, 5632])
    w_down = din("w_down", [2, 2816, D])
    lru_bd = din("lru_bd", [2, 12, 128, 128])
    smalls = din("smalls", [2, 128, NS])
    lamv = din("lamv", [2, 128, 4, 32])
    cmats = din("cmats", [7, 128, 128])
    rope = din("rope", [4, 128, T])
    out = nc.dram_tensor("out", [D, NLAT], F32, kind="ExternalOutput").ap()

    skind = "ExternalOutput" if dbg else "Internal"

    def dscr(name, shape, dt):
        return nc.dram_tensor(name, shape, dt, kind=skind).ap()

    xTs = dscr("xTs", [D, T], F32)
    qk = dscr("qk", [9, 128, T], BF16)
    vtok = dscr("vtok", [T, 768], BF16)
    cxs = dscr("cxs", [3, 128, T], F32)
    cgs = dscr("cgs", [3, 128, T], BF16)
    mix = dscr("mix", [D, T], BF16)
    h2s = dscr("h2s", [D, T + 4], BF16)
    modd = dscr("modd", [128, 96], F32) if dbg else None

    S = Sched()
    groups = [(0, 256, 1)] + [(256 + 512 * i, 512, 0) for i in range(8)]

    with (
        nc.sbuf_tensor("arena", [128, ARENA], U8) as arena,
        nc.sbuf_tensor("cst", [128, 4], F32) as cst,
        nc.sbuf_tensor("cmb", [128, 4, 128], BF16) as cmb,
        nc.sbuf_tensor("msk", [128, 2, 128], BF16) as msk,
        nc.sbuf_tensor("ident", [128, 128], F32) as ident,
        nc.sbuf_tensor("onesb", [128, 128], BF16) as onesb,
        nc.sbuf_tensor("smt", [128, NS], F32) as smt,
        nc.sbuf_tensor("ctile", [128, 8, 2], F32) as ctile,
        nc.sbuf_tensor("sc", [128, 8, 2], F32) as sc,
        nc.sbuf_tensor("modT", [128, 48, 2], F32) as modT,
        nc.sbuf_tensor("A1", [128, 8, 2], F32) as A1,
        nc.sbuf_tensor("A2", [128, 8, 2], F32) as A2,
        nc.sbuf_tensor("lamt", [128, 4, 32], F32) as lamt,
        nc.sbuf_tensor("sm2", [128, 64], F32) as sm2,
        nc.psum_tensor("ps", [128, 8, 512], F32) as ps,
        SemCtx(nc) as (sems, dsems),
    ):
        off = [0]

        def areset():
            off[0] = 0

        def alloc(shape, dt):
            esz = 2 if dt == BF16 else 4
            n = int(np.prod(shape))
            nb = (n * esz + 63) // 64 * 64
            assert off[0] + nb <= ARENA, (off[0], nb)
            a = arena[:, off[0]:off[0] + n * esz].bitcast(dt)
            off[0] += nb
            if len(shape) == 2:
                a = a.rearrange("p (a b) -> p a b", a=shape[0])
            elif len(shape) == 3:
                a = a.rearrange("p (a b c) -> p a b c", a=shape[0], b=shape[1])
            return a

        nb_ = [0]

        def B(name="b"):
            nb_[0] += 1
            return Buf(f"{name}{nb_[0]}")

        psb = [B("ps") for _ in range(8)]
        b_cst, b_cmb, b_msk, b_ident, b_ones, b_smt, b_ct, b_sc, b_modT, b_A, b_lam, b_sm2 = [B("c") for _ in range(12)]

        def psflat(b0, n):
            return ps[:, b0:b0 + (n + 511) // 512, :].rearrange("p b n -> p (b n)")[:, 0:n]

        def rev(ap2d):
            (pst, pn), (fs, fn_) = ap2d.ap
            from concourse.ap import AP
            return AP(ap2d.tensor, ap2d.offset + (fn_ - 1) * fs, [[pst, pn], [-fs, fn_]])

        S.op("pool", lambda g: g.memset(cst[:, 0:1], EPS), writes=[b_cst])
        S.op("pool", lambda g: g.memset(cst[:, 1:2], 1.0), writes=[b_cst])
        S.op("pool", lambda g: g.memset(cst[:, 2:3], 0.0), writes=[b_cst])
        S.op("pool", lambda g: g.memset(onesb[:], 1.0), writes=[b_ones])
        S.dma("pool", lambda g: g.dma_start(out=cmb[:], in_=cmats[0:4].rearrange("c p n -> p c n")), writes=[b_cmb])
        S.dma("pool", lambda g: g.dma_start(out=msk[:], in_=cmats[5:7].rearrange("c p n -> p c n")), writes=[b_msk])
        S.dma("sp", lambda g: g.dma_start(out=ident[:], in_=cmats[4]), writes=[b_ident])
        S.dma("sp", lambda g: g.dma_start(out=ctile[:], in_=cT_in), writes=[b_ct])
        S.op("act", lambda g: g.activation(out=sc[:], in_=ctile[:], func=AF.Silu), reads=[b_ct], writes=[b_sc])

        def run_layer(l):
            last = (l == nlayers - 1)
            ctx_out = not last
            lam_init = 0.8 - 0.6 * math.exp(-0.3 * l)
            xsrc = xT_in if l == 0 else xTs
            xsrc_k = xsrc.rearrange("(k p) t -> p k t", p=128)
            xTs_k = xTs.rearrange("(k p) t -> p k t", p=128)

            sh = {}
            def _ph():
                S.barrier()
                areset()
                S.dma("sp", lambda g, l=l: g.dma_start(out=smt[:], in_=smalls[l]), writes=[b_smt])
                S.dma("sp", lambda g, l=l: g.dma_start(out=lamt[:], in_=lamv[l]), writes=[b_lam])
                bm = alloc([6144], F32)
                modrow = alloc([6144], F32)
                wms = [alloc([8, 512], F32) for _ in range(2)]
                b_bm, b_modrow = B(), B()
                b_wm = [B(), B()]
                S.dma("sp", lambda g, l=l: g.dma_start(out=bm[0:2, :], in_=b_mod2[l]), writes=[b_bm])
                wmk = w_mod[l].rearrange("(k p) n -> p k n", p=128)
                for cg in range(12):
                    wm, bw = wms[cg % 2], b_wm[cg % 2]
                    S.dma("sp", lambda g, wm=wm, cg=cg: g.dma_start(out=wm, in_=wmk[:, :, cg * 512:(cg + 1) * 512]), writes=[bw])

                    def mm(g, wm=wm, cg=cg):
                        for k in range(8):
                            ins = g.matmul(ps[0:2, cg % 2, :], lhsT=sc[:, k, :], rhs=wm[:, k, :], start=(k == 0), stop=(k == 7))
                        return ins
                    S.op("pe", mm, reads=[bw, b_sc], writes=[psb[cg % 2]])
                    S.op("dve", lambda g, cg=cg: g.tensor_tensor(out=modrow[0:2, cg * 512:(cg + 1) * 512], in0=ps[0:2, cg % 2, :],
                                                                 in1=bm[0:2, cg * 512:(cg + 1) * 512], op=ALU.add),
                         reads=[psb[cg % 2], b_bm], writes=[b_modrow])

                def tr(g):
                    for j in range(48):
                        ins = g.transpose(ps[:, 2, 2 * j:2 * j + 2], modrow[0:2, j * 128:(j + 1) * 128], ident[0:2, 0:2])
                    return ins
                S.op("pe", tr, reads=[b_modrow, b_ident], writes=[psb[2]])
                S.op("dve", lambda g: g.tensor_copy(out=modT[:].rearrange("p a b -> p (a b)"), in_=ps[:, 2, 0:96]), reads=[psb[2]], writes=[b_modT])
                for v in range(2):
                    S.op("dve", lambda g, v=v: g.scalar_tensor_tensor(out=A1[:, :, v], in0=modT[:, 8:16, v], scalar=1.0, in1=smt[:, 0:8],
                                                                      op0=ALU.add, op1=ALU.mult), reads=[b_modT, b_smt], writes=[b_A])
                    S.op("dve", lambda g, v=v: g.scalar_tensor_tensor(out=A2[:, :, v], in0=modT[:, 32:40, v], scalar=1.0, in1=smt[:, 8:16],
                                                                      op0=ALU.add, op1=ALU.mult), reads=[b_modT, b_smt], writes=[b_A])
                if dbg and l == 0:
                    S.dma("pool", lambda g: g.dma_start(out=modd, in_=modT[:].rearrange("p a b -> p (a b)")), reads=[b_modT])
                S.op("act", lambda g: g.mul(out=sm2[:, 0:1], in_=smt[:, 20:21], mul=float(1.0 - lam_init)), reads=[b_smt], writes=[b_sm2])
                S.op("act", lambda g: g.activation(out=sm2[:, 2:8], in_=smt[:, 21:27], func=AF.Exp), reads=[b_smt], writes=[b_sm2])
                S.op("dve", lambda g: g.tensor_tensor(out=lamt[:, 0, :], in0=lamt[:, 0, :], in1=lamt[:, 1, :], op=ALU.mult), reads=[b_lam], writes=[b_lam])
                S.op("dve", lambda g: g.tensor_tensor(out=lamt[:, 2, :], in0=lamt[:, 2, :], in1=lamt[:, 3, :], op=ALU.mult), reads=[b_lam], writes=[b_lam])
                S.op("dve", lambda g: g.tensor_reduce(out=sm2[:, 20:21], in_=lamt[:, 0, :], axis=mybir.AxisListType.X, op=ALU.add), reads=[b_lam], writes=[b_sm2])
                S.op("dve", lambda g: g.tensor_reduce(out=sm2[:, 21:22], in_=lamt[:, 2, :], axis=mybir.AxisListType.X, op=ALU.add), reads=[b_lam], writes=[b_sm2])
                S.op("act", lambda g: g.activation(out=sm2[:, 22:24], in_=sm2[:, 20:22], func=AF.Exp), reads=[b_sm2], writes=[b_sm2])
                S.op("dve", lambda g: g.scalar_tensor_tensor(out=sm2[:, 1:2], in0=sm2[:, 23:24], scalar=float(-lam_init), in1=sm2[:, 22:23],
                                                             op0=ALU.add, op1=ALU.subtract), reads=[b_sm2], writes=[b_sm2])
                L_ = smt[:, 69:75]
                S.op("dve", lambda g: g.tensor_scalar_mul(out=sm2[:, 24:30], in0=L_, scalar1=-1.0), reads=[b_smt], writes=[b_sm2])
                S.op("dve", lambda g: g.tensor_tensor(out=sm2[:, 48:54], in0=sm2[:, 24:30], in1=L_, op=ALU.max), reads=[b_smt, b_sm2], writes=[b_sm2])
                S.op("act", lambda g: g.activation(out=sm2[:, 30:36], in_=sm2[:, 48:54], func=AF.Exp, scale=-1.0), reads=[b_sm2], writes=[b_sm2])
                S.op("dve", lambda g: g.tensor_scalar_add(out=sm2[:, 36:42], in0=sm2[:, 30:36], scalar1=1.0), reads=[b_sm2], writes=[b_sm2])
                S.op("act", lambda g: g.activation(out=sm2[:, 42:48], in_=sm2[:, 36:42], func=AF.Ln), reads=[b_sm2], writes=[b_sm2])
                S.op("dve", lambda g: g.tensor_scalar(out=sm2[:, 36:42], in0=sm2[:, 36:42], scalar1=-1.0, scalar2=1e-30, op0=ALU.add, op1=ALU.max),
                     reads=[b_sm2], writes=[b_sm2])
                S.op("dve", lambda g: g.reciprocal(out=sm2[:, 54:60], in_=sm2[:, 36:42]), reads=[b_sm2], writes=[b_sm2])
                S.op("dve", lambda g: g.tensor_tensor(out=sm2[:, 30:36], in0=sm2[:, 30:36], in1=sm2[:, 54:60], op=ALU.mult), reads=[b_sm2], writes=[b_sm2])
                S.op("dve", lambda g: g.tensor_tensor(out=sm2[:, 30:36], in0=sm2[:, 30:36], in1=sm2[:, 42:48], op=ALU.mult), reads=[b_sm2], writes=[b_sm2])
                S.op("dve", lambda g: g.tensor_scalar_max(out=sm2[:, 24:30], in0=sm2[:, 24:30], scalar1=0.0), reads=[b_sm2], writes=[b_sm2])
                S.op("dve", lambda g: g.tensor_tensor(out=sm2[:, 24:30], in0=sm2[:, 24:30], in1=sm2[:, 30:36], op=ALU.add), reads=[b_sm2], writes=[b_sm2])
                S.op("dve", lambda g: g.tensor_scalar_mul(out=sm2[:, 8:14], in0=sm2[:, 24:30], scalar1=-8.0), reads=[b_sm2], writes=[b_sm2])
                S.op("dve", lambda g: g.tensor_scalar_mul(out=sm2[:, 14:20], in0=sm2[:, 24:30], scalar1=-16.0), reads=[b_sm2], writes=[b_sm2])
                if stop_after == ("M", l):
                    return True

                return False
            if _ph():
                return True
            def _ph():
                S.barrier()
                areset()
                win = alloc([8, 2304], BF16)
                b_winh = [B(), B()]
                b_win = b_winh[0]
                wik = w_in[l].rearrange("(k p) n -> p k n", p=128)
                for hh in range(2):
                    S.dma("pool", lambda g, hh=hh: g.dma_start(out=win[:, :, hh * 1152:(hh + 1) * 1152], in_=wik[:, :, hh * 1152:(hh + 1) * 1152]), writes=[b_winh[hh]])
                xgs = [(alloc([8, 512], F32), B()) for _ in range(2)]
                rps = [(alloc([4, 512], F32), B()) for _ in range(2)]
                hTs = [(alloc([8, 512], BF16), B()) for _ in range(2)]
                sq, b_sq = alloc([8, 512], BF16), B()
                tmp8, b_tmp8 = alloc([8, 512], F32), B()
                rstd, b_rstd = alloc([512], F32), B()
                NSET = 3
                sets = [dict(qf=alloc([2, 512], F32), sqb=alloc([2, 512], BF16), rr=alloc([2, 512], F32), qn=alloc([2, 512], F32), qo=alloc([2, 512], BF16),
                             b={n: B() for n in ("qf", "sqb", "rr", "qn", "qo")}, pb=1 + 2 * i) for i in range(NSET)]
                vos = [(alloc([6, 128], BF16), B()) for _ in range(2)]
                for (vo, bvo) in vos:
                    S.op("pool", lambda g, vo=vo: g.memset(vo[:, :, 64:128], 1.0), writes=[bvo])
                ropek = rope.rearrange("r p t -> p r t")
                from concourse.ap import AP as _AP

                def bc3(ap2, nb):
                    (pst, pn), (fs, fn_) = ap2.ap
                    return _AP(ap2.tensor, ap2.offset, [[pst, pn], [0, nb], [fs, fn_]])

                batches = [(0, 2, 16, 0), (2, 2, 17, 0), (4, 2, 18, 1), (6, 1, 18, 1), (7, 2, 19, 1)]

                def prologue(gi):
                    t0, W, v = groups[gi]
                    xg, bxg = xgs[gi % 2]
                    rp, brp = rps[gi % 2]
                    hT, bhT = hTs[gi % 2]

                    def s0():
                        S.dma("sp", lambda g: g.dma_start(out=xg[:, :, 0:W], in_=xsrc_k[:, :, t0:t0 + W]), writes=[bxg])
                        S.dma("sp", lambda g: g.dma_start(out=rp[:, :, 0:W], in_=ropek[:, :, t0:t0 + W]), writes=[brp])

                    def s1():
                        S.op("act", lambda g: g.activation(out=sq[:, :, 0:W], in_=xg[:, :, 0:W], func=AF.Square), reads=[bxg], writes=[b_sq])

                    def s2():
                        def ssmm(g):
                            for k in range(8):
                                ins = g.matmul(ps[:, 0, 0:W], lhsT=onesb[:], rhs=sq[:, k, 0:W], start=(k == 0), stop=(k == 7))
                            return ins
                        S.op("pe", ssmm, reads=[b_sq, b_ones], writes=[psb[0]])

                    def s3():
                        S.op("act", lambda g: g.activation(out=rstd[:, 0:W], in_=ps[:, 0, 0:W], func=AF.Ln, scale=1.0 / D, bias=cst[:, 0:1]),
                             reads=[psb[0], b_cst], writes=[b_rstd])
                        S.op("act", lambda g: g.activation(out=rstd[:, 0:W], in_=rstd[:, 0:W], func=AF.Exp, scale=-0.5), reads=[b_rstd], writes=[b_rstd])

                    def s4():
                        S.op("dve", lambda g: g.tensor_tensor(out=tmp8[:, :, 0:W], in0=xg[:, :, 0:W], in1=bc3(rstd[:, 0:W], 8), op=ALU.mult),
                             reads=[bxg, b_rstd], writes=[b_tmp8])

                    def s5():
                        for k in range(8):
                            S.op("act", lambda g, k=k: g.activation(
                                out=hT[:, k, 0:W], in_=tmp8[:, k, 0:W], func=AF.Identity, scale=A1[:, k, v:v + 1], bias=modT[:, k, v:v + 1]),
                                reads=[b_tmp8, b_modT, b_A], writes=[bhT])
                    return [s0, s1, s2, s3, s4, s5]

                def proj(oc0, nb, pb0, hT, W):
                    def f(g):
                        for i in range(nb):
                            for k in range(8):
                                ins = g.matmul(ps[:, pb0 + i, 0:W], lhsT=win[:, k, (oc0 + i) * 128:(oc0 + i + 1) * 128], rhs=hT[:, k, 0:W], start=(k == 0), stop=(k == 7))
                        return ins
                    return f

                def qkbatch(gi, bi, st):
                    t0, W, v = groups[gi]
                    rp, brp = rps[gi % 2]
                    hT, bhT = hTs[gi % 2]
                    oc0, nb, gcol, mi = batches[bi]
                    bb = st["b"]
                    pb0 = st["pb"]
                    pbs = psb[pb0:pb0 + nb]
                    inv = 1.0 / 32 if mi == 0 else 1.0 / 64
                    rc, rsn = (0, 1) if mi == 0 else (2, 3)
                    qf, sqb, rr, qn, qo = (st[n][:, 0:nb, 0:W] for n in ("qf", "sqb", "rr", "qn", "qo"))
                    pv = ps[:, pb0:pb0 + nb, 0:W]

                    def s0():
                        S.op("pe", proj(oc0, nb, pb0, hT, W), reads=[b_winh[0], b_winh[1], bhT] if oc0 + nb > 9 else [b_winh[0], bhT], writes=pbs)

                    def s1():
                        S.op("act", lambda g: g.activation(out=qf, in_=pv, func=AF.Identity), reads=pbs, writes=[bb["qf"]])
                        S.op("act", lambda g: g.activation(out=sqb, in_=pv, func=AF.Square), reads=pbs, writes=[bb["sqb"]])

                    def s2():
                        def smm(g):
                            for i in range(nb):
                                ins = g.matmul(ps[:, pb0 + i, 0:W], lhsT=cmb[:, mi, :], rhs=st["sqb"][:, i, 0:W], start=True, stop=True)
                            return ins
                        S.op("pe", smm, reads=[bb["sqb"], b_cmb], writes=pbs)

                    def s3():
                        S.op("act", lambda g: g.activation(out=rr, in_=pv, func=AF.Ln, scale=inv, bias=cst[:, 0:1]), reads=pbs + [b_cst], writes=[bb["rr"]])
                        S.op("act", lambda g: g.activation(out=rr, in_=rr, func=AF.Exp, scale=-0.5), reads=[bb["rr"]], writes=[bb["rr"]])

                    def s4():
                        S.op("dve", lambda g: g.scalar_tensor_tensor(out=qn, in0=qf, scalar=smt[:, gcol:gcol + 1], in1=rr, op0=ALU.mult, op1=ALU.mult),
                             reads=[bb["qf"], bb["rr"], b_smt], writes=[bb["qn"]])

                    def s5():
                        S.op("dve", lambda g: g.tensor_copy(out=sqb, in_=qn), reads=[bb["qn"]], writes=[bb["sqb"]])

                    def s6():
                        def rmm(g):
                            for i in range(nb):
                                ins = g.matmul(ps[:, pb0 + i, 0:W], lhsT=cmb[:, 2 + mi, :], rhs=st["sqb"][:, i, 0:W], start=True, stop=True)
                            return ins
                        S.op("pe", rmm, reads=[bb["sqb"], b_cmb], writes=pbs)
                        S.op("pool", lambda g: g.tensor_tensor(out=qf, in0=qn, in1=bc3(rp[:, rc, 0:W], nb), op=ALU.mult), reads=[bb["qn"], brp], writes=[bb["qf"]])

                    def s7():
                        S.op("dve", lambda g: g.tensor_tensor(out=rr, in0=pv, in1=bc3(rp[:, rsn, 0:W], nb), op=ALU.mult), reads=pbs + [brp], writes=[bb["rr"]])

                    def s8():
                        S.op("pool", lambda g: g.tensor_tensor(out=qo, in0=qf, in1=rr, op=ALU.add), reads=[bb["qf"], bb["rr"]], writes=[bb["qo"]])
                        S.dma("sp", lambda g: g.dma_start(out=qk[oc0:oc0 + nb].rearrange("c p t -> p c t")[:, :, t0:t0 + W], in_=qo), reads=[bb["qo"]])
                    return [s0, s1, s2, s3, s4, s5, s6, s7, s8]

                def cbatch(gi, which, st, c0, nb):
                    t0, W, v = groups[gi]
                    hT, bhT = hTs[gi % 2]
                    bb = st["b"]
                    pb0 = st["pb"]
                    pbs = psb[pb0:pb0 + nb]
                    pv = ps[:, pb0:pb0 + nb, 0:W]

                    def s0():
                        S.op("pe", proj((9 if which == 0 else 12) + c0, nb, pb0, hT, W), reads=[b_winh[0], b_winh[1], bhT], writes=pbs)

                    def s1():
                        if which == 0:
                            S.op("act", lambda g: g.activation(out=st["qn"][:, 0:nb, 0:W], in_=pv, func=AF.Identity), reads=pbs, writes=[bb["qn"]])
                            S.dma("sp", lambda g: g.dma_start(out=cxs[c0:c0 + nb].rearrange("c p t -> p c t")[:, :, t0:t0 + W], in_=st["qn"][:, 0:nb, 0:W]), reads=[bb["qn"]])
                        else:
                            S.op("act", lambda g: g.activation(out=st["qo"][:, 0:nb, 0:W], in_=pv, func=AF.Gelu_apprx_tanh), reads=pbs, writes=[bb["qo"]])
                            S.dma("sp", lambda g: g.dma_start(out=cgs[c0:c0 + nb].rearrange("c p t -> p c t")[:, :, t0:t0 + W], in_=st["qo"][:, 0:nb, 0:W]), reads=[bb["qo"]])
                    return [s0, s1]

                def vitem(gi):
                    t0, W, v = groups[gi]
                    hT, bhT = hTs[gi % 2]

                    def mk(tt):
                        def s():
                            def vmm(g):
                                for k in range(8):
                                    ins = g.matmul(ps[:, 7, 0:384], lhsT=hT[:, k, tt * 128:(tt + 1) * 128], rhs=win[:, k, 1920:2304], start=(k == 0), stop=(k == 7))
                                return ins
                            S.op("pe", vmm, reads=[b_winh[1], bhT], writes=[psb[7]])
                            vo, bvo = vos[tt % 2]
                            S.op("dve", lambda g: g.tensor_copy(out=vo[:, :, 0:64], in_=ps[:, 7, 0:384].rearrange("p (h d) -> p h d", h=6)), reads=[psb[7]], writes=[bvo])
                            S.dma("sp", lambda g: g.dma_start(out=vtok[t0 + tt * 128:t0 + (tt + 1) * 128, :], in_=vo[:].rearrange("p h d -> p (h d)")), reads=[bvo])
                        return s
                    return [mk(tt) for tt in range(W // 128)]

                items = []
                pidx = {}
                nset = [0]

                def add(stages, res=None, deps=()):
                    items.append((stages, res, list(deps)))
                    return len(items) - 1

                pidx[0] = add(prologue(0))
                for gi in range(len(groups)):
                    for bi in range(len(batches)):
                        st = sets[nset[0] % NSET]
                        add(qkbatch(gi, bi, st), res=("set", nset[0] % NSET), deps=[pidx[gi]])
                        nset[0] += 1
                        if bi == 1 and gi + 1 < len(groups):
                            pidx[gi + 1] = add(prologue(gi + 1), res=("pro",))
                    for which in range(2):
                        for (c0, nb) in ((0, 2), (2, 1)):
                            st = sets[nset[0] % NSET]
                            add(cbatch(gi, which, st, c0, nb), res=("set", nset[0] % NSET), deps=[pidx[gi]])
                            nset[0] += 1
                    add(vitem(gi), res=("v",), deps=[pidx[gi]])
                SK = 2
                start, end_ = [], []
                resend = {}
                for i, (stages, res, deps) in enumerate(items):
                    s = 0 if i == 0 else start[i - 1] + SK
                    if res is not None and res in resend:
                        s = max(s, resend[res])
                    for dI in deps:
                        s = max(s, end_[dI])
                    start.append(s)
                    end_.append(s + len(stages))
                    if res is not None:
                        resend[res] = s + len(stages)
                tmax = max(end_)
                for t in range(tmax):
                    for i, (stages, res, deps) in enumerate(items):
                        k = t - start[i]
                        if 0 <= k < len(stages):
                            stages[k]()
                if stop_after == ("A", l):
                    return True
                return False
            if _ph():
                return True
            def _ph():
                S.barrier()
                areset()
                bdw, b_bdw = alloc([12, 128], BF16), B()
                S.dma("pool", lambda g: g.dma_start(out=bdw, in_=lru_bd[l].rearrange("c p n -> p c n")), writes=[b_bdw])
                xxs = [(alloc([T], F32), B()) for _ in range(2)]
                xcs = [(alloc([T], F32), B()) for _ in range(2)]
                xcbs = [(alloc([T], BF16), B()) for _ in range(2)]
                rr_, b_r = alloc([T], F32), B()
                ii_, b_i = alloc([T], F32), B()
                aa_, b_a = alloc([T], F32), B()
                mm_, b_m = alloc([T], F32), B()
                hh_ = [alloc([T], F32), alloc([T], F32)]
                b_h = [B(), B()]
                gg_, b_g = alloc([T], BF16), B()
                segs = [(0, NCTX), (NCTX, T)]
                its = [(cc, d) for cc in range(3) for d in range(2)]

                def conv(n):
                    cc, d = its[n]
                    xx, b_xx = xxs[cc % 2]
                    xc, b_xc = xcs[n % 2]
                    xcb, b_xcb = xcbs[n % 2]
                    if d == 0:
                        S.dma("sp", lambda g: g.dma_start(out=xx, in_=cxs[cc]), writes=[b_xx])
                    wcol = lambda k: smt[:, 27 + d * 12 + k * 3 + cc:28 + d * 12 + k * 3 + cc]
                    bcol = smt[:, 51 + d * 3 + cc:52 + d * 3 + cc]
                    S.op("dve", lambda g: g.tensor_scalar(out=xc, in0=xx, scalar1=wcol(3), scalar2=bcol, op0=ALU.mult, op1=ALU.add),
                         reads=[b_xx, b_smt], writes=[b_xc])
                    for k in range(3):
                        s_ = 3 - k
                        for (a_, e_) in segs:
                            if d == 0:
                                dst, src = (a_ + s_, e_), (a_, e_ - s_)
                            else:
                                dst, src = (a_, e_ - s_), (a_ + s_, e_)
                            S.op("dve", lambda g, dst=dst, src=src, k=k: g.scalar_tensor_tensor(
                                out=xc[:, dst[0]:dst[1]], in0=xx[:, src[0]:src[1]], scalar=wcol(k), in1=xc[:, dst[0]:dst[1]], op0=ALU.mult, op1=ALU.add),
                                reads=[b_xx, b_xc, b_smt], writes=[b_xc])
                    S.op("act", lambda g: g.activation(out=xcb, in_=xc, func=AF.Identity), reads=[b_xc], writes=[b_xcb])

                def gates(n):
                    cc, d = its[n]
                    xc, b_xc = xcs[n % 2]
                    xcb, b_xcb = xcbs[n % 2]
                    ia = (d * 2 + 0) * 3 + cc
                    ix = (d * 2 + 1) * 3 + cc
                    for sg0 in range(0, T, 2048):
                        sgw = min(2048, T - sg0)

                        def gmm(g, sg0=sg0, sgw=sgw):
                            for q0 in range(0, sgw, 512):
                                w_ = min(512, sgw - q0)
                                g.matmul(ps[:, q0 // 512, 0:w_], lhsT=bdw[:, ia, :], rhs=xcb[:, sg0 + q0:sg0 + q0 + w_], start=True, stop=True)
                                ins = g.matmul(ps[:, 4 + q0 // 512, 0:w_], lhsT=bdw[:, ix, :], rhs=xcb[:, sg0 + q0:sg0 + q0 + w_], start=True, stop=True)
                            return ins
                        S.op("pe", gmm, reads=[b_xcb, b_bdw], writes=psb)
                        S.op("act", lambda g, sg0=sg0, sgw=sgw: g.activation(
                            out=rr_[:, sg0:sg0 + sgw], in_=psflat(0, sgw), func=AF.Sigmoid, bias=smt[:, 57 + d * 3 + cc:58 + d * 3 + cc]),
                            reads=psb[0:4] + [b_smt], writes=[b_r])
                        S.op("act", lambda g, sg0=sg0, sgw=sgw: g.activation(
                            out=ii_[:, sg0:sg0 + sgw], in_=psflat(4, sgw), func=AF.Sigmoid, bias=smt[:, 63 + d * 3 + cc:64 + d * 3 + cc]),
                            reads=psb[4:8] + [b_smt], writes=[b_i])
                    S.op("act", lambda g: g.activation(out=aa_, in_=rr_, func=AF.Exp, scale=sm2[:, 8 + d * 3 + cc:9 + d * 3 + cc]),
                         reads=[b_r, b_sm2], writes=[b_a])
                    S.op("act", lambda g: g.activation(out=mm_, in_=rr_, func=AF.Exp, scale=sm2[:, 14 + d * 3 + cc:15 + d * 3 + cc]),
                         reads=[b_r, b_sm2], writes=[b_m])
                    S.op("act", lambda g: g.activation(out=mm_, in_=mm_, func=AF.Sqrt, scale=-1.0, bias=cst[:, 1:2]), reads=[b_m, b_cst], writes=[b_m])
                    S.op("pool", lambda g: g.tensor_tensor(out=ii_, in0=ii_, in1=xc, op=ALU.mult), reads=[b_i, b_xc], writes=[b_i])

                def scan(n):
                    cc, d = its[n]
                    S.op("dve", lambda g: g.tensor_tensor(out=mm_, in0=mm_, in1=ii_, op=ALU.mult), reads=[b_m, b_i], writes=[b_m])
                    hd = hh_[d]
                    if d == 0:
                        S.op("dve", lambda g: g.tensor_tensor_scan(out=hd, data0=aa_, data1=mm_, initial=0.0, op0=ALU.mult, op1=ALU.add),
                             reads=[b_a, b_m], writes=[b_h[d]])
                    else:
                        S.op("dve", lambda g: g.tensor_tensor_scan(out=rev(hd[:, 0:NCTX]), data0=rev(aa_[:, 0:NCTX]), data1=rev(mm_[:, 0:NCTX]),
                                                                   initial=0.0, op0=ALU.mult, op1=ALU.add), reads=[b_a, b_m], writes=[b_h[d]])
                        S.op("dve", lambda g: g.tensor_tensor_scan(out=rev(hd[:, NCTX:T]), data0=rev(aa_[:, NCTX:T]), data1=rev(mm_[:, NCTX:T]),
                                                                   initial=hd[:, 0:1], op0=ALU.mult, op1=ALU.add), reads=[b_a, b_m, b_h[d]], writes=[b_h[d]])
                        S.dma("sp", lambda g: g.dma_start(out=gg_, in_=cgs[cc]), writes=[b_g])
                        S.op("pool", lambda g: g.tensor_tensor(out=hh_[0], in0=hh_[0], in1=hh_[1], op=ALU.add), reads=[b_h[0], b_h[1]], writes=[b_h[0]])
                        yy_, b_y = xcbs[n % 2]
                        S.op("pool", lambda g: g.tensor_tensor(out=yy_, in0=hh_[0], in1=gg_, op=ALU.mult), reads=[b_h[0], b_g], writes=[b_y])
                        S.dma("pool", lambda g: g.dma_start(out=mix[640 + cc * 128:640 + (cc + 1) * 128, :], in_=yy_), reads=[b_y])

                conv(0)
                for n in range(len(its)):
                    gates(n)
                    if n + 1 < len(its):
                        conv(n + 1)
                    scan(n)
                if stop_after == ("B3", l):
                    return True
                return False
            if _ph():
                return True
            def _ph():
                S.barrier()
                areset()
                Vt = alloc([34, 768], BF16)
                b_Vtp = [B() for _ in range(4)]
                vtk = vtok.rearrange("(kt p) c -> p kt c", p=128)
                b1_mark = off[0]
                sh.update(Vt=Vt, b_Vt=b_Vtp, b1_mark=b1_mark)
                KT, b_KT = alloc([2, T], BF16), B()
                S.dma("sp", lambda g: g.dma_start(out=KT, in_=qk[2:4].rearrange("c p t -> p c t")), writes=[b_KT])
                for q4 in range(0, 34, 9):
                    q5 = min(34, q4 + 9)
                    S.dma("sp", lambda g, q4=q4, q5=q5: g.dma_start(out=Vt[:, q4:q5, :], in_=vtk[:, q4:q5, :]), writes=[b_Vtp[q4 // 9]])
                QTR = Rot([(alloc([2, 512], BF16), B()) for _ in range(2)])
                ER = [Rot([(alloc([2, 512], BF16), B()) for _ in range(2)]) for _ in range(2)]
                evR = Rot([dict(o=alloc([4, 512], F32), l=alloc([4, 512], F32), bo=B(), bl=B()) for _ in range(2)])
                finR = Rot([dict(oo=alloc([512], F32), osq=alloc([512], BF16), rr=alloc([512], F32), y=alloc([512], BF16),
                                 b={n: B() for n in ("oo", "osq", "rr", "y")}) for _ in range(4)])
                sbR = Rot([0, 1, 2])
                pending = []

                def flush():
                    while pending:
                        pending.pop(0)()
                for gi, (t0, W, v) in enumerate(groups):
                    if gi == 0 and not ctx_out:
                        continue
                    nkt = 2 if gi == 0 else 34
                    QT, bQT = QTR.next()
                    S.dma("sp", lambda g, QT=QT, t0=t0, W=W: g.dma_start(out=QT[:, :, 0:W], in_=qk[0:2].rearrange("c p t -> p c t")[:, :, t0:t0 + W]), writes=[bQT])
                    for c in range(2):
                        Ecur = [None] * 2
                        Eprev = [None] * 2
                        for kt in range(nkt + 1):
                            if kt == min(22, nkt):
                                flush()
                            if kt < nkt:
                                for p in range(2):
                                    E, bE = ER[p].next()
                                    Ecur[p] = (E, bE)

                                    def qk2(g, p=p, c=c, kt=kt, QT=QT, W=W):
                                        for j in (2 * p, 2 * p + 1):
                                            ins = g.matmul(ps[:, j, 0:W], lhsT=KT[32 * j:32 * j + 32, c, kt * 128:(kt + 1) * 128], rhs=QT[32 * j:32 * j + 32, c, 0:W],
                                                           start=True, stop=True, tile_position=(32 * j, 0))
                                        return ins
                                    S.op("pe", qk2, reads=[b_KT, bQT], writes=[psb[2 * p], psb[2 * p + 1]])
                                    S.op("act", lambda g, p=p, E=E, W=W: g.activation(out=E[:, :, 0:W], in_=ps[:, 2 * p:2 * p + 2, 0:W], func=AF.Exp, scale=SCALE_A),
                                         reads=[psb[2 * p], psb[2 * p + 1]], writes=[bE])
                            if kt >= 1:
                                for p in range(2):
                                    E, bE = Eprev[p]
                                    h = 2 * c + p

                                    def pv2(g, p=p, E=E, h=h, kt=kt, W=W, nkt=nkt):
                                        for m in range(2):
                                            ins = g.matmul(ps[:, 4 + 2 * p + m, 0:W], lhsT=Vt[:, kt - 1, h * 128:(h + 1) * 128], rhs=E[:, m, 0:W], start=(kt == 1), stop=(kt == nkt))
                                        return ins
                                    S.op("pe", pv2, reads=[bE, b_Vtp[(kt - 1) // 9]], writes=[psb[4 + 2 * p], psb[5 + 2 * p]])
                            Eprev = list(Ecur)
                        ev = evR.next()
                        S.op("dve", lambda g, ev=ev, W=W: g.tensor_copy(out=ev["o"][0:64, :, 0:W], in_=ps[0:64, 4:8, 0:W]), reads=psb[4:8], writes=[ev["bo"]])
                        S.op("dve", lambda g, ev=ev, W=W: g.tensor_copy(out=ev["l"][0:64, :, 0:W], in_=ps[64:128, 4:8, 0:W]), reads=psb[4:8], writes=[ev["bl"]])
                        S.op("dve", lambda g, ev=ev, W=W: g.reciprocal(out=ev["l"][0:64, :, 0:W], in_=ev["l"][0:64, :, 0:W]), reads=[ev["bl"]], writes=[ev["bl"]])
                        S.op("pool", lambda g, ev=ev, W=W: g.tensor_tensor(out=ev["o"][0:64, :, 0:W], in0=ev["o"][0:64, :, 0:W], in1=ev["l"][0:64, :, 0:W], op=ALU.mult),
                             reads=[ev["bo"], ev["bl"]], writes=[ev["bo"]])
                        for hh in range(2):
                            h = 2 * c + hh
                            f = finR.next()
                            fb = f["b"]
                            S.op("dve", lambda g, f=f, ev=ev, hh=hh, W=W: g.scalar_tensor_tensor(
                                out=f["oo"][0:64, 0:W], in0=ev["o"][0:64, 2 * hh + 1, 0:W], scalar=sm2[0:64, 1:2], in1=ev["o"][0:64, 2 * hh, 0:W],
                                op0=ALU.mult, op1=ALU.add), reads=[ev["bo"], b_sm2], writes=[fb["oo"]])
                            S.op("pool", lambda g, f=f, W=W: g.tensor_tensor(out=f["osq"][0:64, 0:W], in0=f["oo"][0:64, 0:W], in1=f["oo"][0:64, 0:W], op=ALU.mult),
                                 reads=[fb["oo"]], writes=[fb["osq"]])

                            def late(f=f, fb=fb, h=h, t0=t0, W=W):
                                S.op("pe", lambda g: g.matmul(ps[0:64, 0, 0:W], lhsT=onesb[0:64, 0:64], rhs=f["osq"][0:64, 0:W], start=True, stop=True),
                                     reads=[fb["osq"], b_ones], writes=[psb[0]])
                                S.op("act", lambda g: g.activation(out=f["rr"][0:64, 0:W], in_=ps[0:64, 0, 0:W], func=AF.Ln, scale=1.0 / 64, bias=cst[0:64, 0:1]),
                                     reads=[psb[0], b_cst], writes=[fb["rr"]])
                                S.op("act", lambda g: g.activation(out=f["rr"][0:64, 0:W], in_=f["rr"][0:64, 0:W], func=AF.Exp, scale=-0.5),
                                     reads=[fb["rr"]], writes=[fb["rr"]])
                                S.op("dve", lambda g: g.scalar_tensor_tensor(out=f["y"][0:64, 0:W], in0=f["oo"][0:64, 0:W], scalar=sm2[0:64, 0:1], in1=f["rr"][0:64, 0:W],
                                                                             op0=ALU.mult, op1=ALU.mult), reads=[fb["oo"], fb["rr"], b_sm2], writes=[fb["y"]])
                                S.dma("pool", lambda g: g.dma_start(out=mix[h * 64:(h + 1) * 64, t0:t0 + W], in_=f["y"][0:64, 0:W]), reads=[fb["y"]])
                            pending.append(late)
                flush()
                if stop_after == ("B1", l):
                    return True
                return False
            if _ph():
                return True
            def _ph():
                Vt, b_Vt, b1_mark = sh["Vt"], sh["b_Vt"], sh["b1_mark"]
                S.barrier()
                WSZ = 2 * 22528 + 22528
                off[0] = ARENA - WSZ
                ws0 = dict(wg=alloc([8, 1408], BF16), wv=alloc([8, 1408], BF16), wd=alloc([11, D], BF16), b_wg=B(), b_wv=B(), b_wd=B())
                wuk = w_up[l].rearrange("(k p) n -> p k n", p=128)
                S.dma("pool", lambda g: g.dma_start(out=ws0["wg"], in_=wuk[:, :, 0:1408]), writes=[ws0["b_wg"]])
                S.dma("pool", lambda g: g.dma_start(out=ws0["wv"], in_=wuk[:, :, 2816:2816 + 1408]), writes=[ws0["b_wv"]])
                S.dma("pool", lambda g: g.dma_start(out=ws0["wd"], in_=w_down[l][0:1408, :].rearrange("(c p) n -> p c n", p=128)), writes=[ws0["b_wd"]])
                sh["ws0"] = ws0
                off[0] = b1_mark
                KB, b_KB = alloc([2, T], BF16), B()
                S.dma("sp", lambda g: g.dma_start(out=KB, in_=qk[7:9].rearrange("c p t -> p c t")), writes=[b_KB])
                QBs = [(alloc([3, 512], BF16), B()) for _ in range(2)]
                EBR = Rot([(alloc([640], BF16), B()) for _ in range(6)])
                fbR = Rot([dict(ls=alloc([512], F32), rl=alloc([512], F32), y=alloc([512], BF16), b={n: B() for n in ("ls", "rl", "y")}) for _ in range(3)])
                sbR = Rot([0, 2, 4])
                abR = Rot([6, 7])
                assert off[0] <= ARENA - WSZ, off[0]
                units = []
                ng = 0
                for gi, (t0, W, v) in enumerate(groups):
                    if gi == 0 and not ctx_out:
                        continue
                    QB, bQB = QBs[ng % 2]
                    ng += 1
                    for hq in range(6):
                        ab = abR.next()
                        nqb = W // 128
                        for qb in range(nqb):
                            tt = t0 // 128 + qb
                            if gi == 0:
                                keys = [(0, None), (1, None)]
                            else:
                                n = tt - 2
                                keys = [(0, None), (1, None)]
                                if n > 0:
                                    keys.append((tt - 1, 0))
                                keys.append((tt, None))
                                if n < 31:
                                    keys.append((tt + 1, 1))
                            units.append(dict(gi=gi, t0=t0, W=W, hq=hq, qb=qb, keys=keys, ab=ab, QB=QB, bQB=bQB, first=(hq == 0 and qb == 0), lastq=(qb == nqb - 1)))

                def front(u):
                    t0, W, hq, qb, keys, QB, bQB = u["t0"], u["W"], u["hq"], u["qb"], u["keys"], u["QB"], u["bQB"]
                    c, half, kv = hq // 2, hq % 2, hq // 3
                    p0 = half * 64
                    if u["first"]:
                        S.dma("sp", lambda g: g.dma_start(out=QB[:, :, 0:W], in_=qk[4:7].rearrange("c p t -> p c t")[:, :, t0:t0 + W]), writes=[bQB])
                    nk = len(keys)
                    b0 = sbR.next()

                    def qkmm(g):
                        for i, (kt, m) in enumerate(keys):
                            ins = g.matmul(ps[:, b0 + i // 4, (i % 4) * 128:(i % 4 + 1) * 128], lhsT=KB[p0:p0 + 64, kv, kt * 128:(kt + 1) * 128],
                                           rhs=QB[p0:p0 + 64, c, qb * 128:(qb + 1) * 128], start=True, stop=True, tile_position=(p0, 0))
                        return ins
                    S.op("pe", qkmm, reads=[b_KB, bQB], writes=[psb[b0], psb[b0 + 1]])
                    E, bE = EBR.next()
                    u["E"], u["bE"] = E, bE
                    S.op("act", lambda g: g.activation(out=E[:, 0:nk * 128], in_=psflat(b0, nk * 128), func=AF.Exp, scale=SCALE_B),
                         reads=[psb[b0], psb[b0 + 1]], writes=[bE])
                    for i, (kt, m) in enumerate(keys):
                        if m is not None:
                            S.op("dve", lambda g, i=i, m=m: g.tensor_tensor(out=E[:, i * 128:(i + 1) * 128], in0=E[:, i * 128:(i + 1) * 128], in1=msk[:, m, :], op=ALU.mult),
                                 reads=[bE, b_msk], writes=[bE])

                def back(u):
                    t0, W, hq, qb, keys, ab = u["t0"], u["W"], u["hq"], u["qb"], u["keys"], u["ab"]
                    kv = hq // 3
                    E, bE = u["E"], u["bE"]

                    def pvmm(g):
                        for i, (kt, m) in enumerate(keys):
                            ins = g.matmul(ps[:, ab, qb * 128:(qb + 1) * 128], lhsT=Vt[:, kt, (4 + kv) * 128:(5 + kv) * 128], rhs=E[:, i * 128:(i + 1) * 128],
                                           start=(i == 0), stop=(i == len(keys) - 1))
                        return ins
                    S.op("pe", pvmm, reads=[bE] + b_Vt, writes=[psb[ab]])
                    if u["lastq"]:
                        f = fbR.next()
                        fb = f["b"]
                        S.op("act", lambda g: g.activation(out=f["rl"][64:128, 0:W], in_=ps[64:128, ab, 0:W], func=AF.Ln, bias=sm2[64:128, 2 + hq:3 + hq]),
                             reads=[psb[ab], b_sm2], writes=[fb["rl"]])
                        S.op("act", lambda g: g.activation(out=f["rl"][64:128, 0:W], in_=f["rl"][64:128, 0:W], func=AF.Exp, scale=-1.0), reads=[fb["rl"]], writes=[fb["rl"]])
                        S.op("dve", lambda g: g.tensor_tensor(out=f["y"][0:64, 0:W], in0=ps[0:64, ab, 0:W], in1=f["rl"][64:128, 0:W], op=ALU.mult),
                             reads=[psb[ab], fb["rl"]], writes=[fb["y"]])
                        S.dma("pool", lambda g: g.dma_start(out=mix[256 + hq * 64:256 + (hq + 1) * 64, t0:t0 + W], in_=f["y"][0:64, 0:W]), reads=[fb["y"]])

                LAG = 3
                for idx in range(len(units) + LAG):
                    if idx < len(units):
                        front(units[idx])
                    if idx >= LAG:
                        back(units[idx - LAG])
                if stop_after == ("B2", l):
                    return True
                return False
            if _ph():
                return True
            def _ph():
                S.barrier()
                areset()
                WSZ = 2 * 22528 + 22528
                wout, b_wout = alloc([8, D], BF16), B()
                S.dma("pool", lambda g: g.dma_start(out=wout, in_=w_out[l].rearrange("(k p) n -> p k n", p=128)), writes=[b_wout])
                mxs = [(alloc([8, 512], BF16), B()) for _ in range(2)]
                xgs = [(alloc([8, 512], F32), B()) for _ in range(2)]
                h2s_ = [(alloc([8, 514], BF16), B()) for _ in range(2)]
                sq, b_sq = alloc([8, 512], BF16), B()
                tmp8, b_tmp8 = alloc([8, 512], F32), B()
                rstd, b_rstd = alloc([512], F32), B()
                assert off[0] <= ARENA - WSZ
                for (h2, bh2) in h2s_:
                    S.op("pool", lambda g, h2=h2: g.memset(h2, 0.0), writes=[bh2])
                mixk = mix.rearrange("(k p) t -> p k t", p=128)
                h2sk = h2s.rearrange("(k p) t -> p k t", p=128)
                pbR = Rot([1, 2, 3, 4, 5, 6])
                from concourse.ap import AP as _AP

                def bc3(ap2, nb):
                    (pst, pn), (fs, fn_) = ap2.ap
                    return _AP(ap2.tensor, ap2.offset, [[pst, pn], [0, nb], [fs, fn_]])
                glist = [(gi, g_) for gi, g_ in enumerate(groups) if not (gi == 0 and not ctx_out)]

                def front(n):
                    gi, (t0, W, v) = glist[n]
                    mx, bmx = mxs[n % 2]
                    xg, bxg = xgs[n % 2]
                    for kh in range(2):
                        S.dma("sp", lambda g, kh=kh: g.dma_start(out=mx[:, 4 * kh:4 * kh + 4, 0:W], in_=mixk[:, 4 * kh:4 * kh + 4, t0:t0 + W]), writes=[bmx], acc=(kh > 0))
                    S.dma("sp", lambda g: g.dma_start(out=xg[:, :, 0:W], in_=xsrc_k[:, :, t0:t0 + W]), writes=[bxg])
                    for j in range(8):
                        pb = pbR.next()

                        def omm(g, j=j, pb=pb):
                            for c in range(8):
                                ins = g.matmul(ps[:, pb, 0:W], lhsT=wout[:, c, j * 128:(j + 1) * 128], rhs=mx[:, c, 0:W], start=(c == 0), stop=(c == 7))
                            return ins
                        S.op("pe", omm, reads=[b_wout, bmx], writes=[psb[pb]])
                        S.op("dve", lambda g, j=j, pb=pb: g.scalar_tensor_tensor(
                            out=xg[:, j, 0:W], in0=ps[:, pb, 0:W], scalar=modT[:, 16 + j, v:v + 1], in1=xg[:, j, 0:W], op0=ALU.mult, op1=ALU.add),
                            reads=[psb[pb], bxg, b_modT], writes=[bxg])
                    S.dma("pool", lambda g: g.dma_start(out=xTs_k[:, :, t0:t0 + W], in_=xg[:, :, 0:W]), reads=[bxg])

                def back(n):
                    gi, (t0, W, v) = glist[n]
                    xg, bxg = xgs[n % 2]
                    h2, bh2 = h2s_[n % 2]
                    S.op("act", lambda g: g.activation(out=sq[:, :, 0:W], in_=xg[:, :, 0:W], func=AF.Square), reads=[bxg], writes=[b_sq])

                    def ssmm2(g):
                        for k in range(8):
                            ins = g.matmul(ps[:, 0, 0:W], lhsT=onesb[:], rhs=sq[:, k, 0:W], start=(k == 0), stop=(k == 7))
                        return ins
                    S.op("pe", ssmm2, reads=[b_sq, b_ones], writes=[psb[0]])
                    S.op("act", lambda g: g.activation(out=rstd[:, 0:W], in_=ps[:, 0, 0:W], func=AF.Ln, scale=1.0 / D, bias=cst[:, 0:1]),
                         reads=[psb[0], b_cst], writes=[b_rstd])
                    S.op("act", lambda g: g.activation(out=rstd[:, 0:W], in_=rstd[:, 0:W], func=AF.Exp, scale=-0.5), reads=[b_rstd], writes=[b_rstd])
                    S.op("dve", lambda g: g.tensor_tensor(out=tmp8[:, :, 0:W], in0=xg[:, :, 0:W], in1=bc3(rstd[:, 0:W], 8), op=ALU.mult),
                         reads=[bxg, b_rstd], writes=[b_tmp8])
                    for k in range(8):
                        S.op("act", lambda g, k=k: g.activation(
                            out=h2[:, k, 1:1 + W], in_=tmp8[:, k, 0:W], func=AF.Identity, scale=A2[:, k, v:v + 1], bias=modT[:, 24 + k, v:v + 1]),
                            reads=[b_tmp8, b_modT, b_A], writes=[bh2])
                    if gi == 0:
                        S.dma("pool", lambda g: g.dma_start(out=h2sk[:, :, 0:258], in_=h2[:, :, 0:258]), reads=[bh2])
                    elif gi == 1:
                        S.dma("pool", lambda g: g.dma_start(out=h2sk[:, :, 258:258 + 513], in_=h2[:, :, 0:513]), reads=[bh2])
                    elif gi == 8:
                        S.dma("pool", lambda g: g.dma_start(out=h2sk[:, :, t0 + 3:t0 + 3 + 513], in_=h2[:, :, 1:514]), reads=[bh2])
                    else:
                        S.dma("pool", lambda g: g.dma_start(out=h2sk[:, :, t0 + 3:t0 + 3 + 512], in_=h2[:, :, 1:513]), reads=[bh2])

                for n in range(len(glist) + 1):
                    if n < len(glist):
                        front(n)
                    if n >= 1:
                        back(n - 1)
                if stop_after == ("C1", l):
                    return True
                return False
            if _ph():
                return True
            def _ph():
                h2sk = h2s.rearrange("(k p) t -> p k t", p=128)
                wins = []
                if ctx_out:
                    wins.append((0, 258, 0, 256, 1))
                for i in range(9):
                    wo = min(510, NLAT - 510 * i)
                    wins.append((258 + 510 * i, wo + 2, NCTX + 510 * i, wo, 0))
                S.barrier()
                areset()
                wsets = []
                wuk = w_up[l].rearrange("(k p) n -> p k n", p=128)
                wsets.append(sh["ws0"])
                for hf_ in range(1, 2):
                    ws = dict(wg=alloc([8, 1408], BF16), wv=alloc([8, 1408], BF16), wd=alloc([11, D], BF16), b_wg=B(), b_wv=B(), b_wd=B())
                    wsets.append(ws)
                    S.dma("pool", lambda g, hf_=hf_, ws=ws: g.dma_start(out=ws["wg"], in_=wuk[:, :, hf_ * 1408:(hf_ + 1) * 1408]), writes=[ws["b_wg"]])
                    S.dma("pool", lambda g, hf_=hf_, ws=ws: g.dma_start(out=ws["wv"], in_=wuk[:, :, 2816 + hf_ * 1408:2816 + (hf_ + 1) * 1408]), writes=[ws["b_wv"]])
                    S.dma("pool", lambda g, hf_=hf_, ws=ws: g.dma_start(out=ws["wd"], in_=w_down[l][hf_ * 1408:(hf_ + 1) * 1408, :].rearrange("(c p) n -> p c n", p=128)), writes=[ws["b_wd"]])
                hwR = Rot([(alloc([8, 512], BF16), B()) for _ in range(2)])
                xgR = Rot([(alloc([8, 512], F32), B()) for _ in range(1)])
                act, b_act = alloc([11, 512], BF16), B()
                cvR = Rot([(alloc([512], F32), B()) for _ in range(3)])
                sgR = Rot([(alloc([512], F32), B()) for _ in range(3)])
                gvR = Rot([(0, 1), (2, 3), (4, 5)])
                dbR = Rot([6, 7])
                bxw = [B() for _ in wins]
                assert off[0] <= ARENA - (2 * 22528 + 22528), off[0]
                acts = [(act, b_act), (alloc([11, 512], BF16), B())]
                hws = hwR.items
                outk = out.rearrange("(k p) t -> p k t", p=128)
                seq = [(hf, wi) for hf in range(2) for wi in range(len(wins))]
                if stop_after == ("C20", l):
                    seq = [(0, wi) for wi in range(len(wins))]

                def load_h(n):
                    hf, wi = seq[n]
                    cs, Wn, tk0, Wo, v = wins[wi]
                    hw, bhw = hws[n % 2]
                    S.dma("sp", lambda g: g.dma_start(out=hw[:, :, 0:Wn], in_=h2sk[:, :, cs:cs + Wn]), writes=[bhw])

                def up_chunk(n, ci):
                    hf, wi = seq[n]
                    cs, Wn, tk0, Wo, v = wins[wi]
                    ws = wsets[hf]
                    wg, wv = ws["wg"], ws["wv"]
                    hw, bhw = hws[n % 2]
                    act_, b_act_ = acts[n % 2]
                    c = hf * 11 + ci
                    gb, vb = gvR.next()

                    def umm(g):
                        for k in range(8):
                            g.matmul(ps[:, gb, 0:Wn], lhsT=wg[:, k, ci * 128:(ci + 1) * 128], rhs=hw[:, k, 0:Wn], start=(k == 0), stop=(k == 7))
                        for k in range(8):
                            ins = g.matmul(ps[:, vb, 0:Wo], lhsT=wv[:, k, ci * 128:(ci + 1) * 128], rhs=hw[:, k, 1:1 + Wo], start=(k == 0), stop=(k == 7))
                        return ins
                    S.op("pe", umm, reads=[ws["b_wg"], ws["b_wv"], bhw], writes=[psb[gb], psb[vb]])
                    cv, bcv = cvR.next()
                    sg, bsg = sgR.next()
                    w0 = smt[:, 75 + c:76 + c]
                    w1 = smt[:, 75 + 22 + c:76 + 22 + c]
                    w2 = smt[:, 75 + 44 + c:76 + 44 + c]
                    bb_ = smt[:, 141 + c:142 + c]
                    S.op("act", lambda g: g.activation(out=cv[:, 0:Wo], in_=ps[:, gb, 1:1 + Wo], func=AF.Identity, scale=w1, bias=bb_), reads=[psb[gb], b_smt], writes=[bcv])
                    S.op("dve", lambda g: g.scalar_tensor_tensor(out=cv[:, 0:Wo], in0=ps[:, gb, 0:Wo], scalar=w0, in1=cv[:, 0:Wo], op0=ALU.mult, op1=ALU.add),
                         reads=[psb[gb], bcv, b_smt], writes=[bcv])
                    S.op("dve", lambda g: g.scalar_tensor_tensor(out=cv[:, 0:Wo], in0=ps[:, gb, 2:2 + Wo], scalar=w2, in1=cv[:, 0:Wo], op0=ALU.mult, op1=ALU.add),
                         reads=[psb[gb], bcv, b_smt], writes=[bcv])
                    S.op("act", lambda g: g.activation(out=sg[:, 0:Wo], in_=cv[:, 0:Wo], func=AF.Silu), reads=[bcv], writes=[bsg])
                    S.op("dve", lambda g: g.tensor_tensor(out=act_[:, ci, 0:Wo], in0=ps[:, vb, 0:Wo], in1=sg[:, 0:Wo], op=ALU.mult), reads=[psb[vb], bsg], writes=[b_act_])

                def down(n):
                    hf, wi = seq[n]
                    cs, Wn, tk0, Wo, v = wins[wi]
                    wd, b_wd = wsets[hf]["wd"], wsets[hf]["b_wd"]
                    act_, b_act_ = acts[n % 2]
                    xg, bxg = xgR.items[0]
                    for j in range(8):
                        db = dbR.next()

                        def dmm(g, j=j, db=db):
                            for ci in range(11):
                                ins = g.matmul(ps[:, db, 0:Wo], lhsT=wd[:, ci, j * 128:(j + 1) * 128], rhs=act_[:, ci, 0:Wo], start=(ci == 0), stop=(ci == 10))
                            return ins
                        S.op("pe", dmm, reads=[b_wd, b_act_], writes=[psb[db]])
                        S.op("dve", lambda g, j=j, db=db: g.scalar_tensor_tensor(
                            out=xg[:, j, 0:Wo], in0=ps[:, db, 0:Wo], scalar=modT[:, 40 + j, v:v + 1], in1=xg[:, j, 0:Wo], op0=ALU.mult, op1=ALU.add),
                            reads=[psb[db], bxg, b_modT], writes=[bxg])
                    if last and hf == 1:
                        S.dma("pool", lambda g: g.dma_start(out=outk[:, :, tk0 - NCTX:tk0 - NCTX + Wo], in_=xg[:, :, 0:Wo]), reads=[bxg])
                    else:
                        S.dma("pool", lambda g: g.dma_start(out=xTs_k[:, :, tk0:tk0 + Wo], in_=xg[:, :, 0:Wo]), reads=[bxg], writes=[bxw[wi]])

                def load_x(n):
                    hf, wi = seq[n]
                    cs, Wn, tk0, Wo, v = wins[wi]
                    xg, bxg = xgR.items[0]
                    S.dma("sp", lambda g: g.dma_start(out=xg[:, :, 0:Wo], in_=xTs_k[:, :, tk0:tk0 + Wo]), reads=[bxw[wi]], writes=[bxg])

                NPRE = 2
                load_h(0)
                load_x(0)
                for ci in range(11):
                    up_chunk(0, ci)
                for n in range(len(seq)):
                    if n + 1 < len(seq):
                        load_h(n + 1)
                        for ci in range(NPRE):
                            up_chunk(n + 1, ci)
                    down(n)
                    if n + 1 < len(seq):
                        load_x(n + 1)
                        for ci in range(NPRE, 11):
                            up_chunk(n + 1, ci)
                if stop_after in (("C20", l), ("C21", l)):
                    return True
                if stop_after is not None and stop_after[1] == l:
                    return True
                return False
            if _ph():
                return True
            return False

        for l in range(nlayers):
            if run_layer(l):
                break

        S.barrier()
        S.emit(nc, sems, dsems)
    return nc


def _rope_tables():
    pos = np.arange(NLAT)
    row = (pos // 64).astype(np.float32)
    col = (pos % 64).astype(np.float32)
    tabs = []
    for hd in (32, 64):
        quarter = hd // 4
        half = hd // 2
        inv_freq = (np.float32(10000.0) ** (-np.arange(quarter, dtype=np.float32) / np.float32(quarter))).astype(np.float32)
        ang = np.concatenate([row[:, None] * inv_freq[None, :], col[:, None] * inv_freq[None, :]], axis=-1).astype(np.float32)
        cos, sin = np.cos(ang).astype(np.float32), np.sin(ang).astype(np.float32)
        p = np.arange(128)
        d = p % hd
        j = d % half
        sign = np.where(d < half, -1.0, 1.0).astype(np.float32)
        C = np.ones((128, T), np.float32)
        Sg = np.zeros((128, T), np.float32)
        C[:, NCTX:] = cos[:, j].T
        Sg[:, NCTX:] = sin[:, j].T * sign[:, None]
        tabs += [C, Sg]
    return np.stack(tabs, 0)


def _const_mats():
    p = np.arange(128)
    m = np.zeros((7, 128, 128), np.float32)
    m[0] = (p[:, None] // 32 == p[None, :] // 32)
    m[1] = (p[:, None] // 64 == p[None, :] // 64)
    for idx, hd in ((2, 32), (3, 64)):
        perm = (p // hd) * hd + ((p % hd) + hd // 2) % hd
        m[idx] = (p[:, None] == perm[None, :])
    m[4] = np.eye(128, dtype=np.float32)
    m[5] = (p[:, None] >= p[None, :])
    m[6] = (p[:, None] <= p[None, :])
    return m


def _prep_shared(inp):
    f = lambda a: np.ascontiguousarray(np.asarray(a, dtype=np.float32))
    w_in = f(inp["w_in"])
    cols = np.concatenate([np.arange(0, 256), np.arange(256, 512), np.arange(768, 1152),
                           np.arange(1152, 1216), np.arange(1152, 1216), np.arange(1216, 1280), np.arange(1216, 1280),
                           np.arange(1408, 1792), np.arange(1792, 2176), np.arange(512, 768), np.arange(1280, 1408)])
    assert cols.size == 2304
    w_in_ext = np.ascontiguousarray(w_in[:, :, cols])
    b_mod2 = np.ascontiguousarray(np.repeat(f(inp["b_mod"])[:, None, :], 2, axis=1))
    wa, wx = f(inp["lru_wa"]), f(inp["lru_wx"])
    bd = np.zeros((2, 2, 2, 3, 128, 128), np.float32)
    for cc in range(3):
        for hb in range(2):
            bd[:, :, 0, cc, hb * 64:(hb + 1) * 64, hb * 64:(hb + 1) * 64] = wa[:, :, 2 * cc + hb]
            bd[:, :, 1, cc, hb * 64:(hb + 1) * 64, hb * 64:(hb + 1) * 64] = wx[:, :, 2 * cc + hb]
    bd = np.ascontiguousarray(bd.reshape(2, 12, 128, 128))
    p = np.arange(128)
    sm = np.zeros((2, 128, NS), np.float32)
    sm[:, :, 0:8] = f(inp["norm1_gain"]).reshape(2, 8, 128).transpose(0, 2, 1)
    sm[:, :, 8:16] = f(inp["norm2_gain"]).reshape(2, 8, 128).transpose(0, 2, 1)
    sm[:, :, 16] = f(inp["da_q_gain"])[:, p % 32]
    sm[:, :, 17] = f(inp["da_k_gain"])[:, p % 32]
    sm[:, :, 18] = f(inp["sw_q_gain"])[:, p % 64]
    sm[:, :, 19] = f(inp["sw_k_gain"])[:, p % 64]
    sm[:, :, 20] = f(inp["da_sub_gain"])[:, p % 64]
    sm[:, :, 21:27] = f(inp["sw_sink"])[:, None, :]
    cw = f(inp["lru_conv_w"]).reshape(2, 2, 4, 3, 128)
    sm[:, :, 27:51] = cw.transpose(0, 4, 1, 2, 3).reshape(2, 128, 24)
    for base, name in ((51, "lru_conv_b"), (57, "lru_ba"), (63, "lru_bx"), (69, "lru_lambda")):
        sm[:, :, base:base + 6] = f(inp[name]).reshape(2, 2, 3, 128).transpose(0, 3, 1, 2).reshape(2, 128, 6)
    fw = f(inp["ffn_conv_w"]).reshape(2, 3, 22, 128)
    sm[:, :, 75:141] = fw.transpose(0, 3, 1, 2).reshape(2, 128, 66)
    sm[:, :, 141:163] = f(inp["ffn_conv_b"]).reshape(2, 22, 128).transpose(0, 2, 1)
    lam = np.stack([f(inp["da_lam_q1"]), f(inp["da_lam_k1"]), f(inp["da_lam_q2"]), f(inp["da_lam_k2"])], axis=1)
    lamv = np.ascontiguousarray(np.broadcast_to(lam[:, None], (2, 128, 4, 32)))
    return {
        "w_mod": f(inp["w_mod"]), "b_mod2": b_mod2, "w_in": w_in_ext, "w_out": f(inp["w_out"]), "w_up": f(inp["w_up"]),
        "w_down": f(inp["w_down"]), "lru_bd": bd, "smalls": np.ascontiguousarray(sm), "lamv": lamv,
        "cmats": _const_mats(), "rope": _rope_tables(),
    }


def _prep_core(inp, b):
    x = np.asarray(inp["x"], dtype=np.float32)
    ctx = np.asarray(inp["ctx"], dtype=np.float32)
    c = np.asarray(inp["c"], dtype=np.float32)
    c_ctx = np.asarray(inp["c_ctx"], dtype=np.float32)
    xT = np.ascontiguousarray(np.concatenate([ctx[b].T, x[b].T], axis=1))
    cT = np.ascontiguousarray(np.stack([c[b].reshape(8, 128).T, c_ctx.reshape(8, 128).T], axis=-1))
    return {"xT": xT, "cT": cT}


_CACHE = {}


def kernel(**inputs):
    if "nc" not in _CACHE:
        _CACHE["nc"] = build_program()
    nc = _CACHE["nc"]
    shared = _prep_shared(inputs)
    n = 8
    in_maps = []
    for b in range(n):
        m = dict(shared)
        m.update(_prep_core(inputs, b))
        in_maps.append(m)
    res = run_bass_kernel_spmd(nc, in_maps, core_ids=list(range(n)))
    outs = [np.asarray(r["out"]).T for r in res.results]
    return np.ascontiguousarray(np.stack(outs, axis=0).astype(np.float32))
```
